# Optimizing a Trainium2 kernel written in Bass

```python
import math
import jax, jax.numpy as jnp
from jax import lax
import numpy as np

D_MODEL = 1024
BATCH = 8
SEQ = 4096
DEPTH = 2

PLE_DIM = 256
EPS = 1e-6
A_HEADS = 8
A_HEAD_DIM = 64
A_WIDTH = A_HEADS * A_HEAD_DIM
MOBA_BLOCK = 256
MOBA_TOPK = 3
MOBA_QCHUNK = 32
ROPE_THETA = 500000.0
ROPE_DIM = A_HEAD_DIM // 4
B_HEADS = 4
B_QK_DIM = 128
B_V_DIM = 128
B_WIDTH = B_HEADS * B_V_DIM
RET_CHUNK = 128
RET_THETA = 10000.0
C_HEADS = 8
C_HEAD_DIM = 64
C_WIDTH = C_HEADS * C_HEAD_DIM
C_DECAY_RANK = 64
C_ICLR_RANK = 64
C_GATE_RANK = 128
C_VRES_RANK = 32
C_GN_EPS = C_HEAD_DIM * 1e-5
N_BRANCH = 3
MIX_WIDTH = 512
D_FF = -(-8 * D_MODEL // (3 * 256)) * 256
A_COLS = 3 * A_WIDTH
B_COLS = 2 * B_HEADS * B_QK_DIM + 2 * B_WIDTH
C_COLS = 3 * C_WIDTH + C_DECAY_RANK + C_ICLR_RANK + C_GATE_RANK
G_COLS = N_BRANCH * D_MODEL
N_IN = A_COLS + B_COLS + C_COLS + G_COLS

kernel_name = "hybrid_moba_retention_rwkv7_gated_block"


def split_cols(z, sizes):
    return jnp.split(z, np.cumsum(sizes)[:-1].tolist(), axis=-1)


def rms_norm(x, g):
    xf = x.astype(jnp.float32)
    y = xf * lax.rsqrt(jnp.mean(xf * xf, -1, keepdims=True) + EPS)
    return (y * g.astype(jnp.float32)).astype(x.dtype)


def head_norm(y, eps):
    yf = y.astype(jnp.float32)
    mu = jnp.mean(yf, -1, keepdims=True)
    var = jnp.mean(jnp.square(yf - mu), -1, keepdims=True)
    return (yf - mu) * lax.rsqrt(var + eps)


def lerp_shift(z, mu):
    z_prev = jnp.pad(z, ((0, 0), (1, 0), (0, 0)))[:, :-1]
    return z + (z_prev - z) * mu


def partial_rope(x, cos, sin):
    xr, xp = x[..., :ROPE_DIM], x[..., ROPE_DIM:]
    x1, x2 = xr[..., :ROPE_DIM // 2], xr[..., ROPE_DIM // 2:]
    c, s = cos.astype(x.dtype), sin.astype(x.dtype)
    return jnp.concatenate([x1 * c - x2 * s, x1 * s + x2 * c, xp], -1)


def pair_rotate(x, cos, sin):
    xe, xo = x[..., 0::2], x[..., 1::2]
    c, s = cos.astype(x.dtype), sin.astype(x.dtype)
    return jnp.stack([xe * c - xo * s, xe * s + xo * c], -1).reshape(x.shape)


def moba_attention(q, k, v, cos, sin):
    B, S, H, Dh = q.shape
    q = partial_rope(q.transpose(0, 2, 1, 3), cos, sin)
    k = partial_rope(k.transpose(0, 2, 1, 3), cos, sin)
    v = v.transpose(0, 2, 1, 3)
    nb = -(-S // MOBA_BLOCK)
    pad = nb * MOBA_BLOCK - S
    kb = jnp.pad(k, ((0, 0), (0, 0), (0, pad), (0, 0))).reshape(B, H, nb, MOBA_BLOCK, Dh)
    vb = jnp.pad(v, ((0, 0), (0, 0), (0, pad), (0, 0))).reshape(B, H, nb, MOBA_BLOCK, Dh)
    k_mean = jnp.mean(kb.astype(jnp.float32), axis=3)
    gate = jnp.einsum('bhsd,bhnd->bhsn', q.astype(jnp.float32), k_mean)
    q_blk = jnp.arange(S) // MOBA_BLOCK
    past = jnp.arange(nb)[None, :] < q_blk[:, None]
    gate = jnp.where(past, gate, -jnp.inf)
    topk = min(MOBA_TOPK, nb)
    sel_score, sel_idx = lax.top_k(gate, topk)
    sel_valid = jnp.isfinite(sel_score)

    nqc = S // MOBA_QCHUNK

    def to_chunks(t):
        t = t.reshape((B, H, nqc, MOBA_QCHUNK) + t.shape[3:])
        return jnp.moveaxis(t, 2, 0)

    bi = jnp.arange(B)[:, None, None, None]
    hi = jnp.arange(H)[None, :, None, None]
    scale = Dh ** -0.5
    n_sel = topk * MOBA_BLOCK

    def chunk(args):
        c, qc, idx, valid = args
        kg = kb[bi, hi, idx]
        vg = vb[bi, hi, idx]
        blk = (c * MOBA_QCHUNK) // MOBA_BLOCK
        k_own = lax.dynamic_index_in_dim(kb, blk, axis=2, keepdims=False)
        v_own = lax.dynamic_index_in_dim(vb, blk, axis=2, keepdims=False)
        s_sel = jnp.einsum('bhqd,bhqnkd->bhqnk', qc, kg).astype(jnp.float32) * scale
        s_sel = jnp.where(valid[..., None], s_sel, -jnp.inf)
        s_own = jnp.einsum('bhqd,bhkd->bhqk', qc, k_own).astype(jnp.float32) * scale
        q_pos = c * MOBA_QCHUNK + jnp.arange(MOBA_QCHUNK)
        k_pos = blk * MOBA_BLOCK + jnp.arange(MOBA_BLOCK)
        s_own = jnp.where(k_pos[None, :] <= q_pos[:, None], s_own, -jnp.inf)
        s = jnp.concatenate([s_sel.reshape(B, H, MOBA_QCHUNK, n_sel), s_own], -1)
        pr = jax.nn.softmax(s, axis=-1).astype(v.dtype)
        p_sel = pr[..., :n_sel].reshape(B, H, MOBA_QCHUNK, topk, MOBA_BLOCK)
        p_own = pr[..., n_sel:]
        return (jnp.einsum('bhqnk,bhqnkd->bhqd', p_sel, vg)
                + jnp.einsum('bhqk,bhkd->bhqd', p_own, v_own))

    out = lax.map(chunk, (jnp.arange(nqc), to_chunks(q), to_chunks(sel_idx), to_chunks(sel_valid)))
    out = jnp.moveaxis(out, 0, 2).reshape(B, H, S, Dh)
    return out.transpose(0, 2, 1, 3).reshape(B, S, H * Dh)


def retention(q, k, v, g, cos, sin):
    B, S, H, dk = q.shape
    dv = v.shape[-1]
    q = pair_rotate(q.transpose(0, 2, 1, 3), cos, sin)
    k = pair_rotate(k.transpose(0, 2, 1, 3), cos, sin) * (dk ** -0.5)
    v = v.transpose(0, 2, 1, 3)
    log_gamma = jnp.log(1.0 - 2.0 ** (-5.0 - jnp.arange(H, dtype=jnp.float32)))
    C = RET_CHUNK
    nc = S // C
    qc = q.reshape(B, H, nc, C, dk)
    kc = k.reshape(B, H, nc, C, dk)
    vc = v.reshape(B, H, nc, C, dv)
    i = jnp.arange(C, dtype=jnp.float32)
    diff = i[:, None] - i[None, :]
    decay_mask = jnp.where(diff >= 0, jnp.exp(log_gamma[:, None, None] * jnp.maximum(diff, 0.0)), 0.0)
    qk = jnp.einsum('bhnid,bhnjd->bhnij', qc, kc) * decay_mask[None, :, None]
    y_inner = jnp.einsum('bhnij,bhnje->bhnie', qk, vc)
    zeta = jnp.exp(log_gamma[:, None] * (C - 1 - i))
    kv = jnp.einsum('bhnjd,bhnje->bhnde', kc * zeta[None, :, None, :, None], vc).astype(jnp.float32)
    chunk_decay = jnp.exp(log_gamma * C)[None, :, None, None]

    def step(R, kv_n):
        return R * chunk_decay + kv_n, R

    _, R_prev = lax.scan(step, jnp.zeros((B, H, dk, dv), jnp.float32), jnp.moveaxis(kv, 2, 0))
    R_prev = jnp.moveaxis(R_prev, 0, 2)
    xi = jnp.exp(log_gamma[:, None] * (i + 1.0))
    y_cross = jnp.einsum('bhnid,bhnde->bhnie', qc.astype(jnp.float32), R_prev) * xi[None, :, None, :, None]
    y = (y_inner + y_cross).reshape(B, H, S, dv)
    y = head_norm(y, EPS).transpose(0, 2, 1, 3).reshape(B, S, H * dv)
    return (jax.nn.silu(g) * y).astype(g.dtype)


def rwkv7_scan(r, w, k, v, a, b):
    B, S, H, N = r.shape

    def step(state, inp):
        r_t, w_t, k_t, v_t, a_t, b_t = inp
        sa = jnp.einsum('bhvk,bhk->bhv', state, a_t)
        state = (state * w_t[:, :, None, :] + sa[..., None] * b_t[:, :, None, :]
                 + v_t[..., None] * k_t[:, :, None, :])
        return state, jnp.einsum('bhvk,bhk->bhv', state, r_t)

    xs = (jnp.moveaxis(r.astype(jnp.float32), 1, 0), jnp.moveaxis(w.astype(jnp.float32), 1, 0),
          jnp.moveaxis(k.astype(jnp.float32), 1, 0), jnp.moveaxis(v.astype(jnp.float32), 1, 0),
          jnp.moveaxis(a.astype(jnp.float32), 1, 0), jnp.moveaxis(b.astype(jnp.float32), 1, 0))
    _, y = lax.scan(step, jnp.zeros((B, H, N, N), jnp.float32), xs)
    return jnp.moveaxis(y, 0, 1)


def rwkv7_time_mix(r, k, v, w_lr, a_lr, g_lr, w0, w2, a0, a2, g2, k_k, k_a, r_k, ln_g, ln_b):
    B, S, _ = r.shape

    def heads(t):
        return t.reshape(B, S, C_HEADS, C_HEAD_DIM)

    w_log = -jax.nn.softplus(-(w0 + jnp.tanh(w_lr) @ w2)) - 0.5
    decay = jnp.exp(-jnp.exp(w_log.astype(jnp.float32)))
    a = jax.nn.sigmoid(a0 + a_lr @ a2)
    g = jax.nn.sigmoid(g_lr) @ g2
    kk = heads(k * k_k).astype(jnp.float32)
    kk = kk / jnp.maximum(jnp.sqrt(jnp.sum(kk * kk, -1, keepdims=True)), 1e-12)
    k = k * (1.0 + (a - 1.0) * k_a)
    y = rwkv7_scan(heads(r), heads(decay), heads(k), heads(v), -kk, kk * heads(a))
    y = head_norm(y, C_GN_EPS).reshape(B, S, C_WIDTH) * ln_g + ln_b
    bonus = jnp.sum(heads(r) * heads(k) * r_k, -1, keepdims=True) * heads(v)
    y = y + bonus.reshape(B, S, C_WIDTH)
    return (y * g).astype(r.dtype)


def setup_inputs(seed: int = 0) -> dict:
    key = jax.random.key(seed)
    kit = iter(list(jax.random.split(key, 32)))
    f32 = jnp.float32

    def nrm(shape, scale):
        return jax.random.normal(next(kit), shape, f32) * scale

    def unif(shape, lo, hi):
        return jax.random.uniform(next(kit), shape, f32, lo, hi)

    L, D = DEPTH, D_MODEL
    Lv = DEPTH - 1
    return {
        "x": nrm((BATCH, SEQ, D), 1.0),
        "p": nrm((DEPTH, BATCH, SEQ, PLE_DIM), 1.0),
        "norm_mix_g": 1.0 + nrm((L, D), 0.05),
        "w_in": nrm((L, D, N_IN), D ** -0.5),
        "c_mu": unif((L, C_COLS), 0.0, 1.0),
        "c_w0": unif((L, C_WIDTH), -6.0, -1.0),
        "c_w2": nrm((L, C_DECAY_RANK, C_WIDTH), 0.1),
        "c_a0": nrm((L, C_WIDTH), 0.1),
        "c_a2": nrm((L, C_ICLR_RANK, C_WIDTH), 0.1),
        "c_g2": nrm((L, C_GATE_RANK, C_WIDTH), C_GATE_RANK ** -0.5),
        "c_k_k": 0.85 + nrm((L, C_WIDTH), 0.05),
        "c_k_a": 1.0 + nrm((L, C_WIDTH), 0.05),
        "c_r_k": nrm((L, C_HEADS, C_HEAD_DIM), 0.1),
        "c_ln_g": 1.0 + nrm((L, C_WIDTH), 0.05),
        "c_ln_b": nrm((L, C_WIDTH), 0.01),
        "c_vres_down": nrm((Lv, D, C_VRES_RANK), D ** -0.5),
        "c_vres_mu": unif((Lv, C_VRES_RANK), 0.0, 1.0),
        "c_v0": nrm((Lv, C_WIDTH), 0.1),
        "c_v2": nrm((Lv, C_VRES_RANK, C_WIDTH), 0.1),
        "w_branch": nrm((L, N_BRANCH, MIX_WIDTH, D), MIX_WIDTH ** -0.5),
        "w_out": nrm((L, D, D), D ** -0.5),
        "norm_ffn_g": 1.0 + nrm((L, D), 0.05),
        "w_gate_up": nrm((L, D, 2 * D_FF), D ** -0.5),
        "w_down": nrm((L, D_FF, D), D_FF ** -0.5),
        "norm_ple_g": 1.0 + nrm((L, D), 0.05),
        "w_ple_gate": nrm((L, D, D), D ** -0.5),
        "w_ple_proj": nrm((L, PLE_DIM, D), PLE_DIM ** -0.5),
        "final_norm_g": 1.0 + nrm((D,), 0.05),
    }


def reference(x, p, norm_mix_g, w_in, c_mu, c_w0, c_w2, c_a0, c_a2, c_g2, c_k_k, c_k_a, c_r_k,
              c_ln_g, c_ln_b, c_vres_down, c_vres_mu, c_v0, c_v2, w_branch, w_out, norm_ffn_g,
              w_gate_up, w_down, norm_ple_g, w_ple_gate, w_ple_proj, final_norm_g):
    B, S, D = x.shape
    pos = jnp.arange(S, dtype=jnp.float32)
    inv_a = 1.0 / (ROPE_THETA ** (jnp.arange(0, ROPE_DIM, 2, dtype=jnp.float32) / ROPE_DIM))
    ang_a = pos[:, None] * inv_a[None, :]
    cos_a, sin_a = jnp.cos(ang_a), jnp.sin(ang_a)
    inv_b = 1.0 / (RET_THETA ** jnp.linspace(0.0, 1.0, B_QK_DIM // 2, dtype=jnp.float32))
    ang_b = pos[:, None] * inv_b[None, :]
    cos_b, sin_b = jnp.cos(ang_b), jnp.sin(ang_b)

    v_first = x[..., :C_WIDTH]
    for i in range(DEPTH):
        h = rms_norm(x, norm_mix_g[i])
        if i == 0:
            z = h @ w_in[0]
            za, zb, zc, zg = split_cols(z, [A_COLS, B_COLS, C_COLS, G_COLS])
        else:
            w_cat = jnp.concatenate([w_in[i], c_vres_down[i - 1]], axis=1)
            z = h @ w_cat
            za, zb, zc, zg, zv = split_cols(z, [A_COLS, B_COLS, C_COLS, G_COLS, C_VRES_RANK])

        qa, ka, va = split_cols(za, [A_WIDTH, A_WIDTH, A_WIDTH])
        hs_a = (B, S, A_HEADS, A_HEAD_DIM)
        y_a = moba_attention(qa.reshape(hs_a), ka.reshape(hs_a), va.reshape(hs_a), cos_a, sin_a)

        qb, kb, vb, gb = split_cols(zb, [B_HEADS * B_QK_DIM, B_HEADS * B_QK_DIM, B_WIDTH, B_WIDTH])
        y_b = retention(qb.reshape(B, S, B_HEADS, B_QK_DIM), kb.reshape(B, S, B_HEADS, B_QK_DIM),
                        vb.reshape(B, S, B_HEADS, B_V_DIM), gb, cos_b, sin_b)

        zc = lerp_shift(zc, c_mu[i])
        r_c, k_c, v_c, w_lr, a_lr, g_lr = split_cols(
            zc, [C_WIDTH, C_WIDTH, C_WIDTH, C_DECAY_RANK, C_ICLR_RANK, C_GATE_RANK])
        if i == 0:
            v_first = v_c
        else:
            v_lr = lerp_shift(zv, c_vres_mu[i - 1])
            v_c = v_c + (v_first - v_c) * jax.nn.sigmoid(c_v0[i - 1] + v_lr @ c_v2[i - 1])
        y_c = rwkv7_time_mix(r_c, k_c, v_c, w_lr, a_lr, g_lr, c_w0[i], c_w2[i], c_a0[i], c_a2[i],
                             c_g2[i], c_k_k[i], c_k_a[i], c_r_k[i], c_ln_g[i], c_ln_b[i])

        gates = jax.nn.sigmoid(zg.reshape(B, S, N_BRANCH, D))
        ys = jnp.stack([y_a, y_b, y_c], axis=2)
        branch = jnp.einsum('bsnc,ncd->bsnd', ys, w_branch[i])
        x = x + jnp.sum(gates * branch, axis=2) @ w_out[i]

        h = rms_norm(x, norm_ffn_g[i])
        gt, up = split_cols(h @ w_gate_up[i], [D_FF, D_FF])
        x = x + (jax.nn.silu(gt) * up) @ w_down[i]

        h = rms_norm(x, norm_ple_g[i])
        x = x + (p[i] @ w_ple_proj[i]) * jax.nn.sigmoid(h @ w_ple_gate[i])

    return rms_norm(x, final_norm_g)
```

```python
import math
from contextlib import ExitStack
import numpy as np
import concourse.bass as bass
import concourse.mybir as mybir
from concourse.bass_utils import run_bass_kernel_spmd

F32 = mybir.dt.float32
BF16 = mybir.dt.bfloat16
ALU = mybir.AluOpType
AF = mybir.ActivationFunctionType
AX = mybir.AxisListType

D = 1024
NIN = 8448
DFF = 2816
EPS = 1e-6
NEG = -30000.0
C0 = math.exp(-0.5)
SEM_LIMIT = 30000
import os
PA_STOP = int(os.environ.get("PA_STOP", "9"))
PA_SKIP = os.environ.get("PA_SKIP", "")


class Ctx:
    ENGS = ("pe", "dve", "act", "pool", "sp")

    def __init__(self, nc):
        self.nc = nc
        self.prog = {e: [] for e in self.ENGS}
        self.sems = {}
        self.semval = {}
        self.cur = {}
        self.waited = {e: {} for e in self.ENGS}
        self.res = {}
        self.nsem = 0
        self.ninstr = 0

    def _semkey(self, logical, step):
        sk = self.cur.get(logical)
        if sk is None or self.semval[sk] + step > SEM_LIMIT:
            ep = 0 if sk is None else sk[1] + 1
            sk = (logical, ep)
            self.sems[sk] = self.nc.alloc_semaphore(name=f"s{self.nsem}")
            self.nsem += 1
            self.semval[sk] = 0
            self.cur[logical] = sk
        return sk

    @staticmethod
    def _key(x):
        if isinstance(x, (str, tuple)):
            return x
        if hasattr(x, "tensor"):
            return x.tensor.name
        return x.name

    def _collect(self, reads, writes):
        deps = {}

        def add(d):
            if d is not None:
                deps[d[0]] = max(deps.get(d[0], 0), d[1])
        for r in reads:
            st = self.res.get(r)
            if st:
                add(st["w"])
        for w in writes:
            st = self.res.get(w)
            if st:
                add(st["w"])
                for sk, v in st["r"].items():
                    add((sk, v))
        return deps

    def _emit_waits(self, e, deps):
        for sk, v in deps.items():
            if self.waited[e].get(sk, 0) < v:
                h = self.sems[sk]
                self.prog[e].append(lambda eng, h=h, v=v: eng.wait_ge(h, v))
                self.waited[e][sk] = v

    def _update(self, reads, writes, sk, v):
        for r in reads:
            st = self.res.setdefault(r, {"w": None, "r": {}})
            st["r"][sk] = max(st["r"].get(sk, 0), v)
        for w in writes:
            self.res[w] = {"w": (sk, v), "r": {}}

    def op(self, e, fn, r=(), w=()):
        reads = [self._key(x) for x in r]
        writes = [self._key(x) for x in w]
        writes = writes + [k for k in reads if isinstance(k, str) and k.startswith("bank") and k not in writes]
        deps = self._collect(reads, writes)
        self._emit_waits(e, deps)
        sk = self._semkey(("eng", e), 1)
        self.semval[sk] += 1
        v = self.semval[sk]
        h = self.sems[sk]
        self.prog[e].append(lambda eng, fn=fn, h=h: fn(eng).then_inc(h, 1))
        self._update(reads, writes, sk, v)
        self.ninstr += 1

    def dma(self, q, out, in_, r=None, w=None, semkey=None, **kw):
        reads = [self._key(x) for x in (r if r is not None else [in_])]
        writes = [self._key(x) for x in (w if w is not None else [out])]
        if semkey is None:
            semkey = out.tensor.name
        lk = ("dma", semkey)
        sk = self._semkey(lk, 16)
        deps = self._collect(reads, writes)
        if self.semval[sk] > 0:
            deps[sk] = max(deps.get(sk, 0), self.semval[sk])
        self._emit_waits(q, deps)
        self.semval[sk] += 16
        v = self.semval[sk]
        h = self.sems[sk]
        self.prog[q].append(
            lambda eng, out=out, in_=in_, kw=kw, h=h: eng.dma_start(out=out, in_=in_, **kw).then_inc(h, 16))
        self._update(reads, writes, sk, v)
        self.ninstr += 1

    def barrier(self):
        deps = {sk: v for sk, v in self.semval.items() if v > 0}
        for e in self.ENGS:
            self._emit_waits(e, deps)

    def final_wait(self, e, keys):
        deps = self._collect([self._key(k) for k in keys], ())
        self._emit_waits(e, deps)

    def mm(self, out, lhsT, rhs, start=True, stop=True, r=None, w=None):
        self.op("pe", lambda e: e.matmul(out, lhsT=lhsT, rhs=rhs, start=start, stop=stop),
                r if r is not None else [lhsT, rhs], w if w is not None else [out])

    def tr(self, out, in_, ident, r=None, w=None):
        self.op("pe", lambda e: e.transpose(out=out, in_=in_, identity=ident),
                r if r is not None else [in_, ident], w if w is not None else [out])

    def act(self, out, in_, func, bias=None, scale=None, r=None, w=None, eng="act"):
        kw = {}
        if bias is not None:
            kw["bias"] = bias
        if scale is not None:
            kw["scale"] = scale
        rr = [in_] + [x for x in (bias, scale) if not isinstance(x, (int, float, type(None)))]
        self.op("act", lambda e: e.activation(out=out, in_=in_, func=func, **kw),
                r if r is not None else rr, w if w is not None else [out])

    def tt(self, eng, out, in0, in1, op, r=None, w=None):
        self.op(eng, lambda e: e.tensor_tensor(out=out, in0=in0, in1=in1, op=op),
                r if r is not None else [in0, in1], w if w is not None else [out])

    def ts(self, eng, out, in0, s1, s2, op0, op1=None, r=None, w=None):
        rr = [in0] + [x for x in (s1, s2) if not isinstance(x, (int, float, type(None)))]
        if op1 is None:
            fn = lambda e: e.tensor_scalar(out=out, in0=in0, scalar1=s1, scalar2=None, op0=op0)
        else:
            fn = lambda e: e.tensor_scalar(out=out, in0=in0, scalar1=s1, scalar2=s2, op0=op0, op1=op1)
        self.op(eng, fn, r if r is not None else rr, w if w is not None else [out])

    def stt(self, out, in0, scalar, in1, op0, op1, r=None, w=None):
        rr = [in0, in1] + ([scalar] if not isinstance(scalar, (int, float)) else [])
        self.op("dve", lambda e: e.scalar_tensor_tensor(out=out, in0=in0, scalar=scalar, in1=in1, op0=op0, op1=op1),
                r if r is not None else rr, w if w is not None else [out])

    def cp(self, eng, out, in_, r=None, w=None):
        if eng == "act":
            fn = lambda e: e.copy(out=out, in_=in_)
        else:
            fn = lambda e: e.tensor_copy(out=out, in_=in_)
        self.op(eng, fn, r if r is not None else [in_], w if w is not None else [out])

    def memset(self, eng, ap, val, w=None):
        self.op(eng, lambda e: e.memset(ap, val), [], w if w is not None else [ap])

    def replay(self):
        nc = self.nc
        with nc.Block() as block:
            @block.tensor
            def _(eng):
                for f in self.prog["pe"]:
                    f(eng)

            @block.vector
            def _(eng):
                for f in self.prog["dve"]:
                    f(eng)

            @block.scalar
            def _(eng):
                for f in self.prog["act"]:
                    f(eng)

            @block.gpsimd
            def _(eng):
                for f in self.prog["pool"]:
                    f(eng)

            @block.sync
            def _(eng):
                for f in self.prog["sp"]:
                    f(eng)


def host_consts(S):
    c = {}
    pos = np.arange(S, dtype=np.float32)
    inv_a = (1.0 / (np.float32(500000.0) ** (np.arange(0, 16, 2, dtype=np.float32) / np.float32(16)))).astype(np.float32)
    ang = (pos[:, None] * inv_a[None, :]).astype(np.float32)
    cos_a, sin_a = np.cos(ang).astype(np.float32), np.sin(ang).astype(np.float32)
    ca = np.zeros((S, 2, 2, 8, 8), np.float32)
    ca[:, 0] = cos_a[:, None, None, :]
    ca[:, 1] = sin_a[:, None, None, :]
    c["cA"] = ca.reshape(S, 256)
    inv_b = (1.0 / (np.float32(10000.0) ** np.linspace(0.0, 1.0, 64, dtype=np.float32))).astype(np.float32)
    angb = (pos[:, None] * inv_b[None, :]).astype(np.float32)
    cos_b, sin_b = np.cos(angb).astype(np.float64), np.sin(angb).astype(np.float64)
    lg = np.log(1.0 - 2.0 ** (-5.0 - np.arange(4, dtype=np.float64)))
    i = (np.arange(S) % 128).astype(np.float64)
    gq = np.exp(lg[None, :] * i[:, None])
    gk = np.exp(-lg[None, :] * i[:, None]) * (128.0 ** -0.5)
    cb = np.zeros((S, 4, 4, 64), np.float64)
    cb[:, 0] = cos_b[:, None, :] * gq[:, :, None]
    cb[:, 1] = sin_b[:, None, :] * gq[:, :, None]
    cb[:, 2] = cos_b[:, None, :] * gk[:, :, None]
    cb[:, 3] = sin_b[:, None, :] * gk[:, :, None]
    c["cB"] = cb.reshape(S, 1024).astype(np.float32)
    gam = np.exp(lg)
    rs = np.zeros((128, 12), np.float32)
    rs[:, 0:4] = (gam ** 128)[None, :]
    rs[:, 4:8] = (gam ** 127)[None, :]
    rs[:, 8:12] = gam[None, :]
    c["cRS"] = rs
    E = np.zeros((16, S), np.float32)
    for j in range(16):
        E[j, j * 256:(j + 1) * 256] = 1.0
    c["cE"] = E
    cm = np.zeros((2, 128, 256), np.float32)
    for kt in range(2):
        k = kt * 128 + np.arange(128)[:, None]
        q = np.arange(256)[None, :]
        cm[kt] = np.where(k <= q, 0.0, NEG)
    c["cCM"] = cm.transpose(1, 0, 2).reshape(128, 512)
    j = np.arange(128)[:, None]
    ii = np.arange(128)[None, :]
    m = (j <= ii).astype(np.float32)
    c["cRM"] = np.tile(m[:, None, :], (1, 4, 1)).reshape(128, 512)
    s = (np.arange(128) % 64)[:, None]
    t = np.arange(64)[None, :]
    strict = (s < t).astype(np.float32)
    incl = (s <= t).astype(np.float32)
    am = np.concatenate([strict, incl], axis=1)
    c["cAM"] = np.tile(am[:, None, :], (1, 4, 1)).reshape(128, 512)
    tt_ = (np.arange(128) % 64)[:, None]
    ss_ = np.arange(64)[None, :]
    xm = (ss_ < tt_).astype(np.float32)
    c["cXM"] = np.tile(xm[:, None, :], (1, 4, 1)).reshape(128, 256)
    rm = np.ones((128, 512), np.float32)
    rm[:, ::64] = 0.0
    c["cRST"] = rm
    c["cID"] = np.eye(128, dtype=np.float32)
    bo = np.zeros((128, 128), np.float32)
    bo[:64, :64] = 1.0
    bo[64:, 64:] = 1.0
    c["cBO"] = bo
    return c


CONST_SHAPES = lambda S: {"cA": [S, 256], "cB": [S, 1024], "cRS": [128, 12], "cE": [16, S], "cCM": [128, 512],
                          "cRM": [128, 512], "cAM": [128, 512], "cXM": [128, 256], "cRST": [128, 512],
                          "cID": [128, 128], "cBO": [128, 128]}

COLS = {"mixg": (0, 8), "ffng": (8, 8), "pleg": (16, 8), "fing": (24, 8), "mu": (32, 14), "w0": (46, 4),
        "a0": (50, 4), "kk": (54, 4), "ka": (58, 4), "rk": (62, 4), "v0": (66, 4), "vmu": (70, 1)}
NCOL = 72


def pack_cols(inp, l):
    out = np.zeros((128, NCOL), np.float32)

    def put(name, vec):
        o, n = COLS[name]
        v = np.asarray(vec, np.float32).reshape(-1)
        out[:, o:o + n] = v.reshape(n, 128).T
    put("mixg", inp["norm_mix_g"][l])
    put("ffng", inp["norm_ffn_g"][l])
    put("pleg", inp["norm_ple_g"][l])
    put("fing", inp["final_norm_g"])
    put("mu", inp["c_mu"][l])
    put("w0", inp["c_w0"][l])
    put("a0", inp["c_a0"][l])
    put("kk", inp["c_k_k"][l])
    put("ka", inp["c_k_a"][l])
    put("rk", inp["c_r_k"][l])
    if l >= 1:
        put("v0", inp["c_v0"][l - 1])
        out[0:32, COLS["vmu"][0]] = np.asarray(inp["c_vres_mu"][l - 1], np.float32)
    return out


def pack_ln(inp, l):
    out = np.zeros((128, 2, 4, 64), np.float32)
    for k, name in enumerate(("c_ln_g", "c_ln_b")):
        v = np.asarray(inp[name][l], np.float32).reshape(4, 2, 64)
        for hh in range(2):
            out[hh * 64:(hh + 1) * 64, k] = v[:, hh, :][None]
    return out.reshape(128, 512)


def build(S, depth=2, en="abc", dbg=()):
    NT = S // 128
    NG = S // 512
    NB = S // 256
    nc = bass.Bass("TRN2", target_bir_lowering=False)

    def din(name, shape, dt=F32):
        return nc.dram_tensor(name, list(shape), dt, kind="ExternalInput").ap()

    def dscr(name, shape, dt=F32):
        return nc.dram_tensor(name, list(shape), dt, kind="Internal").ap()

    x_d = din("x", [S, D])
    p_d = din("p", [depth, S, 256])
    w_in = din("w_in", [depth, D, NIN])
    w2_d = din("c_w2", [depth, 64, 512])
    a2_d = din("c_a2", [depth, 64, 512])
    g2_d = din("c_g2", [depth, 128, 512])
    vd_d = din("c_vres_down", [max(depth - 1, 1), D, 32])
    v2_d = din("c_v2", [max(depth - 1, 1), 32, 512])
    wbr_d = din("w_branch", [depth, 3, 512, D])
    wout_d = din("w_out", [depth, D, D])
    wgu_d = din("w_gate_up", [depth, D, 2 * DFF])
    wd_d = din("w_down", [depth, DFF, D])
    wpg_d = din("w_ple_gate", [depth, D, D])
    wpp_d = din("w_ple_proj", [depth, 256, D])
    cols_d = din("cols", [depth, 128, NCOL])
    ln_d = din("lnp", [depth, 128, 512])
    cst = {k: din(k, shp) for k, shp in CONST_SHAPES(S).items()}
    out_d = nc.dram_tensor("out", [S, D], F32, kind="ExternalOutput").ap()

    xT_d = dscr("xT_d", [128, 8, S])
    hT_d = dscr("hT_d", [128, 8, S], BF16)
    yT_d = dscr("yT_d", [3, 128, 4, S], BF16)
    vf_d = dscr("vf_d", [128, 4, S])
    NINX = NIN + 32
    Wb_in = dscr("Wb_in", [depth, 128, 8, NINX], BF16)
    Wb_br = dscr("Wb_br", [depth, 128, 3, 4, D], BF16)
    Wb_out = dscr("Wb_out", [depth, 128, 8, D], BF16)
    Wb_gu = dscr("Wb_gu", [depth, 128, 8, 2 * DFF], BF16)
    Wb_d = dscr("Wb_d", [depth, 128, 22, D], BF16)
    Wb_pg = dscr("Wb_pg", [depth, 128, 8, D], BF16)
    Wb_pp = dscr("Wb_pp", [depth, 128, 2, D], BF16)
    dbg_d = {}
    for name, shp in dbg:
        dbg_d[name] = nc.dram_tensor("dbg_" + name, list(shp), F32, kind="ExternalOutput").ap()

    K = Ctx(nc)
    uniq = [0]
    with ExitStack() as top:
        def sbt(es, name, shape, dt=F32):
            uniq[0] += 1
            return es.enter_context(nc.sbuf_tensor(f"s_{name}_{uniq[0]}", list(shape), dt))

        banks = [top.enter_context(nc.psum_tensor(f"bank{i}", [128, 512], F32)) for i in range(8)]
        bank_rr = [0]

        def nb_(lo=0, hi=8):
            b = banks[lo + bank_rr[0] % (hi - lo)]
            bank_rr[0] += 1
            return b

        ident = sbt(top, "ident", [128, 128])
        identb = sbt(top, "identb", [128, 128], BF16)
        onesb = sbt(top, "onesb", [128, 128], BF16)
        cols = sbt(top, "cols", [128, depth, NCOL])
        K.dma("sp", ident[:], cst["cID"][:, :], r=[], semkey="cload")
        K.cp("dve", identb[:], ident[:])
        K.memset("dve", onesb[:], 1.0)
        for l in range(depth):
            K.dma("sp", cols[:, l, :], cols_d[l], r=[], w=[("cols", l)], semkey="cload")
        colkeys = [("cols", l) for l in range(depth)]

        def col(l, name, j=0, n=1):
            o, _ = COLS[name]
            return cols[:, l, o + j:o + j + n]

        def convert_weights(l):
            for c0 in range(0, NIN, 512):
                cw = min(512, NIN - c0)
                K.dma("pool", Wb_in[l, :, :, c0:c0 + cw], w_in[l, :, c0:c0 + cw].rearrange("(k p) n -> p k n", p=128),
                      r=[], w=[("Wb_in", l, c0 // 512)], semkey="conv")
            if l >= 1:
                K.dma("pool", Wb_in[l, :, :, NIN:NINX], vd_d[l - 1].rearrange("(k p) n -> p k n", p=128),
                      r=[], w=[("Wb_in", l, "v")], semkey="conv")
            for n in range(3):
                for h in range(2):
                    K.dma("pool", Wb_br[l, :, n, :, h * 512:(h + 1) * 512],
                          wbr_d[l, n, :, h * 512:(h + 1) * 512].rearrange("(k p) n -> p k n", p=128), r=[], w=[("Wb_br", l)], semkey="conv")
            for h in range(2):
                K.dma("pool", Wb_out[l, :, :, h * 512:(h + 1) * 512],
                      wout_d[l, :, h * 512:(h + 1) * 512].rearrange("(k p) n -> p k n", p=128), r=[], w=[("Wb_out", l)], semkey="conv")
                K.dma("pool", Wb_pg[l, :, :, h * 512:(h + 1) * 512],
                      wpg_d[l, :, h * 512:(h + 1) * 512].rearrange("(k p) n -> p k n", p=128), r=[], w=[("Wb_pg", l)], semkey="conv")
                K.dma("pool", Wb_pp[l, :, :, h * 512:(h + 1) * 512],
                      wpp_d[l, :, h * 512:(h + 1) * 512].rearrange("(k p) n -> p k n", p=128), r=[], w=[("Wb_pp", l)], semkey="conv")
                for k0 in (0, 11):
                    K.dma("pool", Wb_d[l, :, k0:k0 + 11, h * 512:(h + 1) * 512],
                          wd_d[l, k0 * 128:(k0 + 11) * 128, h * 512:(h + 1) * 512].rearrange("(k p) n -> p k n", p=128),
                          r=[], w=[("Wb_d", l)], semkey="conv")
            for c0 in range(0, 2 * DFF, 512):
                K.dma("pool", Wb_gu[l, :, :, c0:c0 + 512], wgu_d[l, :, c0:c0 + 512].rearrange("(k p) n -> p k n", p=128),
                      r=[], w=[("Wb_gu", l)], semkey="conv")

        def norm_group(es_name, xg, hg_out, l, gname, scratch):
            sq, rstd = scratch
            pb = nb_()
            for c in range(8):
                K.act(sq[:, c, :], xg[:, c, :], AF.Square)
            for c in range(8):
                K.mm(pb[:], onesb[:], sq[:, c, :], start=(c == 0), stop=(c == 7))
            K.act(rstd[:], pb[:], AF.Sqrt, bias=epsc[:, 0:1], scale=1.0 / D)
            K.op("dve", lambda e: e.reciprocal(out=rstd[:], in_=rstd[:]), [rstd], [rstd])
            for c in range(8):
                K.stt(hg_out[:, c, :], xg[:, c, :], col(l, gname, c), rstd[:], ALU.mult, ALU.mult,
                      r=[xg, rstd] + colkeys)

        epsc = sbt(top, "epsc", [128, 4])
        K.memset("dve", epsc[:, 0:1], EPS)
        K.memset("dve", epsc[:, 1:2], 1e-5 * 64)
        K.memset("dve", epsc[:, 2:3], 0.0)

        convert_weights(0)

        with ExitStack() as es:
            xin = [sbt(es, f"xin{i}", [128, D]) for i in range(2)]
            xg2 = [sbt(es, f"xgI{i}", [128, 8, 512]) for i in range(2)]
            hg2 = [sbt(es, f"hgI{i}", [128, 8, 512], BF16) for i in range(2)]
            sq = sbt(es, "sqI", [128, 8, 512], BF16)
            rstd = sbt(es, "rstdI", [128, 512])
            for tg in range(NG):
                xg = xg2[tg % 2]
                hg = hg2[tg % 2]
                for tt_ in range(4):
                    t = tg * 4 + tt_
                    xi = xin[t % 2]
                    K.dma("sp", xi[:], x_d[t * 128:(t + 1) * 128, :], r=[])
                    for half in range(2):
                        pb = nb_()
                        for c in range(4):
                            K.tr(pb[:, c * 128:(c + 1) * 128], xi[:, (half * 4 + c) * 128:(half * 4 + c + 1) * 128], ident[:])
                        K.cp("act" if half else "dve", xg[:, half * 4:(half + 1) * 4, tt_ * 128:(tt_ + 1) * 128],
                             pb[:].rearrange("p (c t) -> p c t", c=4))
                K.dma("sp", xT_d[:, :, tg * 512:(tg + 1) * 512], xg[:], w=[("xT", tg)], semkey="xTst")
                norm_group("I", xg, hg, 0, "mixg", (sq, rstd))
                K.dma("sp", hT_d[:, :, tg * 512:(tg + 1) * 512], hg[:], w=[("hT", tg)], semkey="hTst")

        K.barrier()
        for l in range(depth):
            if l + 1 < depth:
                convert_weights(l + 1)
            last = (l == depth - 1)
            if "a" in en:
                phase_a(nc, K, sbt, banks, nb_, l, S, cst, Wb_in, hT_d, yT_d, ident, identb)
                K.barrier()
            if "b" in en:
                phase_b(nc, K, sbt, banks, nb_, l, S, cst, Wb_in, hT_d, yT_d, ident, identb, epsc)
                K.barrier()
            if "c" in en:
                phase_c(nc, K, sbt, banks, nb_, l, S, cst, Wb_in, hT_d, yT_d, vf_d, ident, identb, epsc, cols, col, colkeys,
                        w2_d, a2_d, g2_d, v2_d, ln_d, dbg_d)
                K.barrier()

            with ExitStack() as es:
                xg2 = [sbt(es, f"xgT{i}", [128, 8, 512]) for i in range(2)]
                hg = sbt(es, "hgT", [128, 8, 512], BF16)
                yg = sbt(es, "ygT", [128, 3, 4, 512], BF16)
                hf = sbt(es, "hfT", [128, 8, 512], BF16)
                mg = sbt(es, "mgT", [128, 8, 512], BF16)
                actT = sbt(es, "actT", [128, 22, 512], BF16)
                sq = sbt(es, "sqT", [128, 8, 512], BF16)
                rstd = sbt(es, "rstdT", [128, 512])
                sg = [sbt(es, f"sgT{i}", [128, 512]) for i in range(2)]
                acc = sbt(es, "accT", [128, 512])
                tmpf = [sbt(es, f"tmpT{i}", [128, 512]) for i in range(2)]
                wst = [sbt(es, f"wst{i}", [128, 8, 1024], BF16) for i in range(2)]
                wdt = [sbt(es, f"wdt{i}", [128, 22, 128], BF16) for i in range(2)]
                wbrt = sbt(es, "wbrt", [128, 3, 4, 128], BF16)
                wppt = sbt(es, "wppt", [128, 2, D], BF16)
                pin = [sbt(es, f"pin{i}", [128, 256]) for i in range(2)]
                pT = sbt(es, "pTT", [128, 2, 512], BF16)
                ot = [sbt(es, f"otT{i}", [128, D]) for i in range(2)]
                wsi = [0]

                def wslab():
                    t_ = wst[wsi[0] % 2]
                    wsi[0] += 1
                    return t_
                K.dma("sp", wppt[:], Wb_pp[l], r=[("Wb_pp", l)], semkey="cload")
                for tg in range(NG):
                    xg = xg2[tg % 2]
                    tok = slice(tg * 512, (tg + 1) * 512)
                    K.dma("sp", xg[:], xT_d[:, :, tok], r=[("xT", tg)])
                    K.dma("sp", hg[:], hT_d[:, :, tok], r=[("hT", tg)])
                    for n in range(3):
                        if "abc"[n] in en:
                            K.dma("sp", yg[:, n], yT_d[n, :, :, tok], r=[("yT", n, tg)], w=[("ygT", n)], semkey="ygT")
                        elif tg == 0:
                            K.memset("pool", yg[:, n], 0.0, w=[("ygT", n)])
                    ygk = [("ygT", n) for n in range(3)]
                    for dc in range(8):
                        ws = wslab()
                        for n in range(3):
                            c0 = 5376 + n * 1024 + dc * 128
                            K.dma("sp", ws[:, :, n * 128:(n + 1) * 128], Wb_in[l, :, :, c0:c0 + 128],
                                  r=[("Wb_in", l, c0 // 512)], w=[ws])
                        K.dma("sp", wbrt[:], Wb_br[l, :, :, :, dc * 128:(dc + 1) * 128], r=[("Wb_br", l)])
                        for n in range(3):
                            pbr = nb_()
                            pgt = nb_()
                            for kc in range(4):
                                K.mm(pbr[:], wbrt[:, n, kc, :], yg[:, n, kc, :], start=(kc == 0), stop=(kc == 3),
                                     r=[wbrt, ("ygT", n)])
                            for kc in range(8):
                                K.mm(pgt[:], ws[:, kc, n * 128:(n + 1) * 128], hg[:, kc, :], start=(kc == 0), stop=(kc == 7))
                            s_ = sg[n % 2]
                            K.act(s_[:], pgt[:], AF.Sigmoid)
                            if n == 0:
                                K.tt("dve", acc[:], s_[:], pbr[:], ALU.mult)
                            elif n == 1:
                                K.tt("dve", tmpf[0][:], s_[:], pbr[:], ALU.mult)
                                K.tt("pool", acc[:], acc[:], tmpf[0][:], ALU.add)
                            else:
                                K.tt("dve", tmpf[1][:], s_[:], pbr[:], ALU.mult)
                                K.tt("dve", mg[:, dc, :], acc[:], tmpf[1][:], ALU.add)
                    for dc in range(8):
                        if dc % 4 == 0:
                            ws = wslab()
                            K.dma("sp", ws[:, :, 0:512], Wb_out[l, :, :, dc * 128:dc * 128 + 512], r=[("Wb_out", l)], w=[ws])
                        po = nb_()
                        for kc in range(8):
                            K.mm(po[:], ws[:, kc, (dc % 4) * 128:(dc % 4 + 1) * 128], mg[:, kc, :], start=(kc == 0), stop=(kc == 7))
                        K.tt("dve", xg[:, dc, :], xg[:, dc, :], po[:], ALU.add)
                    norm_group("T", xg, hf, l, "ffng", (sq, rstd))
                    for f4 in range(0, 22, 4):
                        nf = min(4, 22 - f4)
                        ws = wslab()
                        K.dma("sp", ws[:, :, 0:nf * 128], Wb_gu[l, :, :, f4 * 128:(f4 + nf) * 128], r=[("Wb_gu", l)], w=[ws])
                        K.dma("sp", ws[:, :, 512:512 + nf * 128], Wb_gu[l, :, :, DFF + f4 * 128:DFF + (f4 + nf) * 128],
                              r=[("Wb_gu", l)], w=[ws])
                        for fi in range(nf):
                            fc = f4 + fi
                            pg_ = nb_()
                            pu_ = nb_()
                            for kc in range(8):
                                K.mm(pg_[:], ws[:, kc, fi * 128:(fi + 1) * 128], hf[:, kc, :], start=(kc == 0), stop=(kc == 7))
                            for kc in range(8):
                                K.mm(pu_[:], ws[:, kc, 512 + fi * 128:512 + (fi + 1) * 128], hf[:, kc, :], start=(kc == 0), stop=(kc == 7))
                            s_ = sg[fc % 2]
                            K.act(s_[:], pg_[:], AF.Silu)
                            K.tt("dve", actT[:, fc, :], s_[:], pu_[:], ALU.mult)
                    for dc in range(8):
                        wd_ = wdt[dc % 2]
                        K.dma("sp", wd_[:], Wb_d[l, :, :, dc * 128:(dc + 1) * 128], r=[("Wb_d", l)])
                        pd = nb_()
                        for fc in range(22):
                            K.mm(pd[:], wd_[:, fc, :], actT[:, fc, :], start=(fc == 0), stop=(fc == 21))
                        K.tt("dve", xg[:, dc, :], xg[:, dc, :], pd[:], ALU.add)
                    norm_group("T", xg, hf, l, "pleg", (sq, rstd))
                    for tt_ in range(4):
                        pi = pin[tt_ % 2]
                        K.dma("sp", pi[:], p_d[l, tg * 512 + tt_ * 128: tg * 512 + (tt_ + 1) * 128, :], r=[])
                        pb = nb_()
                        for c in range(2):
                            K.tr(pb[:, c * 128:(c + 1) * 128], pi[:, c * 128:(c + 1) * 128], ident[:])
                        K.cp("act", pT[:, :, tt_ * 128:(tt_ + 1) * 128], pb[:, 0:256].rearrange("p (c t) -> p c t", c=2))
                    for dc in range(8):
                        if dc % 4 == 0:
                            ws = wslab()
                            K.dma("sp", ws[:, :, 0:512], Wb_pg[l, :, :, dc * 128:dc * 128 + 512], r=[("Wb_pg", l)], w=[ws])
                        pg_ = nb_()
                        pp_ = nb_()
                        for kc in range(8):
                            K.mm(pg_[:], ws[:, kc, (dc % 4) * 128:(dc % 4 + 1) * 128], hf[:, kc, :], start=(kc == 0), stop=(kc == 7))
                        for kc in range(2):
                            K.mm(pp_[:], wppt[:, kc, dc * 128:(dc + 1) * 128], pT[:, kc, :], start=(kc == 0), stop=(kc == 1))
                        s_ = sg[dc % 2]
                        K.act(s_[:], pg_[:], AF.Sigmoid)
                        K.tt("dve", tmpf[dc % 2][:], s_[:], pp_[:], ALU.mult)
                        K.tt("pool", xg[:, dc, :], xg[:, dc, :], tmpf[dc % 2][:], ALU.add)
                    if not last:
                        K.dma("sp", xT_d[:, :, tok], xg[:], w=[("xT", tg)], semkey="xTst")
                        norm_group("T", xg, hf, l + 1, "mixg", (sq, rstd))
                        K.dma("sp", hT_d[:, :, tok], hf[:], w=[("hT", tg)], semkey="hTst")
                    else:
                        pb = nb_()
                        for c in range(8):
                            K.act(sq[:, c, :], xg[:, c, :], AF.Square)
                        for c in range(8):
                            K.mm(pb[:], onesb[:], sq[:, c, :], start=(c == 0), stop=(c == 7))
                        K.act(rstd[:], pb[:], AF.Sqrt, bias=epsc[:, 0:1], scale=1.0 / D)
                        K.op("dve", lambda e: e.reciprocal(out=rstd[:], in_=rstd[:]), [rstd], [rstd])
                        for c in range(8):
                            K.stt(xg[:, c, :], xg[:, c, :], col(l, "fing", c), rstd[:], ALU.mult, ALU.mult,
                                  r=[xg, rstd] + colkeys)
                        for tt_ in range(4):
                            o_ = ot[tt_ % 2]
                            for half in range(2):
                                pb2 = nb_()
                                for c in range(4):
                                    K.tr(pb2[:, c * 128:(c + 1) * 128], xg[:, half * 4 + c, tt_ * 128:(tt_ + 1) * 128], ident[:])
                                K.cp("act" if half else "dve", o_[:, half * 512:(half + 1) * 512], pb2[:])
                            K.dma("sp", out_d[tg * 512 + tt_ * 128: tg * 512 + (tt_ + 1) * 128, :], o_[:], w=["out"],
                                  semkey="out")
            K.barrier()
        K.final_wait("sp", ["out"] + ["dbg_" + n for n in dbg_d])
        K.replay()
    return nc, K


def phase_a(nc, K, sbt, banks, nb_, l, S, cst, Wb_in, hT_d, yT_d, ident, identb):
    NT = S // 128
    NB = S // 256
    with ExitStack() as es:
        wA = sbt(es, "wA", [128, 8, 1536], BF16)
        KT = sbt(es, "KT", [80, 8, S], BF16)
        Va = sbt(es, "Va", [128, NT, 8, 65], BF16)
        QT = [sbt(es, f"QT{i}", [80, 8, 256], BF16) for i in range(2)]
        QTf = [sbt(es, f"QTf{i}", [64, 8, 128]) for i in range(2)]
        kms = sbt(es, "kms", [64, 8, 16])
        ktmp = sbt(es, "ktmp", [64, 8])
        hTt = [sbt(es, f"hTtA{i}", [128, 8, 128], BF16) for i in range(2)]
        qk = [sbt(es, f"qkA{i}", [128, 2, 8, 64]) for i in range(2)]
        cs = [sbt(es, f"csA{i}", [128, 2, 128]) for i in range(2)]
        rt = [sbt(es, f"rtA{i}", [128, 128]) for i in range(4)]
        gs = sbt(es, "gsA", [128, 8, 16])
        m8 = sbt(es, "m8A", [128, 8, 8])
        Mp = sbt(es, "MpA", [128, 8, 80], BF16)
        cm = sbt(es, "cmA", [128, 2, 256], BF16)
        PT = [sbt(es, f"PTA{i}", [128, 256], BF16) for i in range(3)]
        ya = [sbt(es, f"yaA{i}", [128, 512], BF16) for i in range(2)]
        rden = sbt(es, "rdenA", [128, 2])
        yTs = [sbt(es, f"yTsA{i}", [128, 4, 128], BF16) for i in range(2)]
        for g in range(3):
            K.dma("sp", wA[:, :, g * 512:(g + 1) * 512], Wb_in[l, :, :, g * 512:(g + 1) * 512], r=[("Wb_in", l, g)],
                  w=[("wA", g)], semkey="wload")
        wAk = [("wA", g) for g in range(3)]
        K.dma("pool", cm[:], cst["cCM"].rearrange("p (k q) -> p k q", k=2), r=[], semkey="cloadp")
        for h in range(8 if "e" not in PA_SKIP else 0):
            for e0 in range(0, S, 2048):
                e1 = min(S, e0 + 2048)
                K.dma("pool", KT[64:80, h, e0:e1], cst["cE"][:, e0:e1], r=[], w=[("KTE", h)], semkey="cloadp")
        KTE = [("KTE", h) for h in range(8)]
        if "m" not in PA_SKIP:
            K.memset("pool", Va[:, :, :, 64:65], 1.0, w=["Va1"])
            K.memset("pool", Mp[:], 0.0)
        pz = [banks[0], banks[1]]
        pTq = [banks[2], banks[3]]
        pS = [banks[4], banks[5]]
        pO = [banks[6], banks[7]]
        pti = 0
        psi = 0
        for t in range(NT):
            b = t // 2
            half = t % 2
            hT = hTt[t % 2]
            K.dma("sp", hT[:], hT_d[:, :, t * 128:(t + 1) * 128], r=[("hT", t // 4)])
            c_ = cs[t % 2]
            K.dma("sp", c_[:], cst["cA"][t * 128:(t + 1) * 128, :].rearrange("p (a b) -> p a b", a=2), r=[])
            q_ = qk[t % 2]
            for g in range(3):
                pb = pz[g % 2]
                for kc in range(8):
                    K.mm(pb[:], hT[:, kc, :], wA[:, kc, g * 512:(g + 1) * 512], start=(kc == 0), stop=(kc == 7),
                         r=[hT, ("wA", g)])
                if g < 2:
                    K.cp("act", q_[:, g], pb[:].rearrange("p (h d) -> p h d", h=8))
                elif "v" not in PA_SKIP:
                    K.cp("act", Va[:, t, :, 0:64], pb[:].rearrange("p (h d) -> p h d", h=8), w=[("Va", t)])
            x1 = q_[:, :, :, 0:8]
            x2 = q_[:, :, :, 8:16]
            co = c_[:, 0, :].rearrange("p (a h d) -> p a h d", a=2, h=8)
            si = c_[:, 1, :].rearrange("p (a h d) -> p a h d", a=2, h=8)
            t1, t2, t3, t4 = [r_[:].rearrange("p (a h d) -> p a h d", a=2, h=8) for r_ in rt]
            if "r" not in PA_SKIP:
                K.tt("dve", t1, x1, co, ALU.mult)
                K.tt("pool", t2, x2, si, ALU.mult)
                K.tt("dve", t3, x1, si, ALU.mult)
                K.tt("pool", t4, x2, co, ALU.mult)
                K.tt("dve", x1, t1, t2, ALU.subtract)
                K.tt("dve", x2, t3, t4, ALU.add)
            for g in range(2 if "t" not in PA_SKIP else 0):
                for hq in range(2):
                    pb = pTq[hq]
                    for h4 in range(4):
                        h = hq * 4 + h4
                        K.tr(pb[0:64, h4 * 128:(h4 + 1) * 128], q_[:, g, h, :], ident[:])
                    src = pb[0:64, :].rearrange("p (h t) -> p h t", h=4)
                    if g == 0:
                        K.cp("act", QT[b % 2][0:64, hq * 4:(hq + 1) * 4, half * 128:(half + 1) * 128], src,
                             w=[("QTq", b % 2)])
                        K.cp("dve", QTf[half][:, hq * 4:(hq + 1) * 4, :], src)
                    else:
                        K.cp("act", KT[0:64, hq * 4:(hq + 1) * 4, t * 128:(t + 1) * 128], src, w=[("KT", t)])
                        K.op("dve", lambda e, src=src, hq=hq: e.tensor_reduce(out=ktmp[:, hq * 4:(hq + 1) * 4], in_=src, axis=AX.X, op=ALU.add),
                             [pb], [ktmp])
                if g == 1:
                    if half == 0:
                        K.cp("dve", kms[:, :, b:b + 1], ktmp[:].rearrange("p (h o) -> p h o", o=1))
                    else:
                        K.tt("dve", kms[:, :, b:b + 1], kms[:, :, b:b + 1], ktmp[:].rearrange("p (h o) -> p h o", o=1), ALU.add)
            if half == 0 or PA_STOP <= 1:
                continue
            Qb = QT[b % 2]
            if b >= 1:
                for hf_ in range(2):
                    pg = banks[2]
                    for h in range(8):
                        K.mm(pg[:, h * 16:(h + 1) * 16], QTf[hf_][0:64, h, :], kms[0:64, h, :])
                    K.cp("dve", gs[:], pg[:, 0:128].rearrange("p (h n) -> p h n", h=8))
                    if b < 16:
                        K.memset("dve", gs[:, :, b:16], -1e30)
                    for h in range(8):
                        K.op("dve", lambda e, h=h: e.max(out=m8[:, h, :], in_=gs[:, h, :]), [gs], [m8])
                    for h in range(8):
                        K.ts("dve", Mp[:, h, 64:80], gs[:, h, :], m8[:, h, 2:3], NEG, ALU.is_lt, ALU.mult)
                    pm = banks[3]
                    pmv = pm[:].bitcast(BF16).rearrange("p (h t) -> p h t", h=8)
                    for h in range(8):
                        K.tr(pmv[0:80, h, :], Mp[:, h, :], identb[:])
                    K.cp("act", Qb[64:80, :, hf_ * 128:(hf_ + 1) * 128], pmv[64:80, :, :], w=[("QTm", b % 2, hf_)])
            Qkeys = [("QTq", b % 2), ("QTm", b % 2, 0), ("QTm", b % 2, 1)]
            if PA_STOP <= 2:
                continue
            nkt = 2 * b + 2
            for h in range(8):
                for kt in range(nkt):
                    own = kt >= 2 * b
                    Kr = 64 if own else 80
                    ps_ = pS[psi % 2]
                    psi += 1
                    K.mm(ps_[:, 0:256], KT[0:Kr, h, kt * 128:(kt + 1) * 128], Qb[0:Kr, h, :], start=True, stop=not own,
                         r=[("KT", kt), ("KTE", h)] + Qkeys)
                    if own:
                        K.mm(ps_[:, 0:256], identb[:], cm[:, kt - 2 * b, :], start=False, stop=True)
                    P_ = PT[pti % 3]
                    pti += 1
                    K.act(P_[:], ps_[:, 0:256], AF.Exp, scale=0.125)
                    for q2 in range(2):
                        if own and kt - 2 * b == 1 and q2 == 0:
                            continue
                        lastk = (2 * b) if q2 == 0 else (2 * b + 1)
                        K.mm(pO[q2][:, 0:65], P_[:, q2 * 128:(q2 + 1) * 128], Va[:, kt, h, :], start=(kt == 0), stop=(kt == lastk),
                             r=[P_, ("Va", kt), "Va1"])
                for q2 in range(2):
                    K.op("dve", lambda e, q2=q2: e.reciprocal(out=rden[:, q2:q2 + 1], in_=pO[q2][:, 64:65]), [pO[q2]], [("rden", q2)])
                    K.ts("dve", ya[q2][:, h * 64:(h + 1) * 64], pO[q2][:, 0:64], rden[:, q2:q2 + 1], None, ALU.mult,
                         r=[pO[q2], ("rden", q2)])
            if PA_STOP <= 3:
                continue
            for q2 in range(2):
                tq = 2 * b + q2
                pb = banks[q2]
                pv = pb[:].bitcast(BF16)[:, 0:512].rearrange("p (c t) -> p c t", c=4)
                for c in range(4):
                    K.tr(pv[:, c, :], ya[q2][:, c * 128:(c + 1) * 128], identb[:])
                K.cp("act", yTs[q2][:], pv)
                K.dma("sp", yT_d[0, :, :, tq * 128:(tq + 1) * 128], yTs[q2][:], w=[("yT", 0, tq // 4)], semkey="yTs")


def phase_b(nc, K, sbt, banks, nb_, l, S, cst, Wb_in, hT_d, yT_d, ident, identb, epsc):
    NT = S // 128
    with ExitStack() as es:
        wB = sbt(es, "wB", [128, 8, 2048], BF16)
        hTt = [sbt(es, f"hTtB{i}", [128, 8, 128], BF16) for i in range(2)]
        cb = [sbt(es, f"cbB{i}", [128, 4, 256]) for i in range(2)]
        qs = [sbt(es, f"qsB{i}", [128, 512]) for i in range(2)]
        rt = [sbt(es, f"rtB{i}", [128, 256]) for i in range(4)]
        qr = [sbt(es, f"qrB{i}", [128, 512], BF16) for i in range(2)]
        QTb = sbt(es, "QTbB", [128, 4, 128], BF16)
        KTb = sbt(es, "KTbB", [128, 4, 128], BF16)
        vb = sbt(es, "vbB", [128, 512], BF16)
        sgb = sbt(es, "sgB", [128, 512])
        Sm = sbt(es, "SmB", [128, 4, 128], BF16)
        rm = sbt(es, "rmB", [128, 4, 128], BF16)
        R = sbt(es, "RB", [128, 4, 128])
        Rg = sbt(es, "RgB", [128, 4, 128], BF16)
        rs = sbt(es, "rsB", [128, 12])
        st = sbt(es, "stB", [128, 4, 6])
        mv = sbt(es, "mvB", [128, 4, 2])
        rstd = sbt(es, "rstdB", [128, 4])
        yn = sbt(es, "ynB", [128, 512])
        yb = sbt(es, "ybB", [128, 512], BF16)
        yTs = [sbt(es, f"yTsB{i}", [128, 4, 128], BF16) for i in range(2)]
        for g in range(4):
            K.dma("sp", wB[:, :, g * 512:(g + 1) * 512], Wb_in[l, :, :, 1536 + g * 512:1536 + (g + 1) * 512],
                  r=[("Wb_in", l, 3 + g)], w=[("wB", g)], semkey="wload")
        K.dma("pool", rm[:], cst["cRM"].rearrange("p (h t) -> p h t", h=4), r=[], semkey="cloadp")
        K.dma("sp", rs[:], cst["cRS"][:, :], r=[], semkey="cload")
        K.memset("dve", R[:], 0.0, w=[("RB", h) for h in range(4)])
        K.memset("pool", Rg[:], 0.0)
        for t in range(NT):
            hT = hTt[t % 2]
            K.dma("sp", hT[:], hT_d[:, :, t * 128:(t + 1) * 128], r=[("hT", t // 4)])
            c_ = cb[t % 2]
            K.dma("sp", c_[:], cst["cB"][t * 128:(t + 1) * 128, :].rearrange("p (a b) -> p a b", a=4), r=[])
            for g in range(4):
                pb = nb_(0, 4)
                for kc in range(8):
                    K.mm(pb[:], hT[:, kc, :], wB[:, kc, g * 512:(g + 1) * 512], start=(kc == 0), stop=(kc == 7),
                         r=[hT, ("wB", g)])
                if g < 2:
                    q_ = qs[g]
                    K.cp("act", q_[:], pb[:])
                    xv = q_[:].rearrange("p (h d two) -> p h d two", h=4, two=2)
                    xe = xv[:, :, :, 0]
                    xo = xv[:, :, :, 1]
                    co = c_[:, 2 * g, :].rearrange("p (h d) -> p h d", h=4)
                    si = c_[:, 2 * g + 1, :].rearrange("p (h d) -> p h d", h=4)
                    t1, t2, t3, t4 = [r_[:].rearrange("p (h d) -> p h d", h=4) for r_ in rt]
                    ov = qr[g][:].rearrange("p (h d two) -> p h d two", h=4, two=2)
                    K.tt("dve", t1, xe, co, ALU.mult)
                    K.tt("pool", t2, xo, si, ALU.mult)
                    K.tt("dve", t3, xe, si, ALU.mult)
                    K.tt("pool", t4, xo, co, ALU.mult)
                    K.tt("dve", ov[:, :, :, 0], t1, t2, ALU.subtract, w=[(f"qrB{g}", 0)], r=[rt[0], rt[1]])
                    K.tt("dve", ov[:, :, :, 1], t3, t4, ALU.add, w=[(f"qrB{g}", 1)], r=[rt[2], rt[3]])
                    pbt = nb_(0, 4)
                    pv = pbt[:].bitcast(BF16)[:, 0:512].rearrange("p (h t) -> p h t", h=4)
                    for h in range(4):
                        K.tr(pv[:, h, :], qr[g][:, h * 128:(h + 1) * 128], identb[:], r=[(f"qrB{g}", 0), (f"qrB{g}", 1), identb])
                    K.cp("act", (QTb if g == 0 else KTb)[:], pv)
                elif g == 2:
                    K.cp("act", vb[:], pb[:])
                else:
                    K.act(sgb[:], pb[:], AF.Silu)
            qrk = [("qrB1", 0), ("qrB1", 1)]
            pS_ = banks[4]
            for h in range(4):
                K.mm(pS_[:, h * 128:(h + 1) * 128], KTb[:, h, :], QTb[:, h, :])
            K.tt("dve", Sm[:], pS_[:].rearrange("p (h t) -> p h t", h=4), rm[:], ALU.mult)
            py = banks[5]
            pkv = banks[6]
            for h in range(4):
                K.mm(py[:, h * 128:(h + 1) * 128], Sm[:, h, :], vb[:, h * 128:(h + 1) * 128], start=True, stop=False)
                K.mm(py[:, h * 128:(h + 1) * 128], QTb[:, h, :], Rg[:, h, :], start=False, stop=True)
            for h in range(4):
                K.mm(pkv[:, h * 128:(h + 1) * 128], qr[1][:, h * 128:(h + 1) * 128], vb[:, h * 128:(h + 1) * 128],
                     r=qrk + [vb])
            for h in range(4):
                K.op("dve", lambda e, h=h: e.bn_stats(out=st[:, h, :], in_=py[:, h * 128:(h + 1) * 128]), [py], [("stB", h)])
                K.op("dve", lambda e, h=h: e.bn_aggr(out=mv[:, h, :], in_=st[:, h, :]), [("stB", h)], [("mvB", h)])
            mvk = [("mvB", h) for h in range(4)]
            K.act(rstd[:], mv[:, :, 1], AF.Sqrt, bias=epsc[:, 0:1], scale=1.0, r=mvk + [epsc])
            K.op("dve", lambda e: e.reciprocal(out=rstd[:], in_=rstd[:]), [rstd], [rstd])
            for h in range(4):
                K.ts("dve", yn[:, h * 128:(h + 1) * 128], py[:, h * 128:(h + 1) * 128], mv[:, h, 0:1], rstd[:, h:h + 1],
                     ALU.subtract, ALU.mult, r=[py, rstd] + mvk, w=[("ynB", h)])
            K.tt("pool", yb[:], yn[:], sgb[:], ALU.mult, r=[("ynB", h) for h in range(4)] + [sgb])
            for h in range(4):
                K.ts("pool", R[:, h, :], R[:, h, :], rs[:, h:h + 1], None, ALU.mult, r=[("RB", h), rs], w=[("RB", h)])
                K.stt(R[:, h, :], pkv[:, h * 128:(h + 1) * 128], rs[:, 4 + h:5 + h], R[:, h, :], ALU.mult, ALU.add,
                      r=[pkv, rs, ("RB", h)], w=[("RB", h)])
                K.act(Rg[:, h, :], R[:, h, :], AF.Identity, scale=float(cst_gamma(h)), r=[("RB", h)], w=[Rg])
            pbt = banks[7]
            pv = pbt[:].bitcast(BF16)[:, 0:512].rearrange("p (c t) -> p c t", c=4)
            for c in range(4):
                K.tr(pv[:, c, :], yb[:, c * 128:(c + 1) * 128], identb[:])
            K.cp("act", yTs[t % 2][:], pv)
            K.dma("sp", yT_d[1, :, :, t * 128:(t + 1) * 128], yTs[t % 2][:], w=[("yT", 1, t // 4)], semkey="yTs")


def cst_gamma(h):
    return 1.0 - 2.0 ** (-5.0 - h)


def phase_c(nc, K, sbt, banks, nb_, l, S, cst, Wb_in, hT_d, yT_d, vf_d, ident, identb, epsc, cols, col, colkeys,
            w2_d, a2_d, g2_d, v2_d, ln_d, dbg_d):
    NG = S // 512
    RW = BF16
    ncc = 15 if l >= 1 else 14
    with ExitStack() as es:
        wC = sbt(es, "wC", [128, 8, 1920], BF16)
        w2b = sbt(es, "w2b", [64, 512], BF16)
        a2b = sbt(es, "a2b", [128, 512], BF16)
        g2b = sbt(es, "g2b", [128, 512], BF16)
        v2b = sbt(es, "v2b", [32, 512], BF16)
        lnp = sbt(es, "lnp", [128, 2, 256])
        am = sbt(es, "amC", [128, 2, 2, 128])
        xm = sbt(es, "xmC", [128, 4, 64])
        rst = sbt(es, "rstC", [128, 512])
        bo = sbt(es, "boC", [128, 128], BF16)
        hg = sbt(es, "hgC", [128, 8, 512], BF16)
        zx = sbt(es, "zxC", [128, 15, 513])
        zl = sbt(es, "zlC", [128, 15, 512])
        tmp = [sbt(es, f"tmpC{i}", [128, 512]) for i in range(3)]
        tw = sbt(es, "twC", [64, 512], BF16)
        al = sbt(es, "alC", [128, 512], BF16)
        sgl = sbt(es, "sglC", [128, 512], BF16)
        vlr = sbt(es, "vlrC", [32, 512], BF16)
        sigw = sbt(es, "sigwC", [128, 512])
        iclr = sbt(es, "iclrC", [128, 512])
        cl = sbt(es, "clC", [128, 512])
        Pinc = sbt(es, "PincC", [128, 512])
        Pexc = sbt(es, "PexcC", [128, 512])
        Pinv = sbt(es, "PinvC", [128, 512])
        kkr = sbt(es, "kkrC", [128, 512])
        sqb = sbt(es, "sqbC", [128, 512], BF16)
        kmod = sbt(es, "kmodC", [128, 512])
        vfp = sbt(es, "vfpC", [128, 4, 512])
        vbf = sbt(es, "vbfC", [128, 4, 512], BF16)
        AR = sbt(es, "ARC", [128, 4, 8, 2, 64], RW)
        BK = sbt(es, "BKC", [128, 4, 8, 2, 64], RW)
        prk = sbt(es, "prkC", [128, 4, 512], BF16)
        pend = sbt(es, "pendC", [128, 4, 8])
        H = sbt(es, "HC", [128, 4, 64])
        Hb = sbt(es, "HbC", [128, 4, 64], RW)
        Amat = sbt(es, "AmatC", [128, 4, 2, 128], RW)
        Xm = [sbt(es, f"XmC{i}", [128, 2, 4, 64], RW) for i in range(2)]
        Wt = [sbt(es, f"WtC{i}", [128, 4, 64], RW) for i in range(2)]
        Vc = sbt(es, "VcC", [128, 4, 64], RW)
        Vcf = sbt(es, "VcfC", [128, 4, 64])
        BKt = sbt(es, "BKtC", [128, 4, 2, 64], RW)
        gt = sbt(es, "gtC", [128, 4, 64])
        bs = sbt(es, "bsC", [128, 4])
        st = sbt(es, "stC", [128, 4, 6])
        mv = sbt(es, "mvC", [128, 4, 2])
        rstd = sbt(es, "rstdC", [128, 4])
        yn = sbt(es, "ynC", [128, 4, 64])
        yc = sbt(es, "ycC", [128, 4, 64], BF16)
        ycT = [sbt(es, f"ycTC{i}", [128, 4, 512], BF16) for i in range(2)]

        nwc = 1792 + (32 if l >= 1 else 0)
        for g in range(0, 1792, 512):
            gw = min(512, 1792 - g)
            K.dma("sp", wC[:, :, g:g + gw], Wb_in[l, :, :, 3584 + g:3584 + g + gw],
                  r=[("Wb_in", l, (3584 + g) // 512)], w=[("wC", g)], semkey="wload")
        wCk = [("wC", g) for g in range(0, 1792, 512)]
        if l >= 1:
            K.dma("sp", wC[:, :, 1792:1824], Wb_in[l, :, :, NIN:NIN + 32], r=[("Wb_in", l, "v")], w=[("wC", "v")], semkey="wload")
            wCk.append(("wC", "v"))
            K.dma("pool", v2b[:], v2_d[l - 1], r=[], semkey="cloadp")
        K.dma("pool", w2b[:], w2_d[l], r=[], semkey="cloadp")
        K.dma("pool", a2b[64:128, :], a2_d[l], r=[], semkey="cloadp")
        K.dma("pool", g2b[:], g2_d[l], r=[], semkey="cloadp")
        K.dma("sp", lnp[:], ln_d[l].rearrange("p (a b) -> p a b", a=2), r=[], semkey="cload")
        K.dma("sp", am[:], cst["cAM"].rearrange("p (a j t) -> p a j t", a=2, j=2), r=[], semkey="cload")
        K.dma("sp", xm[:], cst["cXM"].rearrange("p (a t) -> p a t", a=4), r=[], semkey="cload")
        K.dma("sp", rst[:], cst["cRST"][:, :], r=[], semkey="cload")
        K.dma("pool", bo[:], cst["cBO"][:, :], r=[], semkey="cloadp")
        K.memset("dve", zx[:, :, 0:1], 0.0, w=[zx])
        K.memset("dve", H[:], 0.0)
        K.memset("pool", Hb[:], 0.0)
        lng = lnp[:, 0, :].rearrange("p (a v) -> p a v", a=4)
        lnb = lnp[:, 1, :].rearrange("p (a v) -> p a v", a=4)

        for tg in range(NG):
            tok = slice(tg * 512, (tg + 1) * 512)
            K.dma("sp", hg[:], hT_d[:, :, tok], r=[("hT", tg)])
            for cc in range(ncc):
                pb = nb_(0, 4)
                if cc < 14:
                    for kc in range(8):
                        K.mm(pb[:], wC[:, kc, cc * 128:(cc + 1) * 128], hg[:, kc, :], start=(kc == 0), stop=(kc == 7),
                             r=wCk + [hg])
                    K.cp("act", zx[:, cc, 1:513], pb[:])
                else:
                    for kc in range(8):
                        K.mm(pb[0:32, :], wC[:, kc, 1792:1824], hg[:, kc, :], start=(kc == 0), stop=(kc == 7), r=wCk + [hg])
                    K.cp("act", zx[0:32, cc, 1:513], pb[0:32, :])
            for cc in range(ncc):
                np_ = 128 if cc < 14 else 32
                mu = col(l, "mu", cc) if cc < 14 else col(l, "vmu", 0)
                tp = tmp[cc % 2]
                K.tt("pool" if cc % 2 else "dve", tp[0:np_, :], zx[0:np_, cc, 0:512], zx[0:np_, cc, 1:513], ALU.subtract)
                K.stt(zl[0:np_, cc, :], tp[0:np_, :], mu[0:np_, :], zx[0:np_, cc, 1:513], ALU.mult, ALU.add,
                      r=[tp, zx] + colkeys, w=[("zl", cc)])
            zlk = [("zl", cc) for cc in range(ncc)]
            K.cp("dve", zx[:, :, 0:1], zx[:, :, 512:513], r=[zx] + zlk, w=[zx])
            K.act(tw[:], zl[0:64, 12, :], AF.Tanh, r=[("zl", 12)])
            K.cp("dve", al[64:128, :], zl[64:128, 12, :], r=[("zl", 12)])
            K.act(sgl[:], zl[:, 13, :], AF.Sigmoid, r=[("zl", 13)])
            if l >= 1:
                K.cp("dve", vlr[:], zl[0:32, 14, :], r=[("zl", 14)])
                K.dma("sp", vfp[:], vf_d[:, :, tok], r=[("vf", tg)])
            for pc in range(4):
                rT = zl[:, pc, :]
                kT = zl[:, 4 + pc, :]
                vT = zl[:, 8 + pc, :]
                rk_ = [("zl", pc)]
                kk_ = [("zl", 4 + pc)]
                vk_ = [("zl", 8 + pc)]
                pw = nb_(0, 4)
                K.mm(pw[:], w2b[0:64, pc * 128:(pc + 1) * 128], tw[0:64, :])
                K.act(sigw[:], pw[:], AF.Sigmoid, bias=col(l, "w0", pc), r=[pw] + colkeys)
                pa = nb_(0, 4)
                K.mm(pa[:], a2b[64:128, pc * 128:(pc + 1) * 128], al[64:128, :])
                K.act(iclr[:], pa[:], AF.Sigmoid, bias=col(l, "a0", pc), r=[pa] + colkeys)
                K.op("dve", lambda e: e.tensor_tensor_scan(out=cl[:], data0=rst[:], data1=sigw[:], initial=0.0, op0=ALU.mult, op1=ALU.add),
                     [rst, sigw], [cl])
                K.act(Pinc[:], cl[:], AF.Exp, scale=-C0)
                K.act(Pinv[:], cl[:], AF.Exp, scale=C0)
                K.tt("pool", tmp[2][:], cl[:], sigw[:], ALU.subtract)
                K.act(Pexc[:], tmp[2][:], AF.Exp, scale=-C0)
                K.cp("dve", pend[:, pc, :], Pinc[:].rearrange("p (c t) -> p c t", c=8)[:, :, 63])
                K.ts("dve", kkr[:], kT, col(l, "kk", pc), None, ALU.mult, r=kk_ + colkeys)
                K.act(sqb[:], kkr[:], AF.Square)
                pss = nb_(0, 4)
                K.mm(pss[:], bo[:], sqb[:])
                K.act(tmp[0][:], pss[:], AF.Sqrt)
                K.ts("dve", tmp[0][:], tmp[0][:], 1e-12, None, ALU.max)
                K.op("dve", lambda e: e.reciprocal(out=tmp[0][:], in_=tmp[0][:]), [tmp[0]], [tmp[0]])
                K.tt("dve", kkr[:], kkr[:], tmp[0][:], ALU.mult)
                ARv = AR[:, pc].rearrange("p c j t -> p j c t")
                BKv = BK[:, pc].rearrange("p c j t -> p j c t")
                c3 = lambda ap: ap.rearrange("p (c t) -> p c t", c=8)
                K.stt(ARv[:, 0], c3(kkr[:]), -1.0, c3(Pexc[:]), ALU.mult, ALU.mult, r=[kkr, Pexc], w=[("AR", pc, 0)])
                K.tt("pool", tmp[1][:], kkr[:], iclr[:], ALU.mult)
                K.tt("dve", BKv[:, 0], c3(tmp[1][:]), c3(Pinv[:]), ALU.mult, r=[tmp[1], Pinv], w=[("BK", pc, 0)])
                K.ts("dve", tmp[2][:], iclr[:], 1.0, col(l, "ka", pc), ALU.subtract, ALU.mult, r=[iclr] + colkeys)
                K.stt(kmod[:], tmp[2][:], 1.0, kT, ALU.add, ALU.mult, r=[tmp[2]] + kk_)
                K.tt("dve", BKv[:, 1], c3(kmod[:]), c3(Pinv[:]), ALU.mult, r=[kmod, Pinv], w=[("BK", pc, 1)])
                K.tt("pool", ARv[:, 1], c3(rT), c3(Pinc[:]), ALU.mult, r=rk_ + [Pinc], w=[("AR", pc, 1)])
                K.stt(prk[:, pc, :], rT, col(l, "rk", pc), kmod[:], ALU.mult, ALU.mult, r=rk_ + [kmod] + colkeys, w=[("prk", pc)])
                if l == 0:
                    pass
                else:
                    pv_ = nb_(0, 4)
                    K.mm(pv_[:], v2b[0:32, pc * 128:(pc + 1) * 128], vlr[0:32, :])
                    K.act(tmp[0][:], pv_[:], AF.Sigmoid, bias=col(l, "v0", pc), r=[pv_] + colkeys)
                    K.tt("dve", tmp[1][:], vfp[:, pc, :], vT, ALU.subtract, r=[vfp] + vk_)
                    K.tt("dve", tmp[1][:], tmp[1][:], tmp[0][:], ALU.mult)
                    K.tt("dve", vT, vT, tmp[1][:], ALU.add, r=vk_ + [tmp[1]], w=vk_)
            K.cp("pool", vbf[:], zl[:, 8:12, :], r=[("zl", 8 + i) for i in range(4)])
            if l == 0:
                K.dma("sp", vf_d[:, :, tok], zl[:, 8:12, :], r=[("zl", 8 + i) for i in range(4)], w=[("vf", tg)], semkey="vfst")
            ARk = [("AR", pc, j) for pc in range(4) for j in range(2)]
            BKk = [("BK", pc, j) for pc in range(4) for j in range(2)]
            prkk = [("prk", pc) for pc in range(4)]
            vks = [("zl", 8 + i) for i in range(4)]
            yT_ = ycT[tg % 2]
            for c in range(8):
                ct = slice(c * 64, (c + 1) * 64)

                def hs(hh):
                    return slice(hh * 64, (hh + 1) * 64)
                pV = banks[4]
                pVv = pV[:].bitcast(BF16)[:, 0:256].rearrange("p (a v) -> p a v", a=4)
                for pc in range(4):
                    for hh in range(2):
                        K.tr(pVv[hs(hh), pc, :], vbf[hs(hh), pc, ct], identb[hs(hh), hs(hh)])
                K.cp("act", Vc[:], pVv)
                K.cp("dve", Vcf[:], pVv)
                pB = banks[5]
                pBv = pB[:].bitcast(BF16)[:, 0:512].rearrange("p (a j k) -> p a j k", a=4, j=2)
                for pc in range(4):
                    for hh in range(2):
                        for j in range(2):
                            K.tr(pBv[hs(hh), pc, j, :], BK[hs(hh), pc, c, j, :], identb[hs(hh), hs(hh)], r=BKk + [identb])
                K.cp("act", BKt[:], pBv)
                for pb2 in range(2):
                    pA = banks[6 + pb2]
                    pAv = pA[:].rearrange("p (a j t) -> p a j t", a=2, j=2)
                    for a_ in range(2):
                        pc = pb2 * 2 + a_
                        for hh in range(2):
                            rhs = AR[hs(hh), pc, c].rearrange("p j t -> p (j t)")
                            for j in range(2):
                                K.mm(pAv[hs(hh), a_, j, :], BK[hs(hh), pc, c, j, :], rhs, r=ARk + BKk)
                    K.tt("dve", Amat[:, pb2 * 2:pb2 * 2 + 2], pAv, am[:], ALU.mult, w=[("Amat", pb2)])
                Ak = [("Amat", 0), ("Amat", 1)]
                pX = banks[4]
                for pc in range(4):
                    for hh in range(2):
                        K.mm(pX[hs(hh), pc * 64:(pc + 1) * 64], AR[hs(hh), pc, c, 0, :], BK[hs(hh), pc, c, 0, :], r=ARk + BKk)
                X = Xm[0]
                K.tt("dve", X[:, 0], pX[:, 0:256].rearrange("p (a s) -> p a s", a=4), xm[:], ALU.mult, w=[("X", 0)])
                K.cp("act", X[:, 1], Amat[:, :, 0, 0:64], r=Ak, w=[("Y", 0)])
                pW = banks[5]
                for pc in range(4):
                    for hh in range(2):
                        o_ = pW[hs(hh), pc * 64:(pc + 1) * 64]
                        K.mm(o_, AR[hs(hh), pc, c, 0, :], Hb[hs(hh), pc, :], start=True, stop=False, r=ARk + [Hb])
                        K.mm(o_, Amat[hs(hh), pc, 1, 0:64], Vc[hs(hh), pc, :], start=False, stop=True, r=Ak + [Vc])
                W = Wt[0]
                K.cp("dve", W[:], pW[:, 0:256].rearrange("p (a v) -> p a v", a=4))
                xi = 0
                wi = 0
                for lev in range(6):
                    X = Xm[xi % 2]
                    xk = [("X", xi % 2), ("Y", xi % 2)]
                    pWn = banks[5 + (lev % 2)]
                    for pc in range(4):
                        for hh in range(2):
                            K.mm(pWn[hs(hh), pc * 64:(pc + 1) * 64], X[hs(hh), 1, pc, :], Wt[wi % 2][hs(hh), pc, :], r=xk + [Wt[wi % 2]])
                    K.tt("dve", Wt[(wi + 1) % 2][:], Wt[wi % 2][:], pWn[:, 0:256].rearrange("p (a v) -> p a v", a=4), ALU.add)
                    wi += 1
                    if lev < 5:
                        pXn = banks[4]
                        for pc in range(4):
                            for hh in range(2):
                                K.mm(pXn[hs(hh), pc * 64:(pc + 1) * 64], X[hs(hh), 1, pc, :], X[hs(hh), 0, pc, :], r=xk)
                                K.mm(pXn[hs(hh), 256 + pc * 64:256 + (pc + 1) * 64], X[hs(hh), 0, pc, :], X[hs(hh), 1, pc, :], r=xk)
                        Xn = Xm[(xi + 1) % 2]
                        K.cp("act", Xn[:], pXn[:].rearrange("p (j a s) -> p j a s", j=2, a=4),
                             w=[("X", (xi + 1) % 2), ("Y", (xi + 1) % 2)])
                        xi += 1
                U = Wt[wi % 2]
                pO = banks[6]
                for pc in range(4):
                    for hh in range(2):
                        o_ = pO[hs(hh), pc * 64:(pc + 1) * 64]
                        K.mm(o_, AR[hs(hh), pc, c, 1, :], Hb[hs(hh), pc, :], start=True, stop=False, r=ARk + [Hb])
                        K.mm(o_, Amat[hs(hh), pc, 0, 64:128], U[hs(hh), pc, :], start=False, stop=False, r=Ak + [U])
                        K.mm(o_, Amat[hs(hh), pc, 1, 64:128], Vc[hs(hh), pc, :], start=False, stop=True, r=Ak + [Vc])
                pH = banks[7]
                for pc in range(4):
                    for hh in range(2):
                        o_ = pH[hs(hh), pc * 64:(pc + 1) * 64]
                        K.mm(o_, BKt[hs(hh), pc, 0, :], U[hs(hh), pc, :], start=True, stop=False)
                        K.mm(o_, BKt[hs(hh), pc, 1, :], Vc[hs(hh), pc, :], start=False, stop=True)
                K.tt("dve", H[:], H[:], pH[:, 0:256].rearrange("p (a v) -> p a v", a=4), ALU.add)
                K.tt("dve", H[:], H[:], pend[:, :, c:c + 1].to_broadcast([128, 4, 64]), ALU.mult)
                K.cp("act", Hb[:], H[:])
                pG = banks[4]
                for pc in range(4):
                    for hh in range(2):
                        K.mm(pG[hs(hh), pc * 64:(pc + 1) * 64], sgl[:, ct], g2b[:, (2 * pc + hh) * 64:(2 * pc + hh + 1) * 64])
                        K.mm(pG[hs(hh), 256 + pc:256 + pc + 1], prk[hs(hh), pc, ct], bo[hs(hh), hh * 64:hh * 64 + 1], r=prkk + [bo])
                K.cp("act", gt[:], pG[:, 0:256].rearrange("p (a v) -> p a v", a=4))
                K.cp("dve", bs[:], pG[:, 256:260])
                for pc in range(4):
                    K.op("dve", lambda e, pc=pc: e.bn_stats(out=st[:, pc, :], in_=pO[:, pc * 64:(pc + 1) * 64]), [pO], [("stC", pc)])
                    K.op("dve", lambda e, pc=pc: e.bn_aggr(out=mv[:, pc, :], in_=st[:, pc, :]), [("stC", pc)], [("mvC", pc)])
                mvk = [("mvC", pc) for pc in range(4)]
                K.act(rstd[:], mv[:, :, 1], AF.Sqrt, bias=epsc[:, 1:2], scale=1.0, r=mvk + [epsc])
                K.op("dve", lambda e: e.reciprocal(out=rstd[:], in_=rstd[:]), [rstd], [rstd])
                for pc in range(4):
                    K.ts("dve", yn[:, pc, :], pO[:, pc * 64:(pc + 1) * 64], mv[:, pc, 0:1], rstd[:, pc:pc + 1], ALU.subtract, ALU.mult,
                         r=[pO, rstd] + mvk, w=[("ynC", pc)])
                ynk = [("ynC", pc) for pc in range(4)]
                K.tt("dve", yn[:], yn[:], lng, ALU.mult, r=ynk + [lnp], w=[yn])
                K.tt("pool", yn[:], yn[:], lnb, ALU.add, r=[yn, lnp], w=[yn])
                K.tt("dve", Vcf[:], Vcf[:], bs[:].rearrange("p (a o) -> p a o", o=1).to_broadcast([128, 4, 64]), ALU.mult)
                K.tt("pool", yn[:], yn[:], Vcf[:], ALU.add)
                K.tt("dve", yc[:], yn[:], gt[:], ALU.mult)
                pY = banks[5]
                pYv = pY[:].bitcast(BF16)[:, 0:256].rearrange("p (a t) -> p a t", a=4)
                for pc in range(4):
                    for hh in range(2):
                        K.tr(pYv[hs(hh), pc, :], yc[hs(hh), pc, :], identb[hs(hh), hs(hh)])
                K.cp("act", yT_[:, :, ct], pYv)
            K.dma("sp", yT_d[2, :, :, tok], yT_[:], w=[("yT", 2, tg)], semkey="yTs")


_CACHE = {}


def make_in_maps(inp, S, depth, n_cores):
    consts = host_consts(S)
    colsarr = np.stack([pack_cols(inp, l) for l in range(depth)])
    lnarr = np.stack([pack_ln(inp, l) for l in range(depth)])
    maps = []
    f = lambda a: np.ascontiguousarray(np.asarray(a, np.float32))
    shared = {
        "w_in": f(inp["w_in"]), "c_w2": f(inp["c_w2"]), "c_a2": f(inp["c_a2"]), "c_g2": f(inp["c_g2"]),
        "c_vres_down": f(inp["c_vres_down"]), "c_v2": f(inp["c_v2"]), "w_branch": f(inp["w_branch"]),
        "w_out": f(inp["w_out"]), "w_gate_up": f(inp["w_gate_up"]), "w_down": f(inp["w_down"]),
        "w_ple_gate": f(inp["w_ple_gate"]), "w_ple_proj": f(inp["w_ple_proj"]), "cols": colsarr, "lnp": lnarr,
    }
    shared.update(consts)
    x = np.asarray(inp["x"], np.float32)
    p = np.asarray(inp["p"], np.float32)
    for b in range(n_cores):
        m = dict(shared)
        m["x"] = np.ascontiguousarray(x[b])
        m["p"] = np.ascontiguousarray(p[:, b])
        maps.append(m)
    return maps


def kernel(**inputs):
    x = np.asarray(inputs["x"])
    B, S, _ = x.shape
    depth = np.asarray(inputs["w_in"]).shape[0]
    key = (S, depth)
    if key not in _CACHE:
        _CACHE[key] = build(S, depth)[0]
    nc = _CACHE[key]
    maps = make_in_maps(inputs, S, depth, B)
    res = run_bass_kernel_spmd(nc, maps, core_ids=list(range(B)))
    return np.stack([np.asarray(r["out"], np.float32) for r in res.results], axis=0)
```

```python
import math
from contextlib import ExitStack
import numpy as np
import concourse.bass as bass
import concourse.mybir as mybir
from concourse.bass_utils import run_bass_kernel_spmd

F32 = mybir.dt.float32
BF16 = mybir.dt.bfloat16
ALU = mybir.AluOpType
AF = mybir.ActivationFunctionType
AX = mybir.AxisListType

D = 1024
NIN = 8448
DFF = 2816
EPS = 1e-6
NEG = -30000.0
C0 = math.exp(-0.5)
SEM_LIMIT = 30000
import os
PA_STOP = int(os.environ.get("PA_STOP", "9"))
PA_SKIP = os.environ.get("PA_SKIP", "")


class Ctx:
    ENGS = ("pe", "dve", "act", "pool", "sp")

    def __init__(self, nc):
        self.nc = nc
        self.prog = {e: [] for e in self.ENGS}
        self.sems = {}
        self.semval = {}
        self.cur = {}
        self.waited = {e: {} for e in self.ENGS}
        self.res = {}
        self.nsem = 0
        self.ninstr = 0
        self.banktag = {}

    def _semkey(self, logical, step):
        sk = self.cur.get(logical)
        if sk is None or self.semval[sk] + step > SEM_LIMIT:
            ep = 0 if sk is None else sk[1] + 1
            sk = (logical, ep)
            self.sems[sk] = self.nc.alloc_semaphore(name=f"s{self.nsem}")
            self.nsem += 1
            self.semval[sk] = 0
            self.cur[logical] = sk
        return sk

    @staticmethod
    def _key(x):
        if isinstance(x, (str, tuple)):
            return x
        if hasattr(x, "tensor"):
            return x.tensor.name
        return x.name

    def _collect(self, reads, writes):
        deps = {}

        def add(d):
            if d is not None:
                deps[d[0]] = max(deps.get(d[0], 0), d[1])
        for r in reads:
            st = self.res.get(r)
            if st:
                add(st["w"])
        for w in writes:
            st = self.res.get(w)
            if st:
                add(st["w"])
                for sk, v in st["r"].items():
                    add((sk, v))
        return deps

    def _emit_waits(self, e, deps):
        for sk, v in deps.items():
            if self.waited[e].get(sk, 0) < v:
                h = self.sems[sk]
                self.prog[e].append(lambda eng, h=h, v=v: eng.wait_ge(h, v))
                self.waited[e][sk] = v

    def _update(self, reads, writes, sk, v):
        for r in reads:
            st = self.res.setdefault(r, {"w": None, "r": {}})
            st["r"][sk] = max(st["r"].get(sk, 0), v)
        for w in writes:
            self.res[w] = {"w": (sk, v), "r": {}}

    def op(self, e, fn, r=(), w=(), petag=None):
        reads = [self._key(x) for x in r]
        writes = [self._key(x) for x in w]
        writes = writes + [k for k in reads if isinstance(k, str) and k.startswith("bank") and k not in writes]
        skip = []
        if e == "pe" and petag is not None:
            for k in writes:
                if isinstance(k, str) and k.startswith("bank"):
                    st = self.res.get(k)
                    if st and st["w"] is not None and st["w"][0][0] == ("eng", "pe") and not st["r"] \
                            and self.banktag.get(k) == petag:
                        skip.append(k)
                    self.banktag[k] = petag
        deps = self._collect(reads, [k for k in writes if k not in skip])
        self._emit_waits(e, deps)
        sk = self._semkey(("eng", e), 1)
        self.semval[sk] += 1
        v = self.semval[sk]
        h = self.sems[sk]
        self.prog[e].append(lambda eng, fn=fn, h=h: fn(eng).then_inc(h, 1))
        self._update(reads, writes, sk, v)
        self.ninstr += 1

    def dma(self, q, out, in_, r=None, w=None, semkey=None, **kw):
        reads = [self._key(x) for x in (r if r is not None else [in_])]
        writes = [self._key(x) for x in (w if w is not None else [out])]
        if semkey is None:
            semkey = out.tensor.name
        lk = ("dma", semkey)
        sk = self._semkey(lk, 16)
        deps = self._collect(reads, writes)
        if self.semval[sk] > 0:
            deps[sk] = max(deps.get(sk, 0), self.semval[sk])
        self._emit_waits(q, deps)
        self.semval[sk] += 16
        v = self.semval[sk]
        h = self.sems[sk]
        self.prog[q].append(
            lambda eng, out=out, in_=in_, kw=kw, h=h: eng.dma_start(out=out, in_=in_, **kw).then_inc(h, 16))
        self._update(reads, writes, sk, v)
        self.ninstr += 1

    def barrier(self):
        deps = {sk: v for sk, v in self.semval.items() if v > 0}
        for e in self.ENGS:
            self._emit_waits(e, deps)

    def final_wait(self, e, keys):
        deps = self._collect([self._key(k) for k in keys], ())
        self._emit_waits(e, deps)

    def mm(self, out, lhsT, rhs, start=True, stop=True, r=None, w=None):
        tag = (lhsT.start_partition(), lhsT.partition_size())
        self.op("pe", lambda e: e.matmul(out, lhsT=lhsT, rhs=rhs, start=start, stop=stop),
                r if r is not None else [lhsT, rhs], w if w is not None else [out], petag=tag)

    def tr(self, out, in_, ident, r=None, w=None):
        tag = (in_.start_partition(), in_.partition_size())
        self.op("pe", lambda e: e.transpose(out=out, in_=in_, identity=ident),
                r if r is not None else [in_, ident], w if w is not None else [out], petag=tag)

    def act(self, out, in_, func, bias=None, scale=None, r=None, w=None, eng="act"):
        kw = {}
        if bias is not None:
            kw["bias"] = bias
        if scale is not None:
            kw["scale"] = scale
        rr = [in_] + [x for x in (bias, scale) if not isinstance(x, (int, float, type(None)))]
        self.op("act", lambda e: e.activation(out=out, in_=in_, func=func, **kw),
                r if r is not None else rr, w if w is not None else [out])

    def tt(self, eng, out, in0, in1, op, r=None, w=None):
        self.op(eng, lambda e: e.tensor_tensor(out=out, in0=in0, in1=in1, op=op),
                r if r is not None else [in0, in1], w if w is not None else [out])

    def ts(self, eng, out, in0, s1, s2, op0, op1=None, r=None, w=None):
        rr = [in0] + [x for x in (s1, s2) if not isinstance(x, (int, float, type(None)))]
        if op1 is None:
            fn = lambda e: e.tensor_scalar(out=out, in0=in0, scalar1=s1, scalar2=None, op0=op0)
        else:
            fn = lambda e: e.tensor_scalar(out=out, in0=in0, scalar1=s1, scalar2=s2, op0=op0, op1=op1)
        self.op(eng, fn, r if r is not None else rr, w if w is not None else [out])

    def stt(self, out, in0, scalar, in1, op0, op1, r=None, w=None):
        rr = [in0, in1] + ([scalar] if not isinstance(scalar, (int, float)) else [])
        self.op("dve", lambda e: e.scalar_tensor_tensor(out=out, in0=in0, scalar=scalar, in1=in1, op0=op0, op1=op1),
                r if r is not None else rr, w if w is not None else [out])

    def cp(self, eng, out, in_, r=None, w=None):
        if eng == "act":
            fn = lambda e: e.copy(out=out, in_=in_)
        else:
            fn = lambda e: e.tensor_copy(out=out, in_=in_)
        self.op(eng, fn, r if r is not None else [in_], w if w is not None else [out])

    def memset(self, eng, ap, val, w=None):
        self.op(eng, lambda e: e.memset(ap, val), [], w if w is not None else [ap])

    def replay(self):
        nc = self.nc
        with nc.Block() as block:
            @block.tensor
            def _(eng):
                for f in self.prog["pe"]:
                    f(eng)

            @block.vector
            def _(eng):
                for f in self.prog["dve"]:
                    f(eng)

            @block.scalar
            def _(eng):
                for f in self.prog["act"]:
                    f(eng)

            @block.gpsimd
            def _(eng):
                for f in self.prog["pool"]:
                    f(eng)

            @block.sync
            def _(eng):
                for f in self.prog["sp"]:
                    f(eng)


def host_consts(S):
    c = {}
    pos = np.arange(S, dtype=np.float32)
    inv_a = (1.0 / (np.float32(500000.0) ** (np.arange(0, 16, 2, dtype=np.float32) / np.float32(16)))).astype(np.float32)
    ang = (pos[:, None] * inv_a[None, :]).astype(np.float32)
    cos_a, sin_a = np.cos(ang).astype(np.float32), np.sin(ang).astype(np.float32)
    ca = np.zeros((S, 2, 2, 8, 8), np.float32)
    ca[:, 0] = cos_a[:, None, None, :]
    ca[:, 1] = sin_a[:, None, None, :]
    c["cA"] = ca.reshape(S, 256)
    inv_b = (1.0 / (np.float32(10000.0) ** np.linspace(0.0, 1.0, 64, dtype=np.float32))).astype(np.float32)
    angb = (pos[:, None] * inv_b[None, :]).astype(np.float32)
    cos_b, sin_b = np.cos(angb).astype(np.float64), np.sin(angb).astype(np.float64)
    lg = np.log(1.0 - 2.0 ** (-5.0 - np.arange(4, dtype=np.float64)))
    i = (np.arange(S) % 128).astype(np.float64)
    gq = np.exp(lg[None, :] * i[:, None])
    gk = np.exp(-lg[None, :] * i[:, None]) * (128.0 ** -0.5)
    cb = np.zeros((S, 4, 4, 64), np.float64)
    cb[:, 0] = cos_b[:, None, :] * gq[:, :, None]
    cb[:, 1] = sin_b[:, None, :] * gq[:, :, None]
    cb[:, 2] = cos_b[:, None, :] * gk[:, :, None]
    cb[:, 3] = sin_b[:, None, :] * gk[:, :, None]
    c["cB"] = cb.reshape(S, 1024).astype(np.float32)
    gam = np.exp(lg)
    rs = np.zeros((128, 12), np.float32)
    rs[:, 0:4] = (gam ** 128)[None, :]
    rs[:, 4:8] = (gam ** 127)[None, :]
    rs[:, 8:12] = gam[None, :]
    c["cRS"] = rs
    E = np.zeros((16, S), np.float32)
    for j in range(16):
        E[j, j * 256:(j + 1) * 256] = 1.0
    c["cE"] = E
    cm = np.zeros((2, 128, 256), np.float32)
    for kt in range(2):
        k = kt * 128 + np.arange(128)[:, None]
        q = np.arange(256)[None, :]
        cm[kt] = np.where(k <= q, 0.0, NEG)
    c["cCM"] = cm.transpose(1, 0, 2).reshape(128, 512)
    j = np.arange(128)[:, None]
    ii = np.arange(128)[None, :]
    m = (j <= ii).astype(np.float32)
    c["cRM"] = np.tile(m[:, None, :], (1, 4, 1)).reshape(128, 512)
    s = (np.arange(128) % 64)[:, None]
    t = np.arange(64)[None, :]
    strict = (s < t).astype(np.float32)
    incl = (s <= t).astype(np.float32)
    am = np.concatenate([strict, incl], axis=1)
    c["cAM"] = np.tile(am[:, None, :], (1, 4, 1)).reshape(128, 512)
    tt_ = (np.arange(128) % 64)[:, None]
    ss_ = np.arange(64)[None, :]
    xm = (ss_ < tt_).astype(np.float32)
    c["cXM"] = np.tile(xm[:, None, :], (1, 4, 1)).reshape(128, 256)
    rm = np.ones((128, 512), np.float32)
    rm[:, ::64] = 0.0
    c["cRST"] = rm
    c["cID"] = np.eye(128, dtype=np.float32)
    bo = np.zeros((128, 128), np.float32)
    bo[:64, :64] = 1.0
    bo[64:, 64:] = 1.0
    c["cBO"] = bo
    return c


CONST_SHAPES = lambda S: {"cA": [S, 256], "cB": [S, 1024], "cRS": [128, 12], "cE": [16, S], "cCM": [128, 512],
                          "cRM": [128, 512], "cAM": [128, 512], "cXM": [128, 256], "cRST": [128, 512],
                          "cID": [128, 128], "cBO": [128, 128]}

COLS = {"mixg": (0, 8), "ffng": (8, 8), "pleg": (16, 8), "fing": (24, 8), "mu": (32, 14), "w0": (46, 4),
        "a0": (50, 4), "kk": (54, 4), "ka": (58, 4), "rk": (62, 4), "v0": (66, 4), "vmu": (70, 1)}
NCOL = 72


def pack_cols(inp, l):
    out = np.zeros((128, NCOL), np.float32)

    def put(name, vec):
        o, n = COLS[name]
        v = np.asarray(vec, np.float32).reshape(-1)
        out[:, o:o + n] = v.reshape(n, 128).T
    put("mixg", inp["norm_mix_g"][l])
    put("ffng", inp["norm_ffn_g"][l])
    put("pleg", inp["norm_ple_g"][l])
    put("fing", inp["final_norm_g"])
    put("mu", inp["c_mu"][l])
    put("w0", inp["c_w0"][l])
    put("a0", inp["c_a0"][l])
    put("kk", inp["c_k_k"][l])
    put("ka", inp["c_k_a"][l])
    put("rk", inp["c_r_k"][l])
    if l >= 1:
        put("v0", inp["c_v0"][l - 1])
        out[0:32, COLS["vmu"][0]] = np.asarray(inp["c_vres_mu"][l - 1], np.float32)
    return out


def pack_ln(inp, l):
    out = np.zeros((128, 2, 4, 64), np.float32)
    for k, name in enumerate(("c_ln_g", "c_ln_b")):
        v = np.asarray(inp[name][l], np.float32).reshape(4, 2, 64)
        for hh in range(2):
            out[hh * 64:(hh + 1) * 64, k] = v[:, hh, :][None]
    return out.reshape(128, 512)


def build(S, depth=2, en="abc", dbg=()):
    NT = S // 128
    NG = S // 512
    NB = S // 256
    nc = bass.Bass("TRN2", target_bir_lowering=False)

    def din(name, shape, dt=F32):
        return nc.dram_tensor(name, list(shape), dt, kind="ExternalInput").ap()

    def dscr(name, shape, dt=F32):
        return nc.dram_tensor(name, list(shape), dt, kind="Internal").ap()

    x_d = din("x", [S, D])
    p_d = din("p", [depth, S, 256])
    w_in = din("w_in", [depth, D, NIN])
    w2_d = din("c_w2", [depth, 64, 512])
    a2_d = din("c_a2", [depth, 64, 512])
    g2_d = din("c_g2", [depth, 128, 512])
    vd_d = din("c_vres_down", [max(depth - 1, 1), D, 32])
    v2_d = din("c_v2", [max(depth - 1, 1), 32, 512])
    wbr_d = din("w_branch", [depth, 3, 512, D])
    wout_d = din("w_out", [depth, D, D])
    wgu_d = din("w_gate_up", [depth, D, 2 * DFF])
    wd_d = din("w_down", [depth, DFF, D])
    wpg_d = din("w_ple_gate", [depth, D, D])
    wpp_d = din("w_ple_proj", [depth, 256, D])
    cols_d = din("cols", [depth, 128, NCOL])
    ln_d = din("lnp", [depth, 128, 512])
    cst = {k: din(k, shp) for k, shp in CONST_SHAPES(S).items()}
    out_d = nc.dram_tensor("out", [S, D], F32, kind="ExternalOutput").ap()

    xT_d = dscr("xT_d", [128, 8, S])
    hT_d = dscr("hT_d", [128, 8, S], BF16)
    yT_d = dscr("yT_d", [3, 128, 4, S], BF16)
    vf_d = dscr("vf_d", [128, 4, S])
    NINX = NIN + 32
    Wb_in = dscr("Wb_in", [depth, 128, 8, NINX], BF16)
    Wb_br = dscr("Wb_br", [depth, 128, 3, 4, D], BF16)
    Wb_out = dscr("Wb_out", [depth, 128, 8, D], BF16)
    Wb_gu = dscr("Wb_gu", [depth, 128, 8, 2 * DFF], BF16)
    Wb_d = dscr("Wb_d", [depth, 128, 22, D], BF16)
    Wb_pg = dscr("Wb_pg", [depth, 128, 8, D], BF16)
    Wb_pp = dscr("Wb_pp", [depth, 128, 2, D], BF16)
    dbg_d = {}
    for name, shp in dbg:
        dbg_d[name] = nc.dram_tensor("dbg_" + name, list(shp), F32, kind="ExternalOutput").ap()

    K = Ctx(nc)
    uniq = [0]
    with ExitStack() as top:
        def sbt(es, name, shape, dt=F32):
            uniq[0] += 1
            return es.enter_context(nc.sbuf_tensor(f"s_{name}_{uniq[0]}", list(shape), dt))

        banks = [top.enter_context(nc.psum_tensor(f"bank{i}", [128, 512], F32)) for i in range(8)]
        bank_rr = [0]

        def nb_(lo=0, hi=8):
            b = banks[lo + bank_rr[0] % (hi - lo)]
            bank_rr[0] += 1
            return b

        ident = sbt(top, "ident", [128, 128])
        identb = sbt(top, "identb", [128, 128], BF16)
        onesb = sbt(top, "onesb", [128, 128], BF16)
        cols = sbt(top, "cols", [128, depth, NCOL])
        K.dma("sp", ident[:], cst["cID"][:, :], r=[], semkey="cload")
        K.cp("dve", identb[:], ident[:])
        K.memset("dve", onesb[:], 1.0)
        for l in range(depth):
            K.dma("sp", cols[:, l, :], cols_d[l], r=[], w=[("cols", l)], semkey="cload")
        colkeys = [("cols", l) for l in range(depth)]

        def col(l, name, j=0, n=1):
            o, _ = COLS[name]
            return cols[:, l, o + j:o + j + n]

        def convert_weights(l):
            for c0 in range(0, NIN, 512):
                cw = min(512, NIN - c0)
                K.dma("pool", Wb_in[l, :, :, c0:c0 + cw], w_in[l, :, c0:c0 + cw].rearrange("(k p) n -> p k n", p=128),
                      r=[], w=[("Wb_in", l, c0 // 512)], semkey="conv")
            if l >= 1:
                K.dma("pool", Wb_in[l, :, :, NIN:NINX], vd_d[l - 1].rearrange("(k p) n -> p k n", p=128),
                      r=[], w=[("Wb_in", l, "v")], semkey="conv")
            for n in range(3):
                for h in range(2):
                    K.dma("pool", Wb_br[l, :, n, :, h * 512:(h + 1) * 512],
                          wbr_d[l, n, :, h * 512:(h + 1) * 512].rearrange("(k p) n -> p k n", p=128), r=[], w=[("Wb_br", l)], semkey="conv")
            for h in range(2):
                K.dma("pool", Wb_out[l, :, :, h * 512:(h + 1) * 512],
                      wout_d[l, :, h * 512:(h + 1) * 512].rearrange("(k p) n -> p k n", p=128), r=[], w=[("Wb_out", l)], semkey="conv")
                K.dma("pool", Wb_pg[l, :, :, h * 512:(h + 1) * 512],
                      wpg_d[l, :, h * 512:(h + 1) * 512].rearrange("(k p) n -> p k n", p=128), r=[], w=[("Wb_pg", l)], semkey="conv")
                K.dma("pool", Wb_pp[l, :, :, h * 512:(h + 1) * 512],
                      wpp_d[l, :, h * 512:(h + 1) * 512].rearrange("(k p) n -> p k n", p=128), r=[], w=[("Wb_pp", l)], semkey="conv")
                for k0 in (0, 11):
                    K.dma("pool", Wb_d[l, :, k0:k0 + 11, h * 512:(h + 1) * 512],
                          wd_d[l, k0 * 128:(k0 + 11) * 128, h * 512:(h + 1) * 512].rearrange("(k p) n -> p k n", p=128),
                          r=[], w=[("Wb_d", l)], semkey="conv")
            for c0 in range(0, 2 * DFF, 512):
                K.dma("pool", Wb_gu[l, :, :, c0:c0 + 512], wgu_d[l, :, c0:c0 + 512].rearrange("(k p) n -> p k n", p=128),
                      r=[], w=[("Wb_gu", l)], semkey="conv")

        def norm_group(es_name, xg, hg_out, l, gname, scratch):
            sq, rstd = scratch
            pb = nb_()
            for c in range(8):
                K.act(sq[:, c, :], xg[:, c, :], AF.Square)
            for c in range(8):
                K.mm(pb[:], onesb[:], sq[:, c, :], start=(c == 0), stop=(c == 7))
            K.act(rstd[:], pb[:], AF.Sqrt, bias=epsc[:, 0:1], scale=1.0 / D)
            K.op("dve", lambda e: e.reciprocal(out=rstd[:], in_=rstd[:]), [rstd], [rstd])
            for c in range(8):
                K.stt(hg_out[:, c, :], xg[:, c, :], col(l, gname, c), rstd[:], ALU.mult, ALU.mult,
                      r=[xg, rstd] + colkeys)

        epsc = sbt(top, "epsc", [128, 4])
        K.memset("dve", epsc[:, 0:1], EPS)
        K.memset("dve", epsc[:, 1:2], 1e-5 * 64)
        K.memset("dve", epsc[:, 2:3], 0.0)

        convert_weights(0)

        with ExitStack() as es:
            xin = [sbt(es, f"xin{i}", [128, D]) for i in range(2)]
            xg2 = [sbt(es, f"xgI{i}", [128, 8, 512]) for i in range(2)]
            hg2 = [sbt(es, f"hgI{i}", [128, 8, 512], BF16) for i in range(2)]
            sq = sbt(es, "sqI", [128, 8, 512], BF16)
            rstd = sbt(es, "rstdI", [128, 512])
            for tg in range(NG):
                xg = xg2[tg % 2]
                hg = hg2[tg % 2]
                for tt_ in range(4):
                    t = tg * 4 + tt_
                    xi = xin[t % 2]
                    K.dma("sp", xi[:], x_d[t * 128:(t + 1) * 128, :], r=[])
                    for half in range(2):
                        pb = nb_()
                        for c in range(4):
                            K.tr(pb[:, c * 128:(c + 1) * 128], xi[:, (half * 4 + c) * 128:(half * 4 + c + 1) * 128], ident[:])
                        K.cp("act" if half else "dve", xg[:, half * 4:(half + 1) * 4, tt_ * 128:(tt_ + 1) * 128],
                             pb[:].rearrange("p (c t) -> p c t", c=4))
                K.dma("sp", xT_d[:, :, tg * 512:(tg + 1) * 512], xg[:], w=[("xT", tg)], semkey="xTst")
                norm_group("I", xg, hg, 0, "mixg", (sq, rstd))
                K.dma("sp", hT_d[:, :, tg * 512:(tg + 1) * 512], hg[:], w=[("hT", tg)], semkey="hTst")

        K.barrier()
        for l in range(depth):
            if l + 1 < depth:
                convert_weights(l + 1)
            last = (l == depth - 1)
            if "a" in en:
                phase_a(nc, K, sbt, banks, nb_, l, S, cst, Wb_in, hT_d, yT_d, ident, identb)
                K.barrier()
            if "b" in en:
                phase_b(nc, K, sbt, banks, nb_, l, S, cst, Wb_in, hT_d, yT_d, ident, identb, epsc)
                K.barrier()
            if "c" in en:
                phase_c(nc, K, sbt, banks, nb_, l, S, cst, Wb_in, hT_d, yT_d, vf_d, ident, identb, epsc, cols, col, colkeys,
                        w2_d, a2_d, g2_d, v2_d, ln_d, dbg_d)
                K.barrier()

            with ExitStack() as es:
                xg2 = [sbt(es, f"xgT{i}", [128, 8, 512]) for i in range(2)]
                hg = sbt(es, "hgT", [128, 8, 512], BF16)
                yg = sbt(es, "ygT", [128, 3, 4, 512], BF16)
                hf = sbt(es, "hfT", [128, 8, 512], BF16)
                mg = sbt(es, "mgT", [128, 8, 512], BF16)
                actT = sbt(es, "actT", [128, 22, 512], BF16)
                sq = sbt(es, "sqT", [128, 8, 512], BF16)
                rstd = sbt(es, "rstdT", [128, 512])
                sg = [sbt(es, f"sgT{i}", [128, 512]) for i in range(2)]
                acc = sbt(es, "accT", [128, 512])
                tmpf = [sbt(es, f"tmpT{i}", [128, 512]) for i in range(2)]
                wst = [sbt(es, f"wst{i}", [128, 8, 1024], BF16) for i in range(2)]
                wdt = [sbt(es, f"wdt{i}", [128, 22, 128], BF16) for i in range(2)]
                wbrt = sbt(es, "wbrt", [128, 3, 4, 128], BF16)
                wppt = sbt(es, "wppt", [128, 2, D], BF16)
                pin = [sbt(es, f"pin{i}", [128, 256]) for i in range(2)]
                pT = sbt(es, "pTT", [128, 2, 512], BF16)
                ot = [sbt(es, f"otT{i}", [128, D]) for i in range(2)]
                wsi = [0]

                def wslab():
                    t_ = wst[wsi[0] % 2]
                    wsi[0] += 1
                    return t_
                K.dma("sp", wppt[:], Wb_pp[l], r=[("Wb_pp", l)], semkey="cload")
                for tg in range(NG):
                    xg = xg2[tg % 2]
                    tok = slice(tg * 512, (tg + 1) * 512)
                    K.dma("sp", xg[:], xT_d[:, :, tok], r=[("xT", tg)])
                    K.dma("sp", hg[:], hT_d[:, :, tok], r=[("hT", tg)])
                    for n in range(3):
                        if "abc"[n] in en:
                            K.dma("sp", yg[:, n], yT_d[n, :, :, tok], r=[("yT", n, tg)], w=[("ygT", n)], semkey="ygT")
                        elif tg == 0:
                            K.memset("pool", yg[:, n], 0.0, w=[("ygT", n)])
                    ygk = [("ygT", n) for n in range(3)]
                    for dc in range(8):
                        ws = wslab()
                        for n in range(3):
                            c0 = 5376 + n * 1024 + dc * 128
                            K.dma("sp", ws[:, :, n * 128:(n + 1) * 128], Wb_in[l, :, :, c0:c0 + 128],
                                  r=[("Wb_in", l, c0 // 512)], w=[ws])
                        K.dma("sp", wbrt[:], Wb_br[l, :, :, :, dc * 128:(dc + 1) * 128], r=[("Wb_br", l)])
                        for n in range(3):
                            pbr = nb_()
                            pgt = nb_()
                            for kc in range(4):
                                K.mm(pbr[:], wbrt[:, n, kc, :], yg[:, n, kc, :], start=(kc == 0), stop=(kc == 3),
                                     r=[wbrt, ("ygT", n)])
                            for kc in range(8):
                                K.mm(pgt[:], ws[:, kc, n * 128:(n + 1) * 128], hg[:, kc, :], start=(kc == 0), stop=(kc == 7))
                            s_ = sg[n % 2]
                            K.act(s_[:], pgt[:], AF.Sigmoid)
                            if n == 0:
                                K.tt("dve", acc[:], s_[:], pbr[:], ALU.mult)
                            elif n == 1:
                                K.tt("dve", tmpf[0][:], s_[:], pbr[:], ALU.mult)
                                K.tt("pool", acc[:], acc[:], tmpf[0][:], ALU.add)
                            else:
                                K.tt("dve", tmpf[1][:], s_[:], pbr[:], ALU.mult)
                                K.tt("dve", mg[:, dc, :], acc[:], tmpf[1][:], ALU.add)
                    for dc in range(8):
                        if dc % 4 == 0:
                            ws = wslab()
                            K.dma("sp", ws[:, :, 0:512], Wb_out[l, :, :, dc * 128:dc * 128 + 512], r=[("Wb_out", l)], w=[ws])
                        po = nb_()
                        for kc in range(8):
                            K.mm(po[:], ws[:, kc, (dc % 4) * 128:(dc % 4 + 1) * 128], mg[:, kc, :], start=(kc == 0), stop=(kc == 7))
                        K.tt("dve", xg[:, dc, :], xg[:, dc, :], po[:], ALU.add)
                    norm_group("T", xg, hf, l, "ffng", (sq, rstd))
                    for f4 in range(0, 22, 4):
                        nf = min(4, 22 - f4)
                        ws = wslab()
                        K.dma("sp", ws[:, :, 0:nf * 128], Wb_gu[l, :, :, f4 * 128:(f4 + nf) * 128], r=[("Wb_gu", l)], w=[ws])
                        K.dma("sp", ws[:, :, 512:512 + nf * 128], Wb_gu[l, :, :, DFF + f4 * 128:DFF + (f4 + nf) * 128],
                              r=[("Wb_gu", l)], w=[ws])
                        for fi in range(nf):
                            fc = f4 + fi
                            pg_ = nb_()
                            pu_ = nb_()
                            for kc in range(8):
                                K.mm(pg_[:], ws[:, kc, fi * 128:(fi + 1) * 128], hf[:, kc, :], start=(kc == 0), stop=(kc == 7))
                            for kc in range(8):
                                K.mm(pu_[:], ws[:, kc, 512 + fi * 128:512 + (fi + 1) * 128], hf[:, kc, :], start=(kc == 0), stop=(kc == 7))
                            s_ = sg[fc % 2]
                            K.act(s_[:], pg_[:], AF.Silu)
                            K.tt("dve", actT[:, fc, :], s_[:], pu_[:], ALU.mult)
                    for dc in range(8):
                        wd_ = wdt[dc % 2]
                        K.dma("sp", wd_[:], Wb_d[l, :, :, dc * 128:(dc + 1) * 128], r=[("Wb_d", l)])
                        pd = nb_()
                        for fc in range(22):
                            K.mm(pd[:], wd_[:, fc, :], actT[:, fc, :], start=(fc == 0), stop=(fc == 21))
                        K.tt("dve", xg[:, dc, :], xg[:, dc, :], pd[:], ALU.add)
                    norm_group("T", xg, hf, l, "pleg", (sq, rstd))
                    for tt_ in range(4):
                        pi = pin[tt_ % 2]
                        K.dma("sp", pi[:], p_d[l, tg * 512 + tt_ * 128: tg * 512 + (tt_ + 1) * 128, :], r=[])
                        pb = nb_()
                        for c in range(2):
                            K.tr(pb[:, c * 128:(c + 1) * 128], pi[:, c * 128:(c + 1) * 128], ident[:])
                        K.cp("act", pT[:, :, tt_ * 128:(tt_ + 1) * 128], pb[:, 0:256].rearrange("p (c t) -> p c t", c=2))
                    for dc in range(8):
                        if dc % 4 == 0:
                            ws = wslab()
                            K.dma("sp", ws[:, :, 0:512], Wb_pg[l, :, :, dc * 128:dc * 128 + 512], r=[("Wb_pg", l)], w=[ws])
                        pg_ = nb_()
                        pp_ = nb_()
                        for kc in range(8):
                            K.mm(pg_[:], ws[:, kc, (dc % 4) * 128:(dc % 4 + 1) * 128], hf[:, kc, :], start=(kc == 0), stop=(kc == 7))
                        for kc in range(2):
                            K.mm(pp_[:], wppt[:, kc, dc * 128:(dc + 1) * 128], pT[:, kc, :], start=(kc == 0), stop=(kc == 1))
                        s_ = sg[dc % 2]
                        K.act(s_[:], pg_[:], AF.Sigmoid)
                        K.tt("dve", tmpf[dc % 2][:], s_[:], pp_[:], ALU.mult)
                        K.tt("pool", xg[:, dc, :], xg[:, dc, :], tmpf[dc % 2][:], ALU.add)
                    if not last:
                        K.dma("sp", xT_d[:, :, tok], xg[:], w=[("xT", tg)], semkey="xTst")
                        norm_group("T", xg, hf, l + 1, "mixg", (sq, rstd))
                        K.dma("sp", hT_d[:, :, tok], hf[:], w=[("hT", tg)], semkey="hTst")
                    else:
                        pb = nb_()
                        for c in range(8):
                            K.act(sq[:, c, :], xg[:, c, :], AF.Square)
                        for c in range(8):
                            K.mm(pb[:], onesb[:], sq[:, c, :], start=(c == 0), stop=(c == 7))
                        K.act(rstd[:], pb[:], AF.Sqrt, bias=epsc[:, 0:1], scale=1.0 / D)
                        K.op("dve", lambda e: e.reciprocal(out=rstd[:], in_=rstd[:]), [rstd], [rstd])
                        for c in range(8):
                            K.stt(xg[:, c, :], xg[:, c, :], col(l, "fing", c), rstd[:], ALU.mult, ALU.mult,
                                  r=[xg, rstd] + colkeys)
                        for tt_ in range(4):
                            o_ = ot[tt_ % 2]
                            for half in range(2):
                                pb2 = nb_()
                                for c in range(4):
                                    K.tr(pb2[:, c * 128:(c + 1) * 128], xg[:, half * 4 + c, tt_ * 128:(tt_ + 1) * 128], ident[:])
                                K.cp("act" if half else "dve", o_[:, half * 512:(half + 1) * 512], pb2[:])
                            K.dma("sp", out_d[tg * 512 + tt_ * 128: tg * 512 + (tt_ + 1) * 128, :], o_[:], w=["out"],
                                  semkey="out")
            K.barrier()
        K.final_wait("sp", ["out"] + ["dbg_" + n for n in dbg_d])
        K.replay()
    return nc, K


def phase_a(nc, K, sbt, banks, nb_, l, S, cst, Wb_in, hT_d, yT_d, ident, identb):
    NT = S // 128
    NB = S // 256
    with ExitStack() as es:
        wA = sbt(es, "wA", [128, 8, 1536], BF16)
        KT = sbt(es, "KT", [80, 8, S], BF16)
        Va = sbt(es, "Va", [128, NT, 8, 65], BF16)
        QT = [sbt(es, f"QT{i}", [80, 8, 256], BF16) for i in range(2)]
        QTf = [sbt(es, f"QTf{i}", [64, 8, 128]) for i in range(2)]
        kms = sbt(es, "kms", [64, 8, 16])
        ktmp = sbt(es, "ktmp", [64, 8])
        hTt = [sbt(es, f"hTtA{i}", [128, 8, 128], BF16) for i in range(2)]
        qk = [sbt(es, f"qkA{i}", [128, 2, 8, 64]) for i in range(2)]
        cs = [sbt(es, f"csA{i}", [128, 2, 128]) for i in range(2)]
        rt = [sbt(es, f"rtA{i}", [128, 128]) for i in range(4)]
        gs = sbt(es, "gsA", [128, 8, 16])
        m8 = sbt(es, "m8A", [128, 8, 8])
        Mp = sbt(es, "MpA", [128, 8, 80], BF16)
        cm = sbt(es, "cmA", [128, 2, 256], BF16)
        PT = [sbt(es, f"PTA{i}", [128, 256], BF16) for i in range(3)]
        ya = [sbt(es, f"yaA{i}", [128, 512], BF16) for i in range(2)]
        rden = sbt(es, "rdenA", [128, 2])
        yTs = [sbt(es, f"yTsA{i}", [128, 4, 128], BF16) for i in range(2)]
        for g in range(3):
            K.dma("sp", wA[:, :, g * 512:(g + 1) * 512], Wb_in[l, :, :, g * 512:(g + 1) * 512], r=[("Wb_in", l, g)],
                  w=[("wA", g)], semkey="wload")
        wAk = [("wA", g) for g in range(3)]
        K.dma("pool", cm[:], cst["cCM"].rearrange("p (k q) -> p k q", k=2), r=[], semkey="cloadp")
        for h in range(8 if "e" not in PA_SKIP else 0):
            for e0 in range(0, S, 2048):
                e1 = min(S, e0 + 2048)
                K.dma("pool", KT[64:80, h, e0:e1], cst["cE"][:, e0:e1], r=[], w=[("KTE", h)], semkey="cloadp")
        KTE = [("KTE", h) for h in range(8)]
        if "m" not in PA_SKIP:
            K.memset("pool", Va[:, :, :, 64:65], 1.0, w=["Va1"])
            K.memset("pool", Mp[:], 0.0)
        pz = [banks[0], banks[1]]
        pTq = [banks[2], banks[3]]
        pS = [banks[4], banks[5]]
        pO = [banks[6], banks[7]]
        pti = 0
        psi = 0
        for t in range(NT):
            b = t // 2
            half = t % 2
            hT = hTt[t % 2]
            K.dma("sp", hT[:], hT_d[:, :, t * 128:(t + 1) * 128], r=[("hT", t // 4)])
            c_ = cs[t % 2]
            K.dma("sp", c_[:], cst["cA"][t * 128:(t + 1) * 128, :].rearrange("p (a b) -> p a b", a=2), r=[])
            q_ = qk[t % 2]
            for g in range(3):
                pb = pz[g % 2]
                for kc in range(8):
                    K.mm(pb[:], hT[:, kc, :], wA[:, kc, g * 512:(g + 1) * 512], start=(kc == 0), stop=(kc == 7),
                         r=[hT, ("wA", g)])
                if g < 2:
                    K.cp("act", q_[:, g], pb[:].rearrange("p (h d) -> p h d", h=8))
                elif "v" not in PA_SKIP:
                    K.cp("act", Va[:, t, :, 0:64], pb[:].rearrange("p (h d) -> p h d", h=8), w=[("Va", t)])
            x1 = q_[:, :, :, 0:8]
            x2 = q_[:, :, :, 8:16]
            co = c_[:, 0, :].rearrange("p (a h d) -> p a h d", a=2, h=8)
            si = c_[:, 1, :].rearrange("p (a h d) -> p a h d", a=2, h=8)
            t1, t2, t3, t4 = [r_[:].rearrange("p (a h d) -> p a h d", a=2, h=8) for r_ in rt]
            if "r" not in PA_SKIP:
                K.tt("dve", t1, x1, co, ALU.mult)
                K.tt("pool", t2, x2, si, ALU.mult)
                K.tt("dve", t3, x1, si, ALU.mult)
                K.tt("pool", t4, x2, co, ALU.mult)
                K.tt("dve", x1, t1, t2, ALU.subtract)
                K.tt("dve", x2, t3, t4, ALU.add)
            for g in range(2 if "t" not in PA_SKIP else 0):
                for hq in range(2):
                    pb = pTq[hq]
                    for h4 in range(4):
                        h = hq * 4 + h4
                        K.tr(pb[0:64, h4 * 128:(h4 + 1) * 128], q_[:, g, h, :], ident[:])
                    src = pb[0:64, :].rearrange("p (h t) -> p h t", h=4)
                    if g == 0:
                        K.cp("act", QT[b % 2][0:64, hq * 4:(hq + 1) * 4, half * 128:(half + 1) * 128], src,
                             w=[("QTq", b % 2)])
                        K.cp("dve", QTf[half][:, hq * 4:(hq + 1) * 4, :], src)
                    else:
                        K.cp("act", KT[0:64, hq * 4:(hq + 1) * 4, t * 128:(t + 1) * 128], src, w=[("KT", t)])
                        K.op("dve", lambda e, src=src, hq=hq: e.tensor_reduce(out=ktmp[:, hq * 4:(hq + 1) * 4], in_=src, axis=AX.X, op=ALU.add),
                             [pb], [ktmp])
                if g == 1:
                    if half == 0:
                        K.cp("dve", kms[:, :, b:b + 1], ktmp[:].rearrange("p (h o) -> p h o", o=1))
                    else:
                        K.tt("dve", kms[:, :, b:b + 1], kms[:, :, b:b + 1], ktmp[:].rearrange("p (h o) -> p h o", o=1), ALU.add)
            if half == 0 or PA_STOP <= 1:
                continue
            Qb = QT[b % 2]
            if b >= 1:
                for hf_ in range(2):
                    pg = banks[2]
                    for h in range(8):
                        K.mm(pg[:, h * 16:(h + 1) * 16], QTf[hf_][0:64, h, :], kms[0:64, h, :])
                    K.cp("dve", gs[:], pg[:, 0:128].rearrange("p (h n) -> p h n", h=8))
                    if b < 16:
                        K.memset("dve", gs[:, :, b:16], -1e30)
                    for h in range(8):
                        K.op("dve", lambda e, h=h: e.max(out=m8[:, h, :], in_=gs[:, h, :]), [gs], [m8])
                    for h in range(8):
                        K.ts("dve", Mp[:, h, 64:80], gs[:, h, :], m8[:, h, 2:3], NEG, ALU.is_lt, ALU.mult)
                    pm = banks[3]
                    pmv = pm[:].bitcast(BF16).rearrange("p (h t) -> p h t", h=8)
                    for h in range(8):
                        K.tr(pmv[0:80, h, :], Mp[:, h, :], identb[:])
                    K.cp("act", Qb[64:80, :, hf_ * 128:(hf_ + 1) * 128], pmv[64:80, :, :], w=[("QTm", b % 2, hf_)])
            Qkeys = [("QTq", b % 2), ("QTm", b % 2, 0), ("QTm", b % 2, 1)]
            if PA_STOP <= 2:
                continue
            nkt = 2 * b + 2
            for h in range(8):
                for kt in range(nkt):
                    own = kt >= 2 * b
                    Kr = 64 if own else 80
                    ps_ = pS[psi % 2]
                    psi += 1
                    K.mm(ps_[:, 0:256], KT[0:Kr, h, kt * 128:(kt + 1) * 128], Qb[0:Kr, h, :], start=True, stop=not own,
                         r=[("KT", kt), ("KTE", h)] + Qkeys)
                    if own:
                        K.mm(ps_[:, 0:256], identb[:], cm[:, kt - 2 * b, :], start=False, stop=True)
                    P_ = PT[pti % 3]
                    pti += 1
                    K.act(P_[:], ps_[:, 0:256], AF.Exp, scale=0.125)
                    for q2 in range(2):
                        if own and kt - 2 * b == 1 and q2 == 0:
                            continue
                        lastk = (2 * b) if q2 == 0 else (2 * b + 1)
                        K.mm(pO[q2][:, 0:65], P_[:, q2 * 128:(q2 + 1) * 128], Va[:, kt, h, :], start=(kt == 0), stop=(kt == lastk),
                             r=[P_, ("Va", kt), "Va1"])
                for q2 in range(2):
                    K.op("dve", lambda e, q2=q2: e.reciprocal(out=rden[:, q2:q2 + 1], in_=pO[q2][:, 64:65]), [pO[q2]], [("rden", q2)])
                    K.ts("dve", ya[q2][:, h * 64:(h + 1) * 64], pO[q2][:, 0:64], rden[:, q2:q2 + 1], None, ALU.mult,
                         r=[pO[q2], ("rden", q2)])
            if PA_STOP <= 3:
                continue
            for q2 in range(2):
                tq = 2 * b + q2
                pb = banks[q2]
                pv = pb[:].bitcast(BF16)[:, 0:512].rearrange("p (c t) -> p c t", c=4)
                for c in range(4):
                    K.tr(pv[:, c, :], ya[q2][:, c * 128:(c + 1) * 128], identb[:])
                K.cp("act", yTs[q2][:], pv)
                K.dma("sp", yT_d[0, :, :, tq * 128:(tq + 1) * 128], yTs[q2][:], w=[("yT", 0, tq // 4)], semkey="yTs")


def phase_b(nc, K, sbt, banks, nb_, l, S, cst, Wb_in, hT_d, yT_d, ident, identb, epsc):
    NT = S // 128
    with ExitStack() as es:
        wB = sbt(es, "wB", [128, 8, 2048], BF16)
        hTt = [sbt(es, f"hTtB{i}", [128, 8, 128], BF16) for i in range(2)]
        cb = [sbt(es, f"cbB{i}", [128, 4, 256]) for i in range(2)]
        qs = [sbt(es, f"qsB{i}", [128, 512]) for i in range(2)]
        rt = [sbt(es, f"rtB{i}", [128, 256]) for i in range(4)]
        qr = [sbt(es, f"qrB{i}", [128, 512], BF16) for i in range(2)]
        QTb = sbt(es, "QTbB", [128, 4, 128], BF16)
        KTb = sbt(es, "KTbB", [128, 4, 128], BF16)
        vb = sbt(es, "vbB", [128, 512], BF16)
        sgb = sbt(es, "sgB", [128, 512])
        Sm = sbt(es, "SmB", [128, 4, 128], BF16)
        rm = sbt(es, "rmB", [128, 4, 128], BF16)
        R = sbt(es, "RB", [128, 4, 128])
        Rg = sbt(es, "RgB", [128, 4, 128], BF16)
        rs = sbt(es, "rsB", [128, 12])
        st = sbt(es, "stB", [128, 4, 6])
        mv = sbt(es, "mvB", [128, 4, 2])
        rstd = sbt(es, "rstdB", [128, 4])
        yn = sbt(es, "ynB", [128, 512])
        yb = sbt(es, "ybB", [128, 512], BF16)
        yTs = [sbt(es, f"yTsB{i}", [128, 4, 128], BF16) for i in range(2)]
        for g in range(4):
            K.dma("sp", wB[:, :, g * 512:(g + 1) * 512], Wb_in[l, :, :, 1536 + g * 512:1536 + (g + 1) * 512],
                  r=[("Wb_in", l, 3 + g)], w=[("wB", g)], semkey="wload")
        K.dma("pool", rm[:], cst["cRM"].rearrange("p (h t) -> p h t", h=4), r=[], semkey="cloadp")
        K.dma("sp", rs[:], cst["cRS"][:, :], r=[], semkey="cload")
        K.memset("dve", R[:], 0.0, w=[("RB", h) for h in range(4)])
        K.memset("pool", Rg[:], 0.0)
        for t in range(NT):
            hT = hTt[t % 2]
            K.dma("sp", hT[:], hT_d[:, :, t * 128:(t + 1) * 128], r=[("hT", t // 4)])
            c_ = cb[t % 2]
            K.dma("sp", c_[:], cst["cB"][t * 128:(t + 1) * 128, :].rearrange("p (a b) -> p a b", a=4), r=[])
            for g in range(4):
                pb = nb_(0, 4)
                for kc in range(8):
                    K.mm(pb[:], hT[:, kc, :], wB[:, kc, g * 512:(g + 1) * 512], start=(kc == 0), stop=(kc == 7),
                         r=[hT, ("wB", g)])
                if g < 2:
                    q_ = qs[g]
                    K.cp("act", q_[:], pb[:])
                    xv = q_[:].rearrange("p (h d two) -> p h d two", h=4, two=2)
                    xe = xv[:, :, :, 0]
                    xo = xv[:, :, :, 1]
                    co = c_[:, 2 * g, :].rearrange("p (h d) -> p h d", h=4)
                    si = c_[:, 2 * g + 1, :].rearrange("p (h d) -> p h d", h=4)
                    t1, t2, t3, t4 = [r_[:].rearrange("p (h d) -> p h d", h=4) for r_ in rt]
                    ov = qr[g][:].rearrange("p (h d two) -> p h d two", h=4, two=2)
                    K.tt("dve", t1, xe, co, ALU.mult)
                    K.tt("pool", t2, xo, si, ALU.mult)
                    K.tt("dve", t3, xe, si, ALU.mult)
                    K.tt("pool", t4, xo, co, ALU.mult)
                    K.tt("dve", ov[:, :, :, 0], t1, t2, ALU.subtract, w=[(f"qrB{g}", 0)], r=[rt[0], rt[1]])
                    K.tt("dve", ov[:, :, :, 1], t3, t4, ALU.add, w=[(f"qrB{g}", 1)], r=[rt[2], rt[3]])
                    pbt = nb_(0, 4)
                    pv = pbt[:].bitcast(BF16)[:, 0:512].rearrange("p (h t) -> p h t", h=4)
                    for h in range(4):
                        K.tr(pv[:, h, :], qr[g][:, h * 128:(h + 1) * 128], identb[:], r=[(f"qrB{g}", 0), (f"qrB{g}", 1), identb])
                    K.cp("act", (QTb if g == 0 else KTb)[:], pv)
                elif g == 2:
                    K.cp("act", vb[:], pb[:])
                else:
                    K.act(sgb[:], pb[:], AF.Silu)
            qrk = [("qrB1", 0), ("qrB1", 1)]
            pS_ = banks[4]
            for h in range(4):
                K.mm(pS_[:, h * 128:(h + 1) * 128], KTb[:, h, :], QTb[:, h, :])
            K.tt("dve", Sm[:], pS_[:].rearrange("p (h t) -> p h t", h=4), rm[:], ALU.mult)
            py = banks[5]
            pkv = banks[6]
            for h in range(4):
                K.mm(py[:, h * 128:(h + 1) * 128], Sm[:, h, :], vb[:, h * 128:(h + 1) * 128], start=True, stop=False)
                K.mm(py[:, h * 128:(h + 1) * 128], QTb[:, h, :], Rg[:, h, :], start=False, stop=True)
            for h in range(4):
                K.mm(pkv[:, h * 128:(h + 1) * 128], qr[1][:, h * 128:(h + 1) * 128], vb[:, h * 128:(h + 1) * 128],
                     r=qrk + [vb])
            for h in range(4):
                K.op("dve", lambda e, h=h: e.bn_stats(out=st[:, h, :], in_=py[:, h * 128:(h + 1) * 128]), [py], [("stB", h)])
                K.op("dve", lambda e, h=h: e.bn_aggr(out=mv[:, h, :], in_=st[:, h, :]), [("stB", h)], [("mvB", h)])
            mvk = [("mvB", h) for h in range(4)]
            K.act(rstd[:], mv[:, :, 1], AF.Sqrt, bias=epsc[:, 0:1], scale=1.0, r=mvk + [epsc])
            K.op("dve", lambda e: e.reciprocal(out=rstd[:], in_=rstd[:]), [rstd], [rstd])
            for h in range(4):
                K.ts("dve", yn[:, h * 128:(h + 1) * 128], py[:, h * 128:(h + 1) * 128], mv[:, h, 0:1], rstd[:, h:h + 1],
                     ALU.subtract, ALU.mult, r=[py, rstd] + mvk, w=[("ynB", h)])
            K.tt("pool", yb[:], yn[:], sgb[:], ALU.mult, r=[("ynB", h) for h in range(4)] + [sgb])
            for h in range(4):
                K.ts("pool", R[:, h, :], R[:, h, :], rs[:, h:h + 1], None, ALU.mult, r=[("RB", h), rs], w=[("RB", h)])
                K.stt(R[:, h, :], pkv[:, h * 128:(h + 1) * 128], rs[:, 4 + h:5 + h], R[:, h, :], ALU.mult, ALU.add,
                      r=[pkv, rs, ("RB", h)], w=[("RB", h)])
                K.act(Rg[:, h, :], R[:, h, :], AF.Identity, scale=float(cst_gamma(h)), r=[("RB", h)], w=[Rg])
            pbt = banks[7]
            pv = pbt[:].bitcast(BF16)[:, 0:512].rearrange("p (c t) -> p c t", c=4)
            for c in range(4):
                K.tr(pv[:, c, :], yb[:, c * 128:(c + 1) * 128], identb[:])
            K.cp("act", yTs[t % 2][:], pv)
            K.dma("sp", yT_d[1, :, :, t * 128:(t + 1) * 128], yTs[t % 2][:], w=[("yT", 1, t // 4)], semkey="yTs")


def cst_gamma(h):
    return 1.0 - 2.0 ** (-5.0 - h)


def phase_c(nc, K, sbt, banks, nb_, l, S, cst, Wb_in, hT_d, yT_d, vf_d, ident, identb, epsc, cols, col, colkeys,
            w2_d, a2_d, g2_d, v2_d, ln_d, dbg_d):
    NG = S // 512
    RW = BF16
    ncc = 15 if l >= 1 else 14
    with ExitStack() as es:
        wC = sbt(es, "wC", [128, 8, 1920], BF16)
        w2b = sbt(es, "w2b", [64, 512], BF16)
        a2b = sbt(es, "a2b", [128, 512], BF16)
        g2b = sbt(es, "g2b", [128, 512], BF16)
        v2b = sbt(es, "v2b", [32, 512], BF16)
        lnp = sbt(es, "lnp", [128, 2, 256])
        am = sbt(es, "amC", [128, 2, 2, 128])
        xm = sbt(es, "xmC", [128, 4, 64])
        rst = sbt(es, "rstC", [128, 512])
        bo = sbt(es, "boC", [128, 128], BF16)
        hg = sbt(es, "hgC", [128, 8, 512], BF16)
        zx = sbt(es, "zxC", [128, 15, 513])
        zl = sbt(es, "zlC", [128, 15, 512])
        tmp = [sbt(es, f"tmpC{i}", [128, 512]) for i in range(3)]
        tw = sbt(es, "twC", [64, 512], BF16)
        al = sbt(es, "alC", [128, 512], BF16)
        sgl = sbt(es, "sglC", [128, 512], BF16)
        vlr = sbt(es, "vlrC", [32, 512], BF16)
        sigw = sbt(es, "sigwC", [128, 512])
        iclr = sbt(es, "iclrC", [128, 512])
        cl = sbt(es, "clC", [128, 512])
        Pinc = sbt(es, "PincC", [128, 512])
        Pexc = sbt(es, "PexcC", [128, 512])
        Pinv = sbt(es, "PinvC", [128, 512])
        kkr = sbt(es, "kkrC", [128, 512])
        sqb = sbt(es, "sqbC", [128, 512], BF16)
        kmod = sbt(es, "kmodC", [128, 512])
        vfp = sbt(es, "vfpC", [128, 4, 512])
        vbf = sbt(es, "vbfC", [128, 4, 512], BF16)
        AR = sbt(es, "ARC", [128, 4, 8, 2, 64], RW)
        BK = sbt(es, "BKC", [128, 4, 8, 2, 64], RW)
        prk = sbt(es, "prkC", [128, 4, 512], BF16)
        pend = sbt(es, "pendC", [128, 4, 8])
        H = sbt(es, "HC", [128, 4, 64])
        Hb = sbt(es, "HbC", [128, 4, 64], RW)
        Amat = sbt(es, "AmatC", [128, 4, 2, 128], RW)
        Xm = [sbt(es, f"XmC{i}", [128, 2, 4, 64], RW) for i in range(2)]
        Wt = [sbt(es, f"WtC{i}", [128, 4, 64], RW) for i in range(2)]
        Vc = sbt(es, "VcC", [128, 4, 64], RW)
        Vcf = sbt(es, "VcfC", [128, 4, 64])
        BKt = sbt(es, "BKtC", [128, 4, 2, 64], RW)
        gt = sbt(es, "gtC", [128, 4, 64])
        bs = sbt(es, "bsC", [128, 4])
        st = sbt(es, "stC", [128, 4, 6])
        mv = sbt(es, "mvC", [128, 4, 2])
        rstd = sbt(es, "rstdC", [128, 4])
        yn = sbt(es, "ynC", [128, 4, 64])
        yc = sbt(es, "ycC", [128, 4, 64], BF16)
        ycT = [sbt(es, f"ycTC{i}", [128, 4, 512], BF16) for i in range(2)]

        nwc = 1792 + (32 if l >= 1 else 0)
        for g in range(0, 1792, 512):
            gw = min(512, 1792 - g)
            K.dma("sp", wC[:, :, g:g + gw], Wb_in[l, :, :, 3584 + g:3584 + g + gw],
                  r=[("Wb_in", l, (3584 + g) // 512)], w=[("wC", g)], semkey="wload")
        wCk = [("wC", g) for g in range(0, 1792, 512)]
        if l >= 1:
            K.dma("sp", wC[:, :, 1792:1824], Wb_in[l, :, :, NIN:NIN + 32], r=[("Wb_in", l, "v")], w=[("wC", "v")], semkey="wload")
            wCk.append(("wC", "v"))
            K.dma("pool", v2b[:], v2_d[l - 1], r=[], semkey="cloadp")
        K.dma("pool", w2b[:], w2_d[l], r=[], semkey="cloadp")
        K.dma("pool", a2b[64:128, :], a2_d[l], r=[], semkey="cloadp")
        K.dma("pool", g2b[:], g2_d[l], r=[], semkey="cloadp")
        K.dma("sp", lnp[:], ln_d[l].rearrange("p (a b) -> p a b", a=2), r=[], semkey="cload")
        K.dma("sp", am[:], cst["cAM"].rearrange("p (a j t) -> p a j t", a=2, j=2), r=[], semkey="cload")
        K.dma("sp", xm[:], cst["cXM"].rearrange("p (a t) -> p a t", a=4), r=[], semkey="cload")
        K.dma("sp", rst[:], cst["cRST"][:, :], r=[], semkey="cload")
        K.dma("pool", bo[:], cst["cBO"][:, :], r=[], semkey="cloadp")
        K.memset("dve", zx[:, :, 0:1], 0.0, w=[zx])
        K.memset("dve", H[:], 0.0)
        K.memset("pool", Hb[:], 0.0)
        lng = lnp[:, 0, :].rearrange("p (a v) -> p a v", a=4)
        lnb = lnp[:, 1, :].rearrange("p (a v) -> p a v", a=4)

        for tg in range(NG):
            tok = slice(tg * 512, (tg + 1) * 512)
            K.dma("sp", hg[:], hT_d[:, :, tok], r=[("hT", tg)])
            for cc in range(ncc):
                pb = nb_(0, 4)
                if cc < 14:
                    for kc in range(8):
                        K.mm(pb[:], wC[:, kc, cc * 128:(cc + 1) * 128], hg[:, kc, :], start=(kc == 0), stop=(kc == 7),
                             r=wCk + [hg])
                    K.cp("act", zx[:, cc, 1:513], pb[:])
                else:
                    for kc in range(8):
                        K.mm(pb[0:32, :], wC[:, kc, 1792:1824], hg[:, kc, :], start=(kc == 0), stop=(kc == 7), r=wCk + [hg])
                    K.cp("act", zx[0:32, cc, 1:513], pb[0:32, :])
            for cc in range(ncc):
                np_ = 128 if cc < 14 else 32
                mu = col(l, "mu", cc) if cc < 14 else col(l, "vmu", 0)
                tp = tmp[cc % 2]
                K.tt("pool" if cc % 2 else "dve", tp[0:np_, :], zx[0:np_, cc, 0:512], zx[0:np_, cc, 1:513], ALU.subtract)
                K.stt(zl[0:np_, cc, :], tp[0:np_, :], mu[0:np_, :], zx[0:np_, cc, 1:513], ALU.mult, ALU.add,
                      r=[tp, zx] + colkeys, w=[("zl", cc)])
            zlk = [("zl", cc) for cc in range(ncc)]
            K.cp("dve", zx[:, :, 0:1], zx[:, :, 512:513], r=[zx] + zlk, w=[zx])
            K.act(tw[:], zl[0:64, 12, :], AF.Tanh, r=[("zl", 12)])
            K.cp("dve", al[64:128, :], zl[64:128, 12, :], r=[("zl", 12)])
            K.act(sgl[:], zl[:, 13, :], AF.Sigmoid, r=[("zl", 13)])
            if l >= 1:
                K.cp("dve", vlr[:], zl[0:32, 14, :], r=[("zl", 14)])
                K.dma("sp", vfp[:], vf_d[:, :, tok], r=[("vf", tg)])
            for pc in range(4):
                rT = zl[:, pc, :]
                kT = zl[:, 4 + pc, :]
                vT = zl[:, 8 + pc, :]
                rk_ = [("zl", pc)]
                kk_ = [("zl", 4 + pc)]
                vk_ = [("zl", 8 + pc)]
                pw = nb_(0, 4)
                K.mm(pw[:], w2b[0:64, pc * 128:(pc + 1) * 128], tw[0:64, :])
                K.act(sigw[:], pw[:], AF.Sigmoid, bias=col(l, "w0", pc), r=[pw] + colkeys)
                pa = nb_(0, 4)
                K.mm(pa[:], a2b[64:128, pc * 128:(pc + 1) * 128], al[64:128, :])
                K.act(iclr[:], pa[:], AF.Sigmoid, bias=col(l, "a0", pc), r=[pa] + colkeys)
                K.op("dve", lambda e: e.tensor_tensor_scan(out=cl[:], data0=rst[:], data1=sigw[:], initial=0.0, op0=ALU.mult, op1=ALU.add),
                     [rst, sigw], [cl])
                K.act(Pinc[:], cl[:], AF.Exp, scale=-C0)
                K.act(Pinv[:], cl[:], AF.Exp, scale=C0)
                K.tt("pool", tmp[2][:], cl[:], sigw[:], ALU.subtract)
                K.act(Pexc[:], tmp[2][:], AF.Exp, scale=-C0)
                K.cp("dve", pend[:, pc, :], Pinc[:].rearrange("p (c t) -> p c t", c=8)[:, :, 63])
                K.ts("dve", kkr[:], kT, col(l, "kk", pc), None, ALU.mult, r=kk_ + colkeys)
                K.act(sqb[:], kkr[:], AF.Square)
                pss = nb_(0, 4)
                K.mm(pss[:], bo[:], sqb[:])
                K.act(tmp[0][:], pss[:], AF.Sqrt)
                K.ts("dve", tmp[0][:], tmp[0][:], 1e-12, None, ALU.max)
                K.op("dve", lambda e: e.reciprocal(out=tmp[0][:], in_=tmp[0][:]), [tmp[0]], [tmp[0]])
                K.tt("dve", kkr[:], kkr[:], tmp[0][:], ALU.mult)
                ARv = AR[:, pc].rearrange("p c j t -> p j c t")
                BKv = BK[:, pc].rearrange("p c j t -> p j c t")
                c3 = lambda ap: ap.rearrange("p (c t) -> p c t", c=8)
                K.stt(ARv[:, 0], c3(kkr[:]), -1.0, c3(Pexc[:]), ALU.mult, ALU.mult, r=[kkr, Pexc], w=[("AR", pc, 0)])
                K.tt("pool", tmp[1][:], kkr[:], iclr[:], ALU.mult)
                K.tt("dve", BKv[:, 0], c3(tmp[1][:]), c3(Pinv[:]), ALU.mult, r=[tmp[1], Pinv], w=[("BK", pc, 0)])
                K.ts("dve", tmp[2][:], iclr[:], 1.0, col(l, "ka", pc), ALU.subtract, ALU.mult, r=[iclr] + colkeys)
                K.stt(kmod[:], tmp[2][:], 1.0, kT, ALU.add, ALU.mult, r=[tmp[2]] + kk_)
                K.tt("dve", BKv[:, 1], c3(kmod[:]), c3(Pinv[:]), ALU.mult, r=[kmod, Pinv], w=[("BK", pc, 1)])
                K.tt("pool", ARv[:, 1], c3(rT), c3(Pinc[:]), ALU.mult, r=rk_ + [Pinc], w=[("AR", pc, 1)])
                K.stt(prk[:, pc, :], rT, col(l, "rk", pc), kmod[:], ALU.mult, ALU.mult, r=rk_ + [kmod] + colkeys, w=[("prk", pc)])
                if l == 0:
                    pass
                else:
                    pv_ = nb_(0, 4)
                    K.mm(pv_[:], v2b[0:32, pc * 128:(pc + 1) * 128], vlr[0:32, :])
                    K.act(tmp[0][:], pv_[:], AF.Sigmoid, bias=col(l, "v0", pc), r=[pv_] + colkeys)
                    K.tt("dve", tmp[1][:], vfp[:, pc, :], vT, ALU.subtract, r=[vfp] + vk_)
                    K.tt("dve", tmp[1][:], tmp[1][:], tmp[0][:], ALU.mult)
                    K.tt("dve", vT, vT, tmp[1][:], ALU.add, r=vk_ + [tmp[1]], w=vk_)
            K.cp("pool", vbf[:], zl[:, 8:12, :], r=[("zl", 8 + i) for i in range(4)])
            if l == 0:
                K.dma("sp", vf_d[:, :, tok], zl[:, 8:12, :], r=[("zl", 8 + i) for i in range(4)], w=[("vf", tg)], semkey="vfst")
            ARk = [("AR", pc, j) for pc in range(4) for j in range(2)]
            BKk = [("BK", pc, j) for pc in range(4) for j in range(2)]
            prkk = [("prk", pc) for pc in range(4)]
            vks = [("zl", 8 + i) for i in range(4)]
            yT_ = ycT[tg % 2]
            for c in range(8):
                ct = slice(c * 64, (c + 1) * 64)

                def hs(hh):
                    return slice(hh * 64, (hh + 1) * 64)
                pV = banks[4]
                pVv = pV[:].bitcast(BF16)[:, 0:256].rearrange("p (a v) -> p a v", a=4)
                for hh in range(2):
                    for pc in range(4):
                        K.tr(pVv[hs(hh), pc, :], vbf[hs(hh), pc, ct], identb[hs(hh), hs(hh)])
                K.cp("act", Vc[:], pVv)
                K.cp("dve", Vcf[:], pVv)
                pB = banks[5]
                pBv = pB[:].bitcast(BF16)[:, 0:512].rearrange("p (a j k) -> p a j k", a=4, j=2)
                for hh in range(2):
                    for pc in range(4):
                        for j in range(2):
                            K.tr(pBv[hs(hh), pc, j, :], BK[hs(hh), pc, c, j, :], identb[hs(hh), hs(hh)], r=BKk + [identb])
                K.cp("act", BKt[:], pBv)
                for pb2 in range(2):
                    pA = banks[6 + pb2]
                    pAv = pA[:].rearrange("p (a j t) -> p a j t", a=2, j=2)
                    for hh in range(2):
                        for a_ in range(2):
                            pc = pb2 * 2 + a_
                            rhs = AR[hs(hh), pc, c].rearrange("p j t -> p (j t)")
                            for j in range(2):
                                K.mm(pAv[hs(hh), a_, j, :], BK[hs(hh), pc, c, j, :], rhs, r=ARk + BKk)
                    K.tt("dve", Amat[:, pb2 * 2:pb2 * 2 + 2], pAv, am[:], ALU.mult, w=[("Amat", pb2)])
                Ak = [("Amat", 0), ("Amat", 1)]
                pX = banks[4]
                for hh in range(2):
                    for pc in range(4):
                        K.mm(pX[hs(hh), pc * 64:(pc + 1) * 64], AR[hs(hh), pc, c, 0, :], BK[hs(hh), pc, c, 0, :], r=ARk + BKk)
                X = Xm[0]
                K.tt("dve", X[:, 0], pX[:, 0:256].rearrange("p (a s) -> p a s", a=4), xm[:], ALU.mult, w=[("X", 0)])
                K.cp("act", X[:, 1], Amat[:, :, 0, 0:64], r=Ak, w=[("Y", 0)])
                pW = banks[5]
                for hh in range(2):
                    for pc in range(4):
                        o_ = pW[hs(hh), pc * 64:(pc + 1) * 64]
                        K.mm(o_, AR[hs(hh), pc, c, 0, :], Hb[hs(hh), pc, :], start=True, stop=False, r=ARk + [Hb])
                        K.mm(o_, Amat[hs(hh), pc, 1, 0:64], Vc[hs(hh), pc, :], start=False, stop=True, r=Ak + [Vc])
                W = Wt[0]
                K.cp("dve", W[:], pW[:, 0:256].rearrange("p (a v) -> p a v", a=4))
                xi = 0
                wi = 0
                for lev in range(6):
                    X = Xm[xi % 2]
                    xk = [("X", xi % 2), ("Y", xi % 2)]
                    pWn = banks[5 + (lev % 2)]
                    for pc in range(4):
                        for hh in range(2):
                            K.mm(pWn[hs(hh), pc * 64:(pc + 1) * 64], X[hs(hh), 1, pc, :], Wt[wi % 2][hs(hh), pc, :], r=xk + [Wt[wi % 2]])
                    K.tt("dve", Wt[(wi + 1) % 2][:], Wt[wi % 2][:], pWn[:, 0:256].rearrange("p (a v) -> p a v", a=4), ALU.add)
                    wi += 1
                    if lev < 5:
                        pXn = banks[4]
                        for hh in range(2):
                            for pc in range(4):
                                K.mm(pXn[hs(hh), pc * 64:(pc + 1) * 64], X[hs(hh), 1, pc, :], X[hs(hh), 0, pc, :], r=xk)
                                K.mm(pXn[hs(hh), 256 + pc * 64:256 + (pc + 1) * 64], X[hs(hh), 0, pc, :], X[hs(hh), 1, pc, :], r=xk)
                        Xn = Xm[(xi + 1) % 2]
                        K.cp("act", Xn[:], pXn[:].rearrange("p (j a s) -> p j a s", j=2, a=4),
                             w=[("X", (xi + 1) % 2), ("Y", (xi + 1) % 2)])
                        xi += 1
                U = Wt[wi % 2]
                pO = banks[6]
                for hh in range(2):
                    for pc in range(4):
                        o_ = pO[hs(hh), pc * 64:(pc + 1) * 64]
                        K.mm(o_, AR[hs(hh), pc, c, 1, :], Hb[hs(hh), pc, :], start=True, stop=False, r=ARk + [Hb])
                        K.mm(o_, Amat[hs(hh), pc, 0, 64:128], U[hs(hh), pc, :], start=False, stop=False, r=Ak + [U])
                        K.mm(o_, Amat[hs(hh), pc, 1, 64:128], Vc[hs(hh), pc, :], start=False, stop=True, r=Ak + [Vc])
                pH = banks[7]
                for hh in range(2):
                    for pc in range(4):
                        o_ = pH[hs(hh), pc * 64:(pc + 1) * 64]
                        K.mm(o_, BKt[hs(hh), pc, 0, :], U[hs(hh), pc, :], start=True, stop=False)
                        K.mm(o_, BKt[hs(hh), pc, 1, :], Vc[hs(hh), pc, :], start=False, stop=True)
                K.tt("dve", H[:], H[:], pH[:, 0:256].rearrange("p (a v) -> p a v", a=4), ALU.add)
                K.tt("dve", H[:], H[:], pend[:, :, c:c + 1].to_broadcast([128, 4, 64]), ALU.mult)
                K.cp("act", Hb[:], H[:])
                pG = banks[4]
                for hh in range(2):
                    for pc in range(4):
                        K.mm(pG[hs(hh), pc * 64:(pc + 1) * 64], sgl[:, ct], g2b[:, (2 * pc + hh) * 64:(2 * pc + hh + 1) * 64])
                        K.mm(pG[hs(hh), 256 + pc:256 + pc + 1], prk[hs(hh), pc, ct], bo[hs(hh), hh * 64:hh * 64 + 1], r=prkk + [bo])
                K.cp("act", gt[:], pG[:, 0:256].rearrange("p (a v) -> p a v", a=4))
                K.cp("dve", bs[:], pG[:, 256:260])
                for pc in range(4):
                    K.op("dve", lambda e, pc=pc: e.bn_stats(out=st[:, pc, :], in_=pO[:, pc * 64:(pc + 1) * 64]), [pO], [("stC", pc)])
                    K.op("dve", lambda e, pc=pc: e.bn_aggr(out=mv[:, pc, :], in_=st[:, pc, :]), [("stC", pc)], [("mvC", pc)])
                mvk = [("mvC", pc) for pc in range(4)]
                K.act(rstd[:], mv[:, :, 1], AF.Sqrt, bias=epsc[:, 1:2], scale=1.0, r=mvk + [epsc])
                K.op("dve", lambda e: e.reciprocal(out=rstd[:], in_=rstd[:]), [rstd], [rstd])
                for pc in range(4):
                    K.ts("dve", yn[:, pc, :], pO[:, pc * 64:(pc + 1) * 64], mv[:, pc, 0:1], rstd[:, pc:pc + 1], ALU.subtract, ALU.mult,
                         r=[pO, rstd] + mvk, w=[("ynC", pc)])
                ynk = [("ynC", pc) for pc in range(4)]
                K.tt("dve", yn[:], yn[:], lng, ALU.mult, r=ynk + [lnp], w=[yn])
                K.tt("pool", yn[:], yn[:], lnb, ALU.add, r=[yn, lnp], w=[yn])
                K.tt("dve", Vcf[:], Vcf[:], bs[:].rearrange("p (a o) -> p a o", o=1).to_broadcast([128, 4, 64]), ALU.mult)
                K.tt("pool", yn[:], yn[:], Vcf[:], ALU.add)
                K.tt("dve", yc[:], yn[:], gt[:], ALU.mult)
                pY = banks[5]
                pYv = pY[:].bitcast(BF16)[:, 0:256].rearrange("p (a t) -> p a t", a=4)
                for hh in range(2):
                    for pc in range(4):
                        K.tr(pYv[hs(hh), pc, :], yc[hs(hh), pc, :], identb[hs(hh), hs(hh)])
                K.cp("act", yT_[:, :, ct], pYv)
            K.dma("sp", yT_d[2, :, :, tok], yT_[:], w=[("yT", 2, tg)], semkey="yTs")


_CACHE = {}


def make_in_maps(inp, S, depth, n_cores):
    consts = host_consts(S)
    colsarr = np.stack([pack_cols(inp, l) for l in range(depth)])
    lnarr = np.stack([pack_ln(inp, l) for l in range(depth)])
    maps = []
    f = lambda a: np.ascontiguousarray(np.asarray(a, np.float32))
    shared = {
        "w_in": f(inp["w_in"]), "c_w2": f(inp["c_w2"]), "c_a2": f(inp["c_a2"]), "c_g2": f(inp["c_g2"]),
        "c_vres_down": f(inp["c_vres_down"]), "c_v2": f(inp["c_v2"]), "w_branch": f(inp["w_branch"]),
        "w_out": f(inp["w_out"]), "w_gate_up": f(inp["w_gate_up"]), "w_down": f(inp["w_down"]),
        "w_ple_gate": f(inp["w_ple_gate"]), "w_ple_proj": f(inp["w_ple_proj"]), "cols": colsarr, "lnp": lnarr,
    }
    shared.update(consts)
    x = np.asarray(inp["x"], np.float32)
    p = np.asarray(inp["p"], np.float32)
    for b in range(n_cores):
        m = dict(shared)
        m["x"] = np.ascontiguousarray(x[b])
        m["p"] = np.ascontiguousarray(p[:, b])
        maps.append(m)
    return maps


def kernel(**inputs):
    x = np.asarray(inputs["x"])
    B, S, _ = x.shape
    depth = np.asarray(inputs["w_in"]).shape[0]
    key = (S, depth)
    if key not in _CACHE:
        _CACHE[key] = build(S, depth)[0]
    nc = _CACHE[key]
    maps = make_in_maps(inputs, S, depth, B)
    res = run_bass_kernel_spmd(nc, maps, core_ids=list(range(B)))
    return np.stack([np.asarray(r["out"], np.float32) for r in res.results], axis=0)
```

```python
import math
from contextlib import ExitStack
import numpy as np
import concourse.bass as bass
import concourse.mybir as mybir
from concourse.bass_utils import run_bass_kernel_spmd

F32 = mybir.dt.float32
BF16 = mybir.dt.bfloat16
ALU = mybir.AluOpType
AF = mybir.ActivationFunctionType
AX = mybir.AxisListType

D = 1024
NIN = 8448
DFF = 2816
EPS = 1e-6
NEG = -30000.0
C0 = math.exp(-0.5)
SEM_LIMIT = 30000
import os
PA_STOP = int(os.environ.get("PA_STOP", "9"))
PA_SKIP = os.environ.get("PA_SKIP", "")


class Ctx:
    ENGS = ("pe", "dve", "act", "pool", "sp")

    def __init__(self, nc):
        self.nc = nc
        self.prog = {e: [] for e in self.ENGS}
        self.sems = {}
        self.semval = {}
        self.cur = {}
        self.waited = {e: {} for e in self.ENGS}
        self.res = {}
        self.nsem = 0
        self.ninstr = 0
        self.banktag = {}

    def _semkey(self, logical, step):
        sk = self.cur.get(logical)
        if sk is None or self.semval[sk] + step > SEM_LIMIT:
            ep = 0 if sk is None else sk[1] + 1
            sk = (logical, ep)
            self.sems[sk] = self.nc.alloc_semaphore(name=f"s{self.nsem}")
            self.nsem += 1
            self.semval[sk] = 0
            self.cur[logical] = sk
        return sk

    @staticmethod
    def _key(x):
        if isinstance(x, (str, tuple)):
            return x
        if hasattr(x, "tensor"):
            return x.tensor.name
        return x.name

    def _collect(self, reads, writes):
        deps = {}

        def add(d):
            if d is not None:
                deps[d[0]] = max(deps.get(d[0], 0), d[1])
        for r in reads:
            st = self.res.get(r)
            if st:
                add(st["w"])
        for w in writes:
            st = self.res.get(w)
            if st:
                add(st["w"])
                for sk, v in st["r"].items():
                    add((sk, v))
        return deps

    def _emit_waits(self, e, deps):
        for sk, v in deps.items():
            if self.waited[e].get(sk, 0) < v:
                h = self.sems[sk]
                self.prog[e].append(lambda eng, h=h, v=v: eng.wait_ge(h, v))
                self.waited[e][sk] = v

    def _update(self, reads, writes, sk, v):
        for r in reads:
            st = self.res.setdefault(r, {"w": None, "r": {}})
            st["r"][sk] = max(st["r"].get(sk, 0), v)
        for w in writes:
            self.res[w] = {"w": (sk, v), "r": {}}

    def op(self, e, fn, r=(), w=(), petag=None):
        reads = [self._key(x) for x in r]
        writes = [self._key(x) for x in w]
        writes = writes + [k for k in reads if isinstance(k, str) and k.startswith("bank") and k not in writes]
        skip = []
        if e == "pe" and petag is not None:
            for k in writes:
                if isinstance(k, str) and k.startswith("bank"):
                    st = self.res.get(k)
                    if st and st["w"] is not None and st["w"][0][0] == ("eng", "pe") and not st["r"] \
                            and self.banktag.get(k) == petag:
                        skip.append(k)
                    self.banktag[k] = petag
        deps = self._collect(reads, [k for k in writes if k not in skip])
        self._emit_waits(e, deps)
        sk = self._semkey(("eng", e), 1)
        self.semval[sk] += 1
        v = self.semval[sk]
        h = self.sems[sk]
        self.prog[e].append(lambda eng, fn=fn, h=h: fn(eng).then_inc(h, 1))
        self._update(reads, writes, sk, v)
        self.ninstr += 1

    def dma(self, q, out, in_, r=None, w=None, semkey=None, **kw):
        reads = [self._key(x) for x in (r if r is not None else [in_])]
        writes = [self._key(x) for x in (w if w is not None else [out])]
        if semkey is None:
            semkey = out.tensor.name
        lk = ("dma", semkey)
        sk = self._semkey(lk, 16)
        deps = self._collect(reads, writes)
        if self.semval[sk] > 0:
            deps[sk] = max(deps.get(sk, 0), self.semval[sk])
        self._emit_waits(q, deps)
        self.semval[sk] += 16
        v = self.semval[sk]
        h = self.sems[sk]
        self.prog[q].append(
            lambda eng, out=out, in_=in_, kw=kw, h=h: eng.dma_start(out=out, in_=in_, **kw).then_inc(h, 16))
        self._update(reads, writes, sk, v)
        self.ninstr += 1

    def barrier(self):
        deps = {sk: v for sk, v in self.semval.items() if v > 0}
        for e in self.ENGS:
            self._emit_waits(e, deps)

    def final_wait(self, e, keys):
        deps = self._collect([self._key(k) for k in keys], ())
        self._emit_waits(e, deps)

    def mm(self, out, lhsT, rhs, start=True, stop=True, r=None, w=None):
        tag = (lhsT.start_partition(), lhsT.partition_size())
        self.op("pe", lambda e: e.matmul(out, lhsT=lhsT, rhs=rhs, start=start, stop=stop),
                r if r is not None else [lhsT, rhs], w if w is not None else [out], petag=tag)

    def tr(self, out, in_, ident, r=None, w=None):
        tag = (in_.start_partition(), in_.partition_size())
        self.op("pe", lambda e: e.transpose(out=out, in_=in_, identity=ident),
                r if r is not None else [in_, ident], w if w is not None else [out], petag=tag)

    def act(self, out, in_, func, bias=None, scale=None, r=None, w=None, eng="act"):
        kw = {}
        if bias is not None:
            kw["bias"] = bias
        if scale is not None:
            kw["scale"] = scale
        rr = [in_] + [x for x in (bias, scale) if not isinstance(x, (int, float, type(None)))]
        self.op("act", lambda e: e.activation(out=out, in_=in_, func=func, **kw),
                r if r is not None else rr, w if w is not None else [out])

    def tt(self, eng, out, in0, in1, op, r=None, w=None):
        self.op(eng, lambda e: e.tensor_tensor(out=out, in0=in0, in1=in1, op=op),
                r if r is not None else [in0, in1], w if w is not None else [out])

    def ts(self, eng, out, in0, s1, s2, op0, op1=None, r=None, w=None):
        rr = [in0] + [x for x in (s1, s2) if not isinstance(x, (int, float, type(None)))]
        if op1 is None:
            fn = lambda e: e.tensor_scalar(out=out, in0=in0, scalar1=s1, scalar2=None, op0=op0)
        else:
            fn = lambda e: e.tensor_scalar(out=out, in0=in0, scalar1=s1, scalar2=s2, op0=op0, op1=op1)
        self.op(eng, fn, r if r is not None else rr, w if w is not None else [out])

    def stt(self, out, in0, scalar, in1, op0, op1, r=None, w=None):
        rr = [in0, in1] + ([scalar] if not isinstance(scalar, (int, float)) else [])
        self.op("dve", lambda e: e.scalar_tensor_tensor(out=out, in0=in0, scalar=scalar, in1=in1, op0=op0, op1=op1),
                r if r is not None else rr, w if w is not None else [out])

    def cp(self, eng, out, in_, r=None, w=None):
        if eng == "act":
            fn = lambda e: e.copy(out=out, in_=in_)
        else:
            fn = lambda e: e.tensor_copy(out=out, in_=in_)
        self.op(eng, fn, r if r is not None else [in_], w if w is not None else [out])

    def memset(self, eng, ap, val, w=None):
        self.op(eng, lambda e: e.memset(ap, val), [], w if w is not None else [ap])

    def replay(self):
        nc = self.nc
        with nc.Block() as block:
            @block.tensor
            def _(eng):
                for f in self.prog["pe"]:
                    f(eng)

            @block.vector
            def _(eng):
                for f in self.prog["dve"]:
                    f(eng)

            @block.scalar
            def _(eng):
                for f in self.prog["act"]:
                    f(eng)

            @block.gpsimd
            def _(eng):
                for f in self.prog["pool"]:
                    f(eng)

            @block.sync
            def _(eng):
                for f in self.prog["sp"]:
                    f(eng)


def host_consts(S):
    c = {}
    pos = np.arange(S, dtype=np.float32)
    inv_a = (1.0 / (np.float32(500000.0) ** (np.arange(0, 16, 2, dtype=np.float32) / np.float32(16)))).astype(np.float32)
    ang = (pos[:, None] * inv_a[None, :]).astype(np.float32)
    cos_a, sin_a = np.cos(ang).astype(np.float32), np.sin(ang).astype(np.float32)
    ca = np.zeros((S, 2, 2, 8, 8), np.float32)
    ca[:, 0] = cos_a[:, None, None, :]
    ca[:, 1] = sin_a[:, None, None, :]
    c["cA"] = ca.reshape(S, 256)
    inv_b = (1.0 / (np.float32(10000.0) ** np.linspace(0.0, 1.0, 64, dtype=np.float32))).astype(np.float32)
    angb = (pos[:, None] * inv_b[None, :]).astype(np.float32)
    cos_b, sin_b = np.cos(angb).astype(np.float64), np.sin(angb).astype(np.float64)
    lg = np.log(1.0 - 2.0 ** (-5.0 - np.arange(4, dtype=np.float64)))
    i = (np.arange(S) % 128).astype(np.float64)
    gq = np.exp(lg[None, :] * i[:, None])
    gk = np.exp(-lg[None, :] * i[:, None]) * (128.0 ** -0.5)
    cb = np.zeros((S, 4, 4, 64), np.float64)
    cb[:, 0] = cos_b[:, None, :] * gq[:, :, None]
    cb[:, 1] = sin_b[:, None, :] * gq[:, :, None]
    cb[:, 2] = cos_b[:, None, :] * gk[:, :, None]
    cb[:, 3] = sin_b[:, None, :] * gk[:, :, None]
    c["cB"] = cb.reshape(S, 1024).astype(np.float32)
    gam = np.exp(lg)
    rs = np.zeros((128, 12), np.float32)
    rs[:, 0:4] = (gam ** 128)[None, :]
    rs[:, 4:8] = (gam ** 127)[None, :]
    rs[:, 8:12] = gam[None, :]
    c["cRS"] = rs
    E = np.zeros((16, S), np.float32)
    for j in range(16):
        E[j, j * 256:(j + 1) * 256] = 1.0
    c["cE"] = E
    cm = np.zeros((2, 128, 256), np.float32)
    for kt in range(2):
        k = kt * 128 + np.arange(128)[:, None]
        q = np.arange(256)[None, :]
        cm[kt] = np.where(k <= q, 0.0, NEG)
    c["cCM"] = cm.transpose(1, 0, 2).reshape(128, 512)
    j = np.arange(128)[:, None]
    ii = np.arange(128)[None, :]
    m = (j <= ii).astype(np.float32)
    c["cRM"] = np.tile(m[:, None, :], (1, 4, 1)).reshape(128, 512)
    s = (np.arange(128) % 64)[:, None]
    t = np.arange(64)[None, :]
    strict = (s < t).astype(np.float32)
    incl = (s <= t).astype(np.float32)
    am = np.concatenate([strict, incl], axis=1)
    c["cAM"] = np.tile(am[:, None, :], (1, 4, 1)).reshape(128, 512)
    tt_ = (np.arange(128) % 64)[:, None]
    ss_ = np.arange(64)[None, :]
    xm = (ss_ < tt_).astype(np.float32)
    c["cXM"] = np.tile(xm[:, None, :], (1, 4, 1)).reshape(128, 256)
    rm = np.ones((128, 512), np.float32)
    rm[:, ::64] = 0.0
    c["cRST"] = rm
    c["cID"] = np.eye(128, dtype=np.float32)
    bo = np.zeros((128, 128), np.float32)
    bo[:64, :64] = 1.0
    bo[64:, 64:] = 1.0
    c["cBO"] = bo
    return c


CONST_SHAPES = lambda S: {"cA": [S, 256], "cB": [S, 1024], "cRS": [128, 12], "cE": [16, S], "cCM": [128, 512],
                          "cRM": [128, 512], "cAM": [128, 512], "cXM": [128, 256], "cRST": [128, 512],
                          "cID": [128, 128], "cBO": [128, 128]}

COLS = {"mixg": (0, 8), "ffng": (8, 8), "pleg": (16, 8), "fing": (24, 8), "mu": (32, 14), "w0": (46, 4),
        "a0": (50, 4), "kk": (54, 4), "ka": (58, 4), "rk": (62, 4), "v0": (66, 4), "vmu": (70, 1)}
NCOL = 72


def pack_cols(inp, l):
    out = np.zeros((128, NCOL), np.float32)

    def put(name, vec):
        o, n = COLS[name]
        v = np.asarray(vec, np.float32).reshape(-1)
        out[:, o:o + n] = v.reshape(n, 128).T
    put("mixg", inp["norm_mix_g"][l])
    put("ffng", inp["norm_ffn_g"][l])
    put("pleg", inp["norm_ple_g"][l])
    put("fing", inp["final_norm_g"])
    put("mu", inp["c_mu"][l])
    put("w0", inp["c_w0"][l])
    put("a0", inp["c_a0"][l])
    put("kk", inp["c_k_k"][l])
    put("ka", inp["c_k_a"][l])
    put("rk", inp["c_r_k"][l])
    if l >= 1:
        put("v0", inp["c_v0"][l - 1])
        out[0:32, COLS["vmu"][0]] = np.asarray(inp["c_vres_mu"][l - 1], np.float32)
    return out


def pack_ln(inp, l):
    out = np.zeros((128, 2, 4, 64), np.float32)
    for k, name in enumerate(("c_ln_g", "c_ln_b")):
        v = np.asarray(inp[name][l], np.float32).reshape(4, 2, 64)
        for hh in range(2):
            out[hh * 64:(hh + 1) * 64, k] = v[:, hh, :][None]
    return out.reshape(128, 512)


def build(S, depth=2, en="abc", dbg=()):
    NT = S // 128
    NG = S // 512
    NB = S // 256
    nc = bass.Bass("TRN2", target_bir_lowering=False)

    def din(name, shape, dt=F32):
        return nc.dram_tensor(name, list(shape), dt, kind="ExternalInput").ap()

    def dscr(name, shape, dt=F32):
        return nc.dram_tensor(name, list(shape), dt, kind="Internal").ap()

    x_d = din("x", [S, D])
    p_d = din("p", [depth, S, 256])
    w_in = din("w_in", [depth, D, NIN])
    w2_d = din("c_w2", [depth, 64, 512])
    a2_d = din("c_a2", [depth, 64, 512])
    g2_d = din("c_g2", [depth, 128, 512])
    vd_d = din("c_vres_down", [max(depth - 1, 1), D, 32])
    v2_d = din("c_v2", [max(depth - 1, 1), 32, 512])
    wbr_d = din("w_branch", [depth, 3, 512, D])
    wout_d = din("w_out", [depth, D, D])
    wgu_d = din("w_gate_up", [depth, D, 2 * DFF])
    wd_d = din("w_down", [depth, DFF, D])
    wpg_d = din("w_ple_gate", [depth, D, D])
    wpp_d = din("w_ple_proj", [depth, 256, D])
    cols_d = din("cols", [depth, 128, NCOL])
    ln_d = din("lnp", [depth, 128, 512])
    cst = {k: din(k, shp) for k, shp in CONST_SHAPES(S).items()}
    out_d = nc.dram_tensor("out", [S, D], F32, kind="ExternalOutput").ap()

    xT_d = dscr("xT_d", [128, 8, S])
    hT_d = dscr("hT_d", [128, 8, S], BF16)
    yT_d = dscr("yT_d", [3, 128, 4, S], BF16)
    vf_d = dscr("vf_d", [128, 4, S])
    NINX = NIN + 32
    Wb_in = dscr("Wb_in", [depth, 128, 8, NINX], BF16)
    Wb_br = dscr("Wb_br", [depth, 128, 3, 4, D], BF16)
    Wb_out = dscr("Wb_out", [depth, 128, 8, D], BF16)
    Wb_gu = dscr("Wb_gu", [depth, 128, 8, 2 * DFF], BF16)
    Wb_d = dscr("Wb_d", [depth, 128, 22, D], BF16)
    Wb_pg = dscr("Wb_pg", [depth, 128, 8, D], BF16)
    Wb_pp = dscr("Wb_pp", [depth, 128, 2, D], BF16)
    dbg_d = {}
    for name, shp in dbg:
        dbg_d[name] = nc.dram_tensor("dbg_" + name, list(shp), F32, kind="ExternalOutput").ap()

    K = Ctx(nc)
    uniq = [0]
    with ExitStack() as top:
        def sbt(es, name, shape, dt=F32):
            uniq[0] += 1
            return es.enter_context(nc.sbuf_tensor(f"s_{name}_{uniq[0]}", list(shape), dt))

        banks = [top.enter_context(nc.psum_tensor(f"bank{i}", [128, 512], F32)) for i in range(8)]
        bank_rr = [0]

        def nb_(lo=0, hi=8):
            b = banks[lo + bank_rr[0] % (hi - lo)]
            bank_rr[0] += 1
            return b

        ident = sbt(top, "ident", [128, 128])
        identb = sbt(top, "identb", [128, 128], BF16)
        onesb = sbt(top, "onesb", [128, 128], BF16)
        cols = sbt(top, "cols", [128, depth, NCOL])
        K.dma("sp", ident[:], cst["cID"][:, :], r=[], semkey="cload")
        K.cp("dve", identb[:], ident[:])
        K.memset("dve", onesb[:], 1.0)
        for l in range(depth):
            K.dma("sp", cols[:, l, :], cols_d[l], r=[], w=[("cols", l)], semkey="cload")
        colkeys = [("cols", l) for l in range(depth)]

        def col(l, name, j=0, n=1):
            o, _ = COLS[name]
            return cols[:, l, o + j:o + j + n]

        def convert_weights(l):
            for c0 in range(0, NIN, 512):
                cw = min(512, NIN - c0)
                K.dma("pool", Wb_in[l, :, :, c0:c0 + cw], w_in[l, :, c0:c0 + cw].rearrange("(k p) n -> p k n", p=128),
                      r=[], w=[("Wb_in", l, c0 // 512)], semkey="conv")
            if l >= 1:
                K.dma("pool", Wb_in[l, :, :, NIN:NINX], vd_d[l - 1].rearrange("(k p) n -> p k n", p=128),
                      r=[], w=[("Wb_in", l, "v")], semkey="conv")
            for n in range(3):
                for h in range(2):
                    K.dma("pool", Wb_br[l, :, n, :, h * 512:(h + 1) * 512],
                          wbr_d[l, n, :, h * 512:(h + 1) * 512].rearrange("(k p) n -> p k n", p=128), r=[], w=[("Wb_br", l)], semkey="conv")
            for h in range(2):
                K.dma("pool", Wb_out[l, :, :, h * 512:(h + 1) * 512],
                      wout_d[l, :, h * 512:(h + 1) * 512].rearrange("(k p) n -> p k n", p=128), r=[], w=[("Wb_out", l)], semkey="conv")
                K.dma("pool", Wb_pg[l, :, :, h * 512:(h + 1) * 512],
                      wpg_d[l, :, h * 512:(h + 1) * 512].rearrange("(k p) n -> p k n", p=128), r=[], w=[("Wb_pg", l)], semkey="conv")
                K.dma("pool", Wb_pp[l, :, :, h * 512:(h + 1) * 512],
                      wpp_d[l, :, h * 512:(h + 1) * 512].rearrange("(k p) n -> p k n", p=128), r=[], w=[("Wb_pp", l)], semkey="conv")
                for k0 in (0, 11):
                    K.dma("pool", Wb_d[l, :, k0:k0 + 11, h * 512:(h + 1) * 512],
                          wd_d[l, k0 * 128:(k0 + 11) * 128, h * 512:(h + 1) * 512].rearrange("(k p) n -> p k n", p=128),
                          r=[], w=[("Wb_d", l)], semkey="conv")
            for c0 in range(0, 2 * DFF, 512):
                K.dma("pool", Wb_gu[l, :, :, c0:c0 + 512], wgu_d[l, :, c0:c0 + 512].rearrange("(k p) n -> p k n", p=128),
                      r=[], w=[("Wb_gu", l)], semkey="conv")

        def norm_group(es_name, xg, hg_out, l, gname, scratch):
            sq, rstd = scratch
            pb = nb_()
            for c in range(8):
                K.act(sq[:, c, :], xg[:, c, :], AF.Square)
            for c in range(8):
                K.mm(pb[:], onesb[:], sq[:, c, :], start=(c == 0), stop=(c == 7))
            K.act(rstd[:], pb[:], AF.Sqrt, bias=epsc[:, 0:1], scale=1.0 / D)
            K.op("dve", lambda e: e.reciprocal(out=rstd[:], in_=rstd[:]), [rstd], [rstd])
            for c in range(8):
                K.stt(hg_out[:, c, :], xg[:, c, :], col(l, gname, c), rstd[:], ALU.mult, ALU.mult,
                      r=[xg, rstd] + colkeys)

        epsc = sbt(top, "epsc", [128, 4])
        K.memset("dve", epsc[:, 0:1], EPS)
        K.memset("dve", epsc[:, 1:2], 1e-5 * 64)
        K.memset("dve", epsc[:, 2:3], 0.0)

        convert_weights(0)

        with ExitStack() as es:
            xin = [sbt(es, f"xin{i}", [128, D]) for i in range(2)]
            xg2 = [sbt(es, f"xgI{i}", [128, 8, 512]) for i in range(2)]
            hg2 = [sbt(es, f"hgI{i}", [128, 8, 512], BF16) for i in range(2)]
            sq = sbt(es, "sqI", [128, 8, 512], BF16)
            rstd = sbt(es, "rstdI", [128, 512])
            for tg in range(NG):
                xg = xg2[tg % 2]
                hg = hg2[tg % 2]
                for tt_ in range(4):
                    t = tg * 4 + tt_
                    xi = xin[t % 2]
                    K.dma("sp", xi[:], x_d[t * 128:(t + 1) * 128, :], r=[])
                    for half in range(2):
                        pb = nb_()
                        for c in range(4):
                            K.tr(pb[:, c * 128:(c + 1) * 128], xi[:, (half * 4 + c) * 128:(half * 4 + c + 1) * 128], ident[:])
                        K.cp("act" if half else "dve", xg[:, half * 4:(half + 1) * 4, tt_ * 128:(tt_ + 1) * 128],
                             pb[:].rearrange("p (c t) -> p c t", c=4))
                K.dma("sp", xT_d[:, :, tg * 512:(tg + 1) * 512], xg[:], w=[("xT", tg)], semkey="xTst")
                norm_group("I", xg, hg, 0, "mixg", (sq, rstd))
                K.dma("sp", hT_d[:, :, tg * 512:(tg + 1) * 512], hg[:], w=[("hT", tg)], semkey="hTst")

        K.barrier()
        for l in range(depth):
            if l + 1 < depth:
                convert_weights(l + 1)
            last = (l == depth - 1)
            if "a" in en:
                phase_a(nc, K, sbt, banks, nb_, l, S, cst, Wb_in, hT_d, yT_d, ident, identb)
                K.barrier()
            if "b" in en:
                phase_b(nc, K, sbt, banks, nb_, l, S, cst, Wb_in, hT_d, yT_d, ident, identb, epsc)
                K.barrier()
            if "c" in en:
                phase_c(nc, K, sbt, banks, nb_, l, S, cst, Wb_in, hT_d, yT_d, vf_d, ident, identb, epsc, cols, col, colkeys,
                        w2_d, a2_d, g2_d, v2_d, ln_d, dbg_d)
                K.barrier()

            with ExitStack() as es:
                xg2 = [sbt(es, f"xgT{i}", [128, 8, 512]) for i in range(2)]
                hg = sbt(es, "hgT", [128, 8, 512], BF16)
                yg = sbt(es, "ygT", [128, 3, 4, 512], BF16)
                hf = sbt(es, "hfT", [128, 8, 512], BF16)
                mg = sbt(es, "mgT", [128, 8, 512], BF16)
                actT = sbt(es, "actT", [128, 22, 512], BF16)
                sq = sbt(es, "sqT", [128, 8, 512], BF16)
                rstd = sbt(es, "rstdT", [128, 512])
                sg = [sbt(es, f"sgT{i}", [128, 512]) for i in range(2)]
                acc = sbt(es, "accT", [128, 512])
                tmpf = [sbt(es, f"tmpT{i}", [128, 512]) for i in range(2)]
                wst = [sbt(es, f"wst{i}", [128, 8, 1024], BF16) for i in range(2)]
                wdt = [sbt(es, f"wdt{i}", [128, 22, 128], BF16) for i in range(2)]
                wbrt = sbt(es, "wbrt", [128, 3, 4, 128], BF16)
                wppt = sbt(es, "wppt", [128, 2, D], BF16)
                pin = [sbt(es, f"pin{i}", [128, 256]) for i in range(2)]
                pT = sbt(es, "pTT", [128, 2, 512], BF16)
                ot = [sbt(es, f"otT{i}", [128, D]) for i in range(2)]
                wsi = [0]

                def wslab():
                    t_ = wst[wsi[0] % 2]
                    wsi[0] += 1
                    return t_
                K.dma("sp", wppt[:], Wb_pp[l], r=[("Wb_pp", l)], semkey="cload")
                for tg in range(NG):
                    xg = xg2[tg % 2]
                    tok = slice(tg * 512, (tg + 1) * 512)
                    K.dma("sp", xg[:], xT_d[:, :, tok], r=[("xT", tg)])
                    K.dma("sp", hg[:], hT_d[:, :, tok], r=[("hT", tg)])
                    for n in range(3):
                        if "abc"[n] in en:
                            K.dma("sp", yg[:, n], yT_d[n, :, :, tok], r=[("yT", n, tg)], w=[("ygT", n)], semkey="ygT")
                        elif tg == 0:
                            K.memset("pool", yg[:, n], 0.0, w=[("ygT", n)])
                    ygk = [("ygT", n) for n in range(3)]
                    for dc in range(8):
                        ws = wslab()
                        for n in range(3):
                            c0 = 5376 + n * 1024 + dc * 128
                            K.dma("sp", ws[:, :, n * 128:(n + 1) * 128], Wb_in[l, :, :, c0:c0 + 128],
                                  r=[("Wb_in", l, c0 // 512)], w=[ws])
                        K.dma("sp", wbrt[:], Wb_br[l, :, :, :, dc * 128:(dc + 1) * 128], r=[("Wb_br", l)])
                        for n in range(3):
                            pbr = nb_()
                            pgt = nb_()
                            for kc in range(4):
                                K.mm(pbr[:], wbrt[:, n, kc, :], yg[:, n, kc, :], start=(kc == 0), stop=(kc == 3),
                                     r=[wbrt, ("ygT", n)])
                            for kc in range(8):
                                K.mm(pgt[:], ws[:, kc, n * 128:(n + 1) * 128], hg[:, kc, :], start=(kc == 0), stop=(kc == 7))
                            s_ = sg[n % 2]
                            K.act(s_[:], pgt[:], AF.Sigmoid)
                            if n == 0:
                                K.tt("dve", acc[:], s_[:], pbr[:], ALU.mult)
                            elif n == 1:
                                K.tt("dve", tmpf[0][:], s_[:], pbr[:], ALU.mult)
                                K.tt("pool", acc[:], acc[:], tmpf[0][:], ALU.add)
                            else:
                                K.tt("dve", tmpf[1][:], s_[:], pbr[:], ALU.mult)
                                K.tt("dve", mg[:, dc, :], acc[:], tmpf[1][:], ALU.add)
                    for dc in range(8):
                        if dc % 4 == 0:
                            ws = wslab()
                            K.dma("sp", ws[:, :, 0:512], Wb_out[l, :, :, dc * 128:dc * 128 + 512], r=[("Wb_out", l)], w=[ws])
                        po = nb_()
                        for kc in range(8):
                            K.mm(po[:], ws[:, kc, (dc % 4) * 128:(dc % 4 + 1) * 128], mg[:, kc, :], start=(kc == 0), stop=(kc == 7))
                        K.tt("dve", xg[:, dc, :], xg[:, dc, :], po[:], ALU.add)
                    norm_group("T", xg, hf, l, "ffng", (sq, rstd))
                    for f4 in range(0, 22, 4):
                        nf = min(4, 22 - f4)
                        ws = wslab()
                        K.dma("sp", ws[:, :, 0:nf * 128], Wb_gu[l, :, :, f4 * 128:(f4 + nf) * 128], r=[("Wb_gu", l)], w=[ws])
                        K.dma("sp", ws[:, :, 512:512 + nf * 128], Wb_gu[l, :, :, DFF + f4 * 128:DFF + (f4 + nf) * 128],
                              r=[("Wb_gu", l)], w=[ws])
                        for fi in range(nf):
                            fc = f4 + fi
                            pg_ = nb_()
                            pu_ = nb_()
                            for kc in range(8):
                                K.mm(pg_[:], ws[:, kc, fi * 128:(fi + 1) * 128], hf[:, kc, :], start=(kc == 0), stop=(kc == 7))
                            for kc in range(8):
                                K.mm(pu_[:], ws[:, kc, 512 + fi * 128:512 + (fi + 1) * 128], hf[:, kc, :], start=(kc == 0), stop=(kc == 7))
                            s_ = sg[fc % 2]
                            K.act(s_[:], pg_[:], AF.Silu)
                            K.tt("dve", actT[:, fc, :], s_[:], pu_[:], ALU.mult)
                    for dc in range(8):
                        wd_ = wdt[dc % 2]
                        K.dma("sp", wd_[:], Wb_d[l, :, :, dc * 128:(dc + 1) * 128], r=[("Wb_d", l)])
                        pd = nb_()
                        for fc in range(22):
                            K.mm(pd[:], wd_[:, fc, :], actT[:, fc, :], start=(fc == 0), stop=(fc == 21))
                        K.tt("dve", xg[:, dc, :], xg[:, dc, :], pd[:], ALU.add)
                    norm_group("T", xg, hf, l, "pleg", (sq, rstd))
                    for tt_ in range(4):
                        pi = pin[tt_ % 2]
                        K.dma("sp", pi[:], p_d[l, tg * 512 + tt_ * 128: tg * 512 + (tt_ + 1) * 128, :], r=[])
                        pb = nb_()
                        for c in range(2):
                            K.tr(pb[:, c * 128:(c + 1) * 128], pi[:, c * 128:(c + 1) * 128], ident[:])
                        K.cp("act", pT[:, :, tt_ * 128:(tt_ + 1) * 128], pb[:, 0:256].rearrange("p (c t) -> p c t", c=2))
                    for dc in range(8):
                        if dc % 4 == 0:
                            ws = wslab()
                            K.dma("sp", ws[:, :, 0:512], Wb_pg[l, :, :, dc * 128:dc * 128 + 512], r=[("Wb_pg", l)], w=[ws])
                        pg_ = nb_()
                        pp_ = nb_()
                        for kc in range(8):
                            K.mm(pg_[:], ws[:, kc, (dc % 4) * 128:(dc % 4 + 1) * 128], hf[:, kc, :], start=(kc == 0), stop=(kc == 7))
                        for kc in range(2):
                            K.mm(pp_[:], wppt[:, kc, dc * 128:(dc + 1) * 128], pT[:, kc, :], start=(kc == 0), stop=(kc == 1))
                        s_ = sg[dc % 2]
                        K.act(s_[:], pg_[:], AF.Sigmoid)
                        K.tt("dve", tmpf[dc % 2][:], s_[:], pp_[:], ALU.mult)
                        K.tt("pool", xg[:, dc, :], xg[:, dc, :], tmpf[dc % 2][:], ALU.add)
                    if not last:
                        K.dma("sp", xT_d[:, :, tok], xg[:], w=[("xT", tg)], semkey="xTst")
                        norm_group("T", xg, hf, l + 1, "mixg", (sq, rstd))
                        K.dma("sp", hT_d[:, :, tok], hf[:], w=[("hT", tg)], semkey="hTst")
                    else:
                        pb = nb_()
                        for c in range(8):
                            K.act(sq[:, c, :], xg[:, c, :], AF.Square)
                        for c in range(8):
                            K.mm(pb[:], onesb[:], sq[:, c, :], start=(c == 0), stop=(c == 7))
                        K.act(rstd[:], pb[:], AF.Sqrt, bias=epsc[:, 0:1], scale=1.0 / D)
                        K.op("dve", lambda e: e.reciprocal(out=rstd[:], in_=rstd[:]), [rstd], [rstd])
                        for c in range(8):
                            K.stt(xg[:, c, :], xg[:, c, :], col(l, "fing", c), rstd[:], ALU.mult, ALU.mult,
                                  r=[xg, rstd] + colkeys)
                        for tt_ in range(4):
                            o_ = ot[tt_ % 2]
                            for half in range(2):
                                pb2 = nb_()
                                for c in range(4):
                                    K.tr(pb2[:, c * 128:(c + 1) * 128], xg[:, half * 4 + c, tt_ * 128:(tt_ + 1) * 128], ident[:])
                                K.cp("act" if half else "dve", o_[:, half * 512:(half + 1) * 512], pb2[:])
                            K.dma("sp", out_d[tg * 512 + tt_ * 128: tg * 512 + (tt_ + 1) * 128, :], o_[:], w=["out"],
                                  semkey="out")
            K.barrier()
        K.final_wait("sp", ["out"] + ["dbg_" + n for n in dbg_d])
        K.replay()
    return nc, K


def phase_a(nc, K, sbt, banks, nb_, l, S, cst, Wb_in, hT_d, yT_d, ident, identb):
    NT = S // 128
    NB = S // 256
    with ExitStack() as es:
        wA = sbt(es, "wA", [128, 8, 1536], BF16)
        KT = sbt(es, "KT", [80, 8, S], BF16)
        Va = sbt(es, "Va", [128, NT, 8, 65], BF16)
        QT = [sbt(es, f"QT{i}", [80, 8, 256], BF16) for i in range(2)]
        QTf = [sbt(es, f"QTf{i}", [64, 8, 128]) for i in range(2)]
        kms = sbt(es, "kms", [64, 8, 16])
        ktmp = sbt(es, "ktmp", [64, 8])
        hTt = [sbt(es, f"hTtA{i}", [128, 8, 128], BF16) for i in range(2)]
        qk = [sbt(es, f"qkA{i}", [128, 2, 8, 64]) for i in range(2)]
        cs = [sbt(es, f"csA{i}", [128, 2, 128]) for i in range(2)]
        rt = [sbt(es, f"rtA{i}", [128, 128]) for i in range(4)]
        gs = sbt(es, "gsA", [128, 8, 16])
        m8 = sbt(es, "m8A", [128, 8, 8])
        Mp = sbt(es, "MpA", [128, 8, 80], BF16)
        cm = sbt(es, "cmA", [128, 2, 256], BF16)
        PT = [sbt(es, f"PTA{i}", [128, 256], BF16) for i in range(3)]
        ya = [sbt(es, f"yaA{i}", [128, 512], BF16) for i in range(2)]
        rden = sbt(es, "rdenA", [128, 4])
        yTs = [sbt(es, f"yTsA{i}", [128, 4, 128], BF16) for i in range(2)]
        for g in range(3):
            K.dma("sp", wA[:, :, g * 512:(g + 1) * 512], Wb_in[l, :, :, g * 512:(g + 1) * 512], r=[("Wb_in", l, g)],
                  w=[("wA", g)], semkey="wload")
        wAk = [("wA", g) for g in range(3)]
        K.dma("pool", cm[:], cst["cCM"].rearrange("p (k q) -> p k q", k=2), r=[], semkey="cloadp")
        for h in range(8 if "e" not in PA_SKIP else 0):
            for e0 in range(0, S, 2048):
                e1 = min(S, e0 + 2048)
                K.dma("pool", KT[64:80, h, e0:e1], cst["cE"][:, e0:e1], r=[], w=[("KTE", h)], semkey="cloadp")
        KTE = [("KTE", h) for h in range(8)]
        if "m" not in PA_SKIP:
            K.memset("pool", Va[:, :, :, 64:65], 1.0, w=["Va1"])
            K.memset("pool", Mp[:], 0.0)
        pz = [banks[0], banks[1]]
        pTq = [banks[2], banks[3]]
        pS = [banks[4], banks[5]]
        pO = [banks[6], banks[7]]
        pti = 0
        psi = 0
        for t in range(NT):
            b = t // 2
            half = t % 2
            hT = hTt[t % 2]
            K.dma("sp", hT[:], hT_d[:, :, t * 128:(t + 1) * 128], r=[("hT", t // 4)])
            c_ = cs[t % 2]
            K.dma("sp", c_[:], cst["cA"][t * 128:(t + 1) * 128, :].rearrange("p (a b) -> p a b", a=2), r=[])
            q_ = qk[t % 2]
            for g in range(3):
                pb = pz[g % 2]
                for kc in range(8):
                    K.mm(pb[:], hT[:, kc, :], wA[:, kc, g * 512:(g + 1) * 512], start=(kc == 0), stop=(kc == 7),
                         r=[hT, ("wA", g)])
                if g < 2:
                    K.cp("act", q_[:, g], pb[:].rearrange("p (h d) -> p h d", h=8))
                elif "v" not in PA_SKIP:
                    K.cp("act", Va[:, t, :, 0:64], pb[:].rearrange("p (h d) -> p h d", h=8), w=[("Va", t)])
            x1 = q_[:, :, :, 0:8]
            x2 = q_[:, :, :, 8:16]
            co = c_[:, 0, :].rearrange("p (a h d) -> p a h d", a=2, h=8)
            si = c_[:, 1, :].rearrange("p (a h d) -> p a h d", a=2, h=8)
            t1, t2, t3, t4 = [r_[:].rearrange("p (a h d) -> p a h d", a=2, h=8) for r_ in rt]
            if "r" not in PA_SKIP:
                K.tt("dve", t1, x1, co, ALU.mult)
                K.tt("pool", t2, x2, si, ALU.mult)
                K.tt("dve", t3, x1, si, ALU.mult)
                K.tt("pool", t4, x2, co, ALU.mult)
                K.tt("dve", x1, t1, t2, ALU.subtract)
                K.tt("dve", x2, t3, t4, ALU.add)
            for g in range(2 if "t" not in PA_SKIP else 0):
                for hq in range(2):
                    pb = pTq[hq]
                    for h4 in range(4):
                        h = hq * 4 + h4
                        K.tr(pb[0:64, h4 * 128:(h4 + 1) * 128], q_[:, g, h, :], ident[:])
                    src = pb[0:64, :].rearrange("p (h t) -> p h t", h=4)
                    if g == 0:
                        K.cp("act", QT[b % 2][0:64, hq * 4:(hq + 1) * 4, half * 128:(half + 1) * 128], src,
                             w=[("QTq", b % 2)])
                        K.cp("dve", QTf[half][:, hq * 4:(hq + 1) * 4, :], src)
                    else:
                        K.cp("act", KT[0:64, hq * 4:(hq + 1) * 4, t * 128:(t + 1) * 128], src, w=[("KT", t)])
                        K.op("dve", lambda e, src=src, hq=hq: e.tensor_reduce(out=ktmp[:, hq * 4:(hq + 1) * 4], in_=src, axis=AX.X, op=ALU.add),
                             [pb], [ktmp])
                if g == 1:
                    if half == 0:
                        K.cp("dve", kms[:, :, b:b + 1], ktmp[:].rearrange("p (h o) -> p h o", o=1))
                    else:
                        K.tt("dve", kms[:, :, b:b + 1], kms[:, :, b:b + 1], ktmp[:].rearrange("p (h o) -> p h o", o=1), ALU.add)
            if half == 0 or PA_STOP <= 1:
                continue
            Qb = QT[b % 2]
            if b >= 1:
                for hf_ in range(2):
                    pg = banks[2]
                    for h in range(8):
                        K.mm(pg[:, h * 16:(h + 1) * 16], QTf[hf_][0:64, h, :], kms[0:64, h, :])
                    K.cp("dve", gs[:], pg[:, 0:128].rearrange("p (h n) -> p h n", h=8))
                    if b < 16:
                        K.memset("dve", gs[:, :, b:16], -1e30)
                    for h in range(8):
                        K.op("dve", lambda e, h=h: e.max(out=m8[:, h, :], in_=gs[:, h, :]), [gs], [m8])
                    for h in range(8):
                        K.ts("dve", Mp[:, h, 64:80], gs[:, h, :], m8[:, h, 2:3], NEG, ALU.is_lt, ALU.mult)
                    pm = banks[3]
                    pmv = pm[:].bitcast(BF16).rearrange("p (h t) -> p h t", h=8)
                    for h in range(8):
                        K.tr(pmv[0:80, h, :], Mp[:, h, :], identb[:])
                    K.cp("act", Qb[64:80, :, hf_ * 128:(hf_ + 1) * 128], pmv[64:80, :, :], w=[("QTm", b % 2, hf_)])
            Qkeys = [("QTq", b % 2), ("QTm", b % 2, 0), ("QTm", b % 2, 1)]
            if PA_STOP <= 2:
                continue
            nkt = 2 * b + 2
            steps = [(h, kt) for h in range(8) for kt in range(nkt)]

            def issue_S(i):
                h, kt = steps[i]
                own = kt >= 2 * b
                Kr = 64 if own else 80
                ps_ = pS[i % 2]
                K.mm(ps_[:, 0:256], KT[0:Kr, h, kt * 128:(kt + 1) * 128], Qb[0:Kr, h, :], start=True, stop=not own,
                     r=[("KT", kt), ("KTE", h)] + Qkeys)
                if own:
                    K.mm(ps_[:, 0:256], identb[:], cm[:, kt - 2 * b, :], start=False, stop=True)

            issue_S(0)
            for i, (h, kt) in enumerate(steps):
                own = kt >= 2 * b
                pOh = pO if h % 2 == 0 else [banks[2], banks[3]]
                if i + 1 < len(steps):
                    issue_S(i + 1)
                P_ = PT[i % 3]
                K.act(P_[:], pS[i % 2][:, 0:256], AF.Exp, scale=0.125)
                for q2 in range(2):
                    if own and kt - 2 * b == 1 and q2 == 0:
                        continue
                    lastk = (2 * b) if q2 == 0 else (2 * b + 1)
                    K.mm(pOh[q2][:, 0:65], P_[:, q2 * 128:(q2 + 1) * 128], Va[:, kt, h, :], start=(kt == 0), stop=(kt == lastk),
                         r=[P_, ("Va", kt), "Va1"])
                if kt == nkt - 1:
                    for q2 in range(2):
                        rd = rden[:, (h % 2) * 2 + q2:(h % 2) * 2 + q2 + 1]
                        K.op("dve", lambda e, rd=rd, src=pOh[q2][:, 64:65]: e.reciprocal(out=rd, in_=src), [pOh[q2]], [("rden", h % 2, q2)])
                        K.ts("dve", ya[q2][:, h * 64:(h + 1) * 64], pOh[q2][:, 0:64], rd, None, ALU.mult,
                             r=[pOh[q2], ("rden", h % 2, q2)], w=[("ya", q2, h)])
            yak = [[("ya", q2, h) for h in range(8)] for q2 in range(2)]
            if PA_STOP <= 3:
                continue
            for q2 in range(2):
                tq = 2 * b + q2
                pb = banks[q2]
                pv = pb[:].bitcast(BF16)[:, 0:512].rearrange("p (c t) -> p c t", c=4)
                for c in range(4):
                    K.tr(pv[:, c, :], ya[q2][:, c * 128:(c + 1) * 128], identb[:], r=yak[q2] + [identb])
                K.cp("act", yTs[q2][:], pv)
                K.dma("sp", yT_d[0, :, :, tq * 128:(tq + 1) * 128], yTs[q2][:], w=[("yT", 0, tq // 4)], semkey="yTs")


def phase_b(nc, K, sbt, banks, nb_, l, S, cst, Wb_in, hT_d, yT_d, ident, identb, epsc):
    NT = S // 128
    with ExitStack() as es:
        wB = sbt(es, "wB", [128, 8, 2048], BF16)
        hTt = [sbt(es, f"hTtB{i}", [128, 8, 128], BF16) for i in range(2)]
        cb = [sbt(es, f"cbB{i}", [128, 4, 256]) for i in range(2)]
        qs = [sbt(es, f"qsB{i}", [128, 512]) for i in range(2)]
        rt = [sbt(es, f"rtB{i}", [128, 256]) for i in range(4)]
        qr = [sbt(es, f"qrB{i}", [128, 512], BF16) for i in range(2)]
        QTb = sbt(es, "QTbB", [128, 4, 128], BF16)
        KTb = sbt(es, "KTbB", [128, 4, 128], BF16)
        vb = sbt(es, "vbB", [128, 512], BF16)
        sgb = sbt(es, "sgB", [128, 512])
        Sm = sbt(es, "SmB", [128, 4, 128], BF16)
        rm = sbt(es, "rmB", [128, 4, 128], BF16)
        R = sbt(es, "RB", [128, 4, 128])
        Rg = sbt(es, "RgB", [128, 4, 128], BF16)
        rs = sbt(es, "rsB", [128, 12])
        st = sbt(es, "stB", [128, 4, 6])
        mv = sbt(es, "mvB", [128, 4, 2])
        rstd = sbt(es, "rstdB", [128, 4])
        yn = sbt(es, "ynB", [128, 512])
        yb = sbt(es, "ybB", [128, 512], BF16)
        yTs = [sbt(es, f"yTsB{i}", [128, 4, 128], BF16) for i in range(2)]
        for g in range(4):
            K.dma("sp", wB[:, :, g * 512:(g + 1) * 512], Wb_in[l, :, :, 1536 + g * 512:1536 + (g + 1) * 512],
                  r=[("Wb_in", l, 3 + g)], w=[("wB", g)], semkey="wload")
        K.dma("pool", rm[:], cst["cRM"].rearrange("p (h t) -> p h t", h=4), r=[], semkey="cloadp")
        K.dma("sp", rs[:], cst["cRS"][:, :], r=[], semkey="cload")
        K.memset("dve", R[:], 0.0, w=[("RB", h) for h in range(4)])
        K.memset("pool", Rg[:], 0.0)
        for t in range(NT):
            hT = hTt[t % 2]
            K.dma("sp", hT[:], hT_d[:, :, t * 128:(t + 1) * 128], r=[("hT", t // 4)])
            c_ = cb[t % 2]
            K.dma("sp", c_[:], cst["cB"][t * 128:(t + 1) * 128, :].rearrange("p (a b) -> p a b", a=4), r=[])
            for g in range(4):
                pb = nb_(0, 4)
                for kc in range(8):
                    K.mm(pb[:], hT[:, kc, :], wB[:, kc, g * 512:(g + 1) * 512], start=(kc == 0), stop=(kc == 7),
                         r=[hT, ("wB", g)])
                if g < 2:
                    q_ = qs[g]
                    K.cp("act", q_[:], pb[:])
                    xv = q_[:].rearrange("p (h d two) -> p h d two", h=4, two=2)
                    xe = xv[:, :, :, 0]
                    xo = xv[:, :, :, 1]
                    co = c_[:, 2 * g, :].rearrange("p (h d) -> p h d", h=4)
                    si = c_[:, 2 * g + 1, :].rearrange("p (h d) -> p h d", h=4)
                    t1, t2, t3, t4 = [r_[:].rearrange("p (h d) -> p h d", h=4) for r_ in rt]
                    ov = qr[g][:].rearrange("p (h d two) -> p h d two", h=4, two=2)
                    K.tt("dve", t1, xe, co, ALU.mult)
                    K.tt("pool", t2, xo, si, ALU.mult)
                    K.tt("dve", t3, xe, si, ALU.mult)
                    K.tt("pool", t4, xo, co, ALU.mult)
                    K.tt("dve", ov[:, :, :, 0], t1, t2, ALU.subtract, w=[(f"qrB{g}", 0)], r=[rt[0], rt[1]])
                    K.tt("dve", ov[:, :, :, 1], t3, t4, ALU.add, w=[(f"qrB{g}", 1)], r=[rt[2], rt[3]])
                    pbt = nb_(0, 4)
                    pv = pbt[:].bitcast(BF16)[:, 0:512].rearrange("p (h t) -> p h t", h=4)
                    for h in range(4):
                        K.tr(pv[:, h, :], qr[g][:, h * 128:(h + 1) * 128], identb[:], r=[(f"qrB{g}", 0), (f"qrB{g}", 1), identb])
                    K.cp("act", (QTb if g == 0 else KTb)[:], pv)
                elif g == 2:
                    K.cp("act", vb[:], pb[:])
                else:
                    K.act(sgb[:], pb[:], AF.Silu)
            qrk = [("qrB1", 0), ("qrB1", 1)]
            pS_ = banks[4]
            for h in range(4):
                K.mm(pS_[:, h * 128:(h + 1) * 128], KTb[:, h, :], QTb[:, h, :])
            K.tt("dve", Sm[:], pS_[:].rearrange("p (h t) -> p h t", h=4), rm[:], ALU.mult)
            py = banks[5]
            pkv = banks[6]
            for h in range(4):
                K.mm(py[:, h * 128:(h + 1) * 128], Sm[:, h, :], vb[:, h * 128:(h + 1) * 128], start=True, stop=False)
                K.mm(py[:, h * 128:(h + 1) * 128], QTb[:, h, :], Rg[:, h, :], start=False, stop=True)
            for h in range(4):
                K.mm(pkv[:, h * 128:(h + 1) * 128], qr[1][:, h * 128:(h + 1) * 128], vb[:, h * 128:(h + 1) * 128],
                     r=qrk + [vb])
            for h in range(4):
                K.op("dve", lambda e, h=h: e.bn_stats(out=st[:, h, :], in_=py[:, h * 128:(h + 1) * 128]), [py], [("stB", h)])
                K.op("dve", lambda e, h=h: e.bn_aggr(out=mv[:, h, :], in_=st[:, h, :]), [("stB", h)], [("mvB", h)])
            mvk = [("mvB", h) for h in range(4)]
            K.act(rstd[:], mv[:, :, 1], AF.Sqrt, bias=epsc[:, 0:1], scale=1.0, r=mvk + [epsc])
            K.op("dve", lambda e: e.reciprocal(out=rstd[:], in_=rstd[:]), [rstd], [rstd])
            for h in range(4):
                K.ts("dve", yn[:, h * 128:(h + 1) * 128], py[:, h * 128:(h + 1) * 128], mv[:, h, 0:1], rstd[:, h:h + 1],
                     ALU.subtract, ALU.mult, r=[py, rstd] + mvk, w=[("ynB", h)])
            K.tt("pool", yb[:], yn[:], sgb[:], ALU.mult, r=[("ynB", h) for h in range(4)] + [sgb])
            for h in range(4):
                K.ts("pool", R[:, h, :], R[:, h, :], rs[:, h:h + 1], None, ALU.mult, r=[("RB", h), rs], w=[("RB", h)])
                K.stt(R[:, h, :], pkv[:, h * 128:(h + 1) * 128], rs[:, 4 + h:5 + h], R[:, h, :], ALU.mult, ALU.add,
                      r=[pkv, rs, ("RB", h)], w=[("RB", h)])
                K.act(Rg[:, h, :], R[:, h, :], AF.Identity, scale=float(cst_gamma(h)), r=[("RB", h)], w=[Rg])
            pbt = banks[7]
            pv = pbt[:].bitcast(BF16)[:, 0:512].rearrange("p (c t) -> p c t", c=4)
            for c in range(4):
                K.tr(pv[:, c, :], yb[:, c * 128:(c + 1) * 128], identb[:])
            K.cp("act", yTs[t % 2][:], pv)
            K.dma("sp", yT_d[1, :, :, t * 128:(t + 1) * 128], yTs[t % 2][:], w=[("yT", 1, t // 4)], semkey="yTs")


def cst_gamma(h):
    return 1.0 - 2.0 ** (-5.0 - h)


def phase_c(nc, K, sbt, banks, nb_, l, S, cst, Wb_in, hT_d, yT_d, vf_d, ident, identb, epsc, cols, col, colkeys,
            w2_d, a2_d, g2_d, v2_d, ln_d, dbg_d):
    NG = S // 512
    RW = BF16
    ncc = 15 if l >= 1 else 14
    with ExitStack() as es:
        wC = sbt(es, "wC", [128, 8, 1920], BF16)
        w2b = sbt(es, "w2b", [64, 512], BF16)
        a2b = sbt(es, "a2b", [128, 512], BF16)
        g2b = sbt(es, "g2b", [128, 512], BF16)
        v2b = sbt(es, "v2b", [32, 512], BF16)
        lnp = sbt(es, "lnp", [128, 2, 256])
        am = sbt(es, "amC", [128, 2, 2, 128])
        xm = sbt(es, "xmC", [128, 4, 64])
        rst = sbt(es, "rstC", [128, 512])
        bo = sbt(es, "boC", [128, 128], BF16)
        hg = sbt(es, "hgC", [128, 8, 512], BF16)
        zx = sbt(es, "zxC", [128, 15, 513])
        zl = sbt(es, "zlC", [128, 15, 512])
        tmp = [sbt(es, f"tmpC{i}", [128, 512]) for i in range(3)]
        tw = sbt(es, "twC", [64, 512], BF16)
        al = sbt(es, "alC", [128, 512], BF16)
        sgl = sbt(es, "sglC", [128, 512], BF16)
        vlr = sbt(es, "vlrC", [32, 512], BF16)
        sigw = sbt(es, "sigwC", [128, 512])
        iclr = sbt(es, "iclrC", [128, 512])
        cl = sbt(es, "clC", [128, 512])
        Pinc = sbt(es, "PincC", [128, 512])
        Pexc = sbt(es, "PexcC", [128, 512])
        Pinv = sbt(es, "PinvC", [128, 512])
        kkr = sbt(es, "kkrC", [128, 512])
        sqb = sbt(es, "sqbC", [128, 512], BF16)
        kmod = sbt(es, "kmodC", [128, 512])
        vfp = sbt(es, "vfpC", [128, 4, 512])
        vbf = sbt(es, "vbfC", [128, 4, 512], BF16)
        AR = sbt(es, "ARC", [128, 4, 8, 2, 64], RW)
        BK = sbt(es, "BKC", [128, 4, 8, 2, 64], RW)
        prk = sbt(es, "prkC", [128, 4, 512], BF16)
        pend = sbt(es, "pendC", [128, 4, 8])
        H = sbt(es, "HC", [128, 4, 64])
        Hb = sbt(es, "HbC", [128, 4, 64], RW)
        Amat = sbt(es, "AmatC", [128, 4, 2, 128], RW)
        Xm = [sbt(es, f"XmC{i}", [128, 2, 4, 64], RW) for i in range(2)]
        Wt = [sbt(es, f"WtC{i}", [128, 4, 64], RW) for i in range(2)]
        Vc = sbt(es, "VcC", [128, 4, 64], RW)
        Vcf = sbt(es, "VcfC", [128, 4, 64])
        BKt = sbt(es, "BKtC", [128, 4, 2, 64], RW)
        gt = sbt(es, "gtC", [128, 4, 64])
        bs = sbt(es, "bsC", [128, 4])
        st = sbt(es, "stC", [128, 4, 6])
        mv = sbt(es, "mvC", [128, 4, 2])
        rstd = sbt(es, "rstdC", [128, 4])
        yn = sbt(es, "ynC", [128, 4, 64])
        yc = sbt(es, "ycC", [128, 4, 64], BF16)
        ycT = [sbt(es, f"ycTC{i}", [128, 4, 512], BF16) for i in range(2)]

        nwc = 1792 + (32 if l >= 1 else 0)
        for g in range(0, 1792, 512):
            gw = min(512, 1792 - g)
            K.dma("sp", wC[:, :, g:g + gw], Wb_in[l, :, :, 3584 + g:3584 + g + gw],
                  r=[("Wb_in", l, (3584 + g) // 512)], w=[("wC", g)], semkey="wload")
        wCk = [("wC", g) for g in range(0, 1792, 512)]
        if l >= 1:
            K.dma("sp", wC[:, :, 1792:1824], Wb_in[l, :, :, NIN:NIN + 32], r=[("Wb_in", l, "v")], w=[("wC", "v")], semkey="wload")
            wCk.append(("wC", "v"))
            K.dma("pool", v2b[:], v2_d[l - 1], r=[], semkey="cloadp")
        K.dma("pool", w2b[:], w2_d[l], r=[], semkey="cloadp")
        K.dma("pool", a2b[64:128, :], a2_d[l], r=[], semkey="cloadp")
        K.dma("pool", g2b[:], g2_d[l], r=[], semkey="cloadp")
        K.dma("sp", lnp[:], ln_d[l].rearrange("p (a b) -> p a b", a=2), r=[], semkey="cload")
        K.dma("sp", am[:], cst["cAM"].rearrange("p (a j t) -> p a j t", a=2, j=2), r=[], semkey="cload")
        K.dma("sp", xm[:], cst["cXM"].rearrange("p (a t) -> p a t", a=4), r=[], semkey="cload")
        K.dma("sp", rst[:], cst["cRST"][:, :], r=[], semkey="cload")
        K.dma("pool", bo[:], cst["cBO"][:, :], r=[], semkey="cloadp")
        K.memset("dve", zx[:, :, 0:1], 0.0, w=[zx])
        K.memset("dve", H[:], 0.0)
        K.memset("pool", Hb[:], 0.0)
        lng = lnp[:, 0, :].rearrange("p (a v) -> p a v", a=4)
        lnb = lnp[:, 1, :].rearrange("p (a v) -> p a v", a=4)

        for tg in range(NG):
            tok = slice(tg * 512, (tg + 1) * 512)
            K.dma("sp", hg[:], hT_d[:, :, tok], r=[("hT", tg)])
            for cc in range(ncc):
                pb = nb_(0, 4)
                if cc < 14:
                    for kc in range(8):
                        K.mm(pb[:], wC[:, kc, cc * 128:(cc + 1) * 128], hg[:, kc, :], start=(kc == 0), stop=(kc == 7),
                             r=wCk + [hg])
                    K.cp("act", zx[:, cc, 1:513], pb[:])
                else:
                    for kc in range(8):
                        K.mm(pb[0:32, :], wC[:, kc, 1792:1824], hg[:, kc, :], start=(kc == 0), stop=(kc == 7), r=wCk + [hg])
                    K.cp("act", zx[0:32, cc, 1:513], pb[0:32, :])
            for cc in range(ncc):
                np_ = 128 if cc < 14 else 32
                mu = col(l, "mu", cc) if cc < 14 else col(l, "vmu", 0)
                tp = tmp[cc % 2]
                K.tt("pool" if cc % 2 else "dve", tp[0:np_, :], zx[0:np_, cc, 0:512], zx[0:np_, cc, 1:513], ALU.subtract)
                K.stt(zl[0:np_, cc, :], tp[0:np_, :], mu[0:np_, :], zx[0:np_, cc, 1:513], ALU.mult, ALU.add,
                      r=[tp, zx] + colkeys, w=[("zl", cc)])
            zlk = [("zl", cc) for cc in range(ncc)]
            K.cp("dve", zx[:, :, 0:1], zx[:, :, 512:513], r=[zx] + zlk, w=[zx])
            K.act(tw[:], zl[0:64, 12, :], AF.Tanh, r=[("zl", 12)])
            K.cp("dve", al[64:128, :], zl[64:128, 12, :], r=[("zl", 12)])
            K.act(sgl[:], zl[:, 13, :], AF.Sigmoid, r=[("zl", 13)])
            if l >= 1:
                K.cp("dve", vlr[:], zl[0:32, 14, :], r=[("zl", 14)])
                K.dma("sp", vfp[:], vf_d[:, :, tok], r=[("vf", tg)])
            for pc in range(4):
                rT = zl[:, pc, :]
                kT = zl[:, 4 + pc, :]
                vT = zl[:, 8 + pc, :]
                rk_ = [("zl", pc)]
                kk_ = [("zl", 4 + pc)]
                vk_ = [("zl", 8 + pc)]
                pw = nb_(0, 4)
                K.mm(pw[:], w2b[0:64, pc * 128:(pc + 1) * 128], tw[0:64, :])
                K.act(sigw[:], pw[:], AF.Sigmoid, bias=col(l, "w0", pc), r=[pw] + colkeys)
                pa = nb_(0, 4)
                K.mm(pa[:], a2b[64:128, pc * 128:(pc + 1) * 128], al[64:128, :])
                K.act(iclr[:], pa[:], AF.Sigmoid, bias=col(l, "a0", pc), r=[pa] + colkeys)
                K.op("dve", lambda e: e.tensor_tensor_scan(out=cl[:], data0=rst[:], data1=sigw[:], initial=0.0, op0=ALU.mult, op1=ALU.add),
                     [rst, sigw], [cl])
                K.act(Pinc[:], cl[:], AF.Exp, scale=-C0)
                K.act(Pinv[:], cl[:], AF.Exp, scale=C0)
                K.tt("pool", tmp[2][:], cl[:], sigw[:], ALU.subtract)
                K.act(Pexc[:], tmp[2][:], AF.Exp, scale=-C0)
                K.cp("dve", pend[:, pc, :], Pinc[:].rearrange("p (c t) -> p c t", c=8)[:, :, 63])
                K.ts("dve", kkr[:], kT, col(l, "kk", pc), None, ALU.mult, r=kk_ + colkeys)
                K.act(sqb[:], kkr[:], AF.Square)
                pss = nb_(0, 4)
                K.mm(pss[:], bo[:], sqb[:])
                K.act(tmp[0][:], pss[:], AF.Sqrt)
                K.ts("dve", tmp[0][:], tmp[0][:], 1e-12, None, ALU.max)
                K.op("dve", lambda e: e.reciprocal(out=tmp[0][:], in_=tmp[0][:]), [tmp[0]], [tmp[0]])
                K.tt("dve", kkr[:], kkr[:], tmp[0][:], ALU.mult)
                ARv = AR[:, pc].rearrange("p c j t -> p j c t")
                BKv = BK[:, pc].rearrange("p c j t -> p j c t")
                c3 = lambda ap: ap.rearrange("p (c t) -> p c t", c=8)
                K.stt(ARv[:, 0], c3(kkr[:]), -1.0, c3(Pexc[:]), ALU.mult, ALU.mult, r=[kkr, Pexc], w=[("AR", pc, 0)])
                K.tt("pool", tmp[1][:], kkr[:], iclr[:], ALU.mult)
                K.tt("dve", BKv[:, 0], c3(tmp[1][:]), c3(Pinv[:]), ALU.mult, r=[tmp[1], Pinv], w=[("BK", pc, 0)])
                K.ts("dve", tmp[2][:], iclr[:], 1.0, col(l, "ka", pc), ALU.subtract, ALU.mult, r=[iclr] + colkeys)
                K.stt(kmod[:], tmp[2][:], 1.0, kT, ALU.add, ALU.mult, r=[tmp[2]] + kk_)
                K.tt("dve", BKv[:, 1], c3(kmod[:]), c3(Pinv[:]), ALU.mult, r=[kmod, Pinv], w=[("BK", pc, 1)])
                K.tt("pool", ARv[:, 1], c3(rT), c3(Pinc[:]), ALU.mult, r=rk_ + [Pinc], w=[("AR", pc, 1)])
                K.stt(prk[:, pc, :], rT, col(l, "rk", pc), kmod[:], ALU.mult, ALU.mult, r=rk_ + [kmod] + colkeys, w=[("prk", pc)])
                if l == 0:
                    pass
                else:
                    pv_ = nb_(0, 4)
                    K.mm(pv_[:], v2b[0:32, pc * 128:(pc + 1) * 128], vlr[0:32, :])
                    K.act(tmp[0][:], pv_[:], AF.Sigmoid, bias=col(l, "v0", pc), r=[pv_] + colkeys)
                    K.tt("dve", tmp[1][:], vfp[:, pc, :], vT, ALU.subtract, r=[vfp] + vk_)
                    K.tt("dve", tmp[1][:], tmp[1][:], tmp[0][:], ALU.mult)
                    K.tt("dve", vT, vT, tmp[1][:], ALU.add, r=vk_ + [tmp[1]], w=vk_)
            K.cp("pool", vbf[:], zl[:, 8:12, :], r=[("zl", 8 + i) for i in range(4)])
            if l == 0:
                K.dma("sp", vf_d[:, :, tok], zl[:, 8:12, :], r=[("zl", 8 + i) for i in range(4)], w=[("vf", tg)], semkey="vfst")
            ARk = [("AR", pc, j) for pc in range(4) for j in range(2)]
            BKk = [("BK", pc, j) for pc in range(4) for j in range(2)]
            prkk = [("prk", pc) for pc in range(4)]
            vks = [("zl", 8 + i) for i in range(4)]
            yT_ = ycT[tg % 2]
            for c in range(8):
                ct = slice(c * 64, (c + 1) * 64)

                def hs(hh):
                    return slice(hh * 64, (hh + 1) * 64)
                pV = banks[4]
                pVv = pV[:].bitcast(BF16)[:, 0:256].rearrange("p (a v) -> p a v", a=4)
                for hh in range(2):
                    for pc in range(4):
                        K.tr(pVv[hs(hh), pc, :], vbf[hs(hh), pc, ct], identb[hs(hh), hs(hh)])
                K.cp("act", Vc[:], pVv)
                K.cp("dve", Vcf[:], pVv)
                pB = banks[5]
                pBv = pB[:].bitcast(BF16)[:, 0:512].rearrange("p (a j k) -> p a j k", a=4, j=2)
                for hh in range(2):
                    for pc in range(4):
                        for j in range(2):
                            K.tr(pBv[hs(hh), pc, j, :], BK[hs(hh), pc, c, j, :], identb[hs(hh), hs(hh)], r=BKk + [identb])
                K.cp("act", BKt[:], pBv)
                for pb2 in range(2):
                    pA = banks[6 + pb2]
                    pAv = pA[:].rearrange("p (a j t) -> p a j t", a=2, j=2)
                    for hh in range(2):
                        for a_ in range(2):
                            pc = pb2 * 2 + a_
                            rhs = AR[hs(hh), pc, c].rearrange("p j t -> p (j t)")
                            for j in range(2):
                                K.mm(pAv[hs(hh), a_, j, :], BK[hs(hh), pc, c, j, :], rhs, r=ARk + BKk)
                    K.tt("dve", Amat[:, pb2 * 2:pb2 * 2 + 2], pAv, am[:], ALU.mult, w=[("Amat", pb2)])
                Ak = [("Amat", 0), ("Amat", 1)]
                pX = banks[4]
                for hh in range(2):
                    for pc in range(4):
                        K.mm(pX[hs(hh), pc * 64:(pc + 1) * 64], AR[hs(hh), pc, c, 0, :], BK[hs(hh), pc, c, 0, :], r=ARk + BKk)
                X = Xm[0]
                K.tt("dve", X[:, 0], pX[:, 0:256].rearrange("p (a s) -> p a s", a=4), xm[:], ALU.mult, w=[("X", 0)])
                K.cp("act", X[:, 1], Amat[:, :, 0, 0:64], r=Ak, w=[("Y", 0)])
                pW = banks[5]
                for hh in range(2):
                    for pc in range(4):
                        o_ = pW[hs(hh), pc * 64:(pc + 1) * 64]
                        K.mm(o_, AR[hs(hh), pc, c, 0, :], Hb[hs(hh), pc, :], start=True, stop=False, r=ARk + [Hb])
                        K.mm(o_, Amat[hs(hh), pc, 1, 0:64], Vc[hs(hh), pc, :], start=False, stop=True, r=Ak + [Vc])
                W = Wt[0]
                K.cp("dve", W[:], pW[:, 0:256].rearrange("p (a v) -> p a v", a=4))
                xi = 0
                wi = 0
                for lev in range(6):
                    X = Xm[xi % 2]
                    xk = [("X", xi % 2), ("Y", xi % 2)]
                    pWn = banks[5 + (lev % 2)]
                    for pc in range(4):
                        for hh in range(2):
                            K.mm(pWn[hs(hh), pc * 64:(pc + 1) * 64], X[hs(hh), 1, pc, :], Wt[wi % 2][hs(hh), pc, :], r=xk + [Wt[wi % 2]])
                    K.tt("dve", Wt[(wi + 1) % 2][:], Wt[wi % 2][:], pWn[:, 0:256].rearrange("p (a v) -> p a v", a=4), ALU.add)
                    wi += 1
                    if lev < 5:
                        pXn = banks[4]
                        for hh in range(2):
                            for pc in range(4):
                                K.mm(pXn[hs(hh), pc * 64:(pc + 1) * 64], X[hs(hh), 1, pc, :], X[hs(hh), 0, pc, :], r=xk)
                                K.mm(pXn[hs(hh), 256 + pc * 64:256 + (pc + 1) * 64], X[hs(hh), 0, pc, :], X[hs(hh), 1, pc, :], r=xk)
                        Xn = Xm[(xi + 1) % 2]
                        K.cp("act", Xn[:], pXn[:].rearrange("p (j a s) -> p j a s", j=2, a=4),
                             w=[("X", (xi + 1) % 2), ("Y", (xi + 1) % 2)])
                        xi += 1
                U = Wt[wi % 2]
                pO = banks[6]
                for hh in range(2):
                    for pc in range(4):
                        o_ = pO[hs(hh), pc * 64:(pc + 1) * 64]
                        K.mm(o_, AR[hs(hh), pc, c, 1, :], Hb[hs(hh), pc, :], start=True, stop=False, r=ARk + [Hb])
                        K.mm(o_, Amat[hs(hh), pc, 0, 64:128], U[hs(hh), pc, :], start=False, stop=False, r=Ak + [U])
                        K.mm(o_, Amat[hs(hh), pc, 1, 64:128], Vc[hs(hh), pc, :], start=False, stop=True, r=Ak + [Vc])
                pH = banks[7]
                for hh in range(2):
                    for pc in range(4):
                        o_ = pH[hs(hh), pc * 64:(pc + 1) * 64]
                        K.mm(o_, BKt[hs(hh), pc, 0, :], U[hs(hh), pc, :], start=True, stop=False)
                        K.mm(o_, BKt[hs(hh), pc, 1, :], Vc[hs(hh), pc, :], start=False, stop=True)
                K.tt("dve", H[:], H[:], pH[:, 0:256].rearrange("p (a v) -> p a v", a=4), ALU.add)
                K.tt("dve", H[:], H[:], pend[:, :, c:c + 1].to_broadcast([128, 4, 64]), ALU.mult)
                K.cp("act", Hb[:], H[:])
                pG = banks[4]
                for hh in range(2):
                    for pc in range(4):
                        K.mm(pG[hs(hh), pc * 64:(pc + 1) * 64], sgl[:, ct], g2b[:, (2 * pc + hh) * 64:(2 * pc + hh + 1) * 64])
                        K.mm(pG[hs(hh), 256 + pc:256 + pc + 1], prk[hs(hh), pc, ct], bo[hs(hh), hh * 64:hh * 64 + 1], r=prkk + [bo])
                K.cp("act", gt[:], pG[:, 0:256].rearrange("p (a v) -> p a v", a=4))
                K.cp("dve", bs[:], pG[:, 256:260])
                for pc in range(4):
                    K.op("dve", lambda e, pc=pc: e.bn_stats(out=st[:, pc, :], in_=pO[:, pc * 64:(pc + 1) * 64]), [pO], [("stC", pc)])
                    K.op("dve", lambda e, pc=pc: e.bn_aggr(out=mv[:, pc, :], in_=st[:, pc, :]), [("stC", pc)], [("mvC", pc)])
                mvk = [("mvC", pc) for pc in range(4)]
                K.act(rstd[:], mv[:, :, 1], AF.Sqrt, bias=epsc[:, 1:2], scale=1.0, r=mvk + [epsc])
                K.op("dve", lambda e: e.reciprocal(out=rstd[:], in_=rstd[:]), [rstd], [rstd])
                for pc in range(4):
                    K.ts("dve", yn[:, pc, :], pO[:, pc * 64:(pc + 1) * 64], mv[:, pc, 0:1], rstd[:, pc:pc + 1], ALU.subtract, ALU.mult,
                         r=[pO, rstd] + mvk, w=[("ynC", pc)])
                ynk = [("ynC", pc) for pc in range(4)]
                K.tt("dve", yn[:], yn[:], lng, ALU.mult, r=ynk + [lnp], w=[yn])
                K.tt("pool", yn[:], yn[:], lnb, ALU.add, r=[yn, lnp], w=[yn])
                K.tt("dve", Vcf[:], Vcf[:], bs[:].rearrange("p (a o) -> p a o", o=1).to_broadcast([128, 4, 64]), ALU.mult)
                K.tt("pool", yn[:], yn[:], Vcf[:], ALU.add)
                K.tt("dve", yc[:], yn[:], gt[:], ALU.mult)
                pY = banks[5]
                pYv = pY[:].bitcast(BF16)[:, 0:256].rearrange("p (a t) -> p a t", a=4)
                for hh in range(2):
                    for pc in range(4):
                        K.tr(pYv[hs(hh), pc, :], yc[hs(hh), pc, :], identb[hs(hh), hs(hh)])
                K.cp("act", yT_[:, :, ct], pYv)
            K.dma("sp", yT_d[2, :, :, tok], yT_[:], w=[("yT", 2, tg)], semkey="yTs")


_CACHE = {}


def make_in_maps(inp, S, depth, n_cores):
    consts = host_consts(S)
    colsarr = np.stack([pack_cols(inp, l) for l in range(depth)])
    lnarr = np.stack([pack_ln(inp, l) for l in range(depth)])
    maps = []
    f = lambda a: np.ascontiguousarray(np.asarray(a, np.float32))
    shared = {
        "w_in": f(inp["w_in"]), "c_w2": f(inp["c_w2"]), "c_a2": f(inp["c_a2"]), "c_g2": f(inp["c_g2"]),
        "c_vres_down": f(inp["c_vres_down"]), "c_v2": f(inp["c_v2"]), "w_branch": f(inp["w_branch"]),
        "w_out": f(inp["w_out"]), "w_gate_up": f(inp["w_gate_up"]), "w_down": f(inp["w_down"]),
        "w_ple_gate": f(inp["w_ple_gate"]), "w_ple_proj": f(inp["w_ple_proj"]), "cols": colsarr, "lnp": lnarr,
    }
    shared.update(consts)
    x = np.asarray(inp["x"], np.float32)
    p = np.asarray(inp["p"], np.float32)
    for b in range(n_cores):
        m = dict(shared)
        m["x"] = np.ascontiguousarray(x[b])
        m["p"] = np.ascontiguousarray(p[:, b])
        maps.append(m)
    return maps


def kernel(**inputs):
    x = np.asarray(inputs["x"])
    B, S, _ = x.shape
    depth = np.asarray(inputs["w_in"]).shape[0]
    key = (S, depth)
    if key not in _CACHE:
        _CACHE[key] = build(S, depth)[0]
    nc = _CACHE[key]
    maps = make_in_maps(inputs, S, depth, B)
    res = run_bass_kernel_spmd(nc, maps, core_ids=list(range(B)))
    return np.stack([np.asarray(r["out"], np.float32) for r in res.results], axis=0)
```

```python
import math
from contextlib import ExitStack
import numpy as np
import concourse.bass as bass
import concourse.mybir as mybir
from concourse.bass_utils import run_bass_kernel_spmd

F32 = mybir.dt.float32
BF16 = mybir.dt.bfloat16
ALU = mybir.AluOpType
AF = mybir.ActivationFunctionType
AX = mybir.AxisListType

D = 1024
NIN = 8448
DFF = 2816
EPS = 1e-6
NEG = -30000.0
C0 = math.exp(-0.5)
SEM_LIMIT = 30000
import os
PA_STOP = int(os.environ.get("PA_STOP", "9"))
PA_SKIP = os.environ.get("PA_SKIP", "")


class Ctx:
    ENGS = ("pe", "dve", "act", "pool", "sp")

    def __init__(self, nc):
        self.nc = nc
        self.prog = {e: [] for e in self.ENGS}
        self.sems = {}
        self.semval = {}
        self.cur = {}
        self.waited = {e: {} for e in self.ENGS}
        self.res = {}
        self.nsem = 0
        self.ninstr = 0
        self.banktag = {}

    def _semkey(self, logical, step):
        sk = self.cur.get(logical)
        if sk is None or self.semval[sk] + step > SEM_LIMIT:
            ep = 0 if sk is None else sk[1] + 1
            sk = (logical, ep)
            self.sems[sk] = self.nc.alloc_semaphore(name=f"s{self.nsem}")
            self.nsem += 1
            self.semval[sk] = 0
            self.cur[logical] = sk
        return sk

    @staticmethod
    def _key(x):
        if isinstance(x, (str, tuple)):
            return x
        if hasattr(x, "tensor"):
            return x.tensor.name
        return x.name

    def _collect(self, reads, writes):
        deps = {}

        def add(d):
            if d is not None:
                deps[d[0]] = max(deps.get(d[0], 0), d[1])
        for r in reads:
            st = self.res.get(r)
            if st:
                add(st["w"])
        for w in writes:
            st = self.res.get(w)
            if st:
                add(st["w"])
                for sk, v in st["r"].items():
                    add((sk, v))
        return deps

    def _emit_waits(self, e, deps):
        for sk, v in deps.items():
            if self.waited[e].get(sk, 0) < v:
                h = self.sems[sk]
                self.prog[e].append(lambda eng, h=h, v=v: eng.wait_ge(h, v))
                self.waited[e][sk] = v

    def _update(self, reads, writes, sk, v):
        for r in reads:
            st = self.res.setdefault(r, {"w": None, "r": {}})
            st["r"][sk] = max(st["r"].get(sk, 0), v)
        for w in writes:
            self.res[w] = {"w": (sk, v), "r": {}}

    def op(self, e, fn, r=(), w=(), petag=None):
        reads = [self._key(x) for x in r]
        writes = [self._key(x) for x in w]
        writes = writes + [k for k in reads if isinstance(k, str) and k.startswith("bank") and k not in writes]
        skip = []
        if e == "pe" and petag is not None:
            for k in writes:
                if isinstance(k, str) and k.startswith("bank"):
                    st = self.res.get(k)
                    if st and st["w"] is not None and st["w"][0][0] == ("eng", "pe") and not st["r"] \
                            and self.banktag.get(k) == petag:
                        skip.append(k)
                    self.banktag[k] = petag
        deps = self._collect(reads, [k for k in writes if k not in skip])
        self._emit_waits(e, deps)
        sk = self._semkey(("eng", e), 1)
        self.semval[sk] += 1
        v = self.semval[sk]
        h = self.sems[sk]
        self.prog[e].append(lambda eng, fn=fn, h=h: fn(eng).then_inc(h, 1))
        self._update(reads, writes, sk, v)
        self.ninstr += 1

    def dma(self, q, out, in_, r=None, w=None, semkey=None, **kw):
        reads = [self._key(x) for x in (r if r is not None else [in_])]
        writes = [self._key(x) for x in (w if w is not None else [out])]
        if semkey is None:
            semkey = out.tensor.name
        lk = ("dma", semkey)
        sk = self._semkey(lk, 16)
        deps = self._collect(reads, writes)
        if self.semval[sk] > 0:
            deps[sk] = max(deps.get(sk, 0), self.semval[sk])
        self._emit_waits(q, deps)
        self.semval[sk] += 16
        v = self.semval[sk]
        h = self.sems[sk]
        self.prog[q].append(
            lambda eng, out=out, in_=in_, kw=kw, h=h: eng.dma_start(out=out, in_=in_, **kw).then_inc(h, 16))
        self._update(reads, writes, sk, v)
        self.ninstr += 1

    def barrier(self):
        deps = {sk: v for sk, v in self.semval.items() if v > 0}
        for e in self.ENGS:
            self._emit_waits(e, deps)

    def final_wait(self, e, keys):
        deps = self._collect([self._key(k) for k in keys], ())
        self._emit_waits(e, deps)

    def mm(self, out, lhsT, rhs, start=True, stop=True, r=None, w=None):
        tag = (lhsT.start_partition(), lhsT.partition_size())
        self.op("pe", lambda e: e.matmul(out, lhsT=lhsT, rhs=rhs, start=start, stop=stop),
                r if r is not None else [lhsT, rhs], w if w is not None else [out], petag=tag)

    def tr(self, out, in_, ident, r=None, w=None):
        tag = (in_.start_partition(), in_.partition_size())
        self.op("pe", lambda e: e.transpose(out=out, in_=in_, identity=ident),
                r if r is not None else [in_, ident], w if w is not None else [out], petag=tag)

    def act(self, out, in_, func, bias=None, scale=None, r=None, w=None, eng="act"):
        kw = {}
        if bias is not None:
            kw["bias"] = bias
        if scale is not None:
            kw["scale"] = scale
        rr = [in_] + [x for x in (bias, scale) if not isinstance(x, (int, float, type(None)))]
        self.op("act", lambda e: e.activation(out=out, in_=in_, func=func, **kw),
                r if r is not None else rr, w if w is not None else [out])

    def tt(self, eng, out, in0, in1, op, r=None, w=None):
        self.op(eng, lambda e: e.tensor_tensor(out=out, in0=in0, in1=in1, op=op),
                r if r is not None else [in0, in1], w if w is not None else [out])

    def ts(self, eng, out, in0, s1, s2, op0, op1=None, r=None, w=None):
        rr = [in0] + [x for x in (s1, s2) if not isinstance(x, (int, float, type(None)))]
        if op1 is None:
            fn = lambda e: e.tensor_scalar(out=out, in0=in0, scalar1=s1, scalar2=None, op0=op0)
        else:
            fn = lambda e: e.tensor_scalar(out=out, in0=in0, scalar1=s1, scalar2=s2, op0=op0, op1=op1)
        self.op(eng, fn, r if r is not None else rr, w if w is not None else [out])

    def stt(self, out, in0, scalar, in1, op0, op1, r=None, w=None):
        rr = [in0, in1] + ([scalar] if not isinstance(scalar, (int, float)) else [])
        self.op("dve", lambda e: e.scalar_tensor_tensor(out=out, in0=in0, scalar=scalar, in1=in1, op0=op0, op1=op1),
                r if r is not None else rr, w if w is not None else [out])

    def cp(self, eng, out, in_, r=None, w=None):
        if eng == "act":
            fn = lambda e: e.copy(out=out, in_=in_)
        else:
            fn = lambda e: e.tensor_copy(out=out, in_=in_)
        self.op(eng, fn, r if r is not None else [in_], w if w is not None else [out])

    def memset(self, eng, ap, val, w=None):
        self.op(eng, lambda e: e.memset(ap, val), [], w if w is not None else [ap])

    def replay(self):
        nc = self.nc
        with nc.Block() as block:
            @block.tensor
            def _(eng):
                for f in self.prog["pe"]:
                    f(eng)

            @block.vector
            def _(eng):
                for f in self.prog["dve"]:
                    f(eng)

            @block.scalar
            def _(eng):
                for f in self.prog["act"]:
                    f(eng)

            @block.gpsimd
            def _(eng):
                for f in self.prog["pool"]:
                    f(eng)

            @block.sync
            def _(eng):
                for f in self.prog["sp"]:
                    f(eng)


def host_consts(S):
    c = {}
    pos = np.arange(S, dtype=np.float32)
    inv_a = (1.0 / (np.float32(500000.0) ** (np.arange(0, 16, 2, dtype=np.float32) / np.float32(16)))).astype(np.float32)
    ang = (pos[:, None] * inv_a[None, :]).astype(np.float32)
    cos_a, sin_a = np.cos(ang).astype(np.float32), np.sin(ang).astype(np.float32)
    ca = np.zeros((S, 2, 2, 8, 8), np.float32)
    ca[:, 0] = cos_a[:, None, None, :]
    ca[:, 1] = sin_a[:, None, None, :]
    c["cA"] = ca.reshape(S, 256)
    inv_b = (1.0 / (np.float32(10000.0) ** np.linspace(0.0, 1.0, 64, dtype=np.float32))).astype(np.float32)
    angb = (pos[:, None] * inv_b[None, :]).astype(np.float32)
    cos_b, sin_b = np.cos(angb).astype(np.float64), np.sin(angb).astype(np.float64)
    lg = np.log(1.0 - 2.0 ** (-5.0 - np.arange(4, dtype=np.float64)))
    i = (np.arange(S) % 128).astype(np.float64)
    gq = np.exp(lg[None, :] * i[:, None])
    gk = np.exp(-lg[None, :] * i[:, None]) * (128.0 ** -0.5)
    cb = np.zeros((S, 4, 4, 64), np.float64)
    cb[:, 0] = cos_b[:, None, :] * gq[:, :, None]
    cb[:, 1] = sin_b[:, None, :] * gq[:, :, None]
    cb[:, 2] = cos_b[:, None, :] * gk[:, :, None]
    cb[:, 3] = sin_b[:, None, :] * gk[:, :, None]
    c["cB"] = cb.reshape(S, 1024).astype(np.float32)
    gam = np.exp(lg)
    rs = np.zeros((128, 12), np.float32)
    rs[:, 0:4] = (gam ** 128)[None, :]
    rs[:, 4:8] = (gam ** 127)[None, :]
    rs[:, 8:12] = gam[None, :]
    c["cRS"] = rs
    E = np.zeros((16, S), np.float32)
    for j in range(16):
        E[j, j * 256:(j + 1) * 256] = 1.0
    c["cE"] = E
    cm = np.zeros((2, 128, 256), np.float32)
    for kt in range(2):
        k = kt * 128 + np.arange(128)[:, None]
        q = np.arange(256)[None, :]
        cm[kt] = np.where(k <= q, 0.0, NEG)
    c["cCM"] = cm.transpose(1, 0, 2).reshape(128, 512)
    j = np.arange(128)[:, None]
    ii = np.arange(128)[None, :]
    m = (j <= ii).astype(np.float32)
    c["cRM"] = np.tile(m[:, None, :], (1, 4, 1)).reshape(128, 512)
    s = (np.arange(128) % 64)[:, None]
    t = np.arange(64)[None, :]
    strict = (s < t).astype(np.float32)
    incl = (s <= t).astype(np.float32)
    am = np.concatenate([strict, incl], axis=1)
    c["cAM"] = np.tile(am[:, None, :], (1, 4, 1)).reshape(128, 512)
    tt_ = (np.arange(128) % 64)[:, None]
    ss_ = np.arange(64)[None, :]
    xm = (ss_ < tt_).astype(np.float32)
    c["cXM"] = np.tile(xm[:, None, :], (1, 4, 1)).reshape(128, 256)
    rm = np.ones((128, 512), np.float32)
    rm[:, ::64] = 0.0
    c["cRST"] = rm
    c["cID"] = np.eye(128, dtype=np.float32)
    bo = np.zeros((128, 128), np.float32)
    bo[:64, :64] = 1.0
    bo[64:, 64:] = 1.0
    c["cBO"] = bo
    c["cI4"] = np.tile(np.eye(64, dtype=np.float32)[None, None], (2, 4, 1, 1)).transpose(0, 2, 1, 3).reshape(128, 256)
    return c


CONST_SHAPES = lambda S: {"cA": [S, 256], "cB": [S, 1024], "cRS": [128, 12], "cE": [16, S], "cCM": [128, 512],
                          "cRM": [128, 512], "cAM": [128, 512], "cXM": [128, 256], "cRST": [128, 512],
                          "cID": [128, 128], "cBO": [128, 128], "cI4": [128, 256]}

COLS = {"mixg": (0, 8), "ffng": (8, 8), "pleg": (16, 8), "fing": (24, 8), "mu": (32, 14), "w0": (46, 4),
        "a0": (50, 4), "kk": (54, 4), "ka": (58, 4), "rk": (62, 4), "v0": (66, 4), "vmu": (70, 1)}
NCOL = 72


def pack_cols(inp, l):
    out = np.zeros((128, NCOL), np.float32)

    def put(name, vec):
        o, n = COLS[name]
        v = np.asarray(vec, np.float32).reshape(-1)
        out[:, o:o + n] = v.reshape(n, 128).T
    put("mixg", inp["norm_mix_g"][l])
    put("ffng", inp["norm_ffn_g"][l])
    put("pleg", inp["norm_ple_g"][l])
    put("fing", inp["final_norm_g"])
    put("mu", inp["c_mu"][l])
    put("w0", inp["c_w0"][l])
    put("a0", inp["c_a0"][l])
    put("kk", inp["c_k_k"][l])
    put("ka", inp["c_k_a"][l])
    put("rk", inp["c_r_k"][l])
    if l >= 1:
        put("v0", inp["c_v0"][l - 1])
        out[0:32, COLS["vmu"][0]] = np.asarray(inp["c_vres_mu"][l - 1], np.float32)
    return out


def pack_ln(inp, l):
    out = np.zeros((128, 2, 4, 64), np.float32)
    for k, name in enumerate(("c_ln_g", "c_ln_b")):
        v = np.asarray(inp[name][l], np.float32).reshape(4, 2, 64)
        for hh in range(2):
            out[hh * 64:(hh + 1) * 64, k] = v[:, hh, :][None]
    return out.reshape(128, 512)


def build(S, depth=2, en="abc", dbg=()):
    NT = S // 128
    NG = S // 512
    NB = S // 256
    nc = bass.Bass("TRN2", target_bir_lowering=False)

    def din(name, shape, dt=F32):
        return nc.dram_tensor(name, list(shape), dt, kind="ExternalInput").ap()

    def dscr(name, shape, dt=F32):
        return nc.dram_tensor(name, list(shape), dt, kind="Internal").ap()

    x_d = din("x", [S, D])
    p_d = din("p", [depth, S, 256])
    w_in = din("w_in", [depth, D, NIN])
    w2_d = din("c_w2", [depth, 64, 512])
    a2_d = din("c_a2", [depth, 64, 512])
    g2_d = din("c_g2", [depth, 128, 512])
    vd_d = din("c_vres_down", [max(depth - 1, 1), D, 32])
    v2_d = din("c_v2", [max(depth - 1, 1), 32, 512])
    wbr_d = din("w_branch", [depth, 3, 512, D])
    wout_d = din("w_out", [depth, D, D])
    wgu_d = din("w_gate_up", [depth, D, 2 * DFF])
    wd_d = din("w_down", [depth, DFF, D])
    wpg_d = din("w_ple_gate", [depth, D, D])
    wpp_d = din("w_ple_proj", [depth, 256, D])
    cols_d = din("cols", [depth, 128, NCOL])
    ln_d = din("lnp", [depth, 128, 512])
    cst = {k: din(k, shp) for k, shp in CONST_SHAPES(S).items()}
    out_d = nc.dram_tensor("out", [S, D], F32, kind="ExternalOutput").ap()

    xT_d = dscr("xT_d", [128, 8, S])
    hT_d = dscr("hT_d", [128, 8, S], BF16)
    yT_d = dscr("yT_d", [3, 128, 4, S], BF16)
    vf_d = dscr("vf_d", [128, 4, S])
    NINX = NIN + 32
    Wb_in = dscr("Wb_in", [depth, 128, 8, NINX], BF16)
    Wb_br = dscr("Wb_br", [depth, 128, 3, 4, D], BF16)
    Wb_out = dscr("Wb_out", [depth, 128, 8, D], BF16)
    Wb_gu = dscr("Wb_gu", [depth, 128, 8, 2 * DFF], BF16)
    Wb_d = dscr("Wb_d", [depth, 128, 22, D], BF16)
    Wb_pg = dscr("Wb_pg", [depth, 128, 8, D], BF16)
    Wb_pp = dscr("Wb_pp", [depth, 128, 2, D], BF16)
    dbg_d = {}
    for name, shp in dbg:
        dbg_d[name] = nc.dram_tensor("dbg_" + name, list(shp), F32, kind="ExternalOutput").ap()

    K = Ctx(nc)
    uniq = [0]
    with ExitStack() as top:
        def sbt(es, name, shape, dt=F32):
            uniq[0] += 1
            return es.enter_context(nc.sbuf_tensor(f"s_{name}_{uniq[0]}", list(shape), dt))

        banks = [top.enter_context(nc.psum_tensor(f"bank{i}", [128, 512], F32)) for i in range(8)]
        bank_rr = [0]

        def nb_(lo=0, hi=8):
            b = banks[lo + bank_rr[0] % (hi - lo)]
            bank_rr[0] += 1
            return b

        ident = sbt(top, "ident", [128, 128])
        identb = sbt(top, "identb", [128, 128], BF16)
        onesb = sbt(top, "onesb", [128, 128], BF16)
        cols = sbt(top, "cols", [128, depth, NCOL])
        K.dma("sp", ident[:], cst["cID"][:, :], r=[], semkey="cload")
        K.cp("dve", identb[:], ident[:])
        K.memset("dve", onesb[:], 1.0)
        for l in range(depth):
            K.dma("sp", cols[:, l, :], cols_d[l], r=[], w=[("cols", l)], semkey="cload")
        colkeys = [("cols", l) for l in range(depth)]

        def col(l, name, j=0, n=1):
            o, _ = COLS[name]
            return cols[:, l, o + j:o + j + n]

        def convert_weights(l):
            for c0 in range(0, NIN, 512):
                cw = min(512, NIN - c0)
                K.dma("pool", Wb_in[l, :, :, c0:c0 + cw], w_in[l, :, c0:c0 + cw].rearrange("(k p) n -> p k n", p=128),
                      r=[], w=[("Wb_in", l, c0 // 512)], semkey="conv")
            if l >= 1:
                K.dma("pool", Wb_in[l, :, :, NIN:NINX], vd_d[l - 1].rearrange("(k p) n -> p k n", p=128),
                      r=[], w=[("Wb_in", l, "v")], semkey="conv")
            for n in range(3):
                for h in range(2):
                    K.dma("pool", Wb_br[l, :, n, :, h * 512:(h + 1) * 512],
                          wbr_d[l, n, :, h * 512:(h + 1) * 512].rearrange("(k p) n -> p k n", p=128), r=[], w=[("Wb_br", l)], semkey="conv")
            for h in range(2):
                K.dma("pool", Wb_out[l, :, :, h * 512:(h + 1) * 512],
                      wout_d[l, :, h * 512:(h + 1) * 512].rearrange("(k p) n -> p k n", p=128), r=[], w=[("Wb_out", l)], semkey="conv")
                K.dma("pool", Wb_pg[l, :, :, h * 512:(h + 1) * 512],
                      wpg_d[l, :, h * 512:(h + 1) * 512].rearrange("(k p) n -> p k n", p=128), r=[], w=[("Wb_pg", l)], semkey="conv")
                K.dma("pool", Wb_pp[l, :, :, h * 512:(h + 1) * 512],
                      wpp_d[l, :, h * 512:(h + 1) * 512].rearrange("(k p) n -> p k n", p=128), r=[], w=[("Wb_pp", l)], semkey="conv")
                for k0 in (0, 11):
                    K.dma("pool", Wb_d[l, :, k0:k0 + 11, h * 512:(h + 1) * 512],
                          wd_d[l, k0 * 128:(k0 + 11) * 128, h * 512:(h + 1) * 512].rearrange("(k p) n -> p k n", p=128),
                          r=[], w=[("Wb_d", l)], semkey="conv")
            for c0 in range(0, 2 * DFF, 512):
                K.dma("pool", Wb_gu[l, :, :, c0:c0 + 512], wgu_d[l, :, c0:c0 + 512].rearrange("(k p) n -> p k n", p=128),
                      r=[], w=[("Wb_gu", l)], semkey="conv")

        def norm_group(es_name, xg, hg_out, l, gname, scratch):
            sq, rstd = scratch
            pb = nb_()
            for c in range(8):
                K.act(sq[:, c, :], xg[:, c, :], AF.Square)
            for c in range(8):
                K.mm(pb[:], onesb[:], sq[:, c, :], start=(c == 0), stop=(c == 7))
            K.act(rstd[:], pb[:], AF.Sqrt, bias=epsc[:, 0:1], scale=1.0 / D)
            K.op("dve", lambda e: e.reciprocal(out=rstd[:], in_=rstd[:]), [rstd], [rstd])
            for c in range(8):
                K.stt(hg_out[:, c, :], xg[:, c, :], col(l, gname, c), rstd[:], ALU.mult, ALU.mult,
                      r=[xg, rstd] + colkeys)

        epsc = sbt(top, "epsc", [128, 4])
        K.memset("dve", epsc[:, 0:1], EPS)
        K.memset("dve", epsc[:, 1:2], 1e-5 * 64)
        K.memset("dve", epsc[:, 2:3], 0.0)

        convert_weights(0)

        with ExitStack() as es:
            xin = [sbt(es, f"xin{i}", [128, D]) for i in range(2)]
            xg2 = [sbt(es, f"xgI{i}", [128, 8, 512]) for i in range(2)]
            hg2 = [sbt(es, f"hgI{i}", [128, 8, 512], BF16) for i in range(2)]
            sq = sbt(es, "sqI", [128, 8, 512], BF16)
            rstd = sbt(es, "rstdI", [128, 512])
            for tg in range(NG):
                xg = xg2[tg % 2]
                hg = hg2[tg % 2]
                for tt_ in range(4):
                    t = tg * 4 + tt_
                    xi = xin[t % 2]
                    K.dma("sp", xi[:], x_d[t * 128:(t + 1) * 128, :], r=[])
                    for half in range(2):
                        pb = nb_()
                        for c in range(4):
                            K.tr(pb[:, c * 128:(c + 1) * 128], xi[:, (half * 4 + c) * 128:(half * 4 + c + 1) * 128], ident[:])
                        K.cp("act" if half else "dve", xg[:, half * 4:(half + 1) * 4, tt_ * 128:(tt_ + 1) * 128],
                             pb[:].rearrange("p (c t) -> p c t", c=4))
                K.dma("sp", xT_d[:, :, tg * 512:(tg + 1) * 512], xg[:], w=[("xT", tg)], semkey="xTst")
                norm_group("I", xg, hg, 0, "mixg", (sq, rstd))
                K.dma("sp", hT_d[:, :, tg * 512:(tg + 1) * 512], hg[:], w=[("hT", tg)], semkey="hTst")

        K.barrier()
        for l in range(depth):
            if l + 1 < depth:
                convert_weights(l + 1)
            last = (l == depth - 1)
            if "a" in en:
                phase_a(nc, K, sbt, banks, nb_, l, S, cst, Wb_in, hT_d, yT_d, ident, identb)
                K.barrier()
            if "b" in en:
                phase_b(nc, K, sbt, banks, nb_, l, S, cst, Wb_in, hT_d, yT_d, ident, identb, epsc)
                K.barrier()
            if "c" in en:
                phase_c(nc, K, sbt, banks, nb_, l, S, cst, Wb_in, hT_d, yT_d, vf_d, ident, identb, epsc, cols, col, colkeys,
                        w2_d, a2_d, g2_d, v2_d, ln_d, dbg_d)
                K.barrier()

            with ExitStack() as es:
                xg2 = [sbt(es, f"xgT{i}", [128, 8, 512]) for i in range(2)]
                hg = sbt(es, "hgT", [128, 8, 512], BF16)
                yg = sbt(es, "ygT", [128, 3, 4, 512], BF16)
                hf = sbt(es, "hfT", [128, 8, 512], BF16)
                mg = sbt(es, "mgT", [128, 8, 512], BF16)
                actT = sbt(es, "actT", [128, 22, 512], BF16)
                sq = sbt(es, "sqT", [128, 8, 512], BF16)
                rstd = sbt(es, "rstdT", [128, 512])
                sg = [sbt(es, f"sgT{i}", [128, 512]) for i in range(2)]
                acc = sbt(es, "accT", [128, 512])
                tmpf = [sbt(es, f"tmpT{i}", [128, 512]) for i in range(2)]
                wst = [sbt(es, f"wst{i}", [128, 8, 1024], BF16) for i in range(2)]
                wdt = [sbt(es, f"wdt{i}", [128, 22, 128], BF16) for i in range(2)]
                wbrt = sbt(es, "wbrt", [128, 3, 4, 128], BF16)
                wppt = sbt(es, "wppt", [128, 2, D], BF16)
                pin = [sbt(es, f"pin{i}", [128, 256]) for i in range(2)]
                pT = sbt(es, "pTT", [128, 2, 512], BF16)
                ot = [sbt(es, f"otT{i}", [128, D]) for i in range(2)]
                wsi = [0]

                def wslab():
                    t_ = wst[wsi[0] % 2]
                    wsi[0] += 1
                    return t_
                K.dma("sp", wppt[:], Wb_pp[l], r=[("Wb_pp", l)], semkey="cload")
                for tg in range(NG):
                    xg = xg2[tg % 2]
                    tok = slice(tg * 512, (tg + 1) * 512)
                    K.dma("sp", xg[:], xT_d[:, :, tok], r=[("xT", tg)])
                    K.dma("sp", hg[:], hT_d[:, :, tok], r=[("hT", tg)])
                    for n in range(3):
                        if "abc"[n] in en:
                            K.dma("sp", yg[:, n], yT_d[n, :, :, tok], r=[("yT", n, tg)], w=[("ygT", n)], semkey="ygT")
                        elif tg == 0:
                            K.memset("pool", yg[:, n], 0.0, w=[("ygT", n)])
                    ygk = [("ygT", n) for n in range(3)]
                    for dc in range(8):
                        ws = wslab()
                        for n in range(3):
                            c0 = 5376 + n * 1024 + dc * 128
                            K.dma("sp", ws[:, :, n * 128:(n + 1) * 128], Wb_in[l, :, :, c0:c0 + 128],
                                  r=[("Wb_in", l, c0 // 512)], w=[ws])
                        K.dma("sp", wbrt[:], Wb_br[l, :, :, :, dc * 128:(dc + 1) * 128], r=[("Wb_br", l)])
                        for n in range(3):
                            pbr = nb_()
                            pgt = nb_()
                            for kc in range(4):
                                K.mm(pbr[:], wbrt[:, n, kc, :], yg[:, n, kc, :], start=(kc == 0), stop=(kc == 3),
                                     r=[wbrt, ("ygT", n)])
                            for kc in range(8):
                                K.mm(pgt[:], ws[:, kc, n * 128:(n + 1) * 128], hg[:, kc, :], start=(kc == 0), stop=(kc == 7))
                            s_ = sg[n % 2]
                            K.act(s_[:], pgt[:], AF.Sigmoid)
                            if n == 0:
                                K.tt("dve", acc[:], s_[:], pbr[:], ALU.mult)
                            elif n == 1:
                                K.tt("dve", tmpf[0][:], s_[:], pbr[:], ALU.mult)
                                K.tt("pool", acc[:], acc[:], tmpf[0][:], ALU.add)
                            else:
                                K.tt("dve", tmpf[1][:], s_[:], pbr[:], ALU.mult)
                                K.tt("dve", mg[:, dc, :], acc[:], tmpf[1][:], ALU.add)
                    for dc in range(8):
                        if dc % 4 == 0:
                            ws = wslab()
                            K.dma("sp", ws[:, :, 0:512], Wb_out[l, :, :, dc * 128:dc * 128 + 512], r=[("Wb_out", l)], w=[ws])
                        po = nb_()
                        for kc in range(8):
                            K.mm(po[:], ws[:, kc, (dc % 4) * 128:(dc % 4 + 1) * 128], mg[:, kc, :], start=(kc == 0), stop=(kc == 7))
                        K.tt("dve", xg[:, dc, :], xg[:, dc, :], po[:], ALU.add)
                    norm_group("T", xg, hf, l, "ffng", (sq, rstd))
                    for f4 in range(0, 22, 4):
                        nf = min(4, 22 - f4)
                        ws = wslab()
                        K.dma("sp", ws[:, :, 0:nf * 128], Wb_gu[l, :, :, f4 * 128:(f4 + nf) * 128], r=[("Wb_gu", l)], w=[ws])
                        K.dma("sp", ws[:, :, 512:512 + nf * 128], Wb_gu[l, :, :, DFF + f4 * 128:DFF + (f4 + nf) * 128],
                              r=[("Wb_gu", l)], w=[ws])
                        for fi in range(nf):
                            fc = f4 + fi
                            pg_ = nb_()
                            pu_ = nb_()
                            for kc in range(8):
                                K.mm(pg_[:], ws[:, kc, fi * 128:(fi + 1) * 128], hf[:, kc, :], start=(kc == 0), stop=(kc == 7))
                            for kc in range(8):
                                K.mm(pu_[:], ws[:, kc, 512 + fi * 128:512 + (fi + 1) * 128], hf[:, kc, :], start=(kc == 0), stop=(kc == 7))
                            s_ = sg[fc % 2]
                            K.act(s_[:], pg_[:], AF.Silu)
                            K.tt("dve", actT[:, fc, :], s_[:], pu_[:], ALU.mult)
                    for dc in range(8):
                        wd_ = wdt[dc % 2]
                        K.dma("sp", wd_[:], Wb_d[l, :, :, dc * 128:(dc + 1) * 128], r=[("Wb_d", l)])
                        pd = nb_()
                        for fc in range(22):
                            K.mm(pd[:], wd_[:, fc, :], actT[:, fc, :], start=(fc == 0), stop=(fc == 21))
                        K.tt("dve", xg[:, dc, :], xg[:, dc, :], pd[:], ALU.add)
                    norm_group("T", xg, hf, l, "pleg", (sq, rstd))
                    for tt_ in range(4):
                        pi = pin[tt_ % 2]
                        K.dma("sp", pi[:], p_d[l, tg * 512 + tt_ * 128: tg * 512 + (tt_ + 1) * 128, :], r=[])
                        pb = nb_()
                        for c in range(2):
                            K.tr(pb[:, c * 128:(c + 1) * 128], pi[:, c * 128:(c + 1) * 128], ident[:])
                        K.cp("act", pT[:, :, tt_ * 128:(tt_ + 1) * 128], pb[:, 0:256].rearrange("p (c t) -> p c t", c=2))
                    for dc in range(8):
                        if dc % 4 == 0:
                            ws = wslab()
                            K.dma("sp", ws[:, :, 0:512], Wb_pg[l, :, :, dc * 128:dc * 128 + 512], r=[("Wb_pg", l)], w=[ws])
                        pg_ = nb_()
                        pp_ = nb_()
                        for kc in range(8):
                            K.mm(pg_[:], ws[:, kc, (dc % 4) * 128:(dc % 4 + 1) * 128], hf[:, kc, :], start=(kc == 0), stop=(kc == 7))
                        for kc in range(2):
                            K.mm(pp_[:], wppt[:, kc, dc * 128:(dc + 1) * 128], pT[:, kc, :], start=(kc == 0), stop=(kc == 1))
                        s_ = sg[dc % 2]
                        K.act(s_[:], pg_[:], AF.Sigmoid)
                        K.tt("dve", tmpf[dc % 2][:], s_[:], pp_[:], ALU.mult)
                        K.tt("pool", xg[:, dc, :], xg[:, dc, :], tmpf[dc % 2][:], ALU.add)
                    if not last:
                        K.dma("sp", xT_d[:, :, tok], xg[:], w=[("xT", tg)], semkey="xTst")
                        norm_group("T", xg, hf, l + 1, "mixg", (sq, rstd))
                        K.dma("sp", hT_d[:, :, tok], hf[:], w=[("hT", tg)], semkey="hTst")
                    else:
                        pb = nb_()
                        for c in range(8):
                            K.act(sq[:, c, :], xg[:, c, :], AF.Square)
                        for c in range(8):
                            K.mm(pb[:], onesb[:], sq[:, c, :], start=(c == 0), stop=(c == 7))
                        K.act(rstd[:], pb[:], AF.Sqrt, bias=epsc[:, 0:1], scale=1.0 / D)
                        K.op("dve", lambda e: e.reciprocal(out=rstd[:], in_=rstd[:]), [rstd], [rstd])
                        for c in range(8):
                            K.stt(xg[:, c, :], xg[:, c, :], col(l, "fing", c), rstd[:], ALU.mult, ALU.mult,
                                  r=[xg, rstd] + colkeys)
                        for tt_ in range(4):
                            o_ = ot[tt_ % 2]
                            for half in range(2):
                                pb2 = nb_()
                                for c in range(4):
                                    K.tr(pb2[:, c * 128:(c + 1) * 128], xg[:, half * 4 + c, tt_ * 128:(tt_ + 1) * 128], ident[:])
                                K.cp("act" if half else "dve", o_[:, half * 512:(half + 1) * 512], pb2[:])
                            K.dma("sp", out_d[tg * 512 + tt_ * 128: tg * 512 + (tt_ + 1) * 128, :], o_[:], w=["out"],
                                  semkey="out")
            K.barrier()
        K.final_wait("sp", ["out"] + ["dbg_" + n for n in dbg_d])
        K.replay()
    return nc, K


def phase_a(nc, K, sbt, banks, nb_, l, S, cst, Wb_in, hT_d, yT_d, ident, identb):
    NT = S // 128
    NB = S // 256
    with ExitStack() as es:
        wA = sbt(es, "wA", [128, 8, 1536], BF16)
        KT = sbt(es, "KT", [80, 8, S], BF16)
        Va = sbt(es, "Va", [128, NT, 8, 65], BF16)
        QT = [sbt(es, f"QT{i}", [80, 8, 256], BF16) for i in range(2)]
        QTf = [sbt(es, f"QTf{i}", [64, 8, 128]) for i in range(2)]
        kms = sbt(es, "kms", [64, 8, 16])
        ktmp = sbt(es, "ktmp", [64, 8])
        hTt = [sbt(es, f"hTtA{i}", [128, 8, 128], BF16) for i in range(2)]
        qk = [sbt(es, f"qkA{i}", [128, 2, 8, 64]) for i in range(2)]
        cs = [sbt(es, f"csA{i}", [128, 2, 128]) for i in range(2)]
        rt = [sbt(es, f"rtA{i}", [128, 128]) for i in range(4)]
        gs = sbt(es, "gsA", [128, 8, 16])
        m8 = sbt(es, "m8A", [128, 8, 8])
        Mp = sbt(es, "MpA", [128, 8, 80], BF16)
        cm = sbt(es, "cmA", [128, 2, 256], BF16)
        PT = [sbt(es, f"PTA{i}", [128, 256], BF16) for i in range(3)]
        ya = [sbt(es, f"yaA{i}", [128, 512], BF16) for i in range(2)]
        rden = sbt(es, "rdenA", [128, 4])
        yTs = [sbt(es, f"yTsA{i}", [128, 4, 128], BF16) for i in range(2)]
        for g in range(3):
            K.dma("sp", wA[:, :, g * 512:(g + 1) * 512], Wb_in[l, :, :, g * 512:(g + 1) * 512], r=[("Wb_in", l, g)],
                  w=[("wA", g)], semkey="wload")
        wAk = [("wA", g) for g in range(3)]
        K.dma("pool", cm[:], cst["cCM"].rearrange("p (k q) -> p k q", k=2), r=[], semkey="cloadp")
        for h in range(8 if "e" not in PA_SKIP else 0):
            for e0 in range(0, S, 2048):
                e1 = min(S, e0 + 2048)
                K.dma("pool", KT[64:80, h, e0:e1], cst["cE"][:, e0:e1], r=[], w=[("KTE", h)], semkey="cloadp")
        KTE = [("KTE", h) for h in range(8)]
        if "m" not in PA_SKIP:
            K.memset("pool", Va[:, :, :, 64:65], 1.0, w=["Va1"])
            K.memset("pool", Mp[:], 0.0)
        pz = [banks[0], banks[1]]
        pTq = [banks[2], banks[3]]
        pS = [banks[4], banks[5]]
        pO = [banks[6], banks[7]]
        pti = 0
        psi = 0
        for t in range(NT):
            b = t // 2
            half = t % 2
            hT = hTt[t % 2]
            K.dma("sp", hT[:], hT_d[:, :, t * 128:(t + 1) * 128], r=[("hT", t // 4)])
            c_ = cs[t % 2]
            K.dma("sp", c_[:], cst["cA"][t * 128:(t + 1) * 128, :].rearrange("p (a b) -> p a b", a=2), r=[])
            q_ = qk[t % 2]
            for g in range(3):
                pb = pz[g % 2]
                for kc in range(8):
                    K.mm(pb[:], hT[:, kc, :], wA[:, kc, g * 512:(g + 1) * 512], start=(kc == 0), stop=(kc == 7),
                         r=[hT, ("wA", g)])
                if g < 2:
                    K.cp("act", q_[:, g], pb[:].rearrange("p (h d) -> p h d", h=8))
                elif "v" not in PA_SKIP:
                    K.cp("act", Va[:, t, :, 0:64], pb[:].rearrange("p (h d) -> p h d", h=8), w=[("Va", t)])
            x1 = q_[:, :, :, 0:8]
            x2 = q_[:, :, :, 8:16]
            co = c_[:, 0, :].rearrange("p (a h d) -> p a h d", a=2, h=8)
            si = c_[:, 1, :].rearrange("p (a h d) -> p a h d", a=2, h=8)
            t1, t2, t3, t4 = [r_[:].rearrange("p (a h d) -> p a h d", a=2, h=8) for r_ in rt]
            if "r" not in PA_SKIP:
                K.tt("dve", t1, x1, co, ALU.mult)
                K.tt("pool", t2, x2, si, ALU.mult)
                K.tt("dve", t3, x1, si, ALU.mult)
                K.tt("pool", t4, x2, co, ALU.mult)
                K.tt("dve", x1, t1, t2, ALU.subtract)
                K.tt("dve", x2, t3, t4, ALU.add)
            for g in range(2 if "t" not in PA_SKIP else 0):
                for hq in range(2):
                    pb = pTq[hq]
                    for h4 in range(4):
                        h = hq * 4 + h4
                        K.tr(pb[0:64, h4 * 128:(h4 + 1) * 128], q_[:, g, h, :], ident[:])
                    src = pb[0:64, :].rearrange("p (h t) -> p h t", h=4)
                    if g == 0:
                        K.cp("act", QT[b % 2][0:64, hq * 4:(hq + 1) * 4, half * 128:(half + 1) * 128], src,
                             w=[("QTq", b % 2)])
                        K.cp("dve", QTf[half][:, hq * 4:(hq + 1) * 4, :], src)
                    else:
                        K.cp("act", KT[0:64, hq * 4:(hq + 1) * 4, t * 128:(t + 1) * 128], src, w=[("KT", t)])
                        K.op("dve", lambda e, src=src, hq=hq: e.tensor_reduce(out=ktmp[:, hq * 4:(hq + 1) * 4], in_=src, axis=AX.X, op=ALU.add),
                             [pb], [ktmp])
                if g == 1:
                    if half == 0:
                        K.cp("dve", kms[:, :, b:b + 1], ktmp[:].rearrange("p (h o) -> p h o", o=1))
                    else:
                        K.tt("dve", kms[:, :, b:b + 1], kms[:, :, b:b + 1], ktmp[:].rearrange("p (h o) -> p h o", o=1), ALU.add)
            if half == 0 or PA_STOP <= 1:
                continue
            Qb = QT[b % 2]
            if b >= 1:
                for hf_ in range(2):
                    pg = banks[2]
                    for h in range(8):
                        K.mm(pg[:, h * 16:(h + 1) * 16], QTf[hf_][0:64, h, :], kms[0:64, h, :])
                    K.cp("dve", gs[:], pg[:, 0:128].rearrange("p (h n) -> p h n", h=8))
                    if b < 16:
                        K.memset("dve", gs[:, :, b:16], -1e30)
                    for h in range(8):
                        K.op("dve", lambda e, h=h: e.max(out=m8[:, h, :], in_=gs[:, h, :]), [gs], [m8])
                    for h in range(8):
                        K.ts("dve", Mp[:, h, 64:80], gs[:, h, :], m8[:, h, 2:3], NEG, ALU.is_lt, ALU.mult)
                    pm = banks[3]
                    pmv = pm[:].bitcast(BF16).rearrange("p (h t) -> p h t", h=8)
                    for h in range(8):
                        K.tr(pmv[0:80, h, :], Mp[:, h, :], identb[:])
                    K.cp("act", Qb[64:80, :, hf_ * 128:(hf_ + 1) * 128], pmv[64:80, :, :], w=[("QTm", b % 2, hf_)])
            Qkeys = [("QTq", b % 2), ("QTm", b % 2, 0), ("QTm", b % 2, 1)]
            if PA_STOP <= 2:
                continue
            nkt = 2 * b + 2
            steps = [(h, kt) for h in range(8) for kt in range(nkt)]

            def issue_S(i):
                h, kt = steps[i]
                own = kt >= 2 * b
                Kr = 64 if own else 80
                ps_ = pS[i % 2]
                K.mm(ps_[:, 0:256], KT[0:Kr, h, kt * 128:(kt + 1) * 128], Qb[0:Kr, h, :], start=True, stop=not own,
                     r=[("KT", kt), ("KTE", h)] + Qkeys)
                if own:
                    K.mm(ps_[:, 0:256], identb[:], cm[:, kt - 2 * b, :], start=False, stop=True)

            issue_S(0)
            for i, (h, kt) in enumerate(steps):
                own = kt >= 2 * b
                pOh = pO if h % 2 == 0 else [banks[2], banks[3]]
                if i + 1 < len(steps):
                    issue_S(i + 1)
                P_ = PT[i % 3]
                K.act(P_[:], pS[i % 2][:, 0:256], AF.Exp, scale=0.125)
                for q2 in range(2):
                    if own and kt - 2 * b == 1 and q2 == 0:
                        continue
                    lastk = (2 * b) if q2 == 0 else (2 * b + 1)
                    K.mm(pOh[q2][:, 0:65], P_[:, q2 * 128:(q2 + 1) * 128], Va[:, kt, h, :], start=(kt == 0), stop=(kt == lastk),
                         r=[P_, ("Va", kt), "Va1"])
                if kt == nkt - 1:
                    for q2 in range(2):
                        rd = rden[:, (h % 2) * 2 + q2:(h % 2) * 2 + q2 + 1]
                        K.op("dve", lambda e, rd=rd, src=pOh[q2][:, 64:65]: e.reciprocal(out=rd, in_=src), [pOh[q2]], [("rden", h % 2, q2)])
                        K.ts("dve", ya[q2][:, h * 64:(h + 1) * 64], pOh[q2][:, 0:64], rd, None, ALU.mult,
                             r=[pOh[q2], ("rden", h % 2, q2)], w=[("ya", q2, h)])
            yak = [[("ya", q2, h) for h in range(8)] for q2 in range(2)]
            if PA_STOP <= 3:
                continue
            for q2 in range(2):
                tq = 2 * b + q2
                pb = banks[q2]
                pv = pb[:].bitcast(BF16)[:, 0:512].rearrange("p (c t) -> p c t", c=4)
                for c in range(4):
                    K.tr(pv[:, c, :], ya[q2][:, c * 128:(c + 1) * 128], identb[:], r=yak[q2] + [identb])
                K.cp("act", yTs[q2][:], pv)
                K.dma("sp", yT_d[0, :, :, tq * 128:(tq + 1) * 128], yTs[q2][:], w=[("yT", 0, tq // 4)], semkey="yTs")


def phase_b(nc, K, sbt, banks, nb_, l, S, cst, Wb_in, hT_d, yT_d, ident, identb, epsc):
    NT = S // 128
    with ExitStack() as es:
        wB = sbt(es, "wB", [128, 8, 2048], BF16)
        hTt = [sbt(es, f"hTtB{i}", [128, 8, 128], BF16) for i in range(2)]
        cb = [sbt(es, f"cbB{i}", [128, 4, 256]) for i in range(2)]
        qs = [sbt(es, f"qsB{i}", [128, 512]) for i in range(2)]
        rt = [sbt(es, f"rtB{i}", [128, 256]) for i in range(4)]
        qr = [sbt(es, f"qrB{i}", [128, 512], BF16) for i in range(2)]
        QTb = sbt(es, "QTbB", [128, 4, 128], BF16)
        KTb = sbt(es, "KTbB", [128, 4, 128], BF16)
        vb = sbt(es, "vbB", [128, 512], BF16)
        sgb = sbt(es, "sgB", [128, 512])
        Sm = sbt(es, "SmB", [128, 4, 128], BF16)
        rm = sbt(es, "rmB", [128, 4, 128], BF16)
        R = sbt(es, "RB", [128, 4, 128])
        Rg = sbt(es, "RgB", [128, 4, 128], BF16)
        rs = sbt(es, "rsB", [128, 12])
        st = sbt(es, "stB", [128, 4, 6])
        mv = sbt(es, "mvB", [128, 4, 2])
        rstd = sbt(es, "rstdB", [128, 4])
        yn = sbt(es, "ynB", [128, 512])
        yb = sbt(es, "ybB", [128, 512], BF16)
        yTs = [sbt(es, f"yTsB{i}", [128, 4, 128], BF16) for i in range(2)]
        for g in range(4):
            K.dma("sp", wB[:, :, g * 512:(g + 1) * 512], Wb_in[l, :, :, 1536 + g * 512:1536 + (g + 1) * 512],
                  r=[("Wb_in", l, 3 + g)], w=[("wB", g)], semkey="wload")
        K.dma("pool", rm[:], cst["cRM"].rearrange("p (h t) -> p h t", h=4), r=[], semkey="cloadp")
        K.dma("sp", rs[:], cst["cRS"][:, :], r=[], semkey="cload")
        K.memset("dve", R[:], 0.0, w=[("RB", h) for h in range(4)])
        K.memset("pool", Rg[:], 0.0)
        for t in range(NT):
            hT = hTt[t % 2]
            K.dma("sp", hT[:], hT_d[:, :, t * 128:(t + 1) * 128], r=[("hT", t // 4)])
            c_ = cb[t % 2]
            K.dma("sp", c_[:], cst["cB"][t * 128:(t + 1) * 128, :].rearrange("p (a b) -> p a b", a=4), r=[])
            for g in range(4):
                pb = nb_(0, 4)
                for kc in range(8):
                    K.mm(pb[:], hT[:, kc, :], wB[:, kc, g * 512:(g + 1) * 512], start=(kc == 0), stop=(kc == 7),
                         r=[hT, ("wB", g)])
                if g < 2:
                    q_ = qs[g]
                    K.cp("act", q_[:], pb[:])
                    xv = q_[:].rearrange("p (h d two) -> p h d two", h=4, two=2)
                    xe = xv[:, :, :, 0]
                    xo = xv[:, :, :, 1]
                    co = c_[:, 2 * g, :].rearrange("p (h d) -> p h d", h=4)
                    si = c_[:, 2 * g + 1, :].rearrange("p (h d) -> p h d", h=4)
                    t1, t2, t3, t4 = [r_[:].rearrange("p (h d) -> p h d", h=4) for r_ in rt]
                    ov = qr[g][:].rearrange("p (h d two) -> p h d two", h=4, two=2)
                    K.tt("dve", t1, xe, co, ALU.mult)
                    K.tt("pool", t2, xo, si, ALU.mult)
                    K.tt("dve", t3, xe, si, ALU.mult)
                    K.tt("pool", t4, xo, co, ALU.mult)
                    K.tt("dve", ov[:, :, :, 0], t1, t2, ALU.subtract, w=[(f"qrB{g}", 0)], r=[rt[0], rt[1]])
                    K.tt("dve", ov[:, :, :, 1], t3, t4, ALU.add, w=[(f"qrB{g}", 1)], r=[rt[2], rt[3]])
                    pbt = nb_(0, 4)
                    pv = pbt[:].bitcast(BF16)[:, 0:512].rearrange("p (h t) -> p h t", h=4)
                    for h in range(4):
                        K.tr(pv[:, h, :], qr[g][:, h * 128:(h + 1) * 128], identb[:], r=[(f"qrB{g}", 0), (f"qrB{g}", 1), identb])
                    K.cp("act", (QTb if g == 0 else KTb)[:], pv)
                elif g == 2:
                    K.cp("act", vb[:], pb[:])
                else:
                    K.act(sgb[:], pb[:], AF.Silu)
            qrk = [("qrB1", 0), ("qrB1", 1)]
            pS_ = banks[4]
            for h in range(4):
                K.mm(pS_[:, h * 128:(h + 1) * 128], KTb[:, h, :], QTb[:, h, :])
            K.tt("dve", Sm[:], pS_[:].rearrange("p (h t) -> p h t", h=4), rm[:], ALU.mult)
            py = banks[5]
            pkv = banks[6]
            for h in range(4):
                K.mm(py[:, h * 128:(h + 1) * 128], Sm[:, h, :], vb[:, h * 128:(h + 1) * 128], start=True, stop=False)
                K.mm(py[:, h * 128:(h + 1) * 128], QTb[:, h, :], Rg[:, h, :], start=False, stop=True)
            for h in range(4):
                K.mm(pkv[:, h * 128:(h + 1) * 128], qr[1][:, h * 128:(h + 1) * 128], vb[:, h * 128:(h + 1) * 128],
                     r=qrk + [vb])
            for h in range(4):
                K.op("dve", lambda e, h=h: e.bn_stats(out=st[:, h, :], in_=py[:, h * 128:(h + 1) * 128]), [py], [("stB", h)])
                K.op("dve", lambda e, h=h: e.bn_aggr(out=mv[:, h, :], in_=st[:, h, :]), [("stB", h)], [("mvB", h)])
            mvk = [("mvB", h) for h in range(4)]
            K.act(rstd[:], mv[:, :, 1], AF.Sqrt, bias=epsc[:, 0:1], scale=1.0, r=mvk + [epsc])
            K.op("dve", lambda e: e.reciprocal(out=rstd[:], in_=rstd[:]), [rstd], [rstd])
            for h in range(4):
                K.ts("dve", yn[:, h * 128:(h + 1) * 128], py[:, h * 128:(h + 1) * 128], mv[:, h, 0:1], rstd[:, h:h + 1],
                     ALU.subtract, ALU.mult, r=[py, rstd] + mvk, w=[("ynB", h)])
            K.tt("pool", yb[:], yn[:], sgb[:], ALU.mult, r=[("ynB", h) for h in range(4)] + [sgb])
            for h in range(4):
                K.ts("pool", R[:, h, :], R[:, h, :], rs[:, h:h + 1], None, ALU.mult, r=[("RB", h), rs], w=[("RB", h)])
                K.stt(R[:, h, :], pkv[:, h * 128:(h + 1) * 128], rs[:, 4 + h:5 + h], R[:, h, :], ALU.mult, ALU.add,
                      r=[pkv, rs, ("RB", h)], w=[("RB", h)])
                K.act(Rg[:, h, :], R[:, h, :], AF.Identity, scale=float(cst_gamma(h)), r=[("RB", h)], w=[Rg])
            pbt = banks[7]
            pv = pbt[:].bitcast(BF16)[:, 0:512].rearrange("p (c t) -> p c t", c=4)
            for c in range(4):
                K.tr(pv[:, c, :], yb[:, c * 128:(c + 1) * 128], identb[:])
            K.cp("act", yTs[t % 2][:], pv)
            K.dma("sp", yT_d[1, :, :, t * 128:(t + 1) * 128], yTs[t % 2][:], w=[("yT", 1, t // 4)], semkey="yTs")


def cst_gamma(h):
    return 1.0 - 2.0 ** (-5.0 - h)


def phase_c(nc, K, sbt, banks, nb_, l, S, cst, Wb_in, hT_d, yT_d, vf_d, ident, identb, epsc, cols, col, colkeys,
            w2_d, a2_d, g2_d, v2_d, ln_d, dbg_d):
    NG = S // 512
    RW = BF16
    ncc = 15 if l >= 1 else 14
    with ExitStack() as es:
        wC = sbt(es, "wC", [128, 8, 1920], BF16)
        w2b = sbt(es, "w2b", [64, 512], BF16)
        a2b = sbt(es, "a2b", [128, 512], BF16)
        g2b = sbt(es, "g2b", [128, 512], BF16)
        v2b = sbt(es, "v2b", [32, 512], BF16)
        lnp = sbt(es, "lnp", [128, 2, 256])
        am = sbt(es, "amC", [128, 2, 2, 128])
        xm = sbt(es, "xmC", [128, 4, 64])
        rst = sbt(es, "rstC", [128, 512])
        bo = sbt(es, "boC", [128, 128], BF16)
        hg = sbt(es, "hgC", [128, 8, 512], BF16)
        zxc = [sbt(es, f"zxC{i}", [128, 513]) for i in range(2)]
        lastc = sbt(es, "lastcC", [128, 16])
        zl = sbt(es, "zlC", [128, 15, 512])
        tmp = [sbt(es, f"tmpC{i}", [128, 512]) for i in range(3)]
        tw = sbt(es, "twC", [64, 512], BF16)
        al = sbt(es, "alC", [128, 512], BF16)
        sgl = sbt(es, "sglC", [128, 512], BF16)
        vlr = sbt(es, "vlrC", [32, 512], BF16)
        sigw = sbt(es, "sigwC", [128, 512])
        iclr = sbt(es, "iclrC", [128, 512])
        cl = sbt(es, "clC", [128, 512])
        Pinc = sbt(es, "PincC", [128, 512])
        Pexc = sbt(es, "PexcC", [128, 512])
        Pinv = sbt(es, "PinvC", [128, 512])
        kkr = sbt(es, "kkrC", [128, 512])
        sqb = sbt(es, "sqbC", [128, 512], BF16)
        kmod = sbt(es, "kmodC", [128, 512])
        vfp = sbt(es, "vfpC", [128, 4, 512])
        vbf = sbt(es, "vbfC", [128, 4, 512], BF16)
        AR = sbt(es, "ARC", [128, 4, 8, 2, 64], RW)
        BK = sbt(es, "BKC", [128, 4, 8, 2, 64], RW)
        prk = sbt(es, "prkC", [128, 4, 512], BF16)
        pend = sbt(es, "pendC", [128, 4, 8])
        H = sbt(es, "HC", [128, 4, 64])
        Hb = sbt(es, "HbC", [128, 4, 64], RW)
        Amat = [sbt(es, f"AmatC{i}", [128, 4, 2, 128], RW) for i in range(3)]
        Xm = [[sbt(es, f"XmC{i}_{j}", [128, 2, 4, 64], RW) for j in range(2)] for i in range(3)]
        Wt = [sbt(es, f"WtC{i}", [128, 4, 64], RW) for i in range(2)]
        Vc = [sbt(es, f"VcC{i}", [128, 4, 64], RW) for i in range(3)]
        Vcf = [sbt(es, f"VcfC{i}", [128, 4, 64]) for i in range(3)]
        BKt = [sbt(es, f"BKtC{i}", [128, 4, 2, 64], RW) for i in range(3)]
        Tt = [sbt(es, f"TtC{i}", [128, 4, 64], RW) for i in range(3)]
        gt = [sbt(es, f"gtC{i}", [128, 4, 64]) for i in range(3)]
        bs = [sbt(es, f"bsC{i}", [128, 4]) for i in range(3)]
        i4 = sbt(es, "i4C", [128, 4, 64])
        st = sbt(es, "stC", [128, 4, 6])
        mv = sbt(es, "mvC", [128, 4, 2])
        rstd = sbt(es, "rstdC", [128, 4])
        yn = sbt(es, "ynC", [128, 4, 64])
        yc = sbt(es, "ycC", [128, 4, 64], BF16)
        ycT = [sbt(es, f"ycTC{i}", [128, 4, 512], BF16) for i in range(2)]

        nwc = 1792 + (32 if l >= 1 else 0)
        for g in range(0, 1792, 512):
            gw = min(512, 1792 - g)
            K.dma("sp", wC[:, :, g:g + gw], Wb_in[l, :, :, 3584 + g:3584 + g + gw],
                  r=[("Wb_in", l, (3584 + g) // 512)], w=[("wC", g)], semkey="wload")
        wCk = [("wC", g) for g in range(0, 1792, 512)]
        if l >= 1:
            K.dma("sp", wC[:, :, 1792:1824], Wb_in[l, :, :, NIN:NIN + 32], r=[("Wb_in", l, "v")], w=[("wC", "v")], semkey="wload")
            wCk.append(("wC", "v"))
            K.dma("pool", v2b[:], v2_d[l - 1], r=[], semkey="cloadp")
        K.dma("pool", w2b[:], w2_d[l], r=[], semkey="cloadp")
        K.dma("pool", a2b[64:128, :], a2_d[l], r=[], semkey="cloadp")
        K.dma("pool", g2b[:], g2_d[l], r=[], semkey="cloadp")
        K.dma("sp", lnp[:], ln_d[l].rearrange("p (a b) -> p a b", a=2), r=[], semkey="cload")
        K.dma("sp", am[:], cst["cAM"].rearrange("p (a j t) -> p a j t", a=2, j=2), r=[], semkey="cload")
        K.dma("sp", xm[:], cst["cXM"].rearrange("p (a t) -> p a t", a=4), r=[], semkey="cload")
        K.dma("sp", rst[:], cst["cRST"][:, :], r=[], semkey="cload")
        K.dma("pool", bo[:], cst["cBO"][:, :], r=[], semkey="cloadp")
        K.memset("dve", lastc[:], 0.0, w=[("lastc", cc) for cc in range(16)])
        K.dma("sp", i4[:], cst["cI4"].rearrange("p (a t) -> p a t", a=4), r=[], semkey="cload")
        K.memset("dve", H[:], 0.0)
        K.memset("pool", Hb[:], 0.0)
        lng = lnp[:, 0, :].rearrange("p (a v) -> p a v", a=4)
        lnb = lnp[:, 1, :].rearrange("p (a v) -> p a v", a=4)

        for tg in range(NG):
            tok = slice(tg * 512, (tg + 1) * 512)
            K.dma("sp", hg[:], hT_d[:, :, tok], r=[("hT", tg)])
            for cc in range(ncc):
                pb = nb_(0, 4)
                np_ = 128 if cc < 14 else 32
                zc_ = zxc[cc % 2]
                if cc < 14:
                    for kc in range(8):
                        K.mm(pb[:], wC[:, kc, cc * 128:(cc + 1) * 128], hg[:, kc, :], start=(kc == 0), stop=(kc == 7),
                             r=wCk + [hg])
                else:
                    for kc in range(8):
                        K.mm(pb[0:32, :], wC[:, kc, 1792:1824], hg[:, kc, :], start=(kc == 0), stop=(kc == 7), r=wCk + [hg])
                K.cp("act", zc_[0:np_, 1:513], pb[0:np_, :])
                K.cp("pool", zc_[0:np_, 0:1], lastc[0:np_, cc:cc + 1], r=[("lastc", cc)], w=[zc_])
                mu = col(l, "mu", cc) if cc < 14 else col(l, "vmu", 0)
                tp = tmp[cc % 2]
                K.tt("pool" if cc % 2 else "dve", tp[0:np_, :], zc_[0:np_, 0:512], zc_[0:np_, 1:513], ALU.subtract)
                K.stt(zl[0:np_, cc, :], tp[0:np_, :], mu[0:np_, :], zc_[0:np_, 1:513], ALU.mult, ALU.add,
                      r=[tp, zc_] + colkeys, w=[("zl", cc)])
                K.cp("pool", lastc[0:np_, cc:cc + 1], zc_[0:np_, 512:513], w=[("lastc", cc)])
            zlk = [("zl", cc) for cc in range(ncc)]
            K.act(tw[:], zl[0:64, 12, :], AF.Tanh, r=[("zl", 12)])
            K.cp("dve", al[64:128, :], zl[64:128, 12, :], r=[("zl", 12)])
            K.act(sgl[:], zl[:, 13, :], AF.Sigmoid, r=[("zl", 13)])
            if l >= 1:
                K.cp("dve", vlr[:], zl[0:32, 14, :], r=[("zl", 14)])
                K.dma("sp", vfp[:], vf_d[:, :, tok], r=[("vf", tg)])
            for pc in range(4):
                rT = zl[:, pc, :]
                kT = zl[:, 4 + pc, :]
                vT = zl[:, 8 + pc, :]
                rk_ = [("zl", pc)]
                kk_ = [("zl", 4 + pc)]
                vk_ = [("zl", 8 + pc)]
                pw = nb_(0, 4)
                K.mm(pw[:], w2b[0:64, pc * 128:(pc + 1) * 128], tw[0:64, :])
                K.act(sigw[:], pw[:], AF.Sigmoid, bias=col(l, "w0", pc), r=[pw] + colkeys)
                pa = nb_(0, 4)
                K.mm(pa[:], a2b[64:128, pc * 128:(pc + 1) * 128], al[64:128, :])
                K.act(iclr[:], pa[:], AF.Sigmoid, bias=col(l, "a0", pc), r=[pa] + colkeys)
                K.op("dve", lambda e: e.tensor_tensor_scan(out=cl[:], data0=rst[:], data1=sigw[:], initial=0.0, op0=ALU.mult, op1=ALU.add),
                     [rst, sigw], [cl])
                K.act(Pinc[:], cl[:], AF.Exp, scale=-C0)
                K.act(Pinv[:], cl[:], AF.Exp, scale=C0)
                K.tt("pool", tmp[2][:], cl[:], sigw[:], ALU.subtract)
                K.act(Pexc[:], tmp[2][:], AF.Exp, scale=-C0)
                K.cp("dve", pend[:, pc, :], Pinc[:].rearrange("p (c t) -> p c t", c=8)[:, :, 63])
                K.ts("dve", kkr[:], kT, col(l, "kk", pc), None, ALU.mult, r=kk_ + colkeys)
                K.act(sqb[:], kkr[:], AF.Square)
                pss = nb_(0, 4)
                K.mm(pss[:], bo[:], sqb[:])
                K.act(tmp[0][:], pss[:], AF.Sqrt)
                K.ts("dve", tmp[0][:], tmp[0][:], 1e-12, None, ALU.max)
                K.op("dve", lambda e: e.reciprocal(out=tmp[0][:], in_=tmp[0][:]), [tmp[0]], [tmp[0]])
                K.tt("dve", kkr[:], kkr[:], tmp[0][:], ALU.mult)
                ARv = AR[:, pc].rearrange("p c j t -> p j c t")
                BKv = BK[:, pc].rearrange("p c j t -> p j c t")
                c3 = lambda ap: ap.rearrange("p (c t) -> p c t", c=8)
                K.stt(ARv[:, 0], c3(kkr[:]), -1.0, c3(Pexc[:]), ALU.mult, ALU.mult, r=[kkr, Pexc], w=[("AR", pc, 0)])
                K.tt("pool", tmp[1][:], kkr[:], iclr[:], ALU.mult)
                K.tt("dve", BKv[:, 0], c3(tmp[1][:]), c3(Pinv[:]), ALU.mult, r=[tmp[1], Pinv], w=[("BK", pc, 0)])
                K.ts("dve", tmp[2][:], iclr[:], 1.0, col(l, "ka", pc), ALU.subtract, ALU.mult, r=[iclr] + colkeys)
                K.stt(kmod[:], tmp[2][:], 1.0, kT, ALU.add, ALU.mult, r=[tmp[2]] + kk_)
                K.tt("dve", BKv[:, 1], c3(kmod[:]), c3(Pinv[:]), ALU.mult, r=[kmod, Pinv], w=[("BK", pc, 1)])
                K.tt("pool", ARv[:, 1], c3(rT), c3(Pinc[:]), ALU.mult, r=rk_ + [Pinc], w=[("AR", pc, 1)])
                K.stt(prk[:, pc, :], rT, col(l, "rk", pc), kmod[:], ALU.mult, ALU.mult, r=rk_ + [kmod] + colkeys, w=[("prk", pc)])
                if l == 0:
                    pass
                else:
                    pv_ = nb_(0, 4)
                    K.mm(pv_[:], v2b[0:32, pc * 128:(pc + 1) * 128], vlr[0:32, :])
                    K.act(tmp[0][:], pv_[:], AF.Sigmoid, bias=col(l, "v0", pc), r=[pv_] + colkeys)
                    K.tt("dve", tmp[1][:], vfp[:, pc, :], vT, ALU.subtract, r=[vfp] + vk_)
                    K.tt("dve", tmp[1][:], tmp[1][:], tmp[0][:], ALU.mult)
                    K.tt("dve", vT, vT, tmp[1][:], ALU.add, r=vk_ + [tmp[1]], w=vk_)
            K.cp("pool", vbf[:], zl[:, 8:12, :], r=[("zl", 8 + i) for i in range(4)])
            if l == 0:
                K.dma("sp", vf_d[:, :, tok], zl[:, 8:12, :], r=[("zl", 8 + i) for i in range(4)], w=[("vf", tg)], semkey="vfst")
            ARk = [("AR", pc, j) for pc in range(4) for j in range(2)]
            BKk = [("BK", pc, j) for pc in range(4) for j in range(2)]
            prkk = [("prk", pc) for pc in range(4)]
            vks = [("zl", 8 + i) for i in range(4)]
            yT_ = ycT[tg % 2]
            def hs(hh):
                return slice(hh * 64, (hh + 1) * 64)

            def prep(c, bA, bB):
                sl = c % 3
                ct = slice(c * 64, (c + 1) * 64)
                Vc_, Vcf_, BKt_, Amat_, Tt_, gt_, bs_ = Vc[sl], Vcf[sl], BKt[sl], Amat[sl], Tt[sl], gt[sl], bs[sl]
                X0_, X1_ = Xm[sl]
                pVv = bA[:].bitcast(BF16)[:, 0:256].rearrange("p (a v) -> p a v", a=4)
                for hh in range(2):
                    for pc in range(4):
                        K.tr(pVv[hs(hh), pc, :], vbf[hs(hh), pc, ct], identb[hs(hh), hs(hh)])
                K.cp("act", Vc_[:], pVv)
                K.cp("dve", Vcf_[:], pVv)
                pBv = bB[:].bitcast(BF16)[:, 0:512].rearrange("p (a j k) -> p a j k", a=4, j=2)
                for hh in range(2):
                    for pc in range(4):
                        for j in range(2):
                            K.tr(pBv[hs(hh), pc, j, :], BK[hs(hh), pc, c, j, :], identb[hs(hh), hs(hh)], r=BKk + [identb])
                K.cp("act", BKt_[:], pBv)
                yield
                for pb2, pA in enumerate((bA, bB)):
                    pAv = pA[:].rearrange("p (a j t) -> p a j t", a=2, j=2)
                    for hh in range(2):
                        for a_ in range(2):
                            pc = pb2 * 2 + a_
                            rhs = AR[hs(hh), pc, c].rearrange("p j t -> p (j t)")
                            for j in range(2):
                                K.mm(pAv[hs(hh), a_, j, :], BK[hs(hh), pc, c, j, :], rhs, r=ARk + BKk)
                    K.tt("dve", Amat_[:, pb2 * 2:pb2 * 2 + 2], pAv, am[:], ALU.mult, w=[("Amat", sl, pb2)])
                Ak = [("Amat", sl, 0), ("Amat", sl, 1)]
                yield
                for hh in range(2):
                    for pc in range(4):
                        K.mm(bA[hs(hh), pc * 64:(pc + 1) * 64], AR[hs(hh), pc, c, 0, :], BK[hs(hh), pc, c, 0, :], r=ARk + BKk)
                K.tt("dve", X0_[:, 0], bA[:, 0:256].rearrange("p (a s) -> p a s", a=4), xm[:], ALU.mult, w=[("X", sl, 0)])
                K.cp("act", X0_[:, 1], Amat_[:, :, 0, 0:64], r=Ak, w=[("Y", sl, 0)])
                K.tt("pool", Tt_[:], Amat_[:, :, 0, 0:64], i4[:], ALU.add, r=Ak + [i4])
                for hh in range(2):
                    for pc in range(4):
                        K.mm(bB[hs(hh), pc * 64:(pc + 1) * 64], sgl[:, ct], g2b[:, (2 * pc + hh) * 64:(2 * pc + hh + 1) * 64])
                for hh in range(2):
                    for pc in range(4):
                        K.mm(bB[hs(hh), 256 + pc:256 + pc + 1], prk[hs(hh), pc, ct], bo[hs(hh), hh * 64:hh * 64 + 1], r=prkk + [bo])
                K.cp("act", gt_[:], bB[:, 0:256].rearrange("p (a v) -> p a v", a=4))
                K.cp("dve", bs_[:], bB[:, 256:260])
                yield
                Xs = [X0_, X1_]
                for lev in range(1, 6):
                    Xp = Xs[(lev - 1) % 2]
                    Xn = Xs[lev % 2]
                    xk = [("X", sl, (lev - 1) % 2), ("Y", sl, (lev - 1) % 2)]
                    for hh in range(2):
                        for pc in range(4):
                            K.mm(bA[hs(hh), pc * 64:(pc + 1) * 64], Xp[hs(hh), 1, pc, :], Xp[hs(hh), 0, pc, :], r=xk)
                            if lev < 5:
                                K.mm(bA[hs(hh), 256 + pc * 64:256 + (pc + 1) * 64], Xp[hs(hh), 0, pc, :], Xp[hs(hh), 1, pc, :], r=xk)
                    if lev < 5:
                        K.cp("act", Xn[:], bA[:].rearrange("p (j a s) -> p j a s", j=2, a=4),
                             w=[("X", sl, lev % 2), ("Y", sl, lev % 2)])
                    else:
                        K.cp("act", Xn[:, 0], bA[:, 0:256].rearrange("p (a s) -> p a s", a=4), w=[("X", sl, lev % 2)])
                    yield
                    for hh in range(2):
                        for pc in range(4):
                            K.mm(bB[hs(hh), pc * 64:(pc + 1) * 64], Xn[hs(hh), 0, pc, :], Tt_[hs(hh), pc, :], r=[("X", sl, lev % 2), Tt_])
                    K.tt("dve", Tt_[:], Tt_[:], bB[:, 0:256].rearrange("p (a t) -> p a t", a=4), ALU.add)
                yield

            def seq(c, bA, bB):
                sl = c % 3
                ct = slice(c * 64, (c + 1) * 64)
                Vc_, Vcf_, BKt_, Amat_, Tt_, gt_, bs_ = Vc[sl], Vcf[sl], BKt[sl], Amat[sl], Tt[sl], gt[sl], bs[sl]
                Ak = [("Amat", sl, 0), ("Amat", sl, 1)]
                for hh in range(2):
                    for pc in range(4):
                        o_ = bA[hs(hh), pc * 64:(pc + 1) * 64]
                        K.mm(o_, AR[hs(hh), pc, c, 0, :], Hb[hs(hh), pc, :], start=True, stop=False, r=ARk + [Hb])
                        K.mm(o_, Amat_[hs(hh), pc, 1, 0:64], Vc_[hs(hh), pc, :], start=False, stop=True, r=Ak + [Vc_])
                K.cp("dve", Wt[0][:], bA[:, 0:256].rearrange("p (a v) -> p a v", a=4))
                yield
                for hh in range(2):
                    for pc in range(4):
                        K.mm(bB[hs(hh), pc * 64:(pc + 1) * 64], Tt_[hs(hh), pc, :], Wt[0][hs(hh), pc, :])
                U = Wt[1]
                K.cp("act", U[:], bB[:, 0:256].rearrange("p (a v) -> p a v", a=4))
                yield
                pO = bA
                pH = bB
                for hh in range(2):
                    for pc in range(4):
                        o_ = pO[hs(hh), pc * 64:(pc + 1) * 64]
                        K.mm(o_, AR[hs(hh), pc, c, 1, :], Hb[hs(hh), pc, :], start=True, stop=False, r=ARk + [Hb])
                        K.mm(o_, Amat_[hs(hh), pc, 0, 64:128], U[hs(hh), pc, :], start=False, stop=False, r=Ak + [U])
                        K.mm(o_, Amat_[hs(hh), pc, 1, 64:128], Vc_[hs(hh), pc, :], start=False, stop=True, r=Ak + [Vc_])
                for hh in range(2):
                    for pc in range(4):
                        o_ = pH[hs(hh), pc * 64:(pc + 1) * 64]
                        K.mm(o_, BKt_[hs(hh), pc, 0, :], U[hs(hh), pc, :], start=True, stop=False)
                        K.mm(o_, BKt_[hs(hh), pc, 1, :], Vc_[hs(hh), pc, :], start=False, stop=True)
                K.tt("dve", H[:], H[:], pH[:, 0:256].rearrange("p (a v) -> p a v", a=4), ALU.add)
                K.tt("pool", H[:], H[:], pend[:, :, c:c + 1].to_broadcast([128, 4, 64]), ALU.mult)
                K.cp("act", Hb[:], H[:])
                for pc in range(4):
                    K.op("dve", lambda e, pc=pc: e.bn_stats(out=st[:, pc, :], in_=pO[:, pc * 64:(pc + 1) * 64]), [pO], [("stC", pc)])
                    K.op("dve", lambda e, pc=pc: e.bn_aggr(out=mv[:, pc, :], in_=st[:, pc, :]), [("stC", pc)], [("mvC", pc)])
                mvk = [("mvC", pc) for pc in range(4)]
                K.act(rstd[:], mv[:, :, 1], AF.Sqrt, bias=epsc[:, 1:2], scale=1.0, r=mvk + [epsc])
                K.op("dve", lambda e: e.reciprocal(out=rstd[:], in_=rstd[:]), [rstd], [rstd])
                for pc in range(4):
                    K.ts("dve", yn[:, pc, :], pO[:, pc * 64:(pc + 1) * 64], mv[:, pc, 0:1], rstd[:, pc:pc + 1], ALU.subtract, ALU.mult,
                         r=[pO, rstd] + mvk, w=[("ynC", pc)])
                ynk = [("ynC", pc) for pc in range(4)]
                K.tt("dve", yn[:], yn[:], lng, ALU.mult, r=ynk + [lnp], w=[yn])
                K.tt("pool", yn[:], yn[:], lnb, ALU.add, r=[yn, lnp], w=[yn])
                K.tt("pool", Vcf_[:], Vcf_[:], bs_[:].rearrange("p (a o) -> p a o", o=1).to_broadcast([128, 4, 64]), ALU.mult)
                K.tt("pool", yn[:], yn[:], Vcf_[:], ALU.add)
                K.tt("dve", yc[:], yn[:], gt_[:], ALU.mult)
                yield
                pYv = bB[:].bitcast(BF16)[:, 0:256].rearrange("p (a t) -> p a t", a=4)
                for hh in range(2):
                    for pc in range(4):
                        K.tr(pYv[hs(hh), pc, :], yc[hs(hh), pc, :], identb[hs(hh), hs(hh)])
                K.cp("act", yT_[:, :, ct], pYv)
                yield

            free_sets = [(banks[4], banks[5]), (banks[0], banks[1])]
            pending = list(range(8))
            active = []
            prep_done = set()
            seq_done = -1
            seq_gen = None
            seq_c = 0
            while seq_c < 8:
                while pending and len(active) < 2 and free_sets and pending[0] - 3 <= seq_done:
                    cnew = pending.pop(0)
                    bset = free_sets.pop(0)
                    active.append([cnew, prep(cnew, *bset), bset])
                if seq_gen is None and seq_c in prep_done:
                    seq_gen = seq(seq_c, banks[6], banks[7])
                if seq_gen is not None:
                    try:
                        next(seq_gen)
                    except StopIteration:
                        seq_gen = None
                        seq_done = seq_c
                        seq_c += 1
                for a in list(active):
                    try:
                        next(a[1])
                    except StopIteration:
                        prep_done.add(a[0])
                        free_sets.append(a[2])
                        active.remove(a)
            K.dma("sp", yT_d[2, :, :, tok], yT_[:], w=[("yT", 2, tg)], semkey="yTs")


_CACHE = {}


def make_in_maps(inp, S, depth, n_cores):
    consts = host_consts(S)
    colsarr = np.stack([pack_cols(inp, l) for l in range(depth)])
    lnarr = np.stack([pack_ln(inp, l) for l in range(depth)])
    maps = []
    f = lambda a: np.ascontiguousarray(np.asarray(a, np.float32))
    shared = {
        "w_in": f(inp["w_in"]), "c_w2": f(inp["c_w2"]), "c_a2": f(inp["c_a2"]), "c_g2": f(inp["c_g2"]),
        "c_vres_down": f(inp["c_vres_down"]), "c_v2": f(inp["c_v2"]), "w_branch": f(inp["w_branch"]),
        "w_out": f(inp["w_out"]), "w_gate_up": f(inp["w_gate_up"]), "w_down": f(inp["w_down"]),
        "w_ple_gate": f(inp["w_ple_gate"]), "w_ple_proj": f(inp["w_ple_proj"]), "cols": colsarr, "lnp": lnarr,
    }
    shared.update(consts)
    x = np.asarray(inp["x"], np.float32)
    p = np.asarray(inp["p"], np.float32)
    for b in range(n_cores):
        m = dict(shared)
        m["x"] = np.ascontiguousarray(x[b])
        m["p"] = np.ascontiguousarray(p[:, b])
        maps.append(m)
    return maps


def kernel(**inputs):
    x = np.asarray(inputs["x"])
    B, S, _ = x.shape
    depth = np.asarray(inputs["w_in"]).shape[0]
    key = (S, depth)
    if key not in _CACHE:
        _CACHE[key] = build(S, depth)[0]
    nc = _CACHE[key]
    maps = make_in_maps(inputs, S, depth, B)
    res = run_bass_kernel_spmd(nc, maps, core_ids=list(range(B)))
    return np.stack([np.asarray(r["out"], np.float32) for r in res.results], axis=0)
```

```python
import math
from contextlib import ExitStack
import numpy as np
import concourse.bass as bass
import concourse.mybir as mybir
from concourse.bass_utils import run_bass_kernel_spmd

F32 = mybir.dt.float32
BF16 = mybir.dt.bfloat16
ALU = mybir.AluOpType
AF = mybir.ActivationFunctionType
AX = mybir.AxisListType

D = 1024
NIN = 8448
DFF = 2816
EPS = 1e-6
NEG = -30000.0
C0 = math.exp(-0.5)
SEM_LIMIT = 30000
import os
PA_STOP = int(os.environ.get("PA_STOP", "9"))
PA_SKIP = os.environ.get("PA_SKIP", "")


class Ctx:
    ENGS = ("pe", "dve", "act", "pool", "sp")

    def __init__(self, nc):
        self.nc = nc
        self.prog = {e: [] for e in self.ENGS}
        self.sems = {}
        self.semval = {}
        self.cur = {}
        self.waited = {e: {} for e in self.ENGS}
        self.res = {}
        self.nsem = 0
        self.ninstr = 0
        self.banktag = {}

    def _semkey(self, logical, step):
        sk = self.cur.get(logical)
        if sk is None or self.semval[sk] + step > SEM_LIMIT:
            ep = 0 if sk is None else sk[1] + 1
            sk = (logical, ep)
            self.sems[sk] = self.nc.alloc_semaphore(name=f"s{self.nsem}")
            self.nsem += 1
            self.semval[sk] = 0
            self.cur[logical] = sk
        return sk

    @staticmethod
    def _key(x):
        if isinstance(x, (str, tuple)):
            return x
        if hasattr(x, "tensor"):
            return x.tensor.name
        return x.name

    def _collect(self, reads, writes):
        deps = {}

        def add(d):
            if d is not None:
                deps[d[0]] = max(deps.get(d[0], 0), d[1])
        for r in reads:
            st = self.res.get(r)
            if st:
                add(st["w"])
        for w in writes:
            st = self.res.get(w)
            if st:
                add(st["w"])
                for sk, v in st["r"].items():
                    add((sk, v))
        return deps

    def _emit_waits(self, e, deps):
        for sk, v in deps.items():
            if self.waited[e].get(sk, 0) < v:
                h = self.sems[sk]
                self.prog[e].append(lambda eng, h=h, v=v: eng.wait_ge(h, v))
                self.waited[e][sk] = v

    def _update(self, reads, writes, sk, v):
        for r in reads:
            st = self.res.setdefault(r, {"w": None, "r": {}})
            st["r"][sk] = max(st["r"].get(sk, 0), v)
        for w in writes:
            self.res[w] = {"w": (sk, v), "r": {}}

    def op(self, e, fn, r=(), w=(), petag=None):
        reads = [self._key(x) for x in r]
        writes = [self._key(x) for x in w]
        writes = writes + [k for k in reads if isinstance(k, str) and k.startswith("bank") and k not in writes]
        skip = []
        if e == "pe" and petag is not None:
            for k in writes:
                if isinstance(k, str) and k.startswith("bank"):
                    st = self.res.get(k)
                    if st and st["w"] is not None and st["w"][0][0] == ("eng", "pe") and not st["r"] \
                            and self.banktag.get(k) == petag:
                        skip.append(k)
                    self.banktag[k] = petag
        deps = self._collect(reads, [k for k in writes if k not in skip])
        self._emit_waits(e, deps)
        sk = self._semkey(("eng", e), 1)
        self.semval[sk] += 1
        v = self.semval[sk]
        h = self.sems[sk]
        self.prog[e].append(lambda eng, fn=fn, h=h: fn(eng).then_inc(h, 1))
        self._update(reads, writes, sk, v)
        self.ninstr += 1

    def dma(self, q, out, in_, r=None, w=None, semkey=None, **kw):
        reads = [self._key(x) for x in (r if r is not None else [in_])]
        writes = [self._key(x) for x in (w if w is not None else [out])]
        if semkey is None:
            semkey = out.tensor.name
        lk = ("dma", semkey)
        sk = self._semkey(lk, 16)
        deps = self._collect(reads, writes)
        if self.semval[sk] > 0:
            deps[sk] = max(deps.get(sk, 0), self.semval[sk])
        self._emit_waits(q, deps)
        self.semval[sk] += 16
        v = self.semval[sk]
        h = self.sems[sk]
        self.prog[q].append(
            lambda eng, out=out, in_=in_, kw=kw, h=h: eng.dma_start(out=out, in_=in_, **kw).then_inc(h, 16))
        self._update(reads, writes, sk, v)
        self.ninstr += 1

    def barrier(self):
        deps = {sk: v for sk, v in self.semval.items()
                if v > 0 and not (sk[0][0] == "dma" and str(sk[0][1]).startswith("conv"))}
        for e in self.ENGS:
            self._emit_waits(e, deps)

    def final_wait(self, e, keys):
        deps = self._collect([self._key(k) for k in keys], ())
        self._emit_waits(e, deps)

    def mm(self, out, lhsT, rhs, start=True, stop=True, r=None, w=None):
        tag = (lhsT.start_partition(), lhsT.partition_size())
        self.op("pe", lambda e: e.matmul(out, lhsT=lhsT, rhs=rhs, start=start, stop=stop),
                r if r is not None else [lhsT, rhs], w if w is not None else [out], petag=tag)

    def tr(self, out, in_, ident, r=None, w=None):
        tag = (in_.start_partition(), in_.partition_size())
        self.op("pe", lambda e: e.transpose(out=out, in_=in_, identity=ident),
                r if r is not None else [in_, ident], w if w is not None else [out], petag=tag)

    def act(self, out, in_, func, bias=None, scale=None, r=None, w=None, eng="act"):
        kw = {}
        if bias is not None:
            kw["bias"] = bias
        if scale is not None:
            kw["scale"] = scale
        rr = [in_] + [x for x in (bias, scale) if not isinstance(x, (int, float, type(None)))]
        self.op("act", lambda e: e.activation(out=out, in_=in_, func=func, **kw),
                r if r is not None else rr, w if w is not None else [out])

    def tt(self, eng, out, in0, in1, op, r=None, w=None):
        self.op(eng, lambda e: e.tensor_tensor(out=out, in0=in0, in1=in1, op=op),
                r if r is not None else [in0, in1], w if w is not None else [out])

    def ts(self, eng, out, in0, s1, s2, op0, op1=None, r=None, w=None):
        rr = [in0] + [x for x in (s1, s2) if not isinstance(x, (int, float, type(None)))]
        if op1 is None:
            fn = lambda e: e.tensor_scalar(out=out, in0=in0, scalar1=s1, scalar2=None, op0=op0)
        else:
            fn = lambda e: e.tensor_scalar(out=out, in0=in0, scalar1=s1, scalar2=s2, op0=op0, op1=op1)
        self.op(eng, fn, r if r is not None else rr, w if w is not None else [out])

    def stt(self, out, in0, scalar, in1, op0, op1, r=None, w=None):
        rr = [in0, in1] + ([scalar] if not isinstance(scalar, (int, float)) else [])
        self.op("dve", lambda e: e.scalar_tensor_tensor(out=out, in0=in0, scalar=scalar, in1=in1, op0=op0, op1=op1),
                r if r is not None else rr, w if w is not None else [out])

    def cp(self, eng, out, in_, r=None, w=None):
        if eng == "act":
            fn = lambda e: e.copy(out=out, in_=in_)
        else:
            fn = lambda e: e.tensor_copy(out=out, in_=in_)
        self.op(eng, fn, r if r is not None else [in_], w if w is not None else [out])

    def memset(self, eng, ap, val, w=None):
        self.op(eng, lambda e: e.memset(ap, val), [], w if w is not None else [ap])

    def replay(self):
        nc = self.nc
        with nc.Block() as block:
            @block.tensor
            def _(eng):
                for f in self.prog["pe"]:
                    f(eng)

            @block.vector
            def _(eng):
                for f in self.prog["dve"]:
                    f(eng)

            @block.scalar
            def _(eng):
                for f in self.prog["act"]:
                    f(eng)

            @block.gpsimd
            def _(eng):
                for f in self.prog["pool"]:
                    f(eng)

            @block.sync
            def _(eng):
                for f in self.prog["sp"]:
                    f(eng)


def host_consts(S):
    c = {}
    pos = np.arange(S, dtype=np.float32)
    inv_a = (1.0 / (np.float32(500000.0) ** (np.arange(0, 16, 2, dtype=np.float32) / np.float32(16)))).astype(np.float32)
    ang = (pos[:, None] * inv_a[None, :]).astype(np.float32)
    cos_a, sin_a = np.cos(ang).astype(np.float32), np.sin(ang).astype(np.float32)
    ca = np.zeros((S, 2, 2, 8, 8), np.float32)
    ca[:, 0] = cos_a[:, None, None, :]
    ca[:, 1] = sin_a[:, None, None, :]
    c["cA"] = ca.reshape(S, 256)
    inv_b = (1.0 / (np.float32(10000.0) ** np.linspace(0.0, 1.0, 64, dtype=np.float32))).astype(np.float32)
    angb = (pos[:, None] * inv_b[None, :]).astype(np.float32)
    cos_b, sin_b = np.cos(angb).astype(np.float64), np.sin(angb).astype(np.float64)
    lg = np.log(1.0 - 2.0 ** (-5.0 - np.arange(4, dtype=np.float64)))
    i = (np.arange(S) % 128).astype(np.float64)
    gq = np.exp(lg[None, :] * i[:, None])
    gk = np.exp(-lg[None, :] * i[:, None]) * (128.0 ** -0.5)
    cb = np.zeros((S, 4, 4, 64), np.float64)
    cb[:, 0] = cos_b[:, None, :] * gq[:, :, None]
    cb[:, 1] = sin_b[:, None, :] * gq[:, :, None]
    cb[:, 2] = cos_b[:, None, :] * gk[:, :, None]
    cb[:, 3] = sin_b[:, None, :] * gk[:, :, None]
    c["cB"] = cb.reshape(S, 1024).astype(np.float32)
    gam = np.exp(lg)
    rs = np.zeros((128, 12), np.float32)
    rs[:, 0:4] = (gam ** 128)[None, :]
    rs[:, 4:8] = (gam ** 127)[None, :]
    rs[:, 8:12] = gam[None, :]
    c["cRS"] = rs
    E = np.zeros((16, S), np.float32)
    for j in range(16):
        E[j, j * 256:(j + 1) * 256] = 1.0
    c["cE"] = E
    cm = np.zeros((2, 128, 256), np.float32)
    for kt in range(2):
        k = kt * 128 + np.arange(128)[:, None]
        q = np.arange(256)[None, :]
        cm[kt] = np.where(k <= q, 0.0, NEG)
    c["cCM"] = cm.transpose(1, 0, 2).reshape(128, 512)
    j = np.arange(128)[:, None]
    ii = np.arange(128)[None, :]
    m = (j <= ii).astype(np.float32)
    c["cRM"] = np.tile(m[:, None, :], (1, 4, 1)).reshape(128, 512)
    s = (np.arange(128) % 64)[:, None]
    t = np.arange(64)[None, :]
    strict = (s < t).astype(np.float32)
    incl = (s <= t).astype(np.float32)
    am = np.concatenate([strict, incl], axis=1)
    c["cAM"] = np.tile(am[:, None, :], (1, 4, 1)).reshape(128, 512)
    tt_ = (np.arange(128) % 64)[:, None]
    ss_ = np.arange(64)[None, :]
    xm = (ss_ < tt_).astype(np.float32)
    c["cXM"] = np.tile(xm[:, None, :], (1, 4, 1)).reshape(128, 256)
    rm = np.ones((128, 512), np.float32)
    rm[:, ::64] = 0.0
    c["cRST"] = rm
    c["cID"] = np.eye(128, dtype=np.float32)
    bo = np.zeros((128, 128), np.float32)
    bo[:64, :64] = 1.0
    bo[64:, 64:] = 1.0
    c["cBO"] = bo
    c["cI4"] = np.tile(np.eye(64, dtype=np.float32)[None, None], (2, 4, 1, 1)).transpose(0, 2, 1, 3).reshape(128, 256)
    return c


CONST_SHAPES = lambda S: {"cA": [S, 256], "cB": [S, 1024], "cRS": [128, 12], "cE": [16, S], "cCM": [128, 512],
                          "cRM": [128, 512], "cAM": [128, 512], "cXM": [128, 256], "cRST": [128, 512],
                          "cID": [128, 128], "cBO": [128, 128], "cI4": [128, 256]}

COLS = {"mixg": (0, 8), "ffng": (8, 8), "pleg": (16, 8), "fing": (24, 8), "mu": (32, 14), "w0": (46, 4),
        "a0": (50, 4), "kk": (54, 4), "ka": (58, 4), "rk": (62, 4), "v0": (66, 4), "vmu": (70, 1)}
NCOL = 72


def pack_cols(inp, l):
    out = np.zeros((128, NCOL), np.float32)

    def put(name, vec):
        o, n = COLS[name]
        v = np.asarray(vec, np.float32).reshape(-1)
        out[:, o:o + n] = v.reshape(n, 128).T
    put("mixg", inp["norm_mix_g"][l])
    put("ffng", inp["norm_ffn_g"][l])
    put("pleg", inp["norm_ple_g"][l])
    put("fing", inp["final_norm_g"])
    put("mu", inp["c_mu"][l])
    put("w0", inp["c_w0"][l])
    put("a0", inp["c_a0"][l])
    put("kk", inp["c_k_k"][l])
    put("ka", inp["c_k_a"][l])
    put("rk", inp["c_r_k"][l])
    if l >= 1:
        put("v0", inp["c_v0"][l - 1])
        out[0:32, COLS["vmu"][0]] = np.asarray(inp["c_vres_mu"][l - 1], np.float32)
    return out


def pack_ln(inp, l):
    out = np.zeros((128, 2, 4, 64), np.float32)
    for k, name in enumerate(("c_ln_g", "c_ln_b")):
        v = np.asarray(inp[name][l], np.float32).reshape(4, 2, 64)
        for hh in range(2):
            out[hh * 64:(hh + 1) * 64, k] = v[:, hh, :][None]
    return out.reshape(128, 512)


def build(S, depth=2, en="abc", dbg=()):
    NT = S // 128
    NG = S // 512
    NB = S // 256
    nc = bass.Bass("TRN2", target_bir_lowering=False)

    def din(name, shape, dt=F32):
        return nc.dram_tensor(name, list(shape), dt, kind="ExternalInput").ap()

    def dscr(name, shape, dt=F32):
        return nc.dram_tensor(name, list(shape), dt, kind="Internal").ap()

    x_d = din("x", [S, D])
    p_d = din("p", [depth, S, 256])
    w_in = din("w_in", [depth, D, NIN])
    w2_d = din("c_w2", [depth, 64, 512])
    a2_d = din("c_a2", [depth, 64, 512])
    g2_d = din("c_g2", [depth, 128, 512])
    vd_d = din("c_vres_down", [max(depth - 1, 1), D, 32])
    v2_d = din("c_v2", [max(depth - 1, 1), 32, 512])
    wbr_d = din("w_branch", [depth, 3, 512, D])
    wout_d = din("w_out", [depth, D, D])
    wgu_d = din("w_gate_up", [depth, D, 2 * DFF])
    wd_d = din("w_down", [depth, DFF, D])
    wpg_d = din("w_ple_gate", [depth, D, D])
    wpp_d = din("w_ple_proj", [depth, 256, D])
    cols_d = din("cols", [depth, 128, NCOL])
    ln_d = din("lnp", [depth, 128, 512])
    cst = {k: din(k, shp) for k, shp in CONST_SHAPES(S).items()}
    out_d = nc.dram_tensor("out", [S, D], F32, kind="ExternalOutput").ap()

    xT_d = dscr("xT_d", [128, 8, S])
    hT_d = dscr("hT_d", [128, 8, S], BF16)
    yT_d = dscr("yT_d", [3, 128, 4, S], BF16)
    vf_d = dscr("vf_d", [128, 4, S])
    NINX = NIN + 32
    Wb_in = dscr("Wb_in", [depth, 128, 8, NINX], BF16)
    Wb_br = dscr("Wb_br", [depth, 128, 3, 4, D], BF16)
    Wb_out = dscr("Wb_out", [depth, 128, 8, D], BF16)
    Wb_gu = dscr("Wb_gu", [depth, 128, 8, 2 * DFF], BF16)
    Wb_d = dscr("Wb_d", [depth, 128, 22, D], BF16)
    Wb_pg = dscr("Wb_pg", [depth, 128, 8, D], BF16)
    Wb_pp = dscr("Wb_pp", [depth, 128, 2, D], BF16)
    dbg_d = {}
    for name, shp in dbg:
        dbg_d[name] = nc.dram_tensor("dbg_" + name, list(shp), F32, kind="ExternalOutput").ap()

    K = Ctx(nc)
    uniq = [0]
    with ExitStack() as top:
        def sbt(es, name, shape, dt=F32):
            uniq[0] += 1
            return es.enter_context(nc.sbuf_tensor(f"s_{name}_{uniq[0]}", list(shape), dt))

        banks = [top.enter_context(nc.psum_tensor(f"bank{i}", [128, 512], F32)) for i in range(8)]
        bank_rr = [0]

        def nb_(lo=0, hi=8):
            b = banks[lo + bank_rr[0] % (hi - lo)]
            bank_rr[0] += 1
            return b

        ident = sbt(top, "ident", [128, 128])
        identb = sbt(top, "identb", [128, 128], BF16)
        onesb = sbt(top, "onesb", [128, 128], BF16)
        cols = sbt(top, "cols", [128, depth, NCOL])
        K.dma("sp", ident[:], cst["cID"][:, :], r=[], semkey="cload")
        K.cp("dve", identb[:], ident[:])
        K.memset("dve", onesb[:], 1.0)
        for l in range(depth):
            K.dma("sp", cols[:, l, :], cols_d[l], r=[], w=[("cols", l)], semkey="cload")
        colkeys = [("cols", l) for l in range(depth)]

        def col(l, name, j=0, n=1):
            o, _ = COLS[name]
            return cols[:, l, o + j:o + j + n]

        bg = []
        convn = [0]

        def conv_jobs(l):
            first, rest = [], []
            for c0 in range(0, NIN, 512):
                cw = min(512, NIN - c0)
                job = (l, Wb_in[l, :, :, c0:c0 + cw], w_in[l, :, c0:c0 + cw].rearrange("(k p) n -> p k n", p=128), ("Wb_in", l, c0 // 512))
                (first if c0 < 5632 else rest).append(job)
            if l >= 1:
                first.append((l, Wb_in[l, :, :, NIN:NINX], vd_d[l - 1].rearrange("(k p) n -> p k n", p=128), ("Wb_in", l, "v")))
            for n in range(3):
                for h in range(2):
                    rest.append((l, Wb_br[l, :, n, :, h * 512:(h + 1) * 512],
                                 wbr_d[l, n, :, h * 512:(h + 1) * 512].rearrange("(k p) n -> p k n", p=128), ("Wb_br", l)))
            for h in range(2):
                rest.append((l, Wb_out[l, :, :, h * 512:(h + 1) * 512],
                             wout_d[l, :, h * 512:(h + 1) * 512].rearrange("(k p) n -> p k n", p=128), ("Wb_out", l)))
            for c0 in range(0, 2 * DFF, 512):
                rest.append((l, Wb_gu[l, :, :, c0:c0 + 512], wgu_d[l, :, c0:c0 + 512].rearrange("(k p) n -> p k n", p=128), ("Wb_gu", l)))
            for h in range(2):
                for k0 in (0, 11):
                    rest.append((l, Wb_d[l, :, k0:k0 + 11, h * 512:(h + 1) * 512],
                                 wd_d[l, k0 * 128:(k0 + 11) * 128, h * 512:(h + 1) * 512].rearrange("(k p) n -> p k n", p=128), ("Wb_d", l)))
            for h in range(2):
                rest.append((l, Wb_pg[l, :, :, h * 512:(h + 1) * 512],
                             wpg_d[l, :, h * 512:(h + 1) * 512].rearrange("(k p) n -> p k n", p=128), ("Wb_pg", l)))
                rest.append((l, Wb_pp[l, :, :, h * 512:(h + 1) * 512],
                             wpp_d[l, :, h * 512:(h + 1) * 512].rearrange("(k p) n -> p k n", p=128), ("Wb_pp", l)))
            return [j + (True,) for j in first], [j + (False,) for j in rest]

        def issue_job(job):
            _, o_, i_, key, _f = job
            K.dma("pool", o_, i_, r=[], w=[key], semkey=f"conv{convn[0] % 6}")
            convn[0] += 1

        def feed(n=2):
            for _ in range(n):
                if bg:
                    issue_job(bg.pop(0))

        def flush(l):
            while bg and bg[0][0] <= l:
                issue_job(bg.pop(0))

        def norm_group(es_name, xg, hg_out, l, gname, scratch):
            sq, rstd = scratch
            pb = nb_()
            for c in range(8):
                K.act(sq[:, c, :], xg[:, c, :], AF.Square)
            for c in range(8):
                K.mm(pb[:], onesb[:], sq[:, c, :], start=(c == 0), stop=(c == 7))
            K.act(rstd[:], pb[:], AF.Sqrt, bias=epsc[:, 0:1], scale=1.0 / D)
            K.op("dve", lambda e: e.reciprocal(out=rstd[:], in_=rstd[:]), [rstd], [rstd])
            for c in range(8):
                K.stt(hg_out[:, c, :], xg[:, c, :], col(l, gname, c), rstd[:], ALU.mult, ALU.mult,
                      r=[xg, rstd] + colkeys)

        epsc = sbt(top, "epsc", [128, 4])
        K.memset("dve", epsc[:, 0:1], EPS)
        K.memset("dve", epsc[:, 1:2], 1e-5 * 64)
        K.memset("dve", epsc[:, 2:3], 0.0)

        f0, r0 = conv_jobs(0)
        for j in f0:
            issue_job(j)
        bg.extend(r0)
        for l_ in range(1, depth):
            f_, r_ = conv_jobs(l_)
            bg.extend(f_ + r_)

        with ExitStack() as es:
            xin = [sbt(es, f"xin{i}", [128, D]) for i in range(2)]
            xg2 = [sbt(es, f"xgI{i}", [128, 8, 512]) for i in range(2)]
            hg2 = [sbt(es, f"hgI{i}", [128, 8, 512], BF16) for i in range(2)]
            sq = sbt(es, "sqI", [128, 8, 512], BF16)
            rstd = sbt(es, "rstdI", [128, 512])
            for tg in range(NG):
                xg = xg2[tg % 2]
                hg = hg2[tg % 2]
                for tt_ in range(4):
                    t = tg * 4 + tt_
                    xi = xin[t % 2]
                    K.dma("sp", xi[:], x_d[t * 128:(t + 1) * 128, :], r=[])
                    for half in range(2):
                        pb = nb_()
                        for c in range(4):
                            K.tr(pb[:, c * 128:(c + 1) * 128], xi[:, (half * 4 + c) * 128:(half * 4 + c + 1) * 128], ident[:])
                        K.cp("act" if half else "dve", xg[:, half * 4:(half + 1) * 4, tt_ * 128:(tt_ + 1) * 128],
                             pb[:].rearrange("p (c t) -> p c t", c=4))
                K.dma("sp", xT_d[:, :, tg * 512:(tg + 1) * 512], xg[:], w=[("xT", tg)], semkey="xTst")
                norm_group("I", xg, hg, 0, "mixg", (sq, rstd))
                K.dma("sp", hT_d[:, :, tg * 512:(tg + 1) * 512], hg[:], w=[("hT", tg)], semkey="hTst")

        K.barrier()
        for l in range(depth):
            while bg and (bg[0][0] < l or (bg[0][0] == l and bg[0][4])):
                issue_job(bg.pop(0))
            last = (l == depth - 1)
            if "a" in en:
                phase_a(nc, K, sbt, banks, nb_, l, S, cst, Wb_in, hT_d, yT_d, ident, identb, feed)
                K.barrier()
            if "b" in en:
                phase_b(nc, K, sbt, banks, nb_, l, S, cst, Wb_in, hT_d, yT_d, ident, identb, epsc, feed)
                K.barrier()
            if "c" in en:
                phase_c(nc, K, sbt, banks, nb_, l, S, cst, Wb_in, hT_d, yT_d, vf_d, ident, identb, epsc, cols, col, colkeys,
                        w2_d, a2_d, g2_d, v2_d, ln_d, dbg_d)
                K.barrier()

            flush(l)
            with ExitStack() as es:
                xg2 = [sbt(es, f"xgT{i}", [128, 8, 512]) for i in range(2)]
                hgs = [sbt(es, f"hgT{i}", [128, 8, 512], BF16) for i in range(2)]
                ygs = [sbt(es, f"ygT{i}", [128, 3, 4, 512], BF16) for i in range(2)]
                hf = sbt(es, "hfT", [128, 8, 512], BF16)
                mg = sbt(es, "mgT", [128, 8, 512], BF16)
                actT = sbt(es, "actT", [128, 22, 512], BF16)
                sq = mg
                rstd = sbt(es, "rstdT", [128, 512])
                sg = [sbt(es, f"sgT{i}", [128, 512]) for i in range(2)]
                acc = sbt(es, "accT", [128, 512])
                tmpf = [sbt(es, f"tmpT{i}", [128, 512]) for i in range(2)]
                wst = [sbt(es, f"wst{i}", [128, 8, 1024], BF16) for i in range(2)]
                wdt = [sbt(es, f"wdt{i}", [128, 22, 128], BF16) for i in range(2)]
                wbrt = sbt(es, "wbrt", [128, 3, 4, 128], BF16)
                wppt = sbt(es, "wppt", [128, 2, D], BF16)
                pin = [sbt(es, f"pin{i}", [128, 256]) for i in range(2)]
                pT = sbt(es, "pTT", [128, 2, 512], BF16)
                ot = [sbt(es, f"otT{i}", [128, D]) for i in range(2)]
                wsi = [0]

                def wslab():
                    t_ = wst[wsi[0] % 2]
                    wsi[0] += 1
                    return t_
                K.dma("sp", wppt[:], Wb_pp[l], r=[("Wb_pp", l)], semkey="cload")
                def load_group(g):
                    tk = slice(g * 512, (g + 1) * 512)
                    K.dma("sp", xg2[g % 2][:], xT_d[:, :, tk], r=[("xT", g)])
                    K.dma("sp", hgs[g % 2][:], hT_d[:, :, tk], r=[("hT", g)])
                    for n in range(3):
                        if "abc"[n] in en:
                            K.dma("sp", ygs[g % 2][:, n], yT_d[n, :, :, tk], r=[("yT", n, g)], w=[("ygT", g % 2, n)], semkey=f"ygT{g % 2}")
                        elif g < 2:
                            K.memset("pool", ygs[g % 2][:, n], 0.0, w=[("ygT", g % 2, n)])

                load_group(0)
                for tg in range(NG):
                    xg = xg2[tg % 2]
                    hg = hgs[tg % 2]
                    yg = ygs[tg % 2]
                    tok = slice(tg * 512, (tg + 1) * 512)
                    if tg + 1 < NG:
                        load_group(tg + 1)
                    ygk = [("ygT", n) for n in range(3)]
                    for dc in range(8):
                        ws = wslab()
                        for n in range(3):
                            c0 = 5376 + n * 1024 + dc * 128
                            K.dma("sp", ws[:, :, n * 128:(n + 1) * 128], Wb_in[l, :, :, c0:c0 + 128],
                                  r=[("Wb_in", l, c0 // 512)], w=[ws])
                        K.dma("sp", wbrt[:], Wb_br[l, :, :, :, dc * 128:(dc + 1) * 128], r=[("Wb_br", l)])
                        for n in range(3):
                            pbr = nb_()
                            pgt = nb_()
                            for kc in range(4):
                                K.mm(pbr[:], wbrt[:, n, kc, :], yg[:, n, kc, :], start=(kc == 0), stop=(kc == 3),
                                     r=[wbrt, ("ygT", tg % 2, n)])
                            for kc in range(8):
                                K.mm(pgt[:], ws[:, kc, n * 128:(n + 1) * 128], hg[:, kc, :], start=(kc == 0), stop=(kc == 7))
                            s_ = sg[n % 2]
                            K.act(s_[:], pgt[:], AF.Sigmoid)
                            if n == 0:
                                K.tt("dve", acc[:], s_[:], pbr[:], ALU.mult)
                            elif n == 1:
                                K.tt("dve", tmpf[0][:], s_[:], pbr[:], ALU.mult)
                                K.tt("pool", acc[:], acc[:], tmpf[0][:], ALU.add)
                            else:
                                K.tt("dve", tmpf[1][:], s_[:], pbr[:], ALU.mult)
                                K.tt("dve", mg[:, dc, :], acc[:], tmpf[1][:], ALU.add)
                    for dc in range(8):
                        if dc % 4 == 0:
                            ws = wslab()
                            K.dma("sp", ws[:, :, 0:512], Wb_out[l, :, :, dc * 128:dc * 128 + 512], r=[("Wb_out", l)], w=[ws])
                        po = nb_()
                        for kc in range(8):
                            K.mm(po[:], ws[:, kc, (dc % 4) * 128:(dc % 4 + 1) * 128], mg[:, kc, :], start=(kc == 0), stop=(kc == 7))
                        K.tt("dve", xg[:, dc, :], xg[:, dc, :], po[:], ALU.add)
                    norm_group("T", xg, hf, l, "ffng", (sq, rstd))
                    for f4 in range(0, 22, 4):
                        nf = min(4, 22 - f4)
                        ws = wslab()
                        K.dma("sp", ws[:, :, 0:nf * 128], Wb_gu[l, :, :, f4 * 128:(f4 + nf) * 128], r=[("Wb_gu", l)], w=[ws])
                        K.dma("sp", ws[:, :, 512:512 + nf * 128], Wb_gu[l, :, :, DFF + f4 * 128:DFF + (f4 + nf) * 128],
                              r=[("Wb_gu", l)], w=[ws])
                        for fi in range(nf):
                            fc = f4 + fi
                            pg_ = nb_()
                            pu_ = nb_()
                            for kc in range(8):
                                K.mm(pg_[:], ws[:, kc, fi * 128:(fi + 1) * 128], hf[:, kc, :], start=(kc == 0), stop=(kc == 7))
                            for kc in range(8):
                                K.mm(pu_[:], ws[:, kc, 512 + fi * 128:512 + (fi + 1) * 128], hf[:, kc, :], start=(kc == 0), stop=(kc == 7))
                            s_ = sg[fc % 2]
                            K.act(s_[:], pg_[:], AF.Silu)
                            K.tt("dve", actT[:, fc, :], s_[:], pu_[:], ALU.mult)
                    for dc in range(8):
                        wd_ = wdt[dc % 2]
                        K.dma("sp", wd_[:], Wb_d[l, :, :, dc * 128:(dc + 1) * 128], r=[("Wb_d", l)])
                        pd = nb_()
                        for fc in range(22):
                            K.mm(pd[:], wd_[:, fc, :], actT[:, fc, :], start=(fc == 0), stop=(fc == 21))
                        K.tt("dve", xg[:, dc, :], xg[:, dc, :], pd[:], ALU.add)
                    norm_group("T", xg, hf, l, "pleg", (sq, rstd))
                    for tt_ in range(4):
                        pi = pin[tt_ % 2]
                        K.dma("sp", pi[:], p_d[l, tg * 512 + tt_ * 128: tg * 512 + (tt_ + 1) * 128, :], r=[])
                        pb = nb_()
                        for c in range(2):
                            K.tr(pb[:, c * 128:(c + 1) * 128], pi[:, c * 128:(c + 1) * 128], ident[:])
                        K.cp("act", pT[:, :, tt_ * 128:(tt_ + 1) * 128], pb[:, 0:256].rearrange("p (c t) -> p c t", c=2))
                    for dc in range(8):
                        if dc % 4 == 0:
                            ws = wslab()
                            K.dma("sp", ws[:, :, 0:512], Wb_pg[l, :, :, dc * 128:dc * 128 + 512], r=[("Wb_pg", l)], w=[ws])
                        pg_ = nb_()
                        pp_ = nb_()
                        for kc in range(8):
                            K.mm(pg_[:], ws[:, kc, (dc % 4) * 128:(dc % 4 + 1) * 128], hf[:, kc, :], start=(kc == 0), stop=(kc == 7))
                        for kc in range(2):
                            K.mm(pp_[:], wppt[:, kc, dc * 128:(dc + 1) * 128], pT[:, kc, :], start=(kc == 0), stop=(kc == 1))
                        s_ = sg[dc % 2]
                        K.act(s_[:], pg_[:], AF.Sigmoid)
                        K.tt("dve", tmpf[dc % 2][:], s_[:], pp_[:], ALU.mult)
                        K.tt("pool", xg[:, dc, :], xg[:, dc, :], tmpf[dc % 2][:], ALU.add)
                    if not last:
                        K.dma("sp", xT_d[:, :, tok], xg[:], w=[("xT", tg)], semkey="xTst")
                        norm_group("T", xg, hf, l + 1, "mixg", (sq, rstd))
                        K.dma("sp", hT_d[:, :, tok], hf[:], w=[("hT", tg)], semkey="hTst")
                    else:
                        pb = nb_()
                        for c in range(8):
                            K.act(sq[:, c, :], xg[:, c, :], AF.Square)
                        for c in range(8):
                            K.mm(pb[:], onesb[:], sq[:, c, :], start=(c == 0), stop=(c == 7))
                        K.act(rstd[:], pb[:], AF.Sqrt, bias=epsc[:, 0:1], scale=1.0 / D)
                        K.op("dve", lambda e: e.reciprocal(out=rstd[:], in_=rstd[:]), [rstd], [rstd])
                        for c in range(8):
                            K.stt(xg[:, c, :], xg[:, c, :], col(l, "fing", c), rstd[:], ALU.mult, ALU.mult,
                                  r=[xg, rstd] + colkeys)
                        for tt_ in range(4):
                            o_ = ot[tt_ % 2]
                            for half in range(2):
                                pb2 = nb_()
                                for c in range(4):
                                    K.tr(pb2[:, c * 128:(c + 1) * 128], xg[:, half * 4 + c, tt_ * 128:(tt_ + 1) * 128], ident[:])
                                K.cp("act" if half else "dve", o_[:, half * 512:(half + 1) * 512], pb2[:])
                            K.dma("sp", out_d[tg * 512 + tt_ * 128: tg * 512 + (tt_ + 1) * 128, :], o_[:], w=["out"],
                                  semkey="out")
            K.barrier()
        K.final_wait("sp", ["out"] + ["dbg_" + n for n in dbg_d])
        K.replay()
    return nc, K


def phase_a(nc, K, sbt, banks, nb_, l, S, cst, Wb_in, hT_d, yT_d, ident, identb, feed):
    NT = S // 128
    NB = S // 256
    with ExitStack() as es:
        wA = sbt(es, "wA", [128, 8, 1536], BF16)
        KT = sbt(es, "KT", [80, 8, S], BF16)
        Va = sbt(es, "Va", [128, NT, 8, 65], BF16)
        QT = [sbt(es, f"QT{i}", [80, 8, 256], BF16) for i in range(2)]
        QTf = [sbt(es, f"QTf{i}", [64, 8, 128]) for i in range(2)]
        kms = sbt(es, "kms", [64, 8, 16])
        ktmp = sbt(es, "ktmp", [64, 8])
        hTt = [sbt(es, f"hTtA{i}", [128, 8, 128], BF16) for i in range(2)]
        qk = [sbt(es, f"qkA{i}", [128, 2, 8, 64]) for i in range(2)]
        cs = [sbt(es, f"csA{i}", [128, 2, 128]) for i in range(2)]
        rt = [sbt(es, f"rtA{i}", [128, 128]) for i in range(4)]
        gs = sbt(es, "gsA", [128, 8, 16])
        m8 = sbt(es, "m8A", [128, 8, 8])
        Mp = sbt(es, "MpA", [128, 8, 80], BF16)
        cm = sbt(es, "cmA", [128, 2, 256], BF16)
        PT = [sbt(es, f"PTA{i}", [128, 256], BF16) for i in range(3)]
        ya = [sbt(es, f"yaA{i}", [128, 512], BF16) for i in range(2)]
        rden = sbt(es, "rdenA", [128, 4])
        yTs = [sbt(es, f"yTsA{i}", [128, 4, 128], BF16) for i in range(2)]
        for g in range(3):
            K.dma("sp", wA[:, :, g * 512:(g + 1) * 512], Wb_in[l, :, :, g * 512:(g + 1) * 512], r=[("Wb_in", l, g)],
                  w=[("wA", g)], semkey="wload")
        wAk = [("wA", g) for g in range(3)]
        K.dma("pool", cm[:], cst["cCM"].rearrange("p (k q) -> p k q", k=2), r=[], semkey="cloadp")
        for h in range(8 if "e" not in PA_SKIP else 0):
            for e0 in range(0, S, 2048):
                e1 = min(S, e0 + 2048)
                K.dma("pool", KT[64:80, h, e0:e1], cst["cE"][:, e0:e1], r=[], w=[("KTE", h)], semkey="cloadp")
        KTE = [("KTE", h) for h in range(8)]
        if "m" not in PA_SKIP:
            K.memset("pool", Va[:, :, :, 64:65], 1.0, w=["Va1"])
            K.memset("pool", Mp[:], 0.0)
        pz = [banks[0], banks[1]]
        pTq = [banks[2], banks[3]]
        pS = [banks[4], banks[5]]
        pO = [banks[6], banks[7]]
        pti = 0
        psi = 0
        for t in range(NT):
            b = t // 2
            half = t % 2
            feed(2)
            hT = hTt[t % 2]
            K.dma("sp", hT[:], hT_d[:, :, t * 128:(t + 1) * 128], r=[("hT", t // 4)])
            c_ = cs[t % 2]
            K.dma("sp", c_[:], cst["cA"][t * 128:(t + 1) * 128, :].rearrange("p (a b) -> p a b", a=2), r=[])
            q_ = qk[t % 2]
            for g in range(3):
                pb = pz[g % 2]
                for kc in range(8):
                    K.mm(pb[:], hT[:, kc, :], wA[:, kc, g * 512:(g + 1) * 512], start=(kc == 0), stop=(kc == 7),
                         r=[hT, ("wA", g)])
                if g < 2:
                    K.cp("act", q_[:, g], pb[:].rearrange("p (h d) -> p h d", h=8))
                elif "v" not in PA_SKIP:
                    K.cp("act", Va[:, t, :, 0:64], pb[:].rearrange("p (h d) -> p h d", h=8), w=[("Va", t)])
            x1 = q_[:, :, :, 0:8]
            x2 = q_[:, :, :, 8:16]
            co = c_[:, 0, :].rearrange("p (a h d) -> p a h d", a=2, h=8)
            si = c_[:, 1, :].rearrange("p (a h d) -> p a h d", a=2, h=8)
            t1, t2, t3, t4 = [r_[:].rearrange("p (a h d) -> p a h d", a=2, h=8) for r_ in rt]
            if "r" not in PA_SKIP:
                K.tt("dve", t1, x1, co, ALU.mult)
                K.tt("pool", t2, x2, si, ALU.mult)
                K.tt("dve", t3, x1, si, ALU.mult)
                K.tt("pool", t4, x2, co, ALU.mult)
                K.tt("dve", x1, t1, t2, ALU.subtract)
                K.tt("dve", x2, t3, t4, ALU.add)
            for g in range(2 if "t" not in PA_SKIP else 0):
                for hq in range(2):
                    pb = pTq[hq]
                    for h4 in range(4):
                        h = hq * 4 + h4
                        K.tr(pb[0:64, h4 * 128:(h4 + 1) * 128], q_[:, g, h, :], ident[:])
                    src = pb[0:64, :].rearrange("p (h t) -> p h t", h=4)
                    if g == 0:
                        K.cp("act", QT[b % 2][0:64, hq * 4:(hq + 1) * 4, half * 128:(half + 1) * 128], src,
                             w=[("QTq", b % 2)])
                        K.cp("dve", QTf[half][:, hq * 4:(hq + 1) * 4, :], src)
                    else:
                        K.cp("act", KT[0:64, hq * 4:(hq + 1) * 4, t * 128:(t + 1) * 128], src, w=[("KT", t)])
                        K.op("dve", lambda e, src=src, hq=hq: e.tensor_reduce(out=ktmp[:, hq * 4:(hq + 1) * 4], in_=src, axis=AX.X, op=ALU.add),
                             [pb], [ktmp])
                if g == 1:
                    if half == 0:
                        K.cp("dve", kms[:, :, b:b + 1], ktmp[:].rearrange("p (h o) -> p h o", o=1))
                    else:
                        K.tt("dve", kms[:, :, b:b + 1], kms[:, :, b:b + 1], ktmp[:].rearrange("p (h o) -> p h o", o=1), ALU.add)
            if half == 0 or PA_STOP <= 1:
                continue
            Qb = QT[b % 2]
            if b >= 1:
                for hf_ in range(2):
                    pg = banks[2]
                    for h in range(8):
                        K.mm(pg[:, h * 16:(h + 1) * 16], QTf[hf_][0:64, h, :], kms[0:64, h, :])
                    K.cp("dve", gs[:], pg[:, 0:128].rearrange("p (h n) -> p h n", h=8))
                    if b < 16:
                        K.memset("dve", gs[:, :, b:16], -1e30)
                    for h in range(8):
                        K.op("dve", lambda e, h=h: e.max(out=m8[:, h, :], in_=gs[:, h, :]), [gs], [m8])
                    for h in range(8):
                        K.ts("dve", Mp[:, h, 64:80], gs[:, h, :], m8[:, h, 2:3], NEG, ALU.is_lt, ALU.mult)
                    pm = banks[3]
                    pmv = pm[:].bitcast(BF16).rearrange("p (h t) -> p h t", h=8)
                    for h in range(8):
                        K.tr(pmv[0:80, h, :], Mp[:, h, :], identb[:])
                    K.cp("act", Qb[64:80, :, hf_ * 128:(hf_ + 1) * 128], pmv[64:80, :, :], w=[("QTm", b % 2, hf_)])
            Qkeys = [("QTq", b % 2), ("QTm", b % 2, 0), ("QTm", b % 2, 1)]
            if PA_STOP <= 2:
                continue
            nkt = 2 * b + 2
            steps = [(h, kt) for h in range(8) for kt in range(nkt)]

            def issue_S(i):
                h, kt = steps[i]
                own = kt >= 2 * b
                Kr = 64 if own else 80
                ps_ = pS[i % 2]
                K.mm(ps_[:, 0:256], KT[0:Kr, h, kt * 128:(kt + 1) * 128], Qb[0:Kr, h, :], start=True, stop=not own,
                     r=[("KT", kt), ("KTE", h)] + Qkeys)
                if own:
                    K.mm(ps_[:, 0:256], identb[:], cm[:, kt - 2 * b, :], start=False, stop=True)

            issue_S(0)
            for i, (h, kt) in enumerate(steps):
                own = kt >= 2 * b
                pOh = pO if h % 2 == 0 else [banks[2], banks[3]]
                if i + 1 < len(steps):
                    issue_S(i + 1)
                P_ = PT[i % 3]
                K.act(P_[:], pS[i % 2][:, 0:256], AF.Exp, scale=0.125)
                for q2 in range(2):
                    if own and kt - 2 * b == 1 and q2 == 0:
                        continue
                    lastk = (2 * b) if q2 == 0 else (2 * b + 1)
                    K.mm(pOh[q2][:, 0:65], P_[:, q2 * 128:(q2 + 1) * 128], Va[:, kt, h, :], start=(kt == 0), stop=(kt == lastk),
                         r=[P_, ("Va", kt), "Va1"])
                if kt == nkt - 1:
                    for q2 in range(2):
                        rd = rden[:, (h % 2) * 2 + q2:(h % 2) * 2 + q2 + 1]
                        K.op("dve", lambda e, rd=rd, src=pOh[q2][:, 64:65]: e.reciprocal(out=rd, in_=src), [pOh[q2]], [("rden", h % 2, q2)])
                        K.ts("dve", ya[q2][:, h * 64:(h + 1) * 64], pOh[q2][:, 0:64], rd, None, ALU.mult,
                             r=[pOh[q2], ("rden", h % 2, q2)], w=[("ya", q2, h)])
            yak = [[("ya", q2, h) for h in range(8)] for q2 in range(2)]
            if PA_STOP <= 3:
                continue
            for q2 in range(2):
                tq = 2 * b + q2
                pb = banks[q2]
                pv = pb[:].bitcast(BF16)[:, 0:512].rearrange("p (c t) -> p c t", c=4)
                for c in range(4):
                    K.tr(pv[:, c, :], ya[q2][:, c * 128:(c + 1) * 128], identb[:], r=yak[q2] + [identb])
                K.cp("act", yTs[q2][:], pv)
                K.dma("sp", yT_d[0, :, :, tq * 128:(tq + 1) * 128], yTs[q2][:], w=[("yT", 0, tq // 4)], semkey="yTs")


def phase_b(nc, K, sbt, banks, nb_, l, S, cst, Wb_in, hT_d, yT_d, ident, identb, epsc, feed):
    NT = S // 128
    with ExitStack() as es:
        wB = sbt(es, "wB", [128, 8, 2048], BF16)
        hTt = [sbt(es, f"hTtB{i}", [128, 8, 128], BF16) for i in range(2)]
        cb = [sbt(es, f"cbB{i}", [128, 4, 256]) for i in range(2)]
        qs = [sbt(es, f"qsB{i}", [128, 512]) for i in range(2)]
        rt = [sbt(es, f"rtB{i}", [128, 256]) for i in range(4)]
        qr = [sbt(es, f"qrB{i}", [128, 512], BF16) for i in range(2)]
        QTb = sbt(es, "QTbB", [128, 4, 128], BF16)
        KTb = sbt(es, "KTbB", [128, 4, 128], BF16)
        vb = sbt(es, "vbB", [128, 512], BF16)
        sgb = sbt(es, "sgB", [128, 512])
        Sm = sbt(es, "SmB", [128, 4, 128], BF16)
        rm = sbt(es, "rmB", [128, 4, 128], BF16)
        R = sbt(es, "RB", [128, 4, 128])
        Rg = sbt(es, "RgB", [128, 4, 128], BF16)
        rs = sbt(es, "rsB", [128, 12])
        st = sbt(es, "stB", [128, 4, 6])
        mv = sbt(es, "mvB", [128, 4, 2])
        rstd = sbt(es, "rstdB", [128, 4])
        yn = sbt(es, "ynB", [128, 512])
        yb = sbt(es, "ybB", [128, 512], BF16)
        yTs = [sbt(es, f"yTsB{i}", [128, 4, 128], BF16) for i in range(2)]
        for g in range(4):
            K.dma("sp", wB[:, :, g * 512:(g + 1) * 512], Wb_in[l, :, :, 1536 + g * 512:1536 + (g + 1) * 512],
                  r=[("Wb_in", l, 3 + g)], w=[("wB", g)], semkey="wload")
        K.dma("pool", rm[:], cst["cRM"].rearrange("p (h t) -> p h t", h=4), r=[], semkey="cloadp")
        K.dma("sp", rs[:], cst["cRS"][:, :], r=[], semkey="cload")
        K.memset("dve", R[:], 0.0, w=[("RB", h) for h in range(4)])
        K.memset("pool", Rg[:], 0.0)
        for t in range(NT):
            feed(2)
            hT = hTt[t % 2]
            K.dma("sp", hT[:], hT_d[:, :, t * 128:(t + 1) * 128], r=[("hT", t // 4)])
            c_ = cb[t % 2]
            K.dma("sp", c_[:], cst["cB"][t * 128:(t + 1) * 128, :].rearrange("p (a b) -> p a b", a=4), r=[])
            for g in range(4):
                pb = nb_(0, 4)
                for kc in range(8):
                    K.mm(pb[:], hT[:, kc, :], wB[:, kc, g * 512:(g + 1) * 512], start=(kc == 0), stop=(kc == 7),
                         r=[hT, ("wB", g)])
                if g < 2:
                    q_ = qs[g]
                    K.cp("act", q_[:], pb[:])
                    xv = q_[:].rearrange("p (h d two) -> p h d two", h=4, two=2)
                    xe = xv[:, :, :, 0]
                    xo = xv[:, :, :, 1]
                    co = c_[:, 2 * g, :].rearrange("p (h d) -> p h d", h=4)
                    si = c_[:, 2 * g + 1, :].rearrange("p (h d) -> p h d", h=4)
                    t1, t2, t3, t4 = [r_[:].rearrange("p (h d) -> p h d", h=4) for r_ in rt]
                    ov = qr[g][:].rearrange("p (h d two) -> p h d two", h=4, two=2)
                    K.tt("dve", t1, xe, co, ALU.mult)
                    K.tt("pool", t2, xo, si, ALU.mult)
                    K.tt("dve", t3, xe, si, ALU.mult)
                    K.tt("pool", t4, xo, co, ALU.mult)
                    K.tt("dve", ov[:, :, :, 0], t1, t2, ALU.subtract, w=[(f"qrB{g}", 0)], r=[rt[0], rt[1]])
                    K.tt("dve", ov[:, :, :, 1], t3, t4, ALU.add, w=[(f"qrB{g}", 1)], r=[rt[2], rt[3]])
                    pbt = nb_(0, 4)
                    pv = pbt[:].bitcast(BF16)[:, 0:512].rearrange("p (h t) -> p h t", h=4)
                    for h in range(4):
                        K.tr(pv[:, h, :], qr[g][:, h * 128:(h + 1) * 128], identb[:], r=[(f"qrB{g}", 0), (f"qrB{g}", 1), identb])
                    K.cp("act", (QTb if g == 0 else KTb)[:], pv)
                elif g == 2:
                    K.cp("act", vb[:], pb[:])
                else:
                    K.act(sgb[:], pb[:], AF.Silu)
            qrk = [("qrB1", 0), ("qrB1", 1)]
            pS_ = banks[4]
            for h in range(4):
                K.mm(pS_[:, h * 128:(h + 1) * 128], KTb[:, h, :], QTb[:, h, :])
            K.tt("dve", Sm[:], pS_[:].rearrange("p (h t) -> p h t", h=4), rm[:], ALU.mult)
            py = banks[5]
            pkv = banks[6]
            for h in range(4):
                K.mm(py[:, h * 128:(h + 1) * 128], Sm[:, h, :], vb[:, h * 128:(h + 1) * 128], start=True, stop=False)
                K.mm(py[:, h * 128:(h + 1) * 128], QTb[:, h, :], Rg[:, h, :], start=False, stop=True)
            for h in range(4):
                K.mm(pkv[:, h * 128:(h + 1) * 128], qr[1][:, h * 128:(h + 1) * 128], vb[:, h * 128:(h + 1) * 128],
                     r=qrk + [vb])
            for h in range(4):
                K.op("dve", lambda e, h=h: e.bn_stats(out=st[:, h, :], in_=py[:, h * 128:(h + 1) * 128]), [py], [("stB", h)])
                K.op("dve", lambda e, h=h: e.bn_aggr(out=mv[:, h, :], in_=st[:, h, :]), [("stB", h)], [("mvB", h)])
            mvk = [("mvB", h) for h in range(4)]
            K.act(rstd[:], mv[:, :, 1], AF.Sqrt, bias=epsc[:, 0:1], scale=1.0, r=mvk + [epsc])
            K.op("dve", lambda e: e.reciprocal(out=rstd[:], in_=rstd[:]), [rstd], [rstd])
            for h in range(4):
                K.ts("dve", yn[:, h * 128:(h + 1) * 128], py[:, h * 128:(h + 1) * 128], mv[:, h, 0:1], rstd[:, h:h + 1],
                     ALU.subtract, ALU.mult, r=[py, rstd] + mvk, w=[("ynB", h)])
            K.tt("pool", yb[:], yn[:], sgb[:], ALU.mult, r=[("ynB", h) for h in range(4)] + [sgb])
            for h in range(4):
                K.ts("pool", R[:, h, :], R[:, h, :], rs[:, h:h + 1], None, ALU.mult, r=[("RB", h), rs], w=[("RB", h)])
                K.stt(R[:, h, :], pkv[:, h * 128:(h + 1) * 128], rs[:, 4 + h:5 + h], R[:, h, :], ALU.mult, ALU.add,
                      r=[pkv, rs, ("RB", h)], w=[("RB", h)])
                K.act(Rg[:, h, :], R[:, h, :], AF.Identity, scale=float(cst_gamma(h)), r=[("RB", h)], w=[Rg])
            pbt = banks[7]
            pv = pbt[:].bitcast(BF16)[:, 0:512].rearrange("p (c t) -> p c t", c=4)
            for c in range(4):
                K.tr(pv[:, c, :], yb[:, c * 128:(c + 1) * 128], identb[:])
            K.cp("act", yTs[t % 2][:], pv)
            K.dma("sp", yT_d[1, :, :, t * 128:(t + 1) * 128], yTs[t % 2][:], w=[("yT", 1, t // 4)], semkey="yTs")


def cst_gamma(h):
    return 1.0 - 2.0 ** (-5.0 - h)


def phase_c(nc, K, sbt, banks, nb_, l, S, cst, Wb_in, hT_d, yT_d, vf_d, ident, identb, epsc, cols, col, colkeys,
            w2_d, a2_d, g2_d, v2_d, ln_d, dbg_d):
    NG = S // 512
    RW = BF16
    ncc = 15 if l >= 1 else 14
    with ExitStack() as es:
        wC = sbt(es, "wC", [128, 8, 1920], BF16)
        w2b = sbt(es, "w2b", [64, 512], BF16)
        a2b = sbt(es, "a2b", [128, 512], BF16)
        g2b = sbt(es, "g2b", [128, 512], BF16)
        v2b = sbt(es, "v2b", [32, 512], BF16)
        lnp = sbt(es, "lnp", [128, 2, 256])
        am = sbt(es, "amC", [128, 2, 2, 128])
        xm = sbt(es, "xmC", [128, 4, 64])
        rst = sbt(es, "rstC", [128, 512])
        bo = sbt(es, "boC", [128, 128], BF16)
        hg = sbt(es, "hgC", [128, 8, 512], BF16)
        zxc = [sbt(es, f"zxC{i}", [128, 513]) for i in range(2)]
        lastc = sbt(es, "lastcC", [128, 16])
        zl = sbt(es, "zlC", [128, 15, 512])
        tmp = [sbt(es, f"tmpC{i}", [128, 512]) for i in range(3)]
        tw = sbt(es, "twC", [64, 512], BF16)
        al = sbt(es, "alC", [128, 512], BF16)
        sgl = sbt(es, "sglC", [128, 512], BF16)
        vlr = sbt(es, "vlrC", [32, 512], BF16)
        sigw = sbt(es, "sigwC", [128, 512])
        iclr = sbt(es, "iclrC", [128, 512])
        cl = sbt(es, "clC", [128, 512])
        Pinc = sbt(es, "PincC", [128, 512])
        Pexc = sbt(es, "PexcC", [128, 512])
        Pinv = sbt(es, "PinvC", [128, 512])
        kkr = sbt(es, "kkrC", [128, 512])
        sqb = sbt(es, "sqbC", [128, 512], BF16)
        kmod = sbt(es, "kmodC", [128, 512])
        vfp = sbt(es, "vfpC", [128, 4, 512])
        vbf = sbt(es, "vbfC", [128, 4, 512], BF16)
        AR = sbt(es, "ARC", [128, 4, 8, 2, 64], RW)
        BK = sbt(es, "BKC", [128, 4, 8, 2, 64], RW)
        prk = sbt(es, "prkC", [128, 4, 512], BF16)
        pend = sbt(es, "pendC", [128, 4, 8])
        H = sbt(es, "HC", [128, 4, 64])
        Hb = sbt(es, "HbC", [128, 4, 64], RW)
        Amat = [sbt(es, f"AmatC{i}", [128, 4, 2, 128], RW) for i in range(3)]
        Xm = [[sbt(es, f"XmC{i}_{j}", [128, 2, 4, 64], RW) for j in range(2)] for i in range(3)]
        Wt = [sbt(es, f"WtC{i}", [128, 4, 64], RW) for i in range(2)]
        Vc = [sbt(es, f"VcC{i}", [128, 4, 64], RW) for i in range(3)]
        Vcf = [sbt(es, f"VcfC{i}", [128, 4, 64]) for i in range(3)]
        BKt = [sbt(es, f"BKtC{i}", [128, 4, 2, 64], RW) for i in range(3)]
        Tt = [sbt(es, f"TtC{i}", [128, 4, 64], RW) for i in range(3)]
        gt = [sbt(es, f"gtC{i}", [128, 4, 64]) for i in range(3)]
        bs = [sbt(es, f"bsC{i}", [128, 4]) for i in range(3)]
        i4 = sbt(es, "i4C", [128, 4, 64])
        st = sbt(es, "stC", [128, 4, 6])
        mv = sbt(es, "mvC", [128, 4, 2])
        rstd = sbt(es, "rstdC", [128, 4])
        yn = sbt(es, "ynC", [128, 4, 64])
        yc = sbt(es, "ycC", [128, 4, 64], BF16)
        ycT = [sbt(es, f"ycTC{i}", [128, 4, 512], BF16) for i in range(2)]

        nwc = 1792 + (32 if l >= 1 else 0)
        for g in range(0, 1792, 512):
            gw = min(512, 1792 - g)
            K.dma("sp", wC[:, :, g:g + gw], Wb_in[l, :, :, 3584 + g:3584 + g + gw],
                  r=[("Wb_in", l, (3584 + g) // 512)], w=[("wC", g)], semkey="wload")
        wCk = [("wC", g) for g in range(0, 1792, 512)]
        if l >= 1:
            K.dma("sp", wC[:, :, 1792:1824], Wb_in[l, :, :, NIN:NIN + 32], r=[("Wb_in", l, "v")], w=[("wC", "v")], semkey="wload")
            wCk.append(("wC", "v"))
            K.dma("pool", v2b[:], v2_d[l - 1], r=[], semkey="cloadp")
        K.dma("pool", w2b[:], w2_d[l], r=[], semkey="cloadp")
        K.dma("pool", a2b[64:128, :], a2_d[l], r=[], semkey="cloadp")
        K.dma("pool", g2b[:], g2_d[l], r=[], semkey="cloadp")
        K.dma("sp", lnp[:], ln_d[l].rearrange("p (a b) -> p a b", a=2), r=[], semkey="cload")
        K.dma("sp", am[:], cst["cAM"].rearrange("p (a j t) -> p a j t", a=2, j=2), r=[], semkey="cload")
        K.dma("sp", xm[:], cst["cXM"].rearrange("p (a t) -> p a t", a=4), r=[], semkey="cload")
        K.dma("sp", rst[:], cst["cRST"][:, :], r=[], semkey="cload")
        K.dma("pool", bo[:], cst["cBO"][:, :], r=[], semkey="cloadp")
        K.memset("dve", lastc[:], 0.0, w=[("lastc", cc) for cc in range(16)])
        K.dma("sp", i4[:], cst["cI4"].rearrange("p (a t) -> p a t", a=4), r=[], semkey="cload")
        K.memset("dve", H[:], 0.0)
        K.memset("pool", Hb[:], 0.0)
        lng = lnp[:, 0, :].rearrange("p (a v) -> p a v", a=4)
        lnb = lnp[:, 1, :].rearrange("p (a v) -> p a v", a=4)

        for tg in range(NG):
            tok = slice(tg * 512, (tg + 1) * 512)
            K.dma("sp", hg[:], hT_d[:, :, tok], r=[("hT", tg)])
            for cc in range(ncc):
                pb = nb_(0, 4)
                np_ = 128 if cc < 14 else 32
                zc_ = zxc[cc % 2]
                if cc < 14:
                    for kc in range(8):
                        K.mm(pb[:], wC[:, kc, cc * 128:(cc + 1) * 128], hg[:, kc, :], start=(kc == 0), stop=(kc == 7),
                             r=wCk + [hg])
                else:
                    for kc in range(8):
                        K.mm(pb[0:32, :], wC[:, kc, 1792:1824], hg[:, kc, :], start=(kc == 0), stop=(kc == 7), r=wCk + [hg])
                K.cp("act", zc_[0:np_, 1:513], pb[0:np_, :])
                K.cp("pool", zc_[0:np_, 0:1], lastc[0:np_, cc:cc + 1], r=[("lastc", cc)], w=[zc_])
                mu = col(l, "mu", cc) if cc < 14 else col(l, "vmu", 0)
                tp = tmp[cc % 2]
                K.tt("pool" if cc % 2 else "dve", tp[0:np_, :], zc_[0:np_, 0:512], zc_[0:np_, 1:513], ALU.subtract)
                K.stt(zl[0:np_, cc, :], tp[0:np_, :], mu[0:np_, :], zc_[0:np_, 1:513], ALU.mult, ALU.add,
                      r=[tp, zc_] + colkeys, w=[("zl", cc)])
                K.cp("pool", lastc[0:np_, cc:cc + 1], zc_[0:np_, 512:513], w=[("lastc", cc)])
            zlk = [("zl", cc) for cc in range(ncc)]
            K.act(tw[:], zl[0:64, 12, :], AF.Tanh, r=[("zl", 12)])
            K.cp("dve", al[64:128, :], zl[64:128, 12, :], r=[("zl", 12)])
            K.act(sgl[:], zl[:, 13, :], AF.Sigmoid, r=[("zl", 13)])
            if l >= 1:
                K.cp("dve", vlr[:], zl[0:32, 14, :], r=[("zl", 14)])
                K.dma("sp", vfp[:], vf_d[:, :, tok], r=[("vf", tg)])
            for pc in range(4):
                rT = zl[:, pc, :]
                kT = zl[:, 4 + pc, :]
                vT = zl[:, 8 + pc, :]
                rk_ = [("zl", pc)]
                kk_ = [("zl", 4 + pc)]
                vk_ = [("zl", 8 + pc)]
                pw = nb_(0, 4)
                K.mm(pw[:], w2b[0:64, pc * 128:(pc + 1) * 128], tw[0:64, :])
                K.act(sigw[:], pw[:], AF.Sigmoid, bias=col(l, "w0", pc), r=[pw] + colkeys)
                pa = nb_(0, 4)
                K.mm(pa[:], a2b[64:128, pc * 128:(pc + 1) * 128], al[64:128, :])
                K.act(iclr[:], pa[:], AF.Sigmoid, bias=col(l, "a0", pc), r=[pa] + colkeys)
                K.op("dve", lambda e: e.tensor_tensor_scan(out=cl[:], data0=rst[:], data1=sigw[:], initial=0.0, op0=ALU.mult, op1=ALU.add),
                     [rst, sigw], [cl])
                K.act(Pinc[:], cl[:], AF.Exp, scale=-C0)
                K.act(Pinv[:], cl[:], AF.Exp, scale=C0)
                K.tt("pool", tmp[2][:], cl[:], sigw[:], ALU.subtract)
                K.act(Pexc[:], tmp[2][:], AF.Exp, scale=-C0)
                K.cp("dve", pend[:, pc, :], Pinc[:].rearrange("p (c t) -> p c t", c=8)[:, :, 63])
                K.ts("dve", kkr[:], kT, col(l, "kk", pc), None, ALU.mult, r=kk_ + colkeys)
                K.act(sqb[:], kkr[:], AF.Square)
                pss = nb_(0, 4)
                K.mm(pss[:], bo[:], sqb[:])
                K.act(tmp[0][:], pss[:], AF.Sqrt)
                K.ts("dve", tmp[0][:], tmp[0][:], 1e-12, None, ALU.max)
                K.op("dve", lambda e: e.reciprocal(out=tmp[0][:], in_=tmp[0][:]), [tmp[0]], [tmp[0]])
                K.tt("dve", kkr[:], kkr[:], tmp[0][:], ALU.mult)
                ARv = AR[:, pc].rearrange("p c j t -> p j c t")
                BKv = BK[:, pc].rearrange("p c j t -> p j c t")
                c3 = lambda ap: ap.rearrange("p (c t) -> p c t", c=8)
                K.stt(ARv[:, 0], c3(kkr[:]), -1.0, c3(Pexc[:]), ALU.mult, ALU.mult, r=[kkr, Pexc], w=[("AR", pc, 0)])
                K.tt("pool", tmp[1][:], kkr[:], iclr[:], ALU.mult)
                K.tt("dve", BKv[:, 0], c3(tmp[1][:]), c3(Pinv[:]), ALU.mult, r=[tmp[1], Pinv], w=[("BK", pc, 0)])
                K.ts("dve", tmp[2][:], iclr[:], 1.0, col(l, "ka", pc), ALU.subtract, ALU.mult, r=[iclr] + colkeys)
                K.stt(kmod[:], tmp[2][:], 1.0, kT, ALU.add, ALU.mult, r=[tmp[2]] + kk_)
                K.tt("dve", BKv[:, 1], c3(kmod[:]), c3(Pinv[:]), ALU.mult, r=[kmod, Pinv], w=[("BK", pc, 1)])
                K.tt("pool", ARv[:, 1], c3(rT), c3(Pinc[:]), ALU.mult, r=rk_ + [Pinc], w=[("AR", pc, 1)])
                K.stt(prk[:, pc, :], rT, col(l, "rk", pc), kmod[:], ALU.mult, ALU.mult, r=rk_ + [kmod] + colkeys, w=[("prk", pc)])
                if l == 0:
                    pass
                else:
                    pv_ = nb_(0, 4)
                    K.mm(pv_[:], v2b[0:32, pc * 128:(pc + 1) * 128], vlr[0:32, :])
                    K.act(tmp[0][:], pv_[:], AF.Sigmoid, bias=col(l, "v0", pc), r=[pv_] + colkeys)
                    K.tt("dve", tmp[1][:], vfp[:, pc, :], vT, ALU.subtract, r=[vfp] + vk_)
                    K.tt("dve", tmp[1][:], tmp[1][:], tmp[0][:], ALU.mult)
                    K.tt("dve", vT, vT, tmp[1][:], ALU.add, r=vk_ + [tmp[1]], w=vk_)
            K.cp("pool", vbf[:], zl[:, 8:12, :], r=[("zl", 8 + i) for i in range(4)])
            if l == 0:
                K.dma("sp", vf_d[:, :, tok], zl[:, 8:12, :], r=[("zl", 8 + i) for i in range(4)], w=[("vf", tg)], semkey="vfst")
            ARk = [("AR", pc, j) for pc in range(4) for j in range(2)]
            BKk = [("BK", pc, j) for pc in range(4) for j in range(2)]
            prkk = [("prk", pc) for pc in range(4)]
            vks = [("zl", 8 + i) for i in range(4)]
            yT_ = ycT[tg % 2]
            def hs(hh):
                return slice(hh * 64, (hh + 1) * 64)

            def prep(c, bA, bB):
                sl = c % 3
                ct = slice(c * 64, (c + 1) * 64)
                Vc_, Vcf_, BKt_, Amat_, Tt_, gt_, bs_ = Vc[sl], Vcf[sl], BKt[sl], Amat[sl], Tt[sl], gt[sl], bs[sl]
                X0_, X1_ = Xm[sl]
                pVv = bA[:].bitcast(BF16)[:, 0:256].rearrange("p (a v) -> p a v", a=4)
                for hh in range(2):
                    for pc in range(4):
                        K.tr(pVv[hs(hh), pc, :], vbf[hs(hh), pc, ct], identb[hs(hh), hs(hh)])
                K.cp("act", Vc_[:], pVv)
                K.cp("dve", Vcf_[:], pVv)
                pBv = bB[:].bitcast(BF16)[:, 0:512].rearrange("p (a j k) -> p a j k", a=4, j=2)
                for hh in range(2):
                    for pc in range(4):
                        for j in range(2):
                            K.tr(pBv[hs(hh), pc, j, :], BK[hs(hh), pc, c, j, :], identb[hs(hh), hs(hh)], r=BKk + [identb])
                K.cp("act", BKt_[:], pBv)
                yield
                for pb2, pA in enumerate((bA, bB)):
                    pAv = pA[:].rearrange("p (a j t) -> p a j t", a=2, j=2)
                    for hh in range(2):
                        for a_ in range(2):
                            pc = pb2 * 2 + a_
                            rhs = AR[hs(hh), pc, c].rearrange("p j t -> p (j t)")
                            for j in range(2):
                                K.mm(pAv[hs(hh), a_, j, :], BK[hs(hh), pc, c, j, :], rhs, r=ARk + BKk)
                    K.tt("dve", Amat_[:, pb2 * 2:pb2 * 2 + 2], pAv, am[:], ALU.mult, w=[("Amat", sl, pb2)])
                Ak = [("Amat", sl, 0), ("Amat", sl, 1)]
                yield
                for hh in range(2):
                    for pc in range(4):
                        K.mm(bA[hs(hh), pc * 64:(pc + 1) * 64], AR[hs(hh), pc, c, 0, :], BK[hs(hh), pc, c, 0, :], r=ARk + BKk)
                K.tt("dve", X0_[:, 0], bA[:, 0:256].rearrange("p (a s) -> p a s", a=4), xm[:], ALU.mult, w=[("X", sl, 0)])
                K.cp("act", X0_[:, 1], Amat_[:, :, 0, 0:64], r=Ak, w=[("Y", sl, 0)])
                K.tt("pool", Tt_[:], Amat_[:, :, 0, 0:64], i4[:], ALU.add, r=Ak + [i4])
                for hh in range(2):
                    for pc in range(4):
                        K.mm(bB[hs(hh), pc * 64:(pc + 1) * 64], sgl[:, ct], g2b[:, (2 * pc + hh) * 64:(2 * pc + hh + 1) * 64])
                for hh in range(2):
                    for pc in range(4):
                        K.mm(bB[hs(hh), 256 + pc:256 + pc + 1], prk[hs(hh), pc, ct], bo[hs(hh), hh * 64:hh * 64 + 1], r=prkk + [bo])
                K.cp("act", gt_[:], bB[:, 0:256].rearrange("p (a v) -> p a v", a=4))
                K.cp("dve", bs_[:], bB[:, 256:260])
                yield
                Xs = [X0_, X1_]
                for lev in range(1, 6):
                    Xp = Xs[(lev - 1) % 2]
                    Xn = Xs[lev % 2]
                    xk = [("X", sl, (lev - 1) % 2), ("Y", sl, (lev - 1) % 2)]
                    for hh in range(2):
                        for pc in range(4):
                            K.mm(bA[hs(hh), pc * 64:(pc + 1) * 64], Xp[hs(hh), 1, pc, :], Xp[hs(hh), 0, pc, :], r=xk)
                            if lev < 5:
                                K.mm(bA[hs(hh), 256 + pc * 64:256 + (pc + 1) * 64], Xp[hs(hh), 0, pc, :], Xp[hs(hh), 1, pc, :], r=xk)
                    if lev < 5:
                        K.cp("act", Xn[:], bA[:].rearrange("p (j a s) -> p j a s", j=2, a=4),
                             w=[("X", sl, lev % 2), ("Y", sl, lev % 2)])
                    else:
                        K.cp("act", Xn[:, 0], bA[:, 0:256].rearrange("p (a s) -> p a s", a=4), w=[("X", sl, lev % 2)])
                    yield
                    for hh in range(2):
                        for pc in range(4):
                            K.mm(bB[hs(hh), pc * 64:(pc + 1) * 64], Xn[hs(hh), 0, pc, :], Tt_[hs(hh), pc, :], r=[("X", sl, lev % 2), Tt_])
                    K.tt("dve", Tt_[:], Tt_[:], bB[:, 0:256].rearrange("p (a t) -> p a t", a=4), ALU.add)
                yield

            def seq(c, bA, bB):
                sl = c % 3
                ct = slice(c * 64, (c + 1) * 64)
                Vc_, Vcf_, BKt_, Amat_, Tt_, gt_, bs_ = Vc[sl], Vcf[sl], BKt[sl], Amat[sl], Tt[sl], gt[sl], bs[sl]
                Ak = [("Amat", sl, 0), ("Amat", sl, 1)]
                for hh in range(2):
                    for pc in range(4):
                        o_ = bA[hs(hh), pc * 64:(pc + 1) * 64]
                        K.mm(o_, AR[hs(hh), pc, c, 0, :], Hb[hs(hh), pc, :], start=True, stop=False, r=ARk + [Hb])
                        K.mm(o_, Amat_[hs(hh), pc, 1, 0:64], Vc_[hs(hh), pc, :], start=False, stop=True, r=Ak + [Vc_])
                K.cp("dve", Wt[0][:], bA[:, 0:256].rearrange("p (a v) -> p a v", a=4))
                yield
                for hh in range(2):
                    for pc in range(4):
                        K.mm(bB[hs(hh), pc * 64:(pc + 1) * 64], Tt_[hs(hh), pc, :], Wt[0][hs(hh), pc, :])
                U = Wt[1]
                K.cp("act", U[:], bB[:, 0:256].rearrange("p (a v) -> p a v", a=4))
                yield
                pO = bA
                pH = bB
                for hh in range(2):
                    for pc in range(4):
                        o_ = pO[hs(hh), pc * 64:(pc + 1) * 64]
                        K.mm(o_, AR[hs(hh), pc, c, 1, :], Hb[hs(hh), pc, :], start=True, stop=False, r=ARk + [Hb])
                        K.mm(o_, Amat_[hs(hh), pc, 0, 64:128], U[hs(hh), pc, :], start=False, stop=False, r=Ak + [U])
                        K.mm(o_, Amat_[hs(hh), pc, 1, 64:128], Vc_[hs(hh), pc, :], start=False, stop=True, r=Ak + [Vc_])
                for hh in range(2):
                    for pc in range(4):
                        o_ = pH[hs(hh), pc * 64:(pc + 1) * 64]
                        K.mm(o_, BKt_[hs(hh), pc, 0, :], U[hs(hh), pc, :], start=True, stop=False)
                        K.mm(o_, BKt_[hs(hh), pc, 1, :], Vc_[hs(hh), pc, :], start=False, stop=True)
                K.tt("dve", H[:], H[:], pH[:, 0:256].rearrange("p (a v) -> p a v", a=4), ALU.add)
                K.tt("pool", H[:], H[:], pend[:, :, c:c + 1].to_broadcast([128, 4, 64]), ALU.mult)
                K.cp("act", Hb[:], H[:])
                for pc in range(4):
                    K.op("dve", lambda e, pc=pc: e.bn_stats(out=st[:, pc, :], in_=pO[:, pc * 64:(pc + 1) * 64]), [pO], [("stC", pc)])
                    K.op("dve", lambda e, pc=pc: e.bn_aggr(out=mv[:, pc, :], in_=st[:, pc, :]), [("stC", pc)], [("mvC", pc)])
                mvk = [("mvC", pc) for pc in range(4)]
                K.act(rstd[:], mv[:, :, 1], AF.Sqrt, bias=epsc[:, 1:2], scale=1.0, r=mvk + [epsc])
                K.op("dve", lambda e: e.reciprocal(out=rstd[:], in_=rstd[:]), [rstd], [rstd])
                for pc in range(4):
                    K.ts("dve", yn[:, pc, :], pO[:, pc * 64:(pc + 1) * 64], mv[:, pc, 0:1], rstd[:, pc:pc + 1], ALU.subtract, ALU.mult,
                         r=[pO, rstd] + mvk, w=[("ynC", pc)])
                ynk = [("ynC", pc) for pc in range(4)]
                K.tt("dve", yn[:], yn[:], lng, ALU.mult, r=ynk + [lnp], w=[yn])
                K.tt("pool", yn[:], yn[:], lnb, ALU.add, r=[yn, lnp], w=[yn])
                K.tt("pool", Vcf_[:], Vcf_[:], bs_[:].rearrange("p (a o) -> p a o", o=1).to_broadcast([128, 4, 64]), ALU.mult)
                K.tt("pool", yn[:], yn[:], Vcf_[:], ALU.add)
                K.tt("dve", yc[:], yn[:], gt_[:], ALU.mult)
                yield
                pYv = bB[:].bitcast(BF16)[:, 0:256].rearrange("p (a t) -> p a t", a=4)
                for hh in range(2):
                    for pc in range(4):
                        K.tr(pYv[hs(hh), pc, :], yc[hs(hh), pc, :], identb[hs(hh), hs(hh)])
                K.cp("act", yT_[:, :, ct], pYv)
                yield

            free_sets = [(banks[4], banks[5]), (banks[0], banks[1])]
            pending = list(range(8))
            active = []
            prep_done = set()
            seq_done = -1
            seq_gen = None
            seq_c = 0
            while seq_c < 8:
                while pending and len(active) < 2 and free_sets and pending[0] - 3 <= seq_done:
                    cnew = pending.pop(0)
                    bset = free_sets.pop(0)
                    active.append([cnew, prep(cnew, *bset), bset])
                if seq_gen is None and seq_c in prep_done:
                    seq_gen = seq(seq_c, banks[6], banks[7])
                if seq_gen is not None:
                    try:
                        next(seq_gen)
                    except StopIteration:
                        seq_gen = None
                        seq_done = seq_c
                        seq_c += 1
                for a in list(active):
                    try:
                        next(a[1])
                    except StopIteration:
                        prep_done.add(a[0])
                        free_sets.append(a[2])
                        active.remove(a)
            K.dma("sp", yT_d[2, :, :, tok], yT_[:], w=[("yT", 2, tg)], semkey="yTs")


_CACHE = {}


def make_in_maps(inp, S, depth, n_cores):
    consts = host_consts(S)
    colsarr = np.stack([pack_cols(inp, l) for l in range(depth)])
    lnarr = np.stack([pack_ln(inp, l) for l in range(depth)])
    maps = []
    f = lambda a: np.ascontiguousarray(np.asarray(a, np.float32))
    shared = {
        "w_in": f(inp["w_in"]), "c_w2": f(inp["c_w2"]), "c_a2": f(inp["c_a2"]), "c_g2": f(inp["c_g2"]),
        "c_vres_down": f(inp["c_vres_down"]), "c_v2": f(inp["c_v2"]), "w_branch": f(inp["w_branch"]),
        "w_out": f(inp["w_out"]), "w_gate_up": f(inp["w_gate_up"]), "w_down": f(inp["w_down"]),
        "w_ple_gate": f(inp["w_ple_gate"]), "w_ple_proj": f(inp["w_ple_proj"]), "cols": colsarr, "lnp": lnarr,
    }
    shared.update(consts)
    x = np.asarray(inp["x"], np.float32)
    p = np.asarray(inp["p"], np.float32)
    for b in range(n_cores):
        m = dict(shared)
        m["x"] = np.ascontiguousarray(x[b])
        m["p"] = np.ascontiguousarray(p[:, b])
        maps.append(m)
    return maps


def kernel(**inputs):
    x = np.asarray(inputs["x"])
    B, S, _ = x.shape
    depth = np.asarray(inputs["w_in"]).shape[0]
    key = (S, depth)
    if key not in _CACHE:
        _CACHE[key] = build(S, depth)[0]
    nc = _CACHE[key]
    maps = make_in_maps(inputs, S, depth, B)
    res = run_bass_kernel_spmd(nc, maps, core_ids=list(range(B)))
    return np.stack([np.asarray(r["out"], np.float32) for r in res.results], axis=0)
```

```python
import math
from contextlib import ExitStack
import numpy as np
import concourse.bass as bass
import concourse.mybir as mybir
from concourse.bass_utils import run_bass_kernel_spmd

F32 = mybir.dt.float32
BF16 = mybir.dt.bfloat16
ALU = mybir.AluOpType
AF = mybir.ActivationFunctionType
AX = mybir.AxisListType

D = 1024
NIN = 8448
DFF = 2816
EPS = 1e-6
NEG = -30000.0
C0 = math.exp(-0.5)
SEM_LIMIT = 30000
import os
PA_STOP = int(os.environ.get("PA_STOP", "9"))
PA_SKIP = os.environ.get("PA_SKIP", "")


class Ctx:
    ENGS = ("pe", "dve", "act", "pool", "sp")

    def __init__(self, nc):
        self.nc = nc
        self.prog = {e: [] for e in self.ENGS}
        self.sems = {}
        self.semval = {}
        self.cur = {}
        self.waited = {e: {} for e in self.ENGS}
        self.res = {}
        self.nsem = 0
        self.ninstr = 0
        self.banktag = {}

    def _semkey(self, logical, step):
        sk = self.cur.get(logical)
        if sk is None or self.semval[sk] + step > SEM_LIMIT:
            ep = 0 if sk is None else sk[1] + 1
            sk = (logical, ep)
            self.sems[sk] = self.nc.alloc_semaphore(name=f"s{self.nsem}")
            self.nsem += 1
            self.semval[sk] = 0
            self.cur[logical] = sk
        return sk

    @staticmethod
    def _key(x):
        if isinstance(x, (str, tuple)):
            return x
        if hasattr(x, "tensor"):
            return x.tensor.name
        return x.name

    def _collect(self, reads, writes):
        deps = {}

        def add(d):
            if d is not None:
                deps[d[0]] = max(deps.get(d[0], 0), d[1])
        for r in reads:
            st = self.res.get(r)
            if st:
                add(st["w"])
        for w in writes:
            st = self.res.get(w)
            if st:
                add(st["w"])
                for sk, v in st["r"].items():
                    add((sk, v))
        return deps

    def _emit_waits(self, e, deps):
        for sk, v in deps.items():
            if self.waited[e].get(sk, 0) < v:
                h = self.sems[sk]
                self.prog[e].append(lambda eng, h=h, v=v: eng.wait_ge(h, v))
                self.waited[e][sk] = v

    def _update(self, reads, writes, sk, v):
        for r in reads:
            st = self.res.setdefault(r, {"w": None, "r": {}})
            st["r"][sk] = max(st["r"].get(sk, 0), v)
        for w in writes:
            self.res[w] = {"w": (sk, v), "r": {}}

    def op(self, e, fn, r=(), w=(), petag=None):
        reads = [self._key(x) for x in r]
        writes = [self._key(x) for x in w]
        writes = writes + [k for k in reads if isinstance(k, str) and k.startswith("bank") and k not in writes]
        skip = []
        if e == "pe" and petag is not None:
            for k in writes:
                if isinstance(k, str) and k.startswith("bank"):
                    st = self.res.get(k)
                    if st and st["w"] is not None and st["w"][0][0] == ("eng", "pe") and not st["r"] \
                            and self.banktag.get(k) == petag:
                        skip.append(k)
                    self.banktag[k] = petag
        deps = self._collect(reads, [k for k in writes if k not in skip])
        self._emit_waits(e, deps)
        sk = self._semkey(("eng", e), 1)
        self.semval[sk] += 1
        v = self.semval[sk]
        h = self.sems[sk]
        self.prog[e].append(lambda eng, fn=fn, h=h: fn(eng).then_inc(h, 1))
        self._update(reads, writes, sk, v)
        self.ninstr += 1

    def dma(self, q, out, in_, r=None, w=None, semkey=None, **kw):
        reads = [self._key(x) for x in (r if r is not None else [in_])]
        writes = [self._key(x) for x in (w if w is not None else [out])]
        if semkey is None:
            semkey = out.tensor.name
        lk = ("dma", semkey)
        sk = self._semkey(lk, 16)
        deps = self._collect(reads, writes)
        if self.semval[sk] > 0:
            deps[sk] = max(deps.get(sk, 0), self.semval[sk])
        self._emit_waits(q, deps)
        self.semval[sk] += 16
        v = self.semval[sk]
        h = self.sems[sk]
        self.prog[q].append(
            lambda eng, out=out, in_=in_, kw=kw, h=h: eng.dma_start(out=out, in_=in_, **kw).then_inc(h, 16))
        self._update(reads, writes, sk, v)
        self.ninstr += 1

    def barrier(self):
        deps = {sk: v for sk, v in self.semval.items()
                if v > 0 and not (sk[0][0] == "dma" and str(sk[0][1]).startswith("conv"))}
        for e in self.ENGS:
            self._emit_waits(e, deps)

    def final_wait(self, e, keys):
        deps = self._collect([self._key(k) for k in keys], ())
        self._emit_waits(e, deps)

    def mm(self, out, lhsT, rhs, start=True, stop=True, r=None, w=None):
        tag = (lhsT.start_partition(), lhsT.partition_size())
        self.op("pe", lambda e: e.matmul(out, lhsT=lhsT, rhs=rhs, start=start, stop=stop),
                r if r is not None else [lhsT, rhs], w if w is not None else [out], petag=tag)

    def tr(self, out, in_, ident, r=None, w=None):
        tag = (in_.start_partition(), in_.partition_size())
        self.op("pe", lambda e: e.transpose(out=out, in_=in_, identity=ident),
                r if r is not None else [in_, ident], w if w is not None else [out], petag=tag)

    def act(self, out, in_, func, bias=None, scale=None, r=None, w=None, eng="act"):
        kw = {}
        if bias is not None:
            kw["bias"] = bias
        if scale is not None:
            kw["scale"] = scale
        rr = [in_] + [x for x in (bias, scale) if not isinstance(x, (int, float, type(None)))]
        self.op("act", lambda e: e.activation(out=out, in_=in_, func=func, **kw),
                r if r is not None else rr, w if w is not None else [out])

    def tt(self, eng, out, in0, in1, op, r=None, w=None):
        self.op(eng, lambda e: e.tensor_tensor(out=out, in0=in0, in1=in1, op=op),
                r if r is not None else [in0, in1], w if w is not None else [out])

    def ts(self, eng, out, in0, s1, s2, op0, op1=None, r=None, w=None):
        rr = [in0] + [x for x in (s1, s2) if not isinstance(x, (int, float, type(None)))]
        if op1 is None:
            fn = lambda e: e.tensor_scalar(out=out, in0=in0, scalar1=s1, scalar2=None, op0=op0)
        else:
            fn = lambda e: e.tensor_scalar(out=out, in0=in0, scalar1=s1, scalar2=s2, op0=op0, op1=op1)
        self.op(eng, fn, r if r is not None else rr, w if w is not None else [out])

    def stt(self, out, in0, scalar, in1, op0, op1, r=None, w=None):
        rr = [in0, in1] + ([scalar] if not isinstance(scalar, (int, float)) else [])
        self.op("dve", lambda e: e.scalar_tensor_tensor(out=out, in0=in0, scalar=scalar, in1=in1, op0=op0, op1=op1),
                r if r is not None else rr, w if w is not None else [out])

    def cp(self, eng, out, in_, r=None, w=None):
        if eng == "act":
            fn = lambda e: e.copy(out=out, in_=in_)
        else:
            fn = lambda e: e.tensor_copy(out=out, in_=in_)
        self.op(eng, fn, r if r is not None else [in_], w if w is not None else [out])

    def memset(self, eng, ap, val, w=None):
        self.op(eng, lambda e: e.memset(ap, val), [], w if w is not None else [ap])

    def replay(self):
        nc = self.nc
        with nc.Block() as block:
            @block.tensor
            def _(eng):
                for f in self.prog["pe"]:
                    f(eng)

            @block.vector
            def _(eng):
                for f in self.prog["dve"]:
                    f(eng)

            @block.scalar
            def _(eng):
                for f in self.prog["act"]:
                    f(eng)

            @block.gpsimd
            def _(eng):
                for f in self.prog["pool"]:
                    f(eng)

            @block.sync
            def _(eng):
                for f in self.prog["sp"]:
                    f(eng)


def host_consts(S):
    c = {}
    pos = np.arange(S, dtype=np.float32)
    inv_a = (1.0 / (np.float32(500000.0) ** (np.arange(0, 16, 2, dtype=np.float32) / np.float32(16)))).astype(np.float32)
    ang = (pos[:, None] * inv_a[None, :]).astype(np.float32)
    cos_a, sin_a = np.cos(ang).astype(np.float32), np.sin(ang).astype(np.float32)
    ca = np.zeros((S, 2, 2, 8, 8), np.float32)
    ca[:, 0] = cos_a[:, None, None, :]
    ca[:, 1] = sin_a[:, None, None, :]
    c["cA"] = ca.reshape(S, 256)
    inv_b = (1.0 / (np.float32(10000.0) ** np.linspace(0.0, 1.0, 64, dtype=np.float32))).astype(np.float32)
    angb = (pos[:, None] * inv_b[None, :]).astype(np.float32)
    cos_b, sin_b = np.cos(angb).astype(np.float64), np.sin(angb).astype(np.float64)
    lg = np.log(1.0 - 2.0 ** (-5.0 - np.arange(4, dtype=np.float64)))
    i = (np.arange(S) % 128).astype(np.float64)
    gq = np.exp(lg[None, :] * i[:, None])
    gk = np.exp(-lg[None, :] * i[:, None]) * (128.0 ** -0.5)
    cb = np.zeros((S, 4, 4, 64), np.float64)
    cb[:, 0] = cos_b[:, None, :] * gq[:, :, None]
    cb[:, 1] = sin_b[:, None, :] * gq[:, :, None]
    cb[:, 2] = cos_b[:, None, :] * gk[:, :, None]
    cb[:, 3] = sin_b[:, None, :] * gk[:, :, None]
    c["cB"] = cb.reshape(S, 1024).astype(np.float32)
    gam = np.exp(lg)
    rs = np.zeros((128, 12), np.float32)
    rs[:, 0:4] = (gam ** 128)[None, :]
    rs[:, 4:8] = (gam ** 127)[None, :]
    rs[:, 8:12] = gam[None, :]
    c["cRS"] = rs
    E = np.zeros((16, S), np.float32)
    for j in range(16):
        E[j, j * 256:(j + 1) * 256] = 1.0
    c["cE"] = E
    cm = np.zeros((2, 128, 256), np.float32)
    for kt in range(2):
        k = kt * 128 + np.arange(128)[:, None]
        q = np.arange(256)[None, :]
        cm[kt] = np.where(k <= q, 0.0, NEG)
    c["cCM"] = cm.transpose(1, 0, 2).reshape(128, 512)
    j = np.arange(128)[:, None]
    ii = np.arange(128)[None, :]
    m = (j <= ii).astype(np.float32)
    c["cRM"] = np.tile(m[:, None, :], (1, 4, 1)).reshape(128, 512)
    s = (np.arange(128) % 64)[:, None]
    t = np.arange(64)[None, :]
    strict = (s < t).astype(np.float32)
    incl = (s <= t).astype(np.float32)
    am = np.concatenate([strict, incl], axis=1)
    c["cAM"] = np.tile(am[:, None, :], (1, 4, 1)).reshape(128, 512)
    tt_ = (np.arange(128) % 64)[:, None]
    ss_ = np.arange(64)[None, :]
    xm = (ss_ < tt_).astype(np.float32)
    c["cXM"] = np.tile(xm[:, None, :], (1, 4, 1)).reshape(128, 256)
    rm = np.ones((128, 512), np.float32)
    rm[:, ::64] = 0.0
    c["cRST"] = rm
    c["cID"] = np.eye(128, dtype=np.float32)
    bo = np.zeros((128, 128), np.float32)
    bo[:64, :64] = 1.0
    bo[64:, 64:] = 1.0
    c["cBO"] = bo
    c["cI4"] = np.tile(np.eye(64, dtype=np.float32)[None, None], (2, 4, 1, 1)).transpose(0, 2, 1, 3).reshape(128, 256)
    return c


CONST_SHAPES = lambda S: {"cA": [S, 256], "cB": [S, 1024], "cRS": [128, 12], "cE": [16, S], "cCM": [128, 512],
                          "cRM": [128, 512], "cAM": [128, 512], "cXM": [128, 256], "cRST": [128, 512],
                          "cID": [128, 128], "cBO": [128, 128], "cI4": [128, 256]}

COLS = {"mixg": (0, 8), "ffng": (8, 8), "pleg": (16, 8), "fing": (24, 8), "mu": (32, 14), "w0": (46, 4),
        "a0": (50, 4), "kk": (54, 4), "ka": (58, 4), "rk": (62, 4), "v0": (66, 4), "vmu": (70, 1)}
NCOL = 72


def pack_cols(inp, l):
    out = np.zeros((128, NCOL), np.float32)

    def put(name, vec):
        o, n = COLS[name]
        v = np.asarray(vec, np.float32).reshape(-1)
        out[:, o:o + n] = v.reshape(n, 128).T
    put("mixg", inp["norm_mix_g"][l])
    put("ffng", inp["norm_ffn_g"][l])
    put("pleg", inp["norm_ple_g"][l])
    put("fing", inp["final_norm_g"])
    put("mu", inp["c_mu"][l])
    put("w0", inp["c_w0"][l])
    put("a0", inp["c_a0"][l])
    put("kk", inp["c_k_k"][l])
    put("ka", inp["c_k_a"][l])
    put("rk", inp["c_r_k"][l])
    if l >= 1:
        put("v0", inp["c_v0"][l - 1])
        out[0:32, COLS["vmu"][0]] = np.asarray(inp["c_vres_mu"][l - 1], np.float32)
    return out


def pack_ln(inp, l):
    out = np.zeros((128, 2, 4, 64), np.float32)
    for k, name in enumerate(("c_ln_g", "c_ln_b")):
        v = np.asarray(inp[name][l], np.float32).reshape(4, 2, 64)
        for hh in range(2):
            out[hh * 64:(hh + 1) * 64, k] = v[:, hh, :][None]
    return out.reshape(128, 512)


def build(S, depth=2, en="abc", dbg=()):
    NT = S // 128
    NG = S // 512
    NB = S // 256
    nc = bass.Bass("TRN2", target_bir_lowering=False)

    def din(name, shape, dt=F32):
        return nc.dram_tensor(name, list(shape), dt, kind="ExternalInput").ap()

    def dscr(name, shape, dt=F32):
        return nc.dram_tensor(name, list(shape), dt, kind="Internal").ap()

    x_d = din("x", [S, D])
    p_d = din("p", [depth, S, 256])
    w_in = din("w_in", [depth, D, NIN])
    w2_d = din("c_w2", [depth, 64, 512])
    a2_d = din("c_a2", [depth, 64, 512])
    g2_d = din("c_g2", [depth, 128, 512])
    vd_d = din("c_vres_down", [max(depth - 1, 1), D, 32])
    v2_d = din("c_v2", [max(depth - 1, 1), 32, 512])
    wbr_d = din("w_branch", [depth, 3, 512, D])
    wout_d = din("w_out", [depth, D, D])
    wgu_d = din("w_gate_up", [depth, D, 2 * DFF])
    wd_d = din("w_down", [depth, DFF, D])
    wpg_d = din("w_ple_gate", [depth, D, D])
    wpp_d = din("w_ple_proj", [depth, 256, D])
    cols_d = din("cols", [depth, 128, NCOL])
    ln_d = din("lnp", [depth, 128, 512])
    cst = {k: din(k, shp) for k, shp in CONST_SHAPES(S).items()}
    out_d = nc.dram_tensor("out", [S, D], F32, kind="ExternalOutput").ap()

    xT_d = dscr("xT_d", [128, 8, S])
    hT_d = dscr("hT_d", [128, 8, S], BF16)
    yT_d = dscr("yT_d", [3, 128, 4, S], BF16)
    vf_d = dscr("vf_d", [128, 4, S])
    NINX = NIN + 32
    Wb_in = dscr("Wb_in", [depth, 128, 8, NINX], BF16)
    Wb_br = dscr("Wb_br", [depth, 128, 3, 4, D], BF16)
    Wb_out = dscr("Wb_out", [depth, 128, 8, D], BF16)
    Wb_gu = dscr("Wb_gu", [depth, 128, 8, 2 * DFF], BF16)
    Wb_d = dscr("Wb_d", [depth, 128, 22, D], BF16)
    Wb_pg = dscr("Wb_pg", [depth, 128, 8, D], BF16)
    Wb_pp = dscr("Wb_pp", [depth, 128, 2, D], BF16)
    dbg_d = {}
    for name, shp in dbg:
        dbg_d[name] = nc.dram_tensor("dbg_" + name, list(shp), F32, kind="ExternalOutput").ap()

    K = Ctx(nc)
    uniq = [0]
    with ExitStack() as top:
        def sbt(es, name, shape, dt=F32):
            uniq[0] += 1
            return es.enter_context(nc.sbuf_tensor(f"s_{name}_{uniq[0]}", list(shape), dt))

        banks = [top.enter_context(nc.psum_tensor(f"bank{i}", [128, 512], F32)) for i in range(8)]
        bank_rr = [0]

        def nb_(lo=0, hi=8):
            b = banks[lo + bank_rr[0] % (hi - lo)]
            bank_rr[0] += 1
            return b

        ident = sbt(top, "ident", [128, 128])
        identb = sbt(top, "identb", [128, 128], BF16)
        onesb = sbt(top, "onesb", [128, 128], BF16)
        cols = sbt(top, "cols", [128, depth, NCOL])
        K.dma("sp", ident[:], cst["cID"][:, :], r=[], semkey="cload")
        K.cp("dve", identb[:], ident[:])
        K.memset("dve", onesb[:], 1.0)
        for l in range(depth):
            K.dma("sp", cols[:, l, :], cols_d[l], r=[], w=[("cols", l)], semkey="cload")
        colkeys = [("cols", l) for l in range(depth)]

        def col(l, name, j=0, n=1):
            o, _ = COLS[name]
            return cols[:, l, o + j:o + j + n]

        bg = []
        convn = [0]

        def conv_jobs(l):
            first, rest = [], []
            for c0 in range(0, NIN, 512):
                cw = min(512, NIN - c0)
                job = (l, Wb_in[l, :, :, c0:c0 + cw], w_in[l, :, c0:c0 + cw].rearrange("(k p) n -> p k n", p=128), ("Wb_in", l, c0 // 512))
                (first if c0 < 5632 else rest).append(job)
            if l >= 1:
                first.append((l, Wb_in[l, :, :, NIN:NINX], vd_d[l - 1].rearrange("(k p) n -> p k n", p=128), ("Wb_in", l, "v")))
            for n in range(3):
                for h in range(2):
                    rest.append((l, Wb_br[l, :, n, :, h * 512:(h + 1) * 512],
                                 wbr_d[l, n, :, h * 512:(h + 1) * 512].rearrange("(k p) n -> p k n", p=128), ("Wb_br", l)))
            for h in range(2):
                rest.append((l, Wb_out[l, :, :, h * 512:(h + 1) * 512],
                             wout_d[l, :, h * 512:(h + 1) * 512].rearrange("(k p) n -> p k n", p=128), ("Wb_out", l)))
            for c0 in range(0, 2 * DFF, 512):
                rest.append((l, Wb_gu[l, :, :, c0:c0 + 512], wgu_d[l, :, c0:c0 + 512].rearrange("(k p) n -> p k n", p=128), ("Wb_gu", l)))
            for h in range(2):
                for k0 in (0, 11):
                    rest.append((l, Wb_d[l, :, k0:k0 + 11, h * 512:(h + 1) * 512],
                                 wd_d[l, k0 * 128:(k0 + 11) * 128, h * 512:(h + 1) * 512].rearrange("(k p) n -> p k n", p=128), ("Wb_d", l)))
            for h in range(2):
                rest.append((l, Wb_pg[l, :, :, h * 512:(h + 1) * 512],
                             wpg_d[l, :, h * 512:(h + 1) * 512].rearrange("(k p) n -> p k n", p=128), ("Wb_pg", l)))
                rest.append((l, Wb_pp[l, :, :, h * 512:(h + 1) * 512],
                             wpp_d[l, :, h * 512:(h + 1) * 512].rearrange("(k p) n -> p k n", p=128), ("Wb_pp", l)))
            return [j + (True,) for j in first], [j + (False,) for j in rest]

        def issue_job(job):
            _, o_, i_, key, _f = job
            K.dma("pool", o_, i_, r=[], w=[key], semkey=f"conv{convn[0] % 6}")
            convn[0] += 1

        def feed(n=2):
            for _ in range(n):
                if bg:
                    issue_job(bg.pop(0))

        def flush(l):
            while bg and bg[0][0] <= l:
                issue_job(bg.pop(0))

        def norm_group(es_name, xg, hg_out, l, gname, scratch):
            sq, rstd = scratch[0], scratch[1]
            pb = nb_()
            for c in range(8):
                K.act(sq[:, c, :], xg[:, c, :], AF.Square)
            for c in range(8):
                K.mm(pb[:], onesb[:], sq[:, c, :], start=(c == 0), stop=(c == 7))
            K.act(rstd[:], pb[:], AF.Sqrt, bias=epsc[:, 0:1], scale=1.0 / D)
            K.op("dve", lambda e: e.reciprocal(out=rstd[:], in_=rstd[:]), [rstd], [rstd])
            for c in range(8):
                if len(scratch) > 2 and c % 2 == 1:
                    tp_ = scratch[2][(c // 2) % 2]
                    K.tt("pool", tp_[:], xg[:, c, :], rstd[:], ALU.mult)
                    K.act(hg_out[:, c, :], tp_[:], AF.Identity, scale=col(l, gname, c), r=[tp_] + colkeys)
                else:
                    K.stt(hg_out[:, c, :], xg[:, c, :], col(l, gname, c), rstd[:], ALU.mult, ALU.mult,
                          r=[xg, rstd] + colkeys)

        epsc = sbt(top, "epsc", [128, 4])
        K.memset("dve", epsc[:, 0:1], EPS)
        K.memset("dve", epsc[:, 1:2], 1e-5 * 64)
        K.memset("dve", epsc[:, 2:3], 0.0)

        f0, r0 = conv_jobs(0)
        for j in f0:
            issue_job(j)
        bg.extend(r0)
        for l_ in range(1, depth):
            f_, r_ = conv_jobs(l_)
            bg.extend(f_ + r_)

        with ExitStack() as es:
            xin = [sbt(es, f"xin{i}", [128, D]) for i in range(2)]
            xg2 = [sbt(es, f"xgI{i}", [128, 8, 512]) for i in range(2)]
            hg2 = [sbt(es, f"hgI{i}", [128, 8, 512], BF16) for i in range(2)]
            sq = sbt(es, "sqI", [128, 8, 512], BF16)
            rstd = sbt(es, "rstdI", [128, 512])
            for tg in range(NG):
                xg = xg2[tg % 2]
                hg = hg2[tg % 2]
                for tt_ in range(4):
                    t = tg * 4 + tt_
                    xi = xin[t % 2]
                    K.dma("sp", xi[:], x_d[t * 128:(t + 1) * 128, :], r=[])
                    for half in range(2):
                        pb = nb_()
                        for c in range(4):
                            K.tr(pb[:, c * 128:(c + 1) * 128], xi[:, (half * 4 + c) * 128:(half * 4 + c + 1) * 128], ident[:])
                        K.cp("act" if half else "dve", xg[:, half * 4:(half + 1) * 4, tt_ * 128:(tt_ + 1) * 128],
                             pb[:].rearrange("p (c t) -> p c t", c=4))
                K.dma("sp", xT_d[:, :, tg * 512:(tg + 1) * 512], xg[:], w=[("xT", tg)], semkey="xTst")
                norm_group("I", xg, hg, 0, "mixg", (sq, rstd))
                K.dma("sp", hT_d[:, :, tg * 512:(tg + 1) * 512], hg[:], w=[("hT", tg)], semkey="hTst")

        K.barrier()
        for l in range(depth):
            while bg and (bg[0][0] < l or (bg[0][0] == l and bg[0][4])):
                issue_job(bg.pop(0))
            last = (l == depth - 1)
            if "a" in en:
                phase_a(nc, K, sbt, banks, nb_, l, S, cst, Wb_in, hT_d, yT_d, ident, identb, feed)
                K.barrier()
            if "b" in en:
                phase_b(nc, K, sbt, banks, nb_, l, S, cst, Wb_in, hT_d, yT_d, ident, identb, epsc, feed)
                K.barrier()
            if "c" in en:
                phase_c(nc, K, sbt, banks, nb_, l, S, cst, Wb_in, hT_d, yT_d, vf_d, ident, identb, epsc, cols, col, colkeys,
                        w2_d, a2_d, g2_d, v2_d, ln_d, dbg_d)
                K.barrier()

            flush(l)
            with ExitStack() as es:
                xg2 = [sbt(es, f"xgT{i}", [128, 8, 512]) for i in range(2)]
                hgs = [sbt(es, f"hgT{i}", [128, 8, 512], BF16) for i in range(2)]
                ygs = [sbt(es, f"ygT{i}", [128, 3, 4, 512], BF16) for i in range(2)]
                hf = sbt(es, "hfT", [128, 8, 512], BF16)
                mg = sbt(es, "mgT", [128, 8, 512], BF16)
                actT = sbt(es, "actT", [128, 22, 512], BF16)
                sq = mg
                rstd = sbt(es, "rstdT", [128, 512])
                sg = [sbt(es, f"sgT{i}", [128, 512]) for i in range(2)]
                acc = sbt(es, "accT", [128, 512])
                tmpf = [sbt(es, f"tmpT{i}", [128, 512]) for i in range(2)]
                wst = [sbt(es, f"wst{i}", [128, 8, 1024], BF16) for i in range(2)]
                wdt = [sbt(es, f"wdt{i}", [128, 22, 128], BF16) for i in range(2)]
                wbrt = sbt(es, "wbrt", [128, 3, 4, 128], BF16)
                wppt = sbt(es, "wppt", [128, 2, D], BF16)
                pin = [sbt(es, f"pin{i}", [128, 256]) for i in range(2)]
                pT = sbt(es, "pTT", [128, 2, 512], BF16)
                ot = [sbt(es, f"otT{i}", [128, D]) for i in range(2)]
                wsi = [0]

                def wslab():
                    t_ = wst[wsi[0] % 2]
                    wsi[0] += 1
                    return t_
                K.dma("sp", wppt[:], Wb_pp[l], r=[("Wb_pp", l)], semkey="cload")
                def load_group(g):
                    tk = slice(g * 512, (g + 1) * 512)
                    K.dma("sp", xg2[g % 2][:], xT_d[:, :, tk], r=[("xT", g)])
                    K.dma("sp", hgs[g % 2][:], hT_d[:, :, tk], r=[("hT", g)])
                    for n in range(3):
                        if "abc"[n] in en:
                            K.dma("sp", ygs[g % 2][:, n], yT_d[n, :, :, tk], r=[("yT", n, g)], w=[("ygT", g % 2, n)], semkey=f"ygT{g % 2}")
                        elif g < 2:
                            K.memset("pool", ygs[g % 2][:, n], 0.0, w=[("ygT", g % 2, n)])

                load_group(0)
                for tg in range(NG):
                    xg = xg2[tg % 2]
                    hg = hgs[tg % 2]
                    yg = ygs[tg % 2]
                    tok = slice(tg * 512, (tg + 1) * 512)
                    ygk = [("ygT", n) for n in range(3)]
                    for dc in range(8):
                        ws = wslab()
                        for n in range(3):
                            c0 = 5376 + n * 1024 + dc * 128
                            K.dma("sp", ws[:, :, n * 128:(n + 1) * 128], Wb_in[l, :, :, c0:c0 + 128],
                                  r=[("Wb_in", l, c0 // 512)], w=[ws])
                        K.dma("sp", wbrt[:], Wb_br[l, :, :, :, dc * 128:(dc + 1) * 128], r=[("Wb_br", l)])
                        for n in range(3):
                            pbr = nb_()
                            pgt = nb_()
                            for kc in range(4):
                                K.mm(pbr[:], wbrt[:, n, kc, :], yg[:, n, kc, :], start=(kc == 0), stop=(kc == 3),
                                     r=[wbrt, ("ygT", tg % 2, n)])
                            for kc in range(8):
                                K.mm(pgt[:], ws[:, kc, n * 128:(n + 1) * 128], hg[:, kc, :], start=(kc == 0), stop=(kc == 7))
                            s_ = sg[n % 2]
                            K.act(s_[:], pgt[:], AF.Sigmoid)
                            if n == 0:
                                K.tt("dve", acc[:], s_[:], pbr[:], ALU.mult)
                            elif n == 1:
                                K.tt("dve", tmpf[0][:], s_[:], pbr[:], ALU.mult)
                                K.tt("pool", acc[:], acc[:], tmpf[0][:], ALU.add)
                            else:
                                K.tt("dve", tmpf[1][:], s_[:], pbr[:], ALU.mult)
                                K.tt("dve", mg[:, dc, :], acc[:], tmpf[1][:], ALU.add)
                    for dc in range(8):
                        if dc % 4 == 0:
                            ws = wslab()
                            K.dma("sp", ws[:, :, 0:512], Wb_out[l, :, :, dc * 128:dc * 128 + 512], r=[("Wb_out", l)], w=[ws])
                        po = nb_()
                        for kc in range(8):
                            K.mm(po[:], ws[:, kc, (dc % 4) * 128:(dc % 4 + 1) * 128], mg[:, kc, :], start=(kc == 0), stop=(kc == 7))
                        K.tt("dve", xg[:, dc, :], xg[:, dc, :], po[:], ALU.add)
                    if tg + 1 < NG:
                        load_group(tg + 1)
                    norm_group("T", xg, hf, l, "ffng", (sq, rstd, tmpf))
                    for f4 in range(0, 22, 4):
                        nf = min(4, 22 - f4)
                        ws = wslab()
                        K.dma("sp", ws[:, :, 0:nf * 128], Wb_gu[l, :, :, f4 * 128:(f4 + nf) * 128], r=[("Wb_gu", l)], w=[ws])
                        K.dma("sp", ws[:, :, 512:512 + nf * 128], Wb_gu[l, :, :, DFF + f4 * 128:DFF + (f4 + nf) * 128],
                              r=[("Wb_gu", l)], w=[ws])
                        for fi in range(nf):
                            fc = f4 + fi
                            pg_ = nb_()
                            pu_ = nb_()
                            for kc in range(8):
                                K.mm(pg_[:], ws[:, kc, fi * 128:(fi + 1) * 128], hf[:, kc, :], start=(kc == 0), stop=(kc == 7))
                            for kc in range(8):
                                K.mm(pu_[:], ws[:, kc, 512 + fi * 128:512 + (fi + 1) * 128], hf[:, kc, :], start=(kc == 0), stop=(kc == 7))
                            s_ = sg[fc % 2]
                            K.act(s_[:], pg_[:], AF.Silu)
                            K.tt("dve", actT[:, fc, :], s_[:], pu_[:], ALU.mult)
                    for dc in range(8):
                        wd_ = wdt[dc % 2]
                        K.dma("sp", wd_[:], Wb_d[l, :, :, dc * 128:(dc + 1) * 128], r=[("Wb_d", l)])
                        pd = nb_()
                        for fc in range(22):
                            K.mm(pd[:], wd_[:, fc, :], actT[:, fc, :], start=(fc == 0), stop=(fc == 21))
                        K.tt("dve", xg[:, dc, :], xg[:, dc, :], pd[:], ALU.add)
                    norm_group("T", xg, hf, l, "pleg", (sq, rstd, tmpf))
                    for tt_ in range(4):
                        pi = pin[tt_ % 2]
                        K.dma("sp", pi[:], p_d[l, tg * 512 + tt_ * 128: tg * 512 + (tt_ + 1) * 128, :], r=[])
                        pb = nb_()
                        for c in range(2):
                            K.tr(pb[:, c * 128:(c + 1) * 128], pi[:, c * 128:(c + 1) * 128], ident[:])
                        K.cp("act", pT[:, :, tt_ * 128:(tt_ + 1) * 128], pb[:, 0:256].rearrange("p (c t) -> p c t", c=2))
                    for dc in range(8):
                        if dc % 4 == 0:
                            ws = wslab()
                            K.dma("sp", ws[:, :, 0:512], Wb_pg[l, :, :, dc * 128:dc * 128 + 512], r=[("Wb_pg", l)], w=[ws])
                        pg_ = nb_()
                        pp_ = nb_()
                        for kc in range(8):
                            K.mm(pg_[:], ws[:, kc, (dc % 4) * 128:(dc % 4 + 1) * 128], hf[:, kc, :], start=(kc == 0), stop=(kc == 7))
                        for kc in range(2):
                            K.mm(pp_[:], wppt[:, kc, dc * 128:(dc + 1) * 128], pT[:, kc, :], start=(kc == 0), stop=(kc == 1))
                        s_ = sg[dc % 2]
                        K.act(s_[:], pg_[:], AF.Sigmoid)
                        K.tt("dve", tmpf[dc % 2][:], s_[:], pp_[:], ALU.mult)
                        K.tt("pool", xg[:, dc, :], xg[:, dc, :], tmpf[dc % 2][:], ALU.add)
                    if not last:
                        K.dma("pool", xT_d[:, :, tok], xg[:], w=[("xT", tg)], semkey="xTst")
                        norm_group("T", xg, hf, l + 1, "mixg", (sq, rstd, tmpf))
                        K.dma("pool", hT_d[:, :, tok], hf[:], w=[("hT", tg)], semkey="hTst")
                    else:
                        pb = nb_()
                        for c in range(8):
                            K.act(sq[:, c, :], xg[:, c, :], AF.Square)
                        for c in range(8):
                            K.mm(pb[:], onesb[:], sq[:, c, :], start=(c == 0), stop=(c == 7))
                        K.act(rstd[:], pb[:], AF.Sqrt, bias=epsc[:, 0:1], scale=1.0 / D)
                        K.op("dve", lambda e: e.reciprocal(out=rstd[:], in_=rstd[:]), [rstd], [rstd])
                        for c in range(8):
                            K.stt(xg[:, c, :], xg[:, c, :], col(l, "fing", c), rstd[:], ALU.mult, ALU.mult,
                                  r=[xg, rstd] + colkeys)
                        for tt_ in range(4):
                            o_ = ot[tt_ % 2]
                            for half in range(2):
                                pb2 = nb_()
                                for c in range(4):
                                    K.tr(pb2[:, c * 128:(c + 1) * 128], xg[:, half * 4 + c, tt_ * 128:(tt_ + 1) * 128], ident[:])
                                K.cp("act" if half else "dve", o_[:, half * 512:(half + 1) * 512], pb2[:])
                            K.dma("sp", out_d[tg * 512 + tt_ * 128: tg * 512 + (tt_ + 1) * 128, :], o_[:], w=["out"],
                                  semkey="out")
            K.barrier()
        K.final_wait("sp", ["out"] + ["dbg_" + n for n in dbg_d])
        K.replay()
    return nc, K


def phase_a(nc, K, sbt, banks, nb_, l, S, cst, Wb_in, hT_d, yT_d, ident, identb, feed):
    NT = S // 128
    NB = S // 256
    with ExitStack() as es:
        wA = sbt(es, "wA", [128, 8, 1536], BF16)
        KT = sbt(es, "KT", [80, 8, S], BF16)
        Va = sbt(es, "Va", [128, NT, 8, 65], BF16)
        QT = [sbt(es, f"QT{i}", [80, 8, 256], BF16) for i in range(2)]
        QTf = [sbt(es, f"QTf{i}", [64, 8, 128]) for i in range(2)]
        kms = sbt(es, "kms", [64, 8, 16])
        ktmp = sbt(es, "ktmp", [64, 8])
        hTt = [sbt(es, f"hTtA{i}", [128, 8, 128], BF16) for i in range(2)]
        qk = [sbt(es, f"qkA{i}", [128, 2, 8, 64]) for i in range(2)]
        cs = [sbt(es, f"csA{i}", [128, 2, 128]) for i in range(2)]
        rt = [sbt(es, f"rtA{i}", [128, 128]) for i in range(4)]
        gs = sbt(es, "gsA", [128, 8, 16])
        m8 = sbt(es, "m8A", [128, 8, 8])
        Mp = sbt(es, "MpA", [128, 8, 80], BF16)
        cm = sbt(es, "cmA", [128, 2, 256], BF16)
        PT = [sbt(es, f"PTA{i}", [128, 256], BF16) for i in range(3)]
        ya = [sbt(es, f"yaA{i}", [128, 512], BF16) for i in range(2)]
        rden = sbt(es, "rdenA", [128, 4])
        yTs = [sbt(es, f"yTsA{i}", [128, 4, 128], BF16) for i in range(2)]
        for g in range(3):
            K.dma("sp", wA[:, :, g * 512:(g + 1) * 512], Wb_in[l, :, :, g * 512:(g + 1) * 512], r=[("Wb_in", l, g)],
                  w=[("wA", g)], semkey="wload")
        wAk = [("wA", g) for g in range(3)]
        K.dma("pool", cm[:], cst["cCM"].rearrange("p (k q) -> p k q", k=2), r=[], semkey="cloadp")
        for h in range(8 if "e" not in PA_SKIP else 0):
            for e0 in range(0, S, 2048):
                e1 = min(S, e0 + 2048)
                K.dma("pool", KT[64:80, h, e0:e1], cst["cE"][:, e0:e1], r=[], w=[("KTE", h)], semkey="cloadp")
        KTE = [("KTE", h) for h in range(8)]
        if "m" not in PA_SKIP:
            K.memset("pool", Va[:, :, :, 64:65], 1.0, w=["Va1"])
            K.memset("pool", Mp[:], 0.0)
        pz = [banks[0], banks[1]]
        pTq = [banks[2], banks[3]]
        pS = [banks[4], banks[5]]
        pO = [banks[6], banks[7]]
        pti = 0
        psi = 0
        for t in range(NT):
            b = t // 2
            half = t % 2
            feed(2)
            hT = hTt[t % 2]
            K.dma("sp", hT[:], hT_d[:, :, t * 128:(t + 1) * 128], r=[("hT", t // 4)])
            c_ = cs[t % 2]
            K.dma("sp", c_[:], cst["cA"][t * 128:(t + 1) * 128, :].rearrange("p (a b) -> p a b", a=2), r=[])
            q_ = qk[t % 2]
            for g in range(3):
                pb = pz[g % 2]
                for kc in range(8):
                    K.mm(pb[:], hT[:, kc, :], wA[:, kc, g * 512:(g + 1) * 512], start=(kc == 0), stop=(kc == 7),
                         r=[hT, ("wA", g)])
                if g < 2:
                    K.cp("act", q_[:, g], pb[:].rearrange("p (h d) -> p h d", h=8))
                elif "v" not in PA_SKIP:
                    K.cp("act", Va[:, t, :, 0:64], pb[:].rearrange("p (h d) -> p h d", h=8), w=[("Va", t)])
            x1 = q_[:, :, :, 0:8]
            x2 = q_[:, :, :, 8:16]
            co = c_[:, 0, :].rearrange("p (a h d) -> p a h d", a=2, h=8)
            si = c_[:, 1, :].rearrange("p (a h d) -> p a h d", a=2, h=8)
            t1, t2, t3, t4 = [r_[:].rearrange("p (a h d) -> p a h d", a=2, h=8) for r_ in rt]
            if "r" not in PA_SKIP:
                K.tt("dve", t1, x1, co, ALU.mult)
                K.tt("pool", t2, x2, si, ALU.mult)
                K.tt("dve", t3, x1, si, ALU.mult)
                K.tt("pool", t4, x2, co, ALU.mult)
                K.tt("dve", x1, t1, t2, ALU.subtract)
                K.tt("dve", x2, t3, t4, ALU.add)
            for g in range(2 if "t" not in PA_SKIP else 0):
                for hq in range(2):
                    pb = pTq[hq]
                    for h4 in range(4):
                        h = hq * 4 + h4
                        K.tr(pb[0:64, h4 * 128:(h4 + 1) * 128], q_[:, g, h, :], ident[:])
                    src = pb[0:64, :].rearrange("p (h t) -> p h t", h=4)
                    if g == 0:
                        K.cp("act", QT[b % 2][0:64, hq * 4:(hq + 1) * 4, half * 128:(half + 1) * 128], src,
                             w=[("QTq", b % 2)])
                        K.cp("dve", QTf[half][:, hq * 4:(hq + 1) * 4, :], src)
                    else:
                        K.cp("act", KT[0:64, hq * 4:(hq + 1) * 4, t * 128:(t + 1) * 128], src, w=[("KT", t)])
                        K.op("dve", lambda e, src=src, hq=hq: e.tensor_reduce(out=ktmp[:, hq * 4:(hq + 1) * 4], in_=src, axis=AX.X, op=ALU.add),
                             [pb], [ktmp])
                if g == 1:
                    if half == 0:
                        K.cp("dve", kms[:, :, b:b + 1], ktmp[:].rearrange("p (h o) -> p h o", o=1))
                    else:
                        K.tt("dve", kms[:, :, b:b + 1], kms[:, :, b:b + 1], ktmp[:].rearrange("p (h o) -> p h o", o=1), ALU.add)
            if half == 0 or PA_STOP <= 1:
                continue
            Qb = QT[b % 2]
            if b >= 1:
                for hf_ in range(2):
                    pg = banks[2]
                    for h in range(8):
                        K.mm(pg[:, h * 16:(h + 1) * 16], QTf[hf_][0:64, h, :], kms[0:64, h, :])
                    K.cp("dve", gs[:], pg[:, 0:128].rearrange("p (h n) -> p h n", h=8))
                    if b < 16:
                        K.memset("dve", gs[:, :, b:16], -1e30)
                    for h in range(8):
                        K.op("dve", lambda e, h=h: e.max(out=m8[:, h, :], in_=gs[:, h, :]), [gs], [m8])
                    for h in range(8):
                        K.ts("dve", Mp[:, h, 64:80], gs[:, h, :], m8[:, h, 2:3], NEG, ALU.is_lt, ALU.mult)
                    pm = banks[3]
                    pmv = pm[:].bitcast(BF16).rearrange("p (h t) -> p h t", h=8)
                    for h in range(8):
                        K.tr(pmv[0:80, h, :], Mp[:, h, :], identb[:])
                    K.cp("act", Qb[64:80, :, hf_ * 128:(hf_ + 1) * 128], pmv[64:80, :, :], w=[("QTm", b % 2, hf_)])
            Qkeys = [("QTq", b % 2), ("QTm", b % 2, 0), ("QTm", b % 2, 1)]
            if PA_STOP <= 2:
                continue
            nkt = 2 * b + 2
            steps = [(h, kt) for h in range(8) for kt in range(nkt)]

            def issue_S(i):
                h, kt = steps[i]
                own = kt >= 2 * b
                Kr = 64 if own else 80
                ps_ = pS[i % 2]
                K.mm(ps_[:, 0:256], KT[0:Kr, h, kt * 128:(kt + 1) * 128], Qb[0:Kr, h, :], start=True, stop=not own,
                     r=[("KT", kt), ("KTE", h)] + Qkeys)
                if own:
                    K.mm(ps_[:, 0:256], identb[:], cm[:, kt - 2 * b, :], start=False, stop=True)

            issue_S(0)
            for i, (h, kt) in enumerate(steps):
                own = kt >= 2 * b
                pOh = pO if h % 2 == 0 else [banks[2], banks[3]]
                if i + 1 < len(steps):
                    issue_S(i + 1)
                P_ = PT[i % 3]
                K.act(P_[:], pS[i % 2][:, 0:256], AF.Exp, scale=0.125)
                for q2 in range(2):
                    if own and kt - 2 * b == 1 and q2 == 0:
                        continue
                    lastk = (2 * b) if q2 == 0 else (2 * b + 1)
                    K.mm(pOh[q2][:, 0:65], P_[:, q2 * 128:(q2 + 1) * 128], Va[:, kt, h, :], start=(kt == 0), stop=(kt == lastk),
                         r=[P_, ("Va", kt), "Va1"])
                if kt == nkt - 1:
                    for q2 in range(2):
                        rd = rden[:, (h % 2) * 2 + q2:(h % 2) * 2 + q2 + 1]
                        K.op("dve", lambda e, rd=rd, src=pOh[q2][:, 64:65]: e.reciprocal(out=rd, in_=src), [pOh[q2]], [("rden", h % 2, q2)])
                        K.ts("dve", ya[q2][:, h * 64:(h + 1) * 64], pOh[q2][:, 0:64], rd, None, ALU.mult,
                             r=[pOh[q2], ("rden", h % 2, q2)], w=[("ya", q2, h)])
            yak = [[("ya", q2, h) for h in range(8)] for q2 in range(2)]
            if PA_STOP <= 3:
                continue
            for q2 in range(2):
                tq = 2 * b + q2
                pb = banks[q2]
                pv = pb[:].bitcast(BF16)[:, 0:512].rearrange("p (c t) -> p c t", c=4)
                for c in range(4):
                    K.tr(pv[:, c, :], ya[q2][:, c * 128:(c + 1) * 128], identb[:], r=yak[q2] + [identb])
                K.cp("act", yTs[q2][:], pv)
                K.dma("sp", yT_d[0, :, :, tq * 128:(tq + 1) * 128], yTs[q2][:], w=[("yT", 0, tq // 4)], semkey="yTs")


def phase_b(nc, K, sbt, banks, nb_, l, S, cst, Wb_in, hT_d, yT_d, ident, identb, epsc, feed):
    NT = S // 128
    with ExitStack() as es:
        wB = sbt(es, "wB", [128, 8, 2048], BF16)
        hTt = [sbt(es, f"hTtB{i}", [128, 8, 128], BF16) for i in range(2)]
        cb = [sbt(es, f"cbB{i}", [128, 4, 256]) for i in range(2)]
        qs = [sbt(es, f"qsB{i}", [128, 512]) for i in range(2)]
        rt = [sbt(es, f"rtB{i}", [128, 256]) for i in range(4)]
        qr = [sbt(es, f"qrB{i}", [128, 512], BF16) for i in range(2)]
        QTb = sbt(es, "QTbB", [128, 4, 128], BF16)
        KTb = sbt(es, "KTbB", [128, 4, 128], BF16)
        vb = sbt(es, "vbB", [128, 512], BF16)
        sgb = sbt(es, "sgB", [128, 512])
        Sm = sbt(es, "SmB", [128, 4, 128], BF16)
        rm = sbt(es, "rmB", [128, 4, 128], BF16)
        R = sbt(es, "RB", [128, 4, 128])
        Rg = sbt(es, "RgB", [128, 4, 128], BF16)
        rs = sbt(es, "rsB", [128, 12])
        st = sbt(es, "stB", [128, 4, 6])
        mv = sbt(es, "mvB", [128, 4, 2])
        rstd = sbt(es, "rstdB", [128, 4])
        yn = sbt(es, "ynB", [128, 512])
        yb = sbt(es, "ybB", [128, 512], BF16)
        yTs = [sbt(es, f"yTsB{i}", [128, 4, 128], BF16) for i in range(2)]
        for g in range(4):
            K.dma("sp", wB[:, :, g * 512:(g + 1) * 512], Wb_in[l, :, :, 1536 + g * 512:1536 + (g + 1) * 512],
                  r=[("Wb_in", l, 3 + g)], w=[("wB", g)], semkey="wload")
        K.dma("pool", rm[:], cst["cRM"].rearrange("p (h t) -> p h t", h=4), r=[], semkey="cloadp")
        K.dma("sp", rs[:], cst["cRS"][:, :], r=[], semkey="cload")
        K.memset("dve", R[:], 0.0, w=[("RB", h) for h in range(4)])
        K.memset("pool", Rg[:], 0.0)
        for t in range(NT):
            feed(2)
            hT = hTt[t % 2]
            K.dma("sp", hT[:], hT_d[:, :, t * 128:(t + 1) * 128], r=[("hT", t // 4)])
            c_ = cb[t % 2]
            K.dma("sp", c_[:], cst["cB"][t * 128:(t + 1) * 128, :].rearrange("p (a b) -> p a b", a=4), r=[])
            for g in range(4):
                pb = nb_(0, 4)
                for kc in range(8):
                    K.mm(pb[:], hT[:, kc, :], wB[:, kc, g * 512:(g + 1) * 512], start=(kc == 0), stop=(kc == 7),
                         r=[hT, ("wB", g)])
                if g < 2:
                    q_ = qs[g]
                    K.cp("act", q_[:], pb[:])
                    xv = q_[:].rearrange("p (h d two) -> p h d two", h=4, two=2)
                    xe = xv[:, :, :, 0]
                    xo = xv[:, :, :, 1]
                    co = c_[:, 2 * g, :].rearrange("p (h d) -> p h d", h=4)
                    si = c_[:, 2 * g + 1, :].rearrange("p (h d) -> p h d", h=4)
                    t1, t2, t3, t4 = [r_[:].rearrange("p (h d) -> p h d", h=4) for r_ in rt]
                    ov = qr[g][:].rearrange("p (h d two) -> p h d two", h=4, two=2)
                    K.tt("dve", t1, xe, co, ALU.mult)
                    K.tt("pool", t2, xo, si, ALU.mult)
                    K.tt("dve", t3, xe, si, ALU.mult)
                    K.tt("pool", t4, xo, co, ALU.mult)
                    K.tt("dve", ov[:, :, :, 0], t1, t2, ALU.subtract, w=[(f"qrB{g}", 0)], r=[rt[0], rt[1]])
                    K.tt("dve", ov[:, :, :, 1], t3, t4, ALU.add, w=[(f"qrB{g}", 1)], r=[rt[2], rt[3]])
                    pbt = nb_(0, 4)
                    pv = pbt[:].bitcast(BF16)[:, 0:512].rearrange("p (h t) -> p h t", h=4)
                    for h in range(4):
                        K.tr(pv[:, h, :], qr[g][:, h * 128:(h + 1) * 128], identb[:], r=[(f"qrB{g}", 0), (f"qrB{g}", 1), identb])
                    K.cp("act", (QTb if g == 0 else KTb)[:], pv)
                elif g == 2:
                    K.cp("act", vb[:], pb[:])
                else:
                    K.act(sgb[:], pb[:], AF.Silu)
            qrk = [("qrB1", 0), ("qrB1", 1)]
            pS_ = banks[4]
            for h in range(4):
                K.mm(pS_[:, h * 128:(h + 1) * 128], KTb[:, h, :], QTb[:, h, :])
            K.tt("dve", Sm[:], pS_[:].rearrange("p (h t) -> p h t", h=4), rm[:], ALU.mult)
            py = banks[5]
            pkv = banks[6]
            for h in range(4):
                K.mm(py[:, h * 128:(h + 1) * 128], Sm[:, h, :], vb[:, h * 128:(h + 1) * 128], start=True, stop=False)
                K.mm(py[:, h * 128:(h + 1) * 128], QTb[:, h, :], Rg[:, h, :], start=False, stop=True)
            for h in range(4):
                K.mm(pkv[:, h * 128:(h + 1) * 128], qr[1][:, h * 128:(h + 1) * 128], vb[:, h * 128:(h + 1) * 128],
                     r=qrk + [vb])
            for h in range(4):
                K.op("dve", lambda e, h=h: e.bn_stats(out=st[:, h, :], in_=py[:, h * 128:(h + 1) * 128]), [py], [("stB", h)])
                K.op("dve", lambda e, h=h: e.bn_aggr(out=mv[:, h, :], in_=st[:, h, :]), [("stB", h)], [("mvB", h)])
            mvk = [("mvB", h) for h in range(4)]
            K.act(rstd[:], mv[:, :, 1], AF.Sqrt, bias=epsc[:, 0:1], scale=1.0, r=mvk + [epsc])
            K.op("dve", lambda e: e.reciprocal(out=rstd[:], in_=rstd[:]), [rstd], [rstd])
            for h in range(4):
                K.ts("dve", yn[:, h * 128:(h + 1) * 128], py[:, h * 128:(h + 1) * 128], mv[:, h, 0:1], rstd[:, h:h + 1],
                     ALU.subtract, ALU.mult, r=[py, rstd] + mvk, w=[("ynB", h)])
            K.tt("pool", yb[:], yn[:], sgb[:], ALU.mult, r=[("ynB", h) for h in range(4)] + [sgb])
            for h in range(4):
                K.ts("pool", R[:, h, :], R[:, h, :], rs[:, h:h + 1], None, ALU.mult, r=[("RB", h), rs], w=[("RB", h)])
                K.stt(R[:, h, :], pkv[:, h * 128:(h + 1) * 128], rs[:, 4 + h:5 + h], R[:, h, :], ALU.mult, ALU.add,
                      r=[pkv, rs, ("RB", h)], w=[("RB", h)])
                K.act(Rg[:, h, :], R[:, h, :], AF.Identity, scale=float(cst_gamma(h)), r=[("RB", h)], w=[Rg])
            pbt = banks[7]
            pv = pbt[:].bitcast(BF16)[:, 0:512].rearrange("p (c t) -> p c t", c=4)
            for c in range(4):
                K.tr(pv[:, c, :], yb[:, c * 128:(c + 1) * 128], identb[:])
            K.cp("act", yTs[t % 2][:], pv)
            K.dma("sp", yT_d[1, :, :, t * 128:(t + 1) * 128], yTs[t % 2][:], w=[("yT", 1, t // 4)], semkey="yTs")


def cst_gamma(h):
    return 1.0 - 2.0 ** (-5.0 - h)


def phase_c(nc, K, sbt, banks, nb_, l, S, cst, Wb_in, hT_d, yT_d, vf_d, ident, identb, epsc, cols, col, colkeys,
            w2_d, a2_d, g2_d, v2_d, ln_d, dbg_d):
    NG = S // 512
    RW = BF16
    ncc = 15 if l >= 1 else 14
    with ExitStack() as es:
        wC = sbt(es, "wC", [128, 8, 1920], BF16)
        w2b = sbt(es, "w2b", [64, 512], BF16)
        a2b = sbt(es, "a2b", [128, 512], BF16)
        g2b = sbt(es, "g2b", [128, 512], BF16)
        v2b = sbt(es, "v2b", [32, 512], BF16)
        lnp = sbt(es, "lnp", [128, 2, 256])
        am = sbt(es, "amC", [128, 2, 2, 128])
        xm = sbt(es, "xmC", [128, 4, 64])
        rst = sbt(es, "rstC", [128, 512])
        bo = sbt(es, "boC", [128, 128], BF16)
        hg = sbt(es, "hgC", [128, 8, 512], BF16)
        zxc = [sbt(es, f"zxC{i}", [128, 513]) for i in range(2)]
        lastc = sbt(es, "lastcC", [128, 16])
        zl = sbt(es, "zlC", [128, 15, 512])
        tmp = [sbt(es, f"tmpC{i}", [128, 512]) for i in range(3)]
        tw = sbt(es, "twC", [64, 512], BF16)
        al = sbt(es, "alC", [128, 512], BF16)
        sgl = sbt(es, "sglC", [128, 512], BF16)
        vlr = sbt(es, "vlrC", [32, 512], BF16)
        sigw = sbt(es, "sigwC", [128, 512])
        iclr = sbt(es, "iclrC", [128, 512])
        cl = sbt(es, "clC", [128, 512])
        Pinc = sbt(es, "PincC", [128, 512])
        Pexc = sbt(es, "PexcC", [128, 512])
        Pinv = sbt(es, "PinvC", [128, 512])
        kkr = sbt(es, "kkrC", [128, 512])
        sqb = sbt(es, "sqbC", [128, 512], BF16)
        kmod = sbt(es, "kmodC", [128, 512])
        vfp = sbt(es, "vfpC", [128, 4, 512])
        vbf = sbt(es, "vbfC", [128, 4, 512], BF16)
        AR = sbt(es, "ARC", [128, 4, 8, 2, 64], RW)
        BK = sbt(es, "BKC", [128, 4, 8, 2, 64], RW)
        prk = sbt(es, "prkC", [128, 4, 512], BF16)
        pend = sbt(es, "pendC", [128, 4, 8])
        H = sbt(es, "HC", [128, 4, 64])
        Hb = sbt(es, "HbC", [128, 4, 64], RW)
        Amat = [sbt(es, f"AmatC{i}", [128, 4, 2, 128], RW) for i in range(3)]
        Xm = [[sbt(es, f"XmC{i}_{j}", [128, 2, 4, 64], RW) for j in range(2)] for i in range(3)]
        Wt = [sbt(es, f"WtC{i}", [128, 4, 64], RW) for i in range(2)]
        Vc = [sbt(es, f"VcC{i}", [128, 4, 64], RW) for i in range(3)]
        Vcf = [sbt(es, f"VcfC{i}", [128, 4, 64]) for i in range(3)]
        BKt = [sbt(es, f"BKtC{i}", [128, 4, 2, 64], RW) for i in range(3)]
        Tt = [sbt(es, f"TtC{i}", [128, 4, 64], RW) for i in range(3)]
        gt = [sbt(es, f"gtC{i}", [128, 4, 64]) for i in range(3)]
        bs = [sbt(es, f"bsC{i}", [128, 4]) for i in range(3)]
        i4 = sbt(es, "i4C", [128, 4, 64])
        st = sbt(es, "stC", [128, 4, 6])
        mv = sbt(es, "mvC", [128, 4, 2])
        rstd = sbt(es, "rstdC", [128, 4])
        yn = sbt(es, "ynC", [128, 4, 64])
        yc = sbt(es, "ycC", [128, 4, 64], BF16)
        ycT = [sbt(es, f"ycTC{i}", [128, 4, 512], BF16) for i in range(2)]

        nwc = 1792 + (32 if l >= 1 else 0)
        for g in range(0, 1792, 512):
            gw = min(512, 1792 - g)
            K.dma("sp", wC[:, :, g:g + gw], Wb_in[l, :, :, 3584 + g:3584 + g + gw],
                  r=[("Wb_in", l, (3584 + g) // 512)], w=[("wC", g)], semkey="wload")
        wCk = [("wC", g) for g in range(0, 1792, 512)]
        if l >= 1:
            K.dma("sp", wC[:, :, 1792:1824], Wb_in[l, :, :, NIN:NIN + 32], r=[("Wb_in", l, "v")], w=[("wC", "v")], semkey="wload")
            wCk.append(("wC", "v"))
            K.dma("pool", v2b[:], v2_d[l - 1], r=[], semkey="cloadp")
        K.dma("pool", w2b[:], w2_d[l], r=[], semkey="cloadp")
        K.dma("pool", a2b[64:128, :], a2_d[l], r=[], semkey="cloadp")
        K.dma("pool", g2b[:], g2_d[l], r=[], semkey="cloadp")
        K.dma("sp", lnp[:], ln_d[l].rearrange("p (a b) -> p a b", a=2), r=[], semkey="cload")
        K.dma("sp", am[:], cst["cAM"].rearrange("p (a j t) -> p a j t", a=2, j=2), r=[], semkey="cload")
        K.dma("sp", xm[:], cst["cXM"].rearrange("p (a t) -> p a t", a=4), r=[], semkey="cload")
        K.dma("sp", rst[:], cst["cRST"][:, :], r=[], semkey="cload")
        K.dma("pool", bo[:], cst["cBO"][:, :], r=[], semkey="cloadp")
        K.memset("dve", lastc[:], 0.0, w=[("lastc", cc) for cc in range(16)])
        K.dma("sp", i4[:], cst["cI4"].rearrange("p (a t) -> p a t", a=4), r=[], semkey="cload")
        K.memset("dve", H[:], 0.0)
        K.memset("pool", Hb[:], 0.0)
        lng = lnp[:, 0, :].rearrange("p (a v) -> p a v", a=4)
        lnb = lnp[:, 1, :].rearrange("p (a v) -> p a v", a=4)

        for tg in range(NG):
            tok = slice(tg * 512, (tg + 1) * 512)
            K.dma("sp", hg[:], hT_d[:, :, tok], r=[("hT", tg)])
            for cc in range(ncc):
                pb = nb_(0, 4)
                np_ = 128 if cc < 14 else 32
                zc_ = zxc[cc % 2]
                if cc < 14:
                    for kc in range(8):
                        K.mm(pb[:], wC[:, kc, cc * 128:(cc + 1) * 128], hg[:, kc, :], start=(kc == 0), stop=(kc == 7),
                             r=wCk + [hg])
                else:
                    for kc in range(8):
                        K.mm(pb[0:32, :], wC[:, kc, 1792:1824], hg[:, kc, :], start=(kc == 0), stop=(kc == 7), r=wCk + [hg])
                K.cp("act", zc_[0:np_, 1:513], pb[0:np_, :])
                K.cp("pool", zc_[0:np_, 0:1], lastc[0:np_, cc:cc + 1], r=[("lastc", cc)], w=[zc_])
                mu = col(l, "mu", cc) if cc < 14 else col(l, "vmu", 0)
                tp = tmp[cc % 2]
                K.tt("pool" if cc % 2 else "dve", tp[0:np_, :], zc_[0:np_, 0:512], zc_[0:np_, 1:513], ALU.subtract)
                K.stt(zl[0:np_, cc, :], tp[0:np_, :], mu[0:np_, :], zc_[0:np_, 1:513], ALU.mult, ALU.add,
                      r=[tp, zc_] + colkeys, w=[("zl", cc)])
                K.cp("pool", lastc[0:np_, cc:cc + 1], zc_[0:np_, 512:513], w=[("lastc", cc)])
            zlk = [("zl", cc) for cc in range(ncc)]
            K.act(tw[:], zl[0:64, 12, :], AF.Tanh, r=[("zl", 12)])
            K.cp("dve", al[64:128, :], zl[64:128, 12, :], r=[("zl", 12)])
            K.act(sgl[:], zl[:, 13, :], AF.Sigmoid, r=[("zl", 13)])
            if l >= 1:
                K.cp("dve", vlr[:], zl[0:32, 14, :], r=[("zl", 14)])
                K.dma("sp", vfp[:], vf_d[:, :, tok], r=[("vf", tg)])
            for pc in range(4):
                rT = zl[:, pc, :]
                kT = zl[:, 4 + pc, :]
                vT = zl[:, 8 + pc, :]
                rk_ = [("zl", pc)]
                kk_ = [("zl", 4 + pc)]
                vk_ = [("zl", 8 + pc)]
                pw = nb_(0, 4)
                K.mm(pw[:], w2b[0:64, pc * 128:(pc + 1) * 128], tw[0:64, :])
                K.act(sigw[:], pw[:], AF.Sigmoid, bias=col(l, "w0", pc), r=[pw] + colkeys)
                pa = nb_(0, 4)
                K.mm(pa[:], a2b[64:128, pc * 128:(pc + 1) * 128], al[64:128, :])
                K.act(iclr[:], pa[:], AF.Sigmoid, bias=col(l, "a0", pc), r=[pa] + colkeys)
                K.op("dve", lambda e: e.tensor_tensor_scan(out=cl[:], data0=rst[:], data1=sigw[:], initial=0.0, op0=ALU.mult, op1=ALU.add),
                     [rst, sigw], [cl])
                K.act(Pinc[:], cl[:], AF.Exp, scale=-C0)
                K.act(Pinv[:], cl[:], AF.Exp, scale=C0)
                K.tt("pool", tmp[2][:], cl[:], sigw[:], ALU.subtract)
                K.act(Pexc[:], tmp[2][:], AF.Exp, scale=-C0)
                K.cp("dve", pend[:, pc, :], Pinc[:].rearrange("p (c t) -> p c t", c=8)[:, :, 63])
                K.ts("dve", kkr[:], kT, col(l, "kk", pc), None, ALU.mult, r=kk_ + colkeys)
                K.act(sqb[:], kkr[:], AF.Square)
                pss = nb_(0, 4)
                K.mm(pss[:], bo[:], sqb[:])
                K.act(tmp[0][:], pss[:], AF.Sqrt)
                K.ts("dve", tmp[0][:], tmp[0][:], 1e-12, None, ALU.max)
                K.op("dve", lambda e: e.reciprocal(out=tmp[0][:], in_=tmp[0][:]), [tmp[0]], [tmp[0]])
                K.tt("dve", kkr[:], kkr[:], tmp[0][:], ALU.mult)
                ARv = AR[:, pc].rearrange("p c j t -> p j c t")
                BKv = BK[:, pc].rearrange("p c j t -> p j c t")
                c3 = lambda ap: ap.rearrange("p (c t) -> p c t", c=8)
                K.stt(ARv[:, 0], c3(kkr[:]), -1.0, c3(Pexc[:]), ALU.mult, ALU.mult, r=[kkr, Pexc], w=[("AR", pc, 0)])
                K.tt("pool", tmp[1][:], kkr[:], iclr[:], ALU.mult)
                K.tt("dve", BKv[:, 0], c3(tmp[1][:]), c3(Pinv[:]), ALU.mult, r=[tmp[1], Pinv], w=[("BK", pc, 0)])
                K.ts("dve", tmp[2][:], iclr[:], 1.0, col(l, "ka", pc), ALU.subtract, ALU.mult, r=[iclr] + colkeys)
                K.stt(kmod[:], tmp[2][:], 1.0, kT, ALU.add, ALU.mult, r=[tmp[2]] + kk_)
                K.tt("dve", BKv[:, 1], c3(kmod[:]), c3(Pinv[:]), ALU.mult, r=[kmod, Pinv], w=[("BK", pc, 1)])
                K.tt("pool", ARv[:, 1], c3(rT), c3(Pinc[:]), ALU.mult, r=rk_ + [Pinc], w=[("AR", pc, 1)])
                K.stt(prk[:, pc, :], rT, col(l, "rk", pc), kmod[:], ALU.mult, ALU.mult, r=rk_ + [kmod] + colkeys, w=[("prk", pc)])
                if l == 0:
                    pass
                else:
                    pv_ = nb_(0, 4)
                    K.mm(pv_[:], v2b[0:32, pc * 128:(pc + 1) * 128], vlr[0:32, :])
                    K.act(tmp[0][:], pv_[:], AF.Sigmoid, bias=col(l, "v0", pc), r=[pv_] + colkeys)
                    K.tt("dve", tmp[1][:], vfp[:, pc, :], vT, ALU.subtract, r=[vfp] + vk_)
                    K.tt("dve", tmp[1][:], tmp[1][:], tmp[0][:], ALU.mult)
                    K.tt("dve", vT, vT, tmp[1][:], ALU.add, r=vk_ + [tmp[1]], w=vk_)
            K.cp("pool", vbf[:], zl[:, 8:12, :], r=[("zl", 8 + i) for i in range(4)])
            if l == 0:
                K.dma("sp", vf_d[:, :, tok], zl[:, 8:12, :], r=[("zl", 8 + i) for i in range(4)], w=[("vf", tg)], semkey="vfst")
            ARk = [("AR", pc, j) for pc in range(4) for j in range(2)]
            BKk = [("BK", pc, j) for pc in range(4) for j in range(2)]
            prkk = [("prk", pc) for pc in range(4)]
            vks = [("zl", 8 + i) for i in range(4)]
            yT_ = ycT[tg % 2]
            def hs(hh):
                return slice(hh * 64, (hh + 1) * 64)

            def prep(c, bA, bB):
                sl = c % 3
                ct = slice(c * 64, (c + 1) * 64)
                Vc_, Vcf_, BKt_, Amat_, Tt_, gt_, bs_ = Vc[sl], Vcf[sl], BKt[sl], Amat[sl], Tt[sl], gt[sl], bs[sl]
                X0_, X1_ = Xm[sl]
                pVv = bA[:].bitcast(BF16)[:, 0:256].rearrange("p (a v) -> p a v", a=4)
                for hh in range(2):
                    for pc in range(4):
                        K.tr(pVv[hs(hh), pc, :], vbf[hs(hh), pc, ct], identb[hs(hh), hs(hh)])
                K.cp("act", Vc_[:], pVv)
                K.cp("dve", Vcf_[:], pVv)
                pBv = bB[:].bitcast(BF16)[:, 0:512].rearrange("p (a j k) -> p a j k", a=4, j=2)
                for hh in range(2):
                    for pc in range(4):
                        for j in range(2):
                            K.tr(pBv[hs(hh), pc, j, :], BK[hs(hh), pc, c, j, :], identb[hs(hh), hs(hh)], r=BKk + [identb])
                K.cp("act", BKt_[:], pBv)
                yield
                for pb2, pA in enumerate((bA, bB)):
                    pAv = pA[:].rearrange("p (a j t) -> p a j t", a=2, j=2)
                    for hh in range(2):
                        for a_ in range(2):
                            pc = pb2 * 2 + a_
                            rhs = AR[hs(hh), pc, c].rearrange("p j t -> p (j t)")
                            for j in range(2):
                                K.mm(pAv[hs(hh), a_, j, :], BK[hs(hh), pc, c, j, :], rhs, r=ARk + BKk)
                    K.tt("dve", Amat_[:, pb2 * 2:pb2 * 2 + 2], pAv, am[:], ALU.mult, w=[("Amat", sl, pb2)])
                Ak = [("Amat", sl, 0), ("Amat", sl, 1)]
                yield
                for hh in range(2):
                    for pc in range(4):
                        K.mm(bA[hs(hh), pc * 64:(pc + 1) * 64], AR[hs(hh), pc, c, 0, :], BK[hs(hh), pc, c, 0, :], r=ARk + BKk)
                K.tt("dve", X0_[:, 0], bA[:, 0:256].rearrange("p (a s) -> p a s", a=4), xm[:], ALU.mult, w=[("X", sl, 0)])
                K.cp("act", X0_[:, 1], Amat_[:, :, 0, 0:64], r=Ak, w=[("Y", sl, 0)])
                K.tt("pool", Tt_[:], Amat_[:, :, 0, 0:64], i4[:], ALU.add, r=Ak + [i4])
                for hh in range(2):
                    for pc in range(4):
                        K.mm(bB[hs(hh), pc * 64:(pc + 1) * 64], sgl[:, ct], g2b[:, (2 * pc + hh) * 64:(2 * pc + hh + 1) * 64])
                for hh in range(2):
                    for pc in range(4):
                        K.mm(bB[hs(hh), 256 + pc:256 + pc + 1], prk[hs(hh), pc, ct], bo[hs(hh), hh * 64:hh * 64 + 1], r=prkk + [bo])
                K.cp("act", gt_[:], bB[:, 0:256].rearrange("p (a v) -> p a v", a=4))
                K.cp("dve", bs_[:], bB[:, 256:260])
                yield
                Xs = [X0_, X1_]
                for lev in range(1, 6):
                    Xp = Xs[(lev - 1) % 2]
                    Xn = Xs[lev % 2]
                    xk = [("X", sl, (lev - 1) % 2), ("Y", sl, (lev - 1) % 2)]
                    for hh in range(2):
                        for pc in range(4):
                            K.mm(bA[hs(hh), pc * 64:(pc + 1) * 64], Xp[hs(hh), 1, pc, :], Xp[hs(hh), 0, pc, :], r=xk)
                            if lev < 5:
                                K.mm(bA[hs(hh), 256 + pc * 64:256 + (pc + 1) * 64], Xp[hs(hh), 0, pc, :], Xp[hs(hh), 1, pc, :], r=xk)
                    if lev < 5:
                        K.cp("act", Xn[:], bA[:].rearrange("p (j a s) -> p j a s", j=2, a=4),
                             w=[("X", sl, lev % 2), ("Y", sl, lev % 2)])
                    else:
                        K.cp("act", Xn[:, 0], bA[:, 0:256].rearrange("p (a s) -> p a s", a=4), w=[("X", sl, lev % 2)])
                    yield
                    for hh in range(2):
                        for pc in range(4):
                            K.mm(bB[hs(hh), pc * 64:(pc + 1) * 64], Xn[hs(hh), 0, pc, :], Tt_[hs(hh), pc, :], r=[("X", sl, lev % 2), Tt_])
                    K.tt("dve", Tt_[:], Tt_[:], bB[:, 0:256].rearrange("p (a t) -> p a t", a=4), ALU.add)
                yield

            def seq(c, bA, bB):
                sl = c % 3
                ct = slice(c * 64, (c + 1) * 64)
                Vc_, Vcf_, BKt_, Amat_, Tt_, gt_, bs_ = Vc[sl], Vcf[sl], BKt[sl], Amat[sl], Tt[sl], gt[sl], bs[sl]
                Ak = [("Amat", sl, 0), ("Amat", sl, 1)]
                for hh in range(2):
                    for pc in range(4):
                        o_ = bA[hs(hh), pc * 64:(pc + 1) * 64]
                        K.mm(o_, AR[hs(hh), pc, c, 0, :], Hb[hs(hh), pc, :], start=True, stop=False, r=ARk + [Hb])
                        K.mm(o_, Amat_[hs(hh), pc, 1, 0:64], Vc_[hs(hh), pc, :], start=False, stop=True, r=Ak + [Vc_])
                K.cp("dve", Wt[0][:], bA[:, 0:256].rearrange("p (a v) -> p a v", a=4))
                yield
                for hh in range(2):
                    for pc in range(4):
                        K.mm(bB[hs(hh), pc * 64:(pc + 1) * 64], Tt_[hs(hh), pc, :], Wt[0][hs(hh), pc, :])
                U = Wt[1]
                K.cp("act", U[:], bB[:, 0:256].rearrange("p (a v) -> p a v", a=4))
                yield
                pO = bA
                pH = bB
                for hh in range(2):
                    for pc in range(4):
                        o_ = pO[hs(hh), pc * 64:(pc + 1) * 64]
                        K.mm(o_, AR[hs(hh), pc, c, 1, :], Hb[hs(hh), pc, :], start=True, stop=False, r=ARk + [Hb])
                        K.mm(o_, Amat_[hs(hh), pc, 0, 64:128], U[hs(hh), pc, :], start=False, stop=False, r=Ak + [U])
                        K.mm(o_, Amat_[hs(hh), pc, 1, 64:128], Vc_[hs(hh), pc, :], start=False, stop=True, r=Ak + [Vc_])
                for hh in range(2):
                    for pc in range(4):
                        o_ = pH[hs(hh), pc * 64:(pc + 1) * 64]
                        K.mm(o_, BKt_[hs(hh), pc, 0, :], U[hs(hh), pc, :], start=True, stop=False)
                        K.mm(o_, BKt_[hs(hh), pc, 1, :], Vc_[hs(hh), pc, :], start=False, stop=True)
                K.tt("dve", H[:], H[:], pH[:, 0:256].rearrange("p (a v) -> p a v", a=4), ALU.add)
                K.tt("pool", H[:], H[:], pend[:, :, c:c + 1].to_broadcast([128, 4, 64]), ALU.mult)
                K.cp("act", Hb[:], H[:])
                for pc in range(4):
                    K.op("dve", lambda e, pc=pc: e.bn_stats(out=st[:, pc, :], in_=pO[:, pc * 64:(pc + 1) * 64]), [pO], [("stC", pc)])
                    K.op("dve", lambda e, pc=pc: e.bn_aggr(out=mv[:, pc, :], in_=st[:, pc, :]), [("stC", pc)], [("mvC", pc)])
                mvk = [("mvC", pc) for pc in range(4)]
                K.act(rstd[:], mv[:, :, 1], AF.Sqrt, bias=epsc[:, 1:2], scale=1.0, r=mvk + [epsc])
                K.op("dve", lambda e: e.reciprocal(out=rstd[:], in_=rstd[:]), [rstd], [rstd])
                for pc in range(4):
                    K.ts("dve", yn[:, pc, :], pO[:, pc * 64:(pc + 1) * 64], mv[:, pc, 0:1], rstd[:, pc:pc + 1], ALU.subtract, ALU.mult,
                         r=[pO, rstd] + mvk, w=[("ynC", pc)])
                ynk = [("ynC", pc) for pc in range(4)]
                K.tt("dve", yn[:], yn[:], lng, ALU.mult, r=ynk + [lnp], w=[yn])
                K.tt("pool", yn[:], yn[:], lnb, ALU.add, r=[yn, lnp], w=[yn])
                K.tt("pool", Vcf_[:], Vcf_[:], bs_[:].rearrange("p (a o) -> p a o", o=1).to_broadcast([128, 4, 64]), ALU.mult)
                K.tt("pool", yn[:], yn[:], Vcf_[:], ALU.add)
                K.tt("dve", yc[:], yn[:], gt_[:], ALU.mult)
                yield
                pYv = bB[:].bitcast(BF16)[:, 0:256].rearrange("p (a t) -> p a t", a=4)
                for hh in range(2):
                    for pc in range(4):
                        K.tr(pYv[hs(hh), pc, :], yc[hs(hh), pc, :], identb[hs(hh), hs(hh)])
                K.cp("act", yT_[:, :, ct], pYv)
                yield

            free_sets = [(banks[4], banks[5]), (banks[0], banks[1])]
            pending = list(range(8))
            active = []
            prep_done = set()
            seq_done = -1
            seq_gen = None
            seq_c = 0
            while seq_c < 8:
                while pending and len(active) < 2 and free_sets and pending[0] - 3 <= seq_done:
                    cnew = pending.pop(0)
                    bset = free_sets.pop(0)
                    active.append([cnew, prep(cnew, *bset), bset])
                if seq_gen is None and seq_c in prep_done:
                    seq_gen = seq(seq_c, banks[6], banks[7])
                if seq_gen is not None:
                    try:
                        next(seq_gen)
                    except StopIteration:
                        seq_gen = None
                        seq_done = seq_c
                        seq_c += 1
                for a in list(active):
                    try:
                        next(a[1])
                    except StopIteration:
                        prep_done.add(a[0])
                        free_sets.append(a[2])
                        active.remove(a)
            K.dma("sp", yT_d[2, :, :, tok], yT_[:], w=[("yT", 2, tg)], semkey="yTs")


_CACHE = {}


def make_in_maps(inp, S, depth, n_cores):
    consts = host_consts(S)
    colsarr = np.stack([pack_cols(inp, l) for l in range(depth)])
    lnarr = np.stack([pack_ln(inp, l) for l in range(depth)])
    maps = []
    f = lambda a: np.ascontiguousarray(np.asarray(a, np.float32))
    shared = {
        "w_in": f(inp["w_in"]), "c_w2": f(inp["c_w2"]), "c_a2": f(inp["c_a2"]), "c_g2": f(inp["c_g2"]),
        "c_vres_down": f(inp["c_vres_down"]), "c_v2": f(inp["c_v2"]), "w_branch": f(inp["w_branch"]),
        "w_out": f(inp["w_out"]), "w_gate_up": f(inp["w_gate_up"]), "w_down": f(inp["w_down"]),
        "w_ple_gate": f(inp["w_ple_gate"]), "w_ple_proj": f(inp["w_ple_proj"]), "cols": colsarr, "lnp": lnarr,
    }
    shared.update(consts)
    x = np.asarray(inp["x"], np.float32)
    p = np.asarray(inp["p"], np.float32)
    for b in range(n_cores):
        m = dict(shared)
        m["x"] = np.ascontiguousarray(x[b])
        m["p"] = np.ascontiguousarray(p[:, b])
        maps.append(m)
    return maps


def kernel(**inputs):
    x = np.asarray(inputs["x"])
    B, S, _ = x.shape
    depth = np.asarray(inputs["w_in"]).shape[0]
    key = (S, depth)
    if key not in _CACHE:
        _CACHE[key] = build(S, depth)[0]
    nc = _CACHE[key]
    maps = make_in_maps(inputs, S, depth, B)
    res = run_bass_kernel_spmd(nc, maps, core_ids=list(range(B)))
    return np.stack([np.asarray(r["out"], np.float32) for r in res.results], axis=0)
```

```python
import math
from contextlib import ExitStack
import numpy as np
import concourse.bass as bass
import concourse.mybir as mybir
from concourse.bass_utils import run_bass_kernel_spmd

F32 = mybir.dt.float32
BF16 = mybir.dt.bfloat16
ALU = mybir.AluOpType
AF = mybir.ActivationFunctionType
AX = mybir.AxisListType

D = 1024
NIN = 8448
DFF = 2816
EPS = 1e-6
NEG = -30000.0
C0 = math.exp(-0.5)
SEM_LIMIT = 30000
import os
PA_STOP = int(os.environ.get("PA_STOP", "9"))
PA_SKIP = os.environ.get("PA_SKIP", "")


class Ctx:
    ENGS = ("pe", "dve", "act", "pool", "sp")

    def __init__(self, nc):
        self.nc = nc
        self.prog = {e: [] for e in self.ENGS}
        self.sems = {}
        self.semval = {}
        self.cur = {}
        self.waited = {e: {} for e in self.ENGS}
        self.res = {}
        self.nsem = 0
        self.ninstr = 0
        self.banktag = {}

    def _semkey(self, logical, step):
        sk = self.cur.get(logical)
        if sk is None or self.semval[sk] + step > SEM_LIMIT:
            ep = 0 if sk is None else sk[1] + 1
            sk = (logical, ep)
            self.sems[sk] = self.nc.alloc_semaphore(name=f"s{self.nsem}")
            self.nsem += 1
            self.semval[sk] = 0
            self.cur[logical] = sk
        return sk

    @staticmethod
    def _key(x):
        if isinstance(x, (str, tuple)):
            return x
        if hasattr(x, "tensor"):
            return x.tensor.name
        return x.name

    def _collect(self, reads, writes):
        deps = {}

        def add(d):
            if d is not None:
                deps[d[0]] = max(deps.get(d[0], 0), d[1])
        for r in reads:
            st = self.res.get(r)
            if st:
                add(st["w"])
        for w in writes:
            st = self.res.get(w)
            if st:
                add(st["w"])
                for sk, v in st["r"].items():
                    add((sk, v))
        return deps

    def _emit_waits(self, e, deps):
        for sk, v in deps.items():
            if self.waited[e].get(sk, 0) < v:
                h = self.sems[sk]
                self.prog[e].append(lambda eng, h=h, v=v: eng.wait_ge(h, v))
                self.waited[e][sk] = v

    def _update(self, reads, writes, sk, v):
        for r in reads:
            st = self.res.setdefault(r, {"w": None, "r": {}})
            st["r"][sk] = max(st["r"].get(sk, 0), v)
        for w in writes:
            self.res[w] = {"w": (sk, v), "r": {}}

    def op(self, e, fn, r=(), w=(), petag=None):
        reads = [self._key(x) for x in r]
        writes = [self._key(x) for x in w]
        writes = writes + [k for k in reads if isinstance(k, str) and k.startswith("bank") and k not in writes]
        skip = []
        if e == "pe" and petag is not None:
            for k in writes:
                if isinstance(k, str) and k.startswith("bank"):
                    st = self.res.get(k)
                    if st and st["w"] is not None and st["w"][0][0] == ("eng", "pe") and not st["r"] \
                            and self.banktag.get(k) == petag:
                        skip.append(k)
                    self.banktag[k] = petag
        deps = self._collect(reads, [k for k in writes if k not in skip])
        self._emit_waits(e, deps)
        sk = self._semkey(("eng", e), 1)
        self.semval[sk] += 1
        v = self.semval[sk]
        h = self.sems[sk]
        self.prog[e].append(lambda eng, fn=fn, h=h: fn(eng).then_inc(h, 1))
        self._update(reads, writes, sk, v)
        self.ninstr += 1

    def dma(self, q, out, in_, r=None, w=None, semkey=None, **kw):
        reads = [self._key(x) for x in (r if r is not None else [in_])]
        writes = [self._key(x) for x in (w if w is not None else [out])]
        if semkey is None:
            semkey = out.tensor.name
        lk = ("dma", semkey)
        sk = self._semkey(lk, 16)
        deps = self._collect(reads, writes)
        if self.semval[sk] > 0:
            deps[sk] = max(deps.get(sk, 0), self.semval[sk])
        self._emit_waits(q, deps)
        self.semval[sk] += 16
        v = self.semval[sk]
        h = self.sems[sk]
        self.prog[q].append(
            lambda eng, out=out, in_=in_, kw=kw, h=h: eng.dma_start(out=out, in_=in_, **kw).then_inc(h, 16))
        self._update(reads, writes, sk, v)
        self.ninstr += 1

    def barrier(self):
        deps = {sk: v for sk, v in self.semval.items()
                if v > 0 and not (sk[0][0] == "dma" and str(sk[0][1]).startswith("conv"))}
        for e in self.ENGS:
            self._emit_waits(e, deps)

    def final_wait(self, e, keys):
        deps = self._collect([self._key(k) for k in keys], ())
        self._emit_waits(e, deps)

    def mm(self, out, lhsT, rhs, start=True, stop=True, r=None, w=None):
        tag = (lhsT.start_partition(), lhsT.partition_size())
        self.op("pe", lambda e: e.matmul(out, lhsT=lhsT, rhs=rhs, start=start, stop=stop),
                r if r is not None else [lhsT, rhs], w if w is not None else [out], petag=tag)

    def tr(self, out, in_, ident, r=None, w=None):
        tag = (in_.start_partition(), in_.partition_size())
        self.op("pe", lambda e: e.transpose(out=out, in_=in_, identity=ident),
                r if r is not None else [in_, ident], w if w is not None else [out], petag=tag)

    def act(self, out, in_, func, bias=None, scale=None, r=None, w=None, eng="act"):
        kw = {}
        if bias is not None:
            kw["bias"] = bias
        if scale is not None:
            kw["scale"] = scale
        rr = [in_] + [x for x in (bias, scale) if not isinstance(x, (int, float, type(None)))]
        self.op("act", lambda e: e.activation(out=out, in_=in_, func=func, **kw),
                r if r is not None else rr, w if w is not None else [out])

    def tt(self, eng, out, in0, in1, op, r=None, w=None):
        self.op(eng, lambda e: e.tensor_tensor(out=out, in0=in0, in1=in1, op=op),
                r if r is not None else [in0, in1], w if w is not None else [out])

    def ts(self, eng, out, in0, s1, s2, op0, op1=None, r=None, w=None):
        rr = [in0] + [x for x in (s1, s2) if not isinstance(x, (int, float, type(None)))]
        if op1 is None:
            fn = lambda e: e.tensor_scalar(out=out, in0=in0, scalar1=s1, scalar2=None, op0=op0)
        else:
            fn = lambda e: e.tensor_scalar(out=out, in0=in0, scalar1=s1, scalar2=s2, op0=op0, op1=op1)
        self.op(eng, fn, r if r is not None else rr, w if w is not None else [out])

    def stt(self, out, in0, scalar, in1, op0, op1, r=None, w=None):
        rr = [in0, in1] + ([scalar] if not isinstance(scalar, (int, float)) else [])
        self.op("dve", lambda e: e.scalar_tensor_tensor(out=out, in0=in0, scalar=scalar, in1=in1, op0=op0, op1=op1),
                r if r is not None else rr, w if w is not None else [out])

    def cp(self, eng, out, in_, r=None, w=None):
        if eng == "act":
            fn = lambda e: e.copy(out=out, in_=in_)
        else:
            fn = lambda e: e.tensor_copy(out=out, in_=in_)
        self.op(eng, fn, r if r is not None else [in_], w if w is not None else [out])

    def memset(self, eng, ap, val, w=None):
        self.op(eng, lambda e: e.memset(ap, val), [], w if w is not None else [ap])

    def replay(self):
        nc = self.nc
        with nc.Block() as block:
            @block.tensor
            def _(eng):
                for f in self.prog["pe"]:
                    f(eng)

            @block.vector
            def _(eng):
                for f in self.prog["dve"]:
                    f(eng)

            @block.scalar
            def _(eng):
                for f in self.prog["act"]:
                    f(eng)

            @block.gpsimd
            def _(eng):
                for f in self.prog["pool"]:
                    f(eng)

            @block.sync
            def _(eng):
                for f in self.prog["sp"]:
                    f(eng)


def host_consts(S):
    c = {}
    pos = np.arange(S, dtype=np.float32)
    inv_a = (1.0 / (np.float32(500000.0) ** (np.arange(0, 16, 2, dtype=np.float32) / np.float32(16)))).astype(np.float32)
    ang = (pos[:, None] * inv_a[None, :]).astype(np.float32)
    cos_a, sin_a = np.cos(ang).astype(np.float32), np.sin(ang).astype(np.float32)
    ca = np.zeros((S, 2, 2, 8, 8), np.float32)
    ca[:, 0] = cos_a[:, None, None, :]
    ca[:, 1] = sin_a[:, None, None, :]
    c["cA"] = ca.reshape(S, 256)
    inv_b = (1.0 / (np.float32(10000.0) ** np.linspace(0.0, 1.0, 64, dtype=np.float32))).astype(np.float32)
    angb = (pos[:, None] * inv_b[None, :]).astype(np.float32)
    cos_b, sin_b = np.cos(angb).astype(np.float64), np.sin(angb).astype(np.float64)
    lg = np.log(1.0 - 2.0 ** (-5.0 - np.arange(4, dtype=np.float64)))
    i = (np.arange(S) % 128).astype(np.float64)
    gq = np.exp(lg[None, :] * i[:, None])
    gk = np.exp(-lg[None, :] * i[:, None]) * (128.0 ** -0.5)
    cb = np.zeros((S, 4, 4, 64), np.float64)
    cb[:, 0] = cos_b[:, None, :] * gq[:, :, None]
    cb[:, 1] = sin_b[:, None, :] * gq[:, :, None]
    cb[:, 2] = cos_b[:, None, :] * gk[:, :, None]
    cb[:, 3] = sin_b[:, None, :] * gk[:, :, None]
    c["cB"] = cb.reshape(S, 1024).astype(np.float32)
    gam = np.exp(lg)
    rs = np.zeros((128, 12), np.float32)
    rs[:, 0:4] = (gam ** 128)[None, :]
    rs[:, 4:8] = (gam ** 127)[None, :]
    rs[:, 8:12] = gam[None, :]
    c["cRS"] = rs
    E = np.zeros((16, S), np.float32)
    for j in range(16):
        E[j, j * 256:(j + 1) * 256] = 1.0
    c["cE"] = E
    cm = np.zeros((2, 128, 256), np.float32)
    for kt in range(2):
        k = kt * 128 + np.arange(128)[:, None]
        q = np.arange(256)[None, :]
        cm[kt] = np.where(k <= q, 0.0, NEG)
    c["cCM"] = cm.transpose(1, 0, 2).reshape(128, 512)
    j = np.arange(128)[:, None]
    ii = np.arange(128)[None, :]
    m = (j <= ii).astype(np.float32)
    c["cRM"] = np.tile(m[:, None, :], (1, 4, 1)).reshape(128, 512)
    s = (np.arange(128) % 64)[:, None]
    t = np.arange(64)[None, :]
    strict = (s < t).astype(np.float32)
    incl = (s <= t).astype(np.float32)
    am = np.concatenate([strict, incl], axis=1)
    c["cAM"] = np.tile(am[:, None, :], (1, 4, 1)).reshape(128, 512)
    tt_ = (np.arange(128) % 64)[:, None]
    ss_ = np.arange(64)[None, :]
    xm = (ss_ < tt_).astype(np.float32)
    c["cXM"] = np.tile(xm[:, None, :], (1, 4, 1)).reshape(128, 256)
    rm = np.ones((128, 512), np.float32)
    rm[:, ::64] = 0.0
    c["cRST"] = rm
    c["cID"] = np.eye(128, dtype=np.float32)
    bo = np.zeros((128, 128), np.float32)
    bo[:64, :64] = 1.0
    bo[64:, 64:] = 1.0
    c["cBO"] = bo
    c["cI4"] = np.tile(np.eye(64, dtype=np.float32)[None, None], (2, 4, 1, 1)).transpose(0, 2, 1, 3).reshape(128, 256)
    return c


CONST_SHAPES = lambda S: {"cA": [S, 256], "cB": [S, 1024], "cRS": [128, 12], "cE": [16, S], "cCM": [128, 512],
                          "cRM": [128, 512], "cAM": [128, 512], "cXM": [128, 256], "cRST": [128, 512],
                          "cID": [128, 128], "cBO": [128, 128], "cI4": [128, 256]}

COLS = {"mixg": (0, 8), "ffng": (8, 8), "pleg": (16, 8), "fing": (24, 8), "mu": (32, 14), "w0": (46, 4),
        "a0": (50, 4), "kk": (54, 4), "ka": (58, 4), "rk": (62, 4), "v0": (66, 4), "vmu": (70, 1)}
NCOL = 72


def pack_cols(inp, l):
    out = np.zeros((128, NCOL), np.float32)

    def put(name, vec):
        o, n = COLS[name]
        v = np.asarray(vec, np.float32).reshape(-1)
        out[:, o:o + n] = v.reshape(n, 128).T
    put("mixg", inp["norm_mix_g"][l])
    put("ffng", inp["norm_ffn_g"][l])
    put("pleg", inp["norm_ple_g"][l])
    put("fing", inp["final_norm_g"])
    put("mu", inp["c_mu"][l])
    put("w0", inp["c_w0"][l])
    put("a0", inp["c_a0"][l])
    put("kk", inp["c_k_k"][l])
    put("ka", inp["c_k_a"][l])
    put("rk", inp["c_r_k"][l])
    if l >= 1:
        put("v0", inp["c_v0"][l - 1])
        out[0:32, COLS["vmu"][0]] = np.asarray(inp["c_vres_mu"][l - 1], np.float32)
    return out


def pack_ln(inp, l):
    out = np.zeros((128, 2, 4, 64), np.float32)
    for k, name in enumerate(("c_ln_g", "c_ln_b")):
        v = np.asarray(inp[name][l], np.float32).reshape(4, 2, 64)
        for hh in range(2):
            out[hh * 64:(hh + 1) * 64, k] = v[:, hh, :][None]
    return out.reshape(128, 512)


def build(S, depth=2, en="abc", dbg=()):
    NT = S // 128
    NG = S // 512
    NB = S // 256
    nc = bass.Bass("TRN2", target_bir_lowering=False)

    def din(name, shape, dt=F32):
        return nc.dram_tensor(name, list(shape), dt, kind="ExternalInput").ap()

    def dscr(name, shape, dt=F32):
        return nc.dram_tensor(name, list(shape), dt, kind="Internal").ap()

    x_d = din("x", [S, D])
    p_d = din("p", [depth, S, 256])
    w_in = din("w_in", [depth, D, NIN])
    w2_d = din("c_w2", [depth, 64, 512])
    a2_d = din("c_a2", [depth, 64, 512])
    g2_d = din("c_g2", [depth, 128, 512])
    vd_d = din("c_vres_down", [max(depth - 1, 1), D, 32])
    v2_d = din("c_v2", [max(depth - 1, 1), 32, 512])
    wbr_d = din("w_branch", [depth, 3, 512, D])
    wout_d = din("w_out", [depth, D, D])
    wgu_d = din("w_gate_up", [depth, D, 2 * DFF])
    wd_d = din("w_down", [depth, DFF, D])
    wpg_d = din("w_ple_gate", [depth, D, D])
    wpp_d = din("w_ple_proj", [depth, 256, D])
    cols_d = din("cols", [depth, 128, NCOL])
    ln_d = din("lnp", [depth, 128, 512])
    cst = {k: din(k, shp) for k, shp in CONST_SHAPES(S).items()}
    out_d = nc.dram_tensor("out", [S, D], F32, kind="ExternalOutput").ap()

    xT_d = dscr("xT_d", [128, 8, S])
    hT_d = dscr("hT_d", [128, 8, S], BF16)
    yT_d = dscr("yT_d", [3, 128, 4, S], BF16)
    vf_d = dscr("vf_d", [128, 4, S])
    NINX = NIN + 32
    Wb_in = dscr("Wb_in", [depth, 128, 8, NINX], BF16)
    Wb_br = dscr("Wb_br", [depth, 128, 3, 4, D], BF16)
    Wb_out = dscr("Wb_out", [depth, 128, 8, D], BF16)
    Wb_gu = dscr("Wb_gu", [depth, 128, 8, 2 * DFF], BF16)
    Wb_d = dscr("Wb_d", [depth, 128, 22, D], BF16)
    Wb_pg = dscr("Wb_pg", [depth, 128, 8, D], BF16)
    Wb_pp = dscr("Wb_pp", [depth, 128, 2, D], BF16)
    dbg_d = {}
    for name, shp in dbg:
        dbg_d[name] = nc.dram_tensor("dbg_" + name, list(shp), F32, kind="ExternalOutput").ap()

    K = Ctx(nc)
    uniq = [0]
    with ExitStack() as top:
        def sbt(es, name, shape, dt=F32):
            uniq[0] += 1
            return es.enter_context(nc.sbuf_tensor(f"s_{name}_{uniq[0]}", list(shape), dt))

        banks = [top.enter_context(nc.psum_tensor(f"bank{i}", [128, 512], F32)) for i in range(8)]
        bank_rr = [0]

        def nb_(lo=0, hi=8):
            b = banks[lo + bank_rr[0] % (hi - lo)]
            bank_rr[0] += 1
            return b

        ident = sbt(top, "ident", [128, 128])
        identb = sbt(top, "identb", [128, 128], BF16)
        onesb = sbt(top, "onesb", [128, 128], BF16)
        cols = sbt(top, "cols", [128, depth, NCOL])
        K.dma("sp", ident[:], cst["cID"][:, :], r=[], semkey="cload")
        K.cp("dve", identb[:], ident[:])
        K.memset("dve", onesb[:], 1.0)
        for l in range(depth):
            K.dma("sp", cols[:, l, :], cols_d[l], r=[], w=[("cols", l)], semkey="cload")
        colkeys = [("cols", l) for l in range(depth)]

        def col(l, name, j=0, n=1):
            o, _ = COLS[name]
            return cols[:, l, o + j:o + j + n]

        bg = []
        convn = [0]

        def conv_jobs(l):
            first, rest = [], []
            for c0 in range(0, NIN, 512):
                cw = min(512, NIN - c0)
                job = (l, Wb_in[l, :, :, c0:c0 + cw], w_in[l, :, c0:c0 + cw].rearrange("(k p) n -> p k n", p=128), ("Wb_in", l, c0 // 512))
                (first if c0 < 5632 else rest).append(job)
            if l >= 1:
                first.append((l, Wb_in[l, :, :, NIN:NINX], vd_d[l - 1].rearrange("(k p) n -> p k n", p=128), ("Wb_in", l, "v")))
            for n in range(3):
                for h in range(2):
                    rest.append((l, Wb_br[l, :, n, :, h * 512:(h + 1) * 512],
                                 wbr_d[l, n, :, h * 512:(h + 1) * 512].rearrange("(k p) n -> p k n", p=128), ("Wb_br", l)))
            for h in range(2):
                rest.append((l, Wb_out[l, :, :, h * 512:(h + 1) * 512],
                             wout_d[l, :, h * 512:(h + 1) * 512].rearrange("(k p) n -> p k n", p=128), ("Wb_out", l)))
            for c0 in range(0, 2 * DFF, 512):
                rest.append((l, Wb_gu[l, :, :, c0:c0 + 512], wgu_d[l, :, c0:c0 + 512].rearrange("(k p) n -> p k n", p=128), ("Wb_gu", l)))
            for h in range(2):
                for k0 in (0, 11):
                    rest.append((l, Wb_d[l, :, k0:k0 + 11, h * 512:(h + 1) * 512],
                                 wd_d[l, k0 * 128:(k0 + 11) * 128, h * 512:(h + 1) * 512].rearrange("(k p) n -> p k n", p=128), ("Wb_d", l)))
            for h in range(2):
                rest.append((l, Wb_pg[l, :, :, h * 512:(h + 1) * 512],
                             wpg_d[l, :, h * 512:(h + 1) * 512].rearrange("(k p) n -> p k n", p=128), ("Wb_pg", l)))
                rest.append((l, Wb_pp[l, :, :, h * 512:(h + 1) * 512],
                             wpp_d[l, :, h * 512:(h + 1) * 512].rearrange("(k p) n -> p k n", p=128), ("Wb_pp", l)))
            return [j + (True,) for j in first], [j + (False,) for j in rest]

        def issue_job(job):
            _, o_, i_, key, _f = job
            K.dma("pool", o_, i_, r=[], w=[key], semkey=f"conv{convn[0] % 6}")
            convn[0] += 1

        def feed(n=2):
            for _ in range(n):
                if bg:
                    issue_job(bg.pop(0))

        def flush(l):
            while bg and bg[0][0] <= l:
                issue_job(bg.pop(0))

        def norm_group(es_name, xg, hg_out, l, gname, scratch):
            sq, rstd = scratch[0], scratch[1]
            pb = nb_()
            for c in range(8):
                K.act(sq[:, c, :], xg[:, c, :], AF.Square)
            for c in range(8):
                K.mm(pb[:], onesb[:], sq[:, c, :], start=(c == 0), stop=(c == 7))
            K.act(rstd[:], pb[:], AF.Sqrt, bias=epsc[:, 0:1], scale=1.0 / D)
            K.op("dve", lambda e: e.reciprocal(out=rstd[:], in_=rstd[:]), [rstd], [rstd])
            for c in range(8):
                if len(scratch) > 2 and c % 2 == 1:
                    tp_ = scratch[2][(c // 2) % 2]
                    K.tt("pool", tp_[:], xg[:, c, :], rstd[:], ALU.mult)
                    K.act(hg_out[:, c, :], tp_[:], AF.Identity, scale=col(l, gname, c), r=[tp_] + colkeys)
                else:
                    K.stt(hg_out[:, c, :], xg[:, c, :], col(l, gname, c), rstd[:], ALU.mult, ALU.mult,
                          r=[xg, rstd] + colkeys)

        epsc = sbt(top, "epsc", [128, 4])
        K.memset("dve", epsc[:, 0:1], EPS)
        K.memset("dve", epsc[:, 1:2], 1e-5 * 64)
        K.memset("dve", epsc[:, 2:3], 0.0)

        f0, r0 = conv_jobs(0)
        for j in f0:
            issue_job(j)
        bg.extend(r0)
        for l_ in range(1, depth):
            f_, r_ = conv_jobs(l_)
            bg.extend(f_ + r_)

        with ExitStack() as es:
            xin = [sbt(es, f"xin{i}", [128, D]) for i in range(2)]
            xg2 = [sbt(es, f"xgI{i}", [128, 8, 512]) for i in range(2)]
            hg2 = [sbt(es, f"hgI{i}", [128, 8, 512], BF16) for i in range(2)]
            sq = sbt(es, "sqI", [128, 8, 512], BF16)
            rstd = sbt(es, "rstdI", [128, 512])
            for tg in range(NG):
                xg = xg2[tg % 2]
                hg = hg2[tg % 2]
                for tt_ in range(4):
                    t = tg * 4 + tt_
                    xi = xin[t % 2]
                    K.dma("sp", xi[:], x_d[t * 128:(t + 1) * 128, :], r=[])
                    for half in range(2):
                        pb = nb_()
                        for c in range(4):
                            K.tr(pb[:, c * 128:(c + 1) * 128], xi[:, (half * 4 + c) * 128:(half * 4 + c + 1) * 128], ident[:])
                        K.cp("act" if half else "dve", xg[:, half * 4:(half + 1) * 4, tt_ * 128:(tt_ + 1) * 128],
                             pb[:].rearrange("p (c t) -> p c t", c=4))
                K.dma("sp", xT_d[:, :, tg * 512:(tg + 1) * 512], xg[:], w=[("xT", tg)], semkey="xTst")
                norm_group("I", xg, hg, 0, "mixg", (sq, rstd))
                K.dma("sp", hT_d[:, :, tg * 512:(tg + 1) * 512], hg[:], w=[("hT", tg)], semkey="hTst")

        K.barrier()
        for l in range(depth):
            while bg and (bg[0][0] < l or (bg[0][0] == l and bg[0][4])):
                issue_job(bg.pop(0))
            last = (l == depth - 1)
            if "a" in en:
                phase_a(nc, K, sbt, banks, nb_, l, S, cst, Wb_in, hT_d, yT_d, ident, identb, feed)
                K.barrier()
            if "b" in en:
                phase_b(nc, K, sbt, banks, nb_, l, S, cst, Wb_in, hT_d, yT_d, ident, identb, epsc, feed)
                K.barrier()
            if "c" in en:
                phase_c(nc, K, sbt, banks, nb_, l, S, cst, Wb_in, hT_d, yT_d, vf_d, ident, identb, epsc, cols, col, colkeys,
                        w2_d, a2_d, g2_d, v2_d, ln_d, dbg_d)
                K.barrier()

            flush(l)
            with ExitStack() as es:
                xg2 = [sbt(es, f"xgT{i}", [128, 8, 512]) for i in range(2)]
                hgs = [sbt(es, f"hgT{i}", [128, 8, 512], BF16) for i in range(2)]
                ygs = [sbt(es, f"ygT{i}", [128, 3, 4, 512], BF16) for i in range(2)]
                hf = sbt(es, "hfT", [128, 8, 512], BF16)
                mg = sbt(es, "mgT", [128, 8, 512], BF16)
                actT = sbt(es, "actT", [128, 22, 512], BF16)
                sq = mg
                rstd = sbt(es, "rstdT", [128, 512])
                sg = [sbt(es, f"sgT{i}", [128, 512]) for i in range(2)]
                acc = sbt(es, "accT", [128, 512])
                tmpf = [sbt(es, f"tmpT{i}", [128, 512]) for i in range(2)]
                wst = [sbt(es, f"wst{i}", [128, 8, 1024], BF16) for i in range(2)]
                wdt = [sbt(es, f"wdt{i}", [128, 22, 128], BF16) for i in range(2)]
                wbrt = sbt(es, "wbrt", [128, 3, 4, 128], BF16)
                wppt = sbt(es, "wppt", [128, 2, D], BF16)
                pin = [sbt(es, f"pin{i}", [128, 256]) for i in range(2)]
                pT = sbt(es, "pTT", [128, 2, 512], BF16)
                ot = [sbt(es, f"otT{i}", [128, D]) for i in range(2)]
                wsi = [0]

                def wslab():
                    t_ = wst[wsi[0] % 2]
                    wsi[0] += 1
                    return t_
                K.dma("sp", wppt[:], Wb_pp[l], r=[("Wb_pp", l)], semkey="cload")
                def load_group(g):
                    tk = slice(g * 512, (g + 1) * 512)
                    K.dma("sp", xg2[g % 2][:], xT_d[:, :, tk], r=[("xT", g)])
                    K.dma("sp", hgs[g % 2][:], hT_d[:, :, tk], r=[("hT", g)])
                    for n in range(3):
                        if "abc"[n] in en:
                            K.dma("sp", ygs[g % 2][:, n], yT_d[n, :, :, tk], r=[("yT", n, g)], w=[("ygT", g % 2, n)], semkey=f"ygT{g % 2}")
                        elif g < 2:
                            K.memset("pool", ygs[g % 2][:, n], 0.0, w=[("ygT", g % 2, n)])

                load_group(0)
                for tg in range(NG):
                    xg = xg2[tg % 2]
                    hg = hgs[tg % 2]
                    yg = ygs[tg % 2]
                    tok = slice(tg * 512, (tg + 1) * 512)
                    ygk = [("ygT", n) for n in range(3)]
                    for dc in range(8):
                        ws = wslab()
                        for n in range(3):
                            c0 = 5376 + n * 1024 + dc * 128
                            K.dma("sp", ws[:, :, n * 128:(n + 1) * 128], Wb_in[l, :, :, c0:c0 + 128],
                                  r=[("Wb_in", l, c0 // 512)], w=[ws])
                        K.dma("sp", wbrt[:], Wb_br[l, :, :, :, dc * 128:(dc + 1) * 128], r=[("Wb_br", l)])
                        for n in range(3):
                            pbr = nb_()
                            pgt = nb_()
                            for kc in range(4):
                                K.mm(pbr[:], wbrt[:, n, kc, :], yg[:, n, kc, :], start=(kc == 0), stop=(kc == 3),
                                     r=[wbrt, ("ygT", tg % 2, n)])
                            for kc in range(8):
                                K.mm(pgt[:], ws[:, kc, n * 128:(n + 1) * 128], hg[:, kc, :], start=(kc == 0), stop=(kc == 7))
                            s_ = sg[n % 2]
                            K.act(s_[:], pgt[:], AF.Sigmoid)
                            if n == 0:
                                K.tt("dve", acc[:], s_[:], pbr[:], ALU.mult)
                            elif n == 1:
                                K.tt("dve", tmpf[0][:], s_[:], pbr[:], ALU.mult)
                                K.tt("pool", acc[:], acc[:], tmpf[0][:], ALU.add)
                            else:
                                K.tt("dve", tmpf[1][:], s_[:], pbr[:], ALU.mult)
                                K.tt("dve", mg[:, dc, :], acc[:], tmpf[1][:], ALU.add)
                    for dc in range(8):
                        if dc % 4 == 0:
                            ws = wslab()
                            K.dma("sp", ws[:, :, 0:512], Wb_out[l, :, :, dc * 128:dc * 128 + 512], r=[("Wb_out", l)], w=[ws])
                        po = nb_()
                        for kc in range(8):
                            K.mm(po[:], ws[:, kc, (dc % 4) * 128:(dc % 4 + 1) * 128], mg[:, kc, :], start=(kc == 0), stop=(kc == 7))
                        K.tt("dve", xg[:, dc, :], xg[:, dc, :], po[:], ALU.add)
                    if tg + 1 < NG:
                        load_group(tg + 1)
                    norm_group("T", xg, hf, l, "ffng", (sq, rstd, tmpf))
                    for f4 in range(0, 22, 4):
                        nf = min(4, 22 - f4)
                        ws = wslab()
                        K.dma("sp", ws[:, :, 0:nf * 128], Wb_gu[l, :, :, f4 * 128:(f4 + nf) * 128], r=[("Wb_gu", l)], w=[ws])
                        K.dma("sp", ws[:, :, 512:512 + nf * 128], Wb_gu[l, :, :, DFF + f4 * 128:DFF + (f4 + nf) * 128],
                              r=[("Wb_gu", l)], w=[ws])
                        for fi in range(nf):
                            fc = f4 + fi
                            pg_ = nb_()
                            pu_ = nb_()
                            for kc in range(8):
                                K.mm(pg_[:], ws[:, kc, fi * 128:(fi + 1) * 128], hf[:, kc, :], start=(kc == 0), stop=(kc == 7))
                            for kc in range(8):
                                K.mm(pu_[:], ws[:, kc, 512 + fi * 128:512 + (fi + 1) * 128], hf[:, kc, :], start=(kc == 0), stop=(kc == 7))
                            s_ = sg[fc % 2]
                            K.act(s_[:], pg_[:], AF.Silu)
                            K.tt("dve", actT[:, fc, :], s_[:], pu_[:], ALU.mult)
                    for dc in range(8):
                        wd_ = wdt[dc % 2]
                        K.dma("sp", wd_[:], Wb_d[l, :, :, dc * 128:(dc + 1) * 128], r=[("Wb_d", l)])
                        pd = nb_()
                        for fc in range(22):
                            K.mm(pd[:], wd_[:, fc, :], actT[:, fc, :], start=(fc == 0), stop=(fc == 21))
                        K.tt("dve", xg[:, dc, :], xg[:, dc, :], pd[:], ALU.add)
                    norm_group("T", xg, hf, l, "pleg", (sq, rstd, tmpf))
                    for tt_ in range(4):
                        pi = pin[tt_ % 2]
                        K.dma("sp", pi[:], p_d[l, tg * 512 + tt_ * 128: tg * 512 + (tt_ + 1) * 128, :], r=[])
                        pb = nb_()
                        for c in range(2):
                            K.tr(pb[:, c * 128:(c + 1) * 128], pi[:, c * 128:(c + 1) * 128], ident[:])
                        K.cp("act", pT[:, :, tt_ * 128:(tt_ + 1) * 128], pb[:, 0:256].rearrange("p (c t) -> p c t", c=2))
                    for dc in range(8):
                        if dc % 4 == 0:
                            ws = wslab()
                            K.dma("sp", ws[:, :, 0:512], Wb_pg[l, :, :, dc * 128:dc * 128 + 512], r=[("Wb_pg", l)], w=[ws])
                        pg_ = nb_()
                        pp_ = nb_()
                        for kc in range(8):
                            K.mm(pg_[:], ws[:, kc, (dc % 4) * 128:(dc % 4 + 1) * 128], hf[:, kc, :], start=(kc == 0), stop=(kc == 7))
                        for kc in range(2):
                            K.mm(pp_[:], wppt[:, kc, dc * 128:(dc + 1) * 128], pT[:, kc, :], start=(kc == 0), stop=(kc == 1))
                        s_ = sg[dc % 2]
                        K.act(s_[:], pg_[:], AF.Sigmoid)
                        K.tt("dve", tmpf[dc % 2][:], s_[:], pp_[:], ALU.mult)
                        K.tt("pool", xg[:, dc, :], xg[:, dc, :], tmpf[dc % 2][:], ALU.add)
                    if not last:
                        K.dma("pool", xT_d[:, :, tok], xg[:], w=[("xT", tg)], semkey="xTst")
                        norm_group("T", xg, hf, l + 1, "mixg", (sq, rstd, tmpf))
                        K.dma("pool", hT_d[:, :, tok], hf[:], w=[("hT", tg)], semkey="hTst")
                    else:
                        pb = nb_()
                        for c in range(8):
                            K.act(sq[:, c, :], xg[:, c, :], AF.Square)
                        for c in range(8):
                            K.mm(pb[:], onesb[:], sq[:, c, :], start=(c == 0), stop=(c == 7))
                        K.act(rstd[:], pb[:], AF.Sqrt, bias=epsc[:, 0:1], scale=1.0 / D)
                        K.op("dve", lambda e: e.reciprocal(out=rstd[:], in_=rstd[:]), [rstd], [rstd])
                        for c in range(8):
                            K.stt(xg[:, c, :], xg[:, c, :], col(l, "fing", c), rstd[:], ALU.mult, ALU.mult,
                                  r=[xg, rstd] + colkeys)
                        for tt_ in range(4):
                            o_ = ot[tt_ % 2]
                            for half in range(2):
                                pb2 = nb_()
                                for c in range(4):
                                    K.tr(pb2[:, c * 128:(c + 1) * 128], xg[:, half * 4 + c, tt_ * 128:(tt_ + 1) * 128], ident[:])
                                K.cp("act" if half else "dve", o_[:, half * 512:(half + 1) * 512], pb2[:])
                            K.dma("sp", out_d[tg * 512 + tt_ * 128: tg * 512 + (tt_ + 1) * 128, :], o_[:], w=["out"],
                                  semkey="out")
            K.barrier()
        K.final_wait("sp", ["out"] + ["dbg_" + n for n in dbg_d])
        K.replay()
    return nc, K


def phase_a(nc, K, sbt, banks, nb_, l, S, cst, Wb_in, hT_d, yT_d, ident, identb, feed):
    NT = S // 128
    NB = S // 256
    with ExitStack() as es:
        wA = sbt(es, "wA", [128, 8, 1536], BF16)
        KT = sbt(es, "KT", [80, 8, S], BF16)
        Va = sbt(es, "Va", [128, NT, 8, 65], BF16)
        QT = [sbt(es, f"QT{i}", [80, 8, 256], BF16) for i in range(2)]
        QTf = [sbt(es, f"QTf{i}", [64, 8, 128]) for i in range(2)]
        kms = sbt(es, "kms", [64, 8, 16])
        ktmp = sbt(es, "ktmp", [64, 8])
        hTt = [sbt(es, f"hTtA{i}", [128, 8, 128], BF16) for i in range(2)]
        qk = [sbt(es, f"qkA{i}", [128, 2, 8, 64]) for i in range(2)]
        cs = [sbt(es, f"csA{i}", [128, 2, 128]) for i in range(2)]
        rt = [sbt(es, f"rtA{i}", [128, 128]) for i in range(4)]
        gs = sbt(es, "gsA", [128, 8, 16])
        m8 = sbt(es, "m8A", [128, 8, 8])
        Mp = sbt(es, "MpA", [128, 8, 80], BF16)
        cm = sbt(es, "cmA", [128, 2, 256], BF16)
        PT = [sbt(es, f"PTA{i}", [128, 256], BF16) for i in range(3)]
        ya = [sbt(es, f"yaA{i}", [128, 512], BF16) for i in range(2)]
        rden = sbt(es, "rdenA", [128, 4])
        yTs = [sbt(es, f"yTsA{i}", [128, 4, 128], BF16) for i in range(2)]
        for g in range(3):
            K.dma("sp", wA[:, :, g * 512:(g + 1) * 512], Wb_in[l, :, :, g * 512:(g + 1) * 512], r=[("Wb_in", l, g)],
                  w=[("wA", g)], semkey="wload")
        wAk = [("wA", g) for g in range(3)]
        K.dma("pool", cm[:], cst["cCM"].rearrange("p (k q) -> p k q", k=2), r=[], semkey="cloadp")
        for h in range(8 if "e" not in PA_SKIP else 0):
            for e0 in range(0, S, 2048):
                e1 = min(S, e0 + 2048)
                K.dma("pool", KT[64:80, h, e0:e1], cst["cE"][:, e0:e1], r=[], w=[("KTE", h)], semkey="cloadp")
        KTE = [("KTE", h) for h in range(8)]
        if "m" not in PA_SKIP:
            K.memset("pool", Va[:, :, :, 64:65], 1.0, w=["Va1"])
            K.memset("pool", Mp[:], 0.0)
        pz = [banks[0], banks[1]]
        pTq = [banks[2], banks[3]]
        pS = [banks[4], banks[5]]
        pO = [banks[6], banks[7]]
        pti = 0
        psi = 0
        for t in range(NT):
            b = t // 2
            half = t % 2
            feed(2)
            hT = hTt[t % 2]
            K.dma("sp", hT[:], hT_d[:, :, t * 128:(t + 1) * 128], r=[("hT", t // 4)])
            c_ = cs[t % 2]
            K.dma("sp", c_[:], cst["cA"][t * 128:(t + 1) * 128, :].rearrange("p (a b) -> p a b", a=2), r=[])
            q_ = qk[t % 2]
            for g in range(3):
                pb = pz[g % 2]
                for kc in range(8):
                    K.mm(pb[:], hT[:, kc, :], wA[:, kc, g * 512:(g + 1) * 512], start=(kc == 0), stop=(kc == 7),
                         r=[hT, ("wA", g)])
                if g < 2:
                    K.cp("act", q_[:, g], pb[:].rearrange("p (h d) -> p h d", h=8))
                elif "v" not in PA_SKIP:
                    K.cp("act", Va[:, t, :, 0:64], pb[:].rearrange("p (h d) -> p h d", h=8), w=[("Va", t)])
            x1 = q_[:, :, :, 0:8]
            x2 = q_[:, :, :, 8:16]
            co = c_[:, 0, :].rearrange("p (a h d) -> p a h d", a=2, h=8)
            si = c_[:, 1, :].rearrange("p (a h d) -> p a h d", a=2, h=8)
            t1, t2, t3, t4 = [r_[:].rearrange("p (a h d) -> p a h d", a=2, h=8) for r_ in rt]
            if "r" not in PA_SKIP:
                K.tt("dve", t1, x1, co, ALU.mult)
                K.tt("pool", t2, x2, si, ALU.mult)
                K.tt("dve", t3, x1, si, ALU.mult)
                K.tt("pool", t4, x2, co, ALU.mult)
                K.tt("dve", x1, t1, t2, ALU.subtract)
                K.tt("dve", x2, t3, t4, ALU.add)
            for g in range(2 if "t" not in PA_SKIP else 0):
                for hq in range(2):
                    pb = pTq[hq]
                    for h4 in range(4):
                        h = hq * 4 + h4
                        K.tr(pb[0:64, h4 * 128:(h4 + 1) * 128], q_[:, g, h, :], ident[:])
                    src = pb[0:64, :].rearrange("p (h t) -> p h t", h=4)
                    if g == 0:
                        K.cp("act", QT[b % 2][0:64, hq * 4:(hq + 1) * 4, half * 128:(half + 1) * 128], src,
                             w=[("QTq", b % 2)])
                        K.cp("dve", QTf[half][:, hq * 4:(hq + 1) * 4, :], src)
                    else:
                        K.cp("act", KT[0:64, hq * 4:(hq + 1) * 4, t * 128:(t + 1) * 128], src, w=[("KT", t)])
                        K.op("dve", lambda e, src=src, hq=hq: e.tensor_reduce(out=ktmp[:, hq * 4:(hq + 1) * 4], in_=src, axis=AX.X, op=ALU.add),
                             [pb], [ktmp])
                if g == 1:
                    if half == 0:
                        K.cp("dve", kms[:, :, b:b + 1], ktmp[:].rearrange("p (h o) -> p h o", o=1))
                    else:
                        K.tt("dve", kms[:, :, b:b + 1], kms[:, :, b:b + 1], ktmp[:].rearrange("p (h o) -> p h o", o=1), ALU.add)
            if half == 0 or PA_STOP <= 1:
                continue
            Qb = QT[b % 2]
            if b >= 1:
                for hf_ in range(2):
                    pg = banks[2]
                    for h in range(8):
                        K.mm(pg[:, h * 16:(h + 1) * 16], QTf[hf_][0:64, h, :], kms[0:64, h, :])
                    K.cp("dve", gs[:], pg[:, 0:128].rearrange("p (h n) -> p h n", h=8))
                    if b < 16:
                        K.memset("dve", gs[:, :, b:16], -1e30)
                    for h in range(8):
                        K.op("dve", lambda e, h=h: e.max(out=m8[:, h, :], in_=gs[:, h, :]), [gs], [m8])
                    for h in range(8):
                        K.ts("dve", Mp[:, h, 64:80], gs[:, h, :], m8[:, h, 2:3], NEG, ALU.is_lt, ALU.mult)
                    pm = banks[3]
                    pmv = pm[:].bitcast(BF16).rearrange("p (h t) -> p h t", h=8)
                    for h in range(8):
                        K.tr(pmv[0:80, h, :], Mp[:, h, :], identb[:])
                    K.cp("act", Qb[64:80, :, hf_ * 128:(hf_ + 1) * 128], pmv[64:80, :, :], w=[("QTm", b % 2, hf_)])
            Qkeys = [("QTq", b % 2), ("QTm", b % 2, 0), ("QTm", b % 2, 1)]
            if PA_STOP <= 2:
                continue
            nkt = 2 * b + 2
            steps = [(h, kt) for h in range(8) for kt in range(nkt)]

            def issue_S(i):
                h, kt = steps[i]
                own = kt >= 2 * b
                Kr = 64 if own else 80
                ps_ = pS[i % 2]
                K.mm(ps_[:, 0:256], KT[0:Kr, h, kt * 128:(kt + 1) * 128], Qb[0:Kr, h, :], start=True, stop=not own,
                     r=[("KT", kt), ("KTE", h)] + Qkeys)
                if own:
                    K.mm(ps_[:, 0:256], identb[:], cm[:, kt - 2 * b, :], start=False, stop=True)

            issue_S(0)
            for i, (h, kt) in enumerate(steps):
                own = kt >= 2 * b
                pOh = pO if h % 2 == 0 else [banks[2], banks[3]]
                if i + 1 < len(steps):
                    issue_S(i + 1)
                P_ = PT[i % 3]
                K.act(P_[:], pS[i % 2][:, 0:256], AF.Exp, scale=0.125)
                for q2 in range(2):
                    if own and kt - 2 * b == 1 and q2 == 0:
                        continue
                    lastk = (2 * b) if q2 == 0 else (2 * b + 1)
                    K.mm(pOh[q2][:, 0:65], P_[:, q2 * 128:(q2 + 1) * 128], Va[:, kt, h, :], start=(kt == 0), stop=(kt == lastk),
                         r=[P_, ("Va", kt), "Va1"])
                if kt == nkt - 1:
                    for q2 in range(2):
                        rd = rden[:, (h % 2) * 2 + q2:(h % 2) * 2 + q2 + 1]
                        K.op("dve", lambda e, rd=rd, src=pOh[q2][:, 64:65]: e.reciprocal(out=rd, in_=src), [pOh[q2]], [("rden", h % 2, q2)])
                        K.ts("dve", ya[q2][:, h * 64:(h + 1) * 64], pOh[q2][:, 0:64], rd, None, ALU.mult,
                             r=[pOh[q2], ("rden", h % 2, q2)], w=[("ya", q2, h)])
            yak = [[("ya", q2, h) for h in range(8)] for q2 in range(2)]
            if PA_STOP <= 3:
                continue
            for q2 in range(2):
                tq = 2 * b + q2
                pb = banks[q2]
                pv = pb[:].bitcast(BF16)[:, 0:512].rearrange("p (c t) -> p c t", c=4)
                for c in range(4):
                    K.tr(pv[:, c, :], ya[q2][:, c * 128:(c + 1) * 128], identb[:], r=yak[q2] + [identb])
                K.cp("act", yTs[q2][:], pv)
                K.dma("sp", yT_d[0, :, :, tq * 128:(tq + 1) * 128], yTs[q2][:], w=[("yT", 0, tq // 4)], semkey="yTs")


def phase_b(nc, K, sbt, banks, nb_, l, S, cst, Wb_in, hT_d, yT_d, ident, identb, epsc, feed):
    NT = S // 128
    with ExitStack() as es:
        wB = sbt(es, "wB", [128, 8, 2048], BF16)
        hTt = [sbt(es, f"hTtB{i}", [128, 8, 128], BF16) for i in range(2)]
        cb = [sbt(es, f"cbB{i}", [128, 4, 256]) for i in range(2)]
        qs = [[sbt(es, f"qsB{i}_{g}", [128, 512]) for g in range(2)] for i in range(2)]
        rt = [sbt(es, f"rtB{i}", [128, 256]) for i in range(4)]
        qr = [[sbt(es, f"qrB{i}_{g}", [128, 512], BF16) for g in range(2)] for i in range(2)]
        QTb = [sbt(es, f"QTbB{i}", [128, 4, 128], BF16) for i in range(2)]
        KTb = [sbt(es, f"KTbB{i}", [128, 4, 128], BF16) for i in range(2)]
        vb = [sbt(es, f"vbB{i}", [128, 512], BF16) for i in range(2)]
        sgb = [sbt(es, f"sgB{i}", [128, 512]) for i in range(2)]
        Sm = [sbt(es, f"SmB{i}", [128, 4, 128], BF16) for i in range(2)]
        kvs = [sbt(es, f"kvsB{i}", [128, 4, 128]) for i in range(2)]
        rm = sbt(es, "rmB", [128, 4, 128], BF16)
        R = sbt(es, "RB", [128, 4, 128])
        Rg = sbt(es, "RgB", [128, 4, 128], BF16)
        rs = sbt(es, "rsB", [128, 12])
        st = sbt(es, "stB", [128, 4, 6])
        mv = sbt(es, "mvB", [128, 4, 2])
        rstd = sbt(es, "rstdB", [128, 4])
        yn = sbt(es, "ynB", [128, 512])
        yb = sbt(es, "ybB", [128, 512], BF16)
        yTs = [sbt(es, f"yTsB{i}", [128, 4, 128], BF16) for i in range(2)]
        for g in range(4):
            K.dma("sp", wB[:, :, g * 512:(g + 1) * 512], Wb_in[l, :, :, 1536 + g * 512:1536 + (g + 1) * 512],
                  r=[("Wb_in", l, 3 + g)], w=[("wB", g)], semkey="wload")
        K.dma("pool", rm[:], cst["cRM"].rearrange("p (h t) -> p h t", h=4), r=[], semkey="cloadp")
        K.dma("sp", rs[:], cst["cRS"][:, :], r=[], semkey="cload")
        K.memset("dve", R[:], 0.0, w=[("RB", h) for h in range(4)])
        K.memset("pool", Rg[:], 0.0)

        def prep(t):
            p = t % 2
            feed(2)
            hT = hTt[p]
            K.dma("sp", hT[:], hT_d[:, :, t * 128:(t + 1) * 128], r=[("hT", t // 4)])
            c_ = cb[p]
            K.dma("sp", c_[:], cst["cB"][t * 128:(t + 1) * 128, :].rearrange("p (a b) -> p a b", a=4), r=[])
            for g in range(2):
                pb = nb_(0, 4)
                for kc in range(8):
                    K.mm(pb[:], hT[:, kc, :], wB[:, kc, g * 512:(g + 1) * 512], start=(kc == 0), stop=(kc == 7),
                         r=[hT, ("wB", g)])
                K.cp("act", qs[p][g][:], pb[:])
            yield
            for g in (2, 3):
                pb = nb_(0, 4)
                for kc in range(8):
                    K.mm(pb[:], hT[:, kc, :], wB[:, kc, g * 512:(g + 1) * 512], start=(kc == 0), stop=(kc == 7),
                         r=[hT, ("wB", g)])
                if g == 2:
                    K.cp("act", vb[p][:], pb[:])
                else:
                    K.act(sgb[p][:], pb[:], AF.Silu)
            for g in range(2):
                q_ = qs[p][g]
                xv = q_[:].rearrange("p (h d two) -> p h d two", h=4, two=2)
                xe = xv[:, :, :, 0]
                xo = xv[:, :, :, 1]
                co = c_[:, 2 * g, :].rearrange("p (h d) -> p h d", h=4)
                si = c_[:, 2 * g + 1, :].rearrange("p (h d) -> p h d", h=4)
                t1, t2, t3, t4 = [r_[:].rearrange("p (h d) -> p h d", h=4) for r_ in rt]
                ov = qr[p][g][:].rearrange("p (h d two) -> p h d two", h=4, two=2)
                K.tt("dve", t1, xe, co, ALU.mult)
                K.tt("pool", t2, xo, si, ALU.mult)
                K.tt("dve", t3, xe, si, ALU.mult)
                K.tt("pool", t4, xo, co, ALU.mult)
                K.tt("dve", ov[:, :, :, 0], t1, t2, ALU.subtract, w=[("qrB", p, g, 0)], r=[rt[0], rt[1]])
                K.tt("pool", ov[:, :, :, 1], t3, t4, ALU.add, w=[("qrB", p, g, 1)], r=[rt[2], rt[3]])
            yield
            for g in range(2):
                pbt = nb_(0, 4)
                pv = pbt[:].bitcast(BF16)[:, 0:512].rearrange("p (h t) -> p h t", h=4)
                for h in range(4):
                    K.tr(pv[:, h, :], qr[p][g][:, h * 128:(h + 1) * 128], identb[:], r=[("qrB", p, g, 0), ("qrB", p, g, 1), identb])
                K.cp("act", (QTb if g == 0 else KTb)[p][:], pv)
            yield
            pS_ = banks[4]
            for h in range(4):
                K.mm(pS_[:, h * 128:(h + 1) * 128], KTb[p][:, h, :], QTb[p][:, h, :])
            K.tt("dve", Sm[p][:], pS_[:].rearrange("p (h t) -> p h t", h=4), rm[:], ALU.mult)
            pkv = banks[6]
            for h in range(4):
                K.mm(pkv[:, h * 128:(h + 1) * 128], qr[p][1][:, h * 128:(h + 1) * 128], vb[p][:, h * 128:(h + 1) * 128],
                     r=[("qrB", p, 1, 0), ("qrB", p, 1, 1), vb[p]])
            K.cp("act", kvs[p][:], pkv[:].rearrange("p (h e) -> p h e", h=4))
            yield

        def seq(t):
            p = t % 2
            py = banks[5]
            for h in range(4):
                K.mm(py[:, h * 128:(h + 1) * 128], Sm[p][:, h, :], vb[p][:, h * 128:(h + 1) * 128], start=True, stop=False)
                K.mm(py[:, h * 128:(h + 1) * 128], QTb[p][:, h, :], Rg[:, h, :], start=False, stop=True)
            for h in range(4):
                K.op("dve", lambda e, h=h: e.bn_stats(out=st[:, h, :], in_=py[:, h * 128:(h + 1) * 128]), [py], [("stB", h)])
                K.op("dve", lambda e, h=h: e.bn_aggr(out=mv[:, h, :], in_=st[:, h, :]), [("stB", h)], [("mvB", h)])
            mvk = [("mvB", h) for h in range(4)]
            K.act(rstd[:], mv[:, :, 1], AF.Sqrt, bias=epsc[:, 0:1], scale=1.0, r=mvk + [epsc])
            K.op("dve", lambda e: e.reciprocal(out=rstd[:], in_=rstd[:]), [rstd], [rstd])
            for h in range(4):
                K.ts("dve", yn[:, h * 128:(h + 1) * 128], py[:, h * 128:(h + 1) * 128], mv[:, h, 0:1], rstd[:, h:h + 1],
                     ALU.subtract, ALU.mult, r=[py, rstd] + mvk, w=[("ynB", h)])
            K.tt("pool", yb[:], yn[:], sgb[p][:], ALU.mult, r=[("ynB", h) for h in range(4)] + [sgb[p]])
            for h in range(4):
                K.ts("pool", R[:, h, :], R[:, h, :], rs[:, h:h + 1], None, ALU.mult, r=[("RB", h), rs], w=[("RB", h)])
                K.stt(R[:, h, :], kvs[p][:, h, :], rs[:, 4 + h:5 + h], R[:, h, :], ALU.mult, ALU.add,
                      r=[kvs[p], rs, ("RB", h)], w=[("RB", h)])
                K.act(Rg[:, h, :], R[:, h, :], AF.Identity, scale=float(cst_gamma(h)), r=[("RB", h)], w=[Rg])
            yield
            pbt = banks[7]
            pv = pbt[:].bitcast(BF16)[:, 0:512].rearrange("p (c t) -> p c t", c=4)
            for c in range(4):
                K.tr(pv[:, c, :], yb[:, c * 128:(c + 1) * 128], identb[:])
            K.cp("act", yTs[p][:], pv)
            K.dma("sp", yT_d[1, :, :, t * 128:(t + 1) * 128], yTs[p][:], w=[("yT", 1, t // 4)], semkey="yTs")
            yield

        def run_all(gens):
            gens = [g for g in gens if g is not None]
            while gens:
                for g in list(gens):
                    try:
                        next(g)
                    except StopIteration:
                        gens.remove(g)

        run_all([prep(0)])
        for t in range(NT):
            run_all([seq(t), prep(t + 1) if t + 1 < NT else None])


def cst_gamma(h):
    return 1.0 - 2.0 ** (-5.0 - h)


def phase_c(nc, K, sbt, banks, nb_, l, S, cst, Wb_in, hT_d, yT_d, vf_d, ident, identb, epsc, cols, col, colkeys,
            w2_d, a2_d, g2_d, v2_d, ln_d, dbg_d):
    NG = S // 512
    RW = BF16
    ncc = 15 if l >= 1 else 14
    with ExitStack() as es:
        wC = sbt(es, "wC", [128, 8, 1920], BF16)
        w2b = sbt(es, "w2b", [64, 512], BF16)
        a2b = sbt(es, "a2b", [128, 512], BF16)
        g2b = sbt(es, "g2b", [128, 512], BF16)
        v2b = sbt(es, "v2b", [32, 512], BF16)
        lnp = sbt(es, "lnp", [128, 2, 256])
        am = sbt(es, "amC", [128, 2, 2, 128])
        xm = sbt(es, "xmC", [128, 4, 64])
        rst = sbt(es, "rstC", [128, 512])
        bo = sbt(es, "boC", [128, 128], BF16)
        hg = sbt(es, "hgC", [128, 8, 512], BF16)
        zxc = [sbt(es, f"zxC{i}", [128, 513]) for i in range(2)]
        lastc = sbt(es, "lastcC", [128, 16])
        zl = sbt(es, "zlC", [128, 15, 512])
        tmp = [sbt(es, f"tmpC{i}", [128, 512]) for i in range(3)]
        tw = sbt(es, "twC", [64, 512], BF16)
        al = sbt(es, "alC", [128, 512], BF16)
        sgl = sbt(es, "sglC", [128, 512], BF16)
        vlr = sbt(es, "vlrC", [32, 512], BF16)
        sigw = sbt(es, "sigwC", [128, 512])
        iclr = sbt(es, "iclrC", [128, 512])
        cl = sbt(es, "clC", [128, 512])
        Pinc = sbt(es, "PincC", [128, 512])
        Pexc = sbt(es, "PexcC", [128, 512])
        Pinv = sbt(es, "PinvC", [128, 512])
        kkr = sbt(es, "kkrC", [128, 512])
        sqb = sbt(es, "sqbC", [128, 512], BF16)
        kmod = sbt(es, "kmodC", [128, 512])
        vfp = sbt(es, "vfpC", [128, 4, 512])
        vbf = sbt(es, "vbfC", [128, 4, 512], BF16)
        AR = sbt(es, "ARC", [128, 4, 8, 2, 64], RW)
        BK = sbt(es, "BKC", [128, 4, 8, 2, 64], RW)
        prk = sbt(es, "prkC", [128, 4, 512], BF16)
        pend = sbt(es, "pendC", [128, 4, 8])
        H = sbt(es, "HC", [128, 4, 64])
        Hb = sbt(es, "HbC", [128, 4, 64], RW)
        Amat = [sbt(es, f"AmatC{i}", [128, 4, 2, 128], RW) for i in range(3)]
        Xm = [[sbt(es, f"XmC{i}_{j}", [128, 2, 4, 64], RW) for j in range(2)] for i in range(3)]
        Wt = [sbt(es, f"WtC{i}", [128, 4, 64], RW) for i in range(2)]
        Vc = [sbt(es, f"VcC{i}", [128, 4, 64], RW) for i in range(3)]
        Vcf = [sbt(es, f"VcfC{i}", [128, 4, 64]) for i in range(3)]
        BKt = [sbt(es, f"BKtC{i}", [128, 4, 2, 64], RW) for i in range(3)]
        Tt = [sbt(es, f"TtC{i}", [128, 4, 64], RW) for i in range(3)]
        gt = [sbt(es, f"gtC{i}", [128, 4, 64]) for i in range(3)]
        bs = [sbt(es, f"bsC{i}", [128, 4]) for i in range(3)]
        i4 = sbt(es, "i4C", [128, 4, 64])
        st = sbt(es, "stC", [128, 4, 6])
        mv = sbt(es, "mvC", [128, 4, 2])
        rstd = sbt(es, "rstdC", [128, 4])
        yn = sbt(es, "ynC", [128, 4, 64])
        yc = sbt(es, "ycC", [128, 4, 64], BF16)
        ycT = [sbt(es, f"ycTC{i}", [128, 4, 512], BF16) for i in range(2)]

        nwc = 1792 + (32 if l >= 1 else 0)
        for g in range(0, 1792, 512):
            gw = min(512, 1792 - g)
            K.dma("sp", wC[:, :, g:g + gw], Wb_in[l, :, :, 3584 + g:3584 + g + gw],
                  r=[("Wb_in", l, (3584 + g) // 512)], w=[("wC", g)], semkey="wload")
        wCk = [("wC", g) for g in range(0, 1792, 512)]
        if l >= 1:
            K.dma("sp", wC[:, :, 1792:1824], Wb_in[l, :, :, NIN:NIN + 32], r=[("Wb_in", l, "v")], w=[("wC", "v")], semkey="wload")
            wCk.append(("wC", "v"))
            K.dma("pool", v2b[:], v2_d[l - 1], r=[], semkey="cloadp")
        K.dma("pool", w2b[:], w2_d[l], r=[], semkey="cloadp")
        K.dma("pool", a2b[64:128, :], a2_d[l], r=[], semkey="cloadp")
        K.dma("pool", g2b[:], g2_d[l], r=[], semkey="cloadp")
        K.dma("sp", lnp[:], ln_d[l].rearrange("p (a b) -> p a b", a=2), r=[], semkey="cload")
        K.dma("sp", am[:], cst["cAM"].rearrange("p (a j t) -> p a j t", a=2, j=2), r=[], semkey="cload")
        K.dma("sp", xm[:], cst["cXM"].rearrange("p (a t) -> p a t", a=4), r=[], semkey="cload")
        K.dma("sp", rst[:], cst["cRST"][:, :], r=[], semkey="cload")
        K.dma("pool", bo[:], cst["cBO"][:, :], r=[], semkey="cloadp")
        K.memset("dve", lastc[:], 0.0, w=[("lastc", cc) for cc in range(16)])
        K.dma("sp", i4[:], cst["cI4"].rearrange("p (a t) -> p a t", a=4), r=[], semkey="cload")
        K.memset("dve", H[:], 0.0)
        K.memset("pool", Hb[:], 0.0)
        lng = lnp[:, 0, :].rearrange("p (a v) -> p a v", a=4)
        lnb = lnp[:, 1, :].rearrange("p (a v) -> p a v", a=4)

        for tg in range(NG):
            tok = slice(tg * 512, (tg + 1) * 512)
            K.dma("sp", hg[:], hT_d[:, :, tok], r=[("hT", tg)])
            for cc in range(ncc):
                pb = nb_(0, 4)
                np_ = 128 if cc < 14 else 32
                zc_ = zxc[cc % 2]
                if cc < 14:
                    for kc in range(8):
                        K.mm(pb[:], wC[:, kc, cc * 128:(cc + 1) * 128], hg[:, kc, :], start=(kc == 0), stop=(kc == 7),
                             r=wCk + [hg])
                else:
                    for kc in range(8):
                        K.mm(pb[0:32, :], wC[:, kc, 1792:1824], hg[:, kc, :], start=(kc == 0), stop=(kc == 7), r=wCk + [hg])
                K.cp("act", zc_[0:np_, 1:513], pb[0:np_, :])
                K.cp("pool", zc_[0:np_, 0:1], lastc[0:np_, cc:cc + 1], r=[("lastc", cc)], w=[zc_])
                mu = col(l, "mu", cc) if cc < 14 else col(l, "vmu", 0)
                tp = tmp[cc % 2]
                K.tt("pool" if cc % 2 else "dve", tp[0:np_, :], zc_[0:np_, 0:512], zc_[0:np_, 1:513], ALU.subtract)
                K.stt(zl[0:np_, cc, :], tp[0:np_, :], mu[0:np_, :], zc_[0:np_, 1:513], ALU.mult, ALU.add,
                      r=[tp, zc_] + colkeys, w=[("zl", cc)])
                K.cp("pool", lastc[0:np_, cc:cc + 1], zc_[0:np_, 512:513], w=[("lastc", cc)])
            zlk = [("zl", cc) for cc in range(ncc)]
            K.act(tw[:], zl[0:64, 12, :], AF.Tanh, r=[("zl", 12)])
            K.cp("dve", al[64:128, :], zl[64:128, 12, :], r=[("zl", 12)])
            K.act(sgl[:], zl[:, 13, :], AF.Sigmoid, r=[("zl", 13)])
            if l >= 1:
                K.cp("dve", vlr[:], zl[0:32, 14, :], r=[("zl", 14)])
                K.dma("sp", vfp[:], vf_d[:, :, tok], r=[("vf", tg)])
            for pc in range(4):
                rT = zl[:, pc, :]
                kT = zl[:, 4 + pc, :]
                vT = zl[:, 8 + pc, :]
                rk_ = [("zl", pc)]
                kk_ = [("zl", 4 + pc)]
                vk_ = [("zl", 8 + pc)]
                pw = nb_(0, 4)
                K.mm(pw[:], w2b[0:64, pc * 128:(pc + 1) * 128], tw[0:64, :])
                K.act(sigw[:], pw[:], AF.Sigmoid, bias=col(l, "w0", pc), r=[pw] + colkeys)
                pa = nb_(0, 4)
                K.mm(pa[:], a2b[64:128, pc * 128:(pc + 1) * 128], al[64:128, :])
                K.act(iclr[:], pa[:], AF.Sigmoid, bias=col(l, "a0", pc), r=[pa] + colkeys)
                K.op("dve", lambda e: e.tensor_tensor_scan(out=cl[:], data0=rst[:], data1=sigw[:], initial=0.0, op0=ALU.mult, op1=ALU.add),
                     [rst, sigw], [cl])
                K.act(Pinc[:], cl[:], AF.Exp, scale=-C0)
                K.act(Pinv[:], cl[:], AF.Exp, scale=C0)
                K.tt("pool", tmp[2][:], cl[:], sigw[:], ALU.subtract)
                K.act(Pexc[:], tmp[2][:], AF.Exp, scale=-C0)
                K.cp("dve", pend[:, pc, :], Pinc[:].rearrange("p (c t) -> p c t", c=8)[:, :, 63])
                K.ts("dve", kkr[:], kT, col(l, "kk", pc), None, ALU.mult, r=kk_ + colkeys)
                K.act(sqb[:], kkr[:], AF.Square)
                pss = nb_(0, 4)
                K.mm(pss[:], bo[:], sqb[:])
                K.act(tmp[0][:], pss[:], AF.Sqrt)
                K.ts("dve", tmp[0][:], tmp[0][:], 1e-12, None, ALU.max)
                K.op("dve", lambda e: e.reciprocal(out=tmp[0][:], in_=tmp[0][:]), [tmp[0]], [tmp[0]])
                K.tt("dve", kkr[:], kkr[:], tmp[0][:], ALU.mult)
                ARv = AR[:, pc].rearrange("p c j t -> p j c t")
                BKv = BK[:, pc].rearrange("p c j t -> p j c t")
                c3 = lambda ap: ap.rearrange("p (c t) -> p c t", c=8)
                K.stt(ARv[:, 0], c3(kkr[:]), -1.0, c3(Pexc[:]), ALU.mult, ALU.mult, r=[kkr, Pexc], w=[("AR", pc, 0)])
                K.tt("pool", tmp[1][:], kkr[:], iclr[:], ALU.mult)
                K.tt("dve", BKv[:, 0], c3(tmp[1][:]), c3(Pinv[:]), ALU.mult, r=[tmp[1], Pinv], w=[("BK", pc, 0)])
                K.ts("dve", tmp[2][:], iclr[:], 1.0, col(l, "ka", pc), ALU.subtract, ALU.mult, r=[iclr] + colkeys)
                K.stt(kmod[:], tmp[2][:], 1.0, kT, ALU.add, ALU.mult, r=[tmp[2]] + kk_)
                K.tt("dve", BKv[:, 1], c3(kmod[:]), c3(Pinv[:]), ALU.mult, r=[kmod, Pinv], w=[("BK", pc, 1)])
                K.tt("pool", ARv[:, 1], c3(rT), c3(Pinc[:]), ALU.mult, r=rk_ + [Pinc], w=[("AR", pc, 1)])
                K.stt(prk[:, pc, :], rT, col(l, "rk", pc), kmod[:], ALU.mult, ALU.mult, r=rk_ + [kmod] + colkeys, w=[("prk", pc)])
                if l == 0:
                    pass
                else:
                    pv_ = nb_(0, 4)
                    K.mm(pv_[:], v2b[0:32, pc * 128:(pc + 1) * 128], vlr[0:32, :])
                    K.act(tmp[0][:], pv_[:], AF.Sigmoid, bias=col(l, "v0", pc), r=[pv_] + colkeys)
                    K.tt("dve", tmp[1][:], vfp[:, pc, :], vT, ALU.subtract, r=[vfp] + vk_)
                    K.tt("dve", tmp[1][:], tmp[1][:], tmp[0][:], ALU.mult)
                    K.tt("dve", vT, vT, tmp[1][:], ALU.add, r=vk_ + [tmp[1]], w=vk_)
            K.cp("pool", vbf[:], zl[:, 8:12, :], r=[("zl", 8 + i) for i in range(4)])
            if l == 0:
                K.dma("sp", vf_d[:, :, tok], zl[:, 8:12, :], r=[("zl", 8 + i) for i in range(4)], w=[("vf", tg)], semkey="vfst")
            ARk = [("AR", pc, j) for pc in range(4) for j in range(2)]
            BKk = [("BK", pc, j) for pc in range(4) for j in range(2)]
            prkk = [("prk", pc) for pc in range(4)]
            vks = [("zl", 8 + i) for i in range(4)]
            yT_ = ycT[tg % 2]
            def hs(hh):
                return slice(hh * 64, (hh + 1) * 64)

            def prep(c, bA, bB):
                sl = c % 3
                ct = slice(c * 64, (c + 1) * 64)
                Vc_, Vcf_, BKt_, Amat_, Tt_, gt_, bs_ = Vc[sl], Vcf[sl], BKt[sl], Amat[sl], Tt[sl], gt[sl], bs[sl]
                X0_, X1_ = Xm[sl]
                pVv = bA[:].bitcast(BF16)[:, 0:256].rearrange("p (a v) -> p a v", a=4)
                for hh in range(2):
                    for pc in range(4):
                        K.tr(pVv[hs(hh), pc, :], vbf[hs(hh), pc, ct], identb[hs(hh), hs(hh)])
                K.cp("act", Vc_[:], pVv)
                K.cp("dve", Vcf_[:], pVv)
                pBv = bB[:].bitcast(BF16)[:, 0:512].rearrange("p (a j k) -> p a j k", a=4, j=2)
                for hh in range(2):
                    for pc in range(4):
                        for j in range(2):
                            K.tr(pBv[hs(hh), pc, j, :], BK[hs(hh), pc, c, j, :], identb[hs(hh), hs(hh)], r=BKk + [identb])
                K.cp("act", BKt_[:], pBv)
                yield
                for pb2, pA in enumerate((bA, bB)):
                    pAv = pA[:].rearrange("p (a j t) -> p a j t", a=2, j=2)
                    for hh in range(2):
                        for a_ in range(2):
                            pc = pb2 * 2 + a_
                            rhs = AR[hs(hh), pc, c].rearrange("p j t -> p (j t)")
                            for j in range(2):
                                K.mm(pAv[hs(hh), a_, j, :], BK[hs(hh), pc, c, j, :], rhs, r=ARk + BKk)
                    K.tt("dve", Amat_[:, pb2 * 2:pb2 * 2 + 2], pAv, am[:], ALU.mult, w=[("Amat", sl, pb2)])
                Ak = [("Amat", sl, 0), ("Amat", sl, 1)]
                yield
                for hh in range(2):
                    for pc in range(4):
                        K.mm(bA[hs(hh), pc * 64:(pc + 1) * 64], AR[hs(hh), pc, c, 0, :], BK[hs(hh), pc, c, 0, :], r=ARk + BKk)
                K.tt("dve", X0_[:, 0], bA[:, 0:256].rearrange("p (a s) -> p a s", a=4), xm[:], ALU.mult, w=[("X", sl, 0)])
                K.cp("act", X0_[:, 1], Amat_[:, :, 0, 0:64], r=Ak, w=[("Y", sl, 0)])
                K.tt("pool", Tt_[:], Amat_[:, :, 0, 0:64], i4[:], ALU.add, r=Ak + [i4])
                for hh in range(2):
                    for pc in range(4):
                        K.mm(bB[hs(hh), pc * 64:(pc + 1) * 64], sgl[:, ct], g2b[:, (2 * pc + hh) * 64:(2 * pc + hh + 1) * 64])
                for hh in range(2):
                    for pc in range(4):
                        K.mm(bB[hs(hh), 256 + pc:256 + pc + 1], prk[hs(hh), pc, ct], bo[hs(hh), hh * 64:hh * 64 + 1], r=prkk + [bo])
                K.cp("act", gt_[:], bB[:, 0:256].rearrange("p (a v) -> p a v", a=4))
                K.cp("dve", bs_[:], bB[:, 256:260])
                yield
                Xs = [X0_, X1_]
                for lev in range(1, 6):
                    Xp = Xs[(lev - 1) % 2]
                    Xn = Xs[lev % 2]
                    xk = [("X", sl, (lev - 1) % 2), ("Y", sl, (lev - 1) % 2)]
                    for hh in range(2):
                        for pc in range(4):
                            K.mm(bA[hs(hh), pc * 64:(pc + 1) * 64], Xp[hs(hh), 1, pc, :], Xp[hs(hh), 0, pc, :], r=xk)
                            if lev < 5:
                                K.mm(bA[hs(hh), 256 + pc * 64:256 + (pc + 1) * 64], Xp[hs(hh), 0, pc, :], Xp[hs(hh), 1, pc, :], r=xk)
                    if lev < 5:
                        K.cp("act", Xn[:], bA[:].rearrange("p (j a s) -> p j a s", j=2, a=4),
                             w=[("X", sl, lev % 2), ("Y", sl, lev % 2)])
                    else:
                        K.cp("act", Xn[:, 0], bA[:, 0:256].rearrange("p (a s) -> p a s", a=4), w=[("X", sl, lev % 2)])
                    yield
                    for hh in range(2):
                        for pc in range(4):
                            K.mm(bB[hs(hh), pc * 64:(pc + 1) * 64], Xn[hs(hh), 0, pc, :], Tt_[hs(hh), pc, :], r=[("X", sl, lev % 2), Tt_])
                    K.tt("dve", Tt_[:], Tt_[:], bB[:, 0:256].rearrange("p (a t) -> p a t", a=4), ALU.add)
                yield

            def seq(c, bA, bB):
                sl = c % 3
                ct = slice(c * 64, (c + 1) * 64)
                Vc_, Vcf_, BKt_, Amat_, Tt_, gt_, bs_ = Vc[sl], Vcf[sl], BKt[sl], Amat[sl], Tt[sl], gt[sl], bs[sl]
                Ak = [("Amat", sl, 0), ("Amat", sl, 1)]
                for hh in range(2):
                    for pc in range(4):
                        o_ = bA[hs(hh), pc * 64:(pc + 1) * 64]
                        K.mm(o_, AR[hs(hh), pc, c, 0, :], Hb[hs(hh), pc, :], start=True, stop=False, r=ARk + [Hb])
                        K.mm(o_, Amat_[hs(hh), pc, 1, 0:64], Vc_[hs(hh), pc, :], start=False, stop=True, r=Ak + [Vc_])
                K.cp("dve", Wt[0][:], bA[:, 0:256].rearrange("p (a v) -> p a v", a=4))
                yield
                for hh in range(2):
                    for pc in range(4):
                        K.mm(bB[hs(hh), pc * 64:(pc + 1) * 64], Tt_[hs(hh), pc, :], Wt[0][hs(hh), pc, :])
                U = Wt[1]
                K.cp("act", U[:], bB[:, 0:256].rearrange("p (a v) -> p a v", a=4))
                yield
                pO = bA
                pH = bB
                for hh in range(2):
                    for pc in range(4):
                        o_ = pO[hs(hh), pc * 64:(pc + 1) * 64]
                        K.mm(o_, AR[hs(hh), pc, c, 1, :], Hb[hs(hh), pc, :], start=True, stop=False, r=ARk + [Hb])
                        K.mm(o_, Amat_[hs(hh), pc, 0, 64:128], U[hs(hh), pc, :], start=False, stop=False, r=Ak + [U])
                        K.mm(o_, Amat_[hs(hh), pc, 1, 64:128], Vc_[hs(hh), pc, :], start=False, stop=True, r=Ak + [Vc_])
                for hh in range(2):
                    for pc in range(4):
                        o_ = pH[hs(hh), pc * 64:(pc + 1) * 64]
                        K.mm(o_, BKt_[hs(hh), pc, 0, :], U[hs(hh), pc, :], start=True, stop=False)
                        K.mm(o_, BKt_[hs(hh), pc, 1, :], Vc_[hs(hh), pc, :], start=False, stop=True)
                K.tt("dve", H[:], H[:], pH[:, 0:256].rearrange("p (a v) -> p a v", a=4), ALU.add)
                K.tt("pool", H[:], H[:], pend[:, :, c:c + 1].to_broadcast([128, 4, 64]), ALU.mult)
                K.cp("act", Hb[:], H[:])
                for pc in range(4):
                    K.op("dve", lambda e, pc=pc: e.bn_stats(out=st[:, pc, :], in_=pO[:, pc * 64:(pc + 1) * 64]), [pO], [("stC", pc)])
                    K.op("dve", lambda e, pc=pc: e.bn_aggr(out=mv[:, pc, :], in_=st[:, pc, :]), [("stC", pc)], [("mvC", pc)])
                mvk = [("mvC", pc) for pc in range(4)]
                K.act(rstd[:], mv[:, :, 1], AF.Sqrt, bias=epsc[:, 1:2], scale=1.0, r=mvk + [epsc])
                K.op("dve", lambda e: e.reciprocal(out=rstd[:], in_=rstd[:]), [rstd], [rstd])
                for pc in range(4):
                    K.ts("dve", yn[:, pc, :], pO[:, pc * 64:(pc + 1) * 64], mv[:, pc, 0:1], rstd[:, pc:pc + 1], ALU.subtract, ALU.mult,
                         r=[pO, rstd] + mvk, w=[("ynC", pc)])
                ynk = [("ynC", pc) for pc in range(4)]
                K.tt("dve", yn[:], yn[:], lng, ALU.mult, r=ynk + [lnp], w=[yn])
                K.tt("pool", yn[:], yn[:], lnb, ALU.add, r=[yn, lnp], w=[yn])
                K.tt("pool", Vcf_[:], Vcf_[:], bs_[:].rearrange("p (a o) -> p a o", o=1).to_broadcast([128, 4, 64]), ALU.mult)
                K.tt("pool", yn[:], yn[:], Vcf_[:], ALU.add)
                K.tt("dve", yc[:], yn[:], gt_[:], ALU.mult)
                yield
                pYv = bB[:].bitcast(BF16)[:, 0:256].rearrange("p (a t) -> p a t", a=4)
                for hh in range(2):
                    for pc in range(4):
                        K.tr(pYv[hs(hh), pc, :], yc[hs(hh), pc, :], identb[hs(hh), hs(hh)])
                K.cp("act", yT_[:, :, ct], pYv)
                yield

            free_sets = [(banks[4], banks[5]), (banks[0], banks[1])]
            pending = list(range(8))
            active = []
            prep_done = set()
            seq_done = -1
            seq_gen = None
            seq_c = 0
            while seq_c < 8:
                while pending and len(active) < 2 and free_sets and pending[0] - 3 <= seq_done:
                    cnew = pending.pop(0)
                    bset = free_sets.pop(0)
                    active.append([cnew, prep(cnew, *bset), bset])
                if seq_gen is None and seq_c in prep_done:
                    seq_gen = seq(seq_c, banks[6], banks[7])
                if seq_gen is not None:
                    try:
                        next(seq_gen)
                    except StopIteration:
                        seq_gen = None
                        seq_done = seq_c
                        seq_c += 1
                for a in list(active):
                    try:
                        next(a[1])
                    except StopIteration:
                        prep_done.add(a[0])
                        free_sets.append(a[2])
                        active.remove(a)
            K.dma("sp", yT_d[2, :, :, tok], yT_[:], w=[("yT", 2, tg)], semkey="yTs")


_CACHE = {}


def make_in_maps(inp, S, depth, n_cores):
    consts = host_consts(S)
    colsarr = np.stack([pack_cols(inp, l) for l in range(depth)])
    lnarr = np.stack([pack_ln(inp, l) for l in range(depth)])
    maps = []
    f = lambda a: np.ascontiguousarray(np.asarray(a, np.float32))
    shared = {
        "w_in": f(inp["w_in"]), "c_w2": f(inp["c_w2"]), "c_a2": f(inp["c_a2"]), "c_g2": f(inp["c_g2"]),
        "c_vres_down": f(inp["c_vres_down"]), "c_v2": f(inp["c_v2"]), "w_branch": f(inp["w_branch"]),
        "w_out": f(inp["w_out"]), "w_gate_up": f(inp["w_gate_up"]), "w_down": f(inp["w_down"]),
        "w_ple_gate": f(inp["w_ple_gate"]), "w_ple_proj": f(inp["w_ple_proj"]), "cols": colsarr, "lnp": lnarr,
    }
    shared.update(consts)
    x = np.asarray(inp["x"], np.float32)
    p = np.asarray(inp["p"], np.float32)
    for b in range(n_cores):
        m = dict(shared)
        m["x"] = np.ascontiguousarray(x[b])
        m["p"] = np.ascontiguousarray(p[:, b])
        maps.append(m)
    return maps


def kernel(**inputs):
    x = np.asarray(inputs["x"])
    B, S, _ = x.shape
    depth = np.asarray(inputs["w_in"]).shape[0]
    key = (S, depth)
    if key not in _CACHE:
        _CACHE[key] = build(S, depth)[0]
    nc = _CACHE[key]
    maps = make_in_maps(inputs, S, depth, B)
    res = run_bass_kernel_spmd(nc, maps, core_ids=list(range(B)))
    return np.stack([np.asarray(r["out"], np.float32) for r in res.results], axis=0)
```
